# Optimizing a Trainium2 kernel written in Bass

```python
import math
import jax
import jax.numpy as jnp
from jax import lax
import numpy as np

D_MODEL = 1024
BATCH = 8
SEQ = 4096
DEPTH = 4
DEC_BATCH = 8
DEC_SEQ = 32
PAST_LEN = 2048

F32 = jnp.float32
CHUNK = 64
Q_BLOCK = 128
N_MIXERS = 4
DEEPNORM_ALPHA = (2.0 * DEPTH) ** 0.25
DEEPNORM_BETA = (8.0 * DEPTH) ** -0.25
LN_EPS = 1e-5
NORM_EPS = 1e-6
NEG_INF = -1e30

H_G = 8
DK_G = D_MODEL // H_G
DV_G = D_MODEL // H_G
CONV_W = 4
H_F = 16
DH_F = D_MODEL // H_F
H_D = 8
DH_D = D_MODEL // (2 * H_D)
DIFF_LAYER = 2
LAMBDA_INIT = 0.8 - 0.6 * math.exp(-0.3 * DIFF_LAYER)
N_BUCKETS = 32
MAX_DISTANCE = 128
H_R = 4
DK_R = D_MODEL // H_R
DV_R = 2 * D_MODEL // H_R
ROPE_BASE = 10000.0

kernel_name = 'hybrid_streaming_encoder_step'


def _layer_norm(x, g, b):
    xf = x.astype(F32)
    mu = jnp.mean(xf, axis=-1, keepdims=True)
    var = jnp.mean(jnp.square(xf - mu), axis=-1, keepdims=True)
    return ((xf - mu) * lax.rsqrt(var + LN_EPS) * g.astype(F32) + b.astype(F32)).astype(x.dtype)


def _rms_norm(x, w):
    xf = x.astype(F32)
    return (xf * lax.rsqrt(jnp.mean(xf * xf, axis=-1, keepdims=True) + NORM_EPS) * w.astype(F32)).astype(x.dtype)


def _l2_normalize(x):
    xf = x.astype(F32)
    return (xf * lax.rsqrt(jnp.sum(xf * xf, axis=-1, keepdims=True) + NORM_EPS)).astype(x.dtype)


def _modulate(x, c, w, b):
    shift, scale, gate = jnp.split(jax.nn.silu(c) @ w + b, 3, axis=-1)
    return x * (1.0 + scale[:, None]) + shift[:, None], 1.0 + gate[:, None]


def _post_norm(x, h, gate, g, b):
    return _layer_norm(DEEPNORM_ALPHA * x + gate * h, g, b)


def _to_chunks(a, L):
    B, T = a.shape[:2]
    a = a.astype(F32).reshape((B, T // L, L) + a.shape[2:])
    return jnp.swapaxes(jnp.swapaxes(a, 0, 1), 2, 3)


def _from_chunks(o):
    o = jnp.swapaxes(jnp.swapaxes(o, 2, 3), 0, 1)
    return o.reshape((o.shape[0], o.shape[1] * o.shape[2]) + o.shape[3:])


def _sweep_query_blocks(fn, start, *per_query):
    B, T = per_query[0].shape[:2]
    lq = min(Q_BLOCK, T)
    n = T // lq
    pos = (start + jnp.arange(T)).reshape(n, lq)
    blocks = tuple(jnp.swapaxes(a.reshape((B, n, lq) + a.shape[2:]), 0, 1) for a in per_query)
    out = lax.map(lambda xs: fn(xs[0], *xs[1:]), (pos,) + blocks)
    return jnp.swapaxes(out, 0, 1).reshape((B, T) + out.shape[3:])


def _t5_bucket(rel):
    nb = N_BUCKETS // 2
    max_exact = nb // 2
    ret = jnp.where(rel > 0, nb, 0)
    n = jnp.abs(rel)
    large = max_exact + (jnp.log(jnp.maximum(n, 1).astype(F32) / max_exact)
                         / math.log(MAX_DISTANCE / max_exact) * (nb - max_exact)).astype(jnp.int32)
    large = jnp.minimum(large, nb - 1)
    return ret + jnp.where(n < max_exact, n, large)


def _rotary(x, pos):
    d = x.shape[-1]
    inv = ROPE_BASE ** (-jnp.arange(0, d, 2, dtype=F32) / d)
    ang = pos.astype(F32)[:, None] * inv[None, :]
    cos = jnp.cos(ang)[None, :, None, :]
    sin = jnp.sin(ang)[None, :, None, :]
    xf = x.astype(F32)
    x1, x2 = xf[..., 0::2], xf[..., 1::2]
    return jnp.stack([x1 * cos - x2 * sin, x1 * sin + x2 * cos], axis=-1).reshape(x.shape).astype(x.dtype)


def _gated_delta_chunked(q, k, v, g, beta, s0):
    B, T, H, DK = q.shape
    DV = v.shape[-1]
    L = min(CHUNK, T)
    qc, kc, vc, gc, bc = (_to_chunks(a, L) for a in (q, k, v, g, beta))
    G = jnp.cumsum(gc, axis=-1)
    idx = jnp.arange(L)
    strict = idx[:, None] > idx[None, :]
    incl = idx[:, None] >= idx[None, :]
    dG = G[..., :, None] - G[..., None, :]
    dec_strict = jnp.exp(jnp.where(strict, dG, NEG_INF))
    dec_incl = jnp.exp(jnp.where(incl, dG, NEG_INF))
    kb = kc * bc[..., None]
    a_mat = jnp.einsum('nbhid,nbhjd->nbhij', kb, kc) * dec_strict + jnp.eye(L, dtype=F32)
    rhs = jnp.concatenate([vc * bc[..., None], kb * jnp.exp(G)[..., None]], axis=-1)
    sol = lax.linalg.triangular_solve(a_mat, rhs, left_side=True, lower=True, unit_diagonal=True)
    u_new, w_dec = sol[..., :DV], sol[..., DV:]
    qk = jnp.einsum('nbhid,nbhjd->nbhij', qc, kc) * dec_incl
    q_dec = qc * jnp.exp(G)[..., None]
    k_dec = kc * jnp.exp(G[..., -1:] - G)[..., None]
    g_tot = jnp.exp(G[..., -1])

    def step(s, xs):
        u_c, w_c, qk_c, qd_c, kd_c, gt_c = xs
        v_res = u_c - jnp.einsum('bhld,bhdv->bhlv', w_c, s)
        o = jnp.einsum('bhld,bhdv->bhlv', qd_c, s) + jnp.einsum('bhij,bhjv->bhiv', qk_c, v_res)
        s = s * gt_c[..., None, None] + jnp.einsum('bhld,bhlv->bhdv', kd_c, v_res)
        return s, o

    s_fin, o = lax.scan(step, s0.astype(F32), (u_new, w_dec, qk, q_dec, k_dec, g_tot))
    return _from_chunks(o), s_fin


def _retention_chunked(q, k, v, s0):
    B, T, H, DK = q.shape
    L = min(CHUNK, T)
    log_gamma = jnp.log1p(-jnp.power(2.0, -5.0 - jnp.arange(H, dtype=F32)))
    idx = jnp.arange(L, dtype=F32)
    rel = idx[:, None] - idx[None, :]
    intra = jnp.where(rel >= 0, jnp.exp(log_gamma[:, None, None] * jnp.maximum(rel, 0.0)), 0.0)
    q_dec = jnp.exp(log_gamma[:, None] * (idx + 1.0))
    k_dec = jnp.exp(log_gamma[:, None] * (L - 1.0 - idx))
    c_dec = jnp.exp(log_gamma * L)
    qc, kc, vc = (_to_chunks(a, L) for a in (q, k, v))

    def step(s, xs):
        qi, ki, vi = xs
        att = jnp.einsum('bhid,bhjd->bhij', qi, ki) * intra
        o = jnp.einsum('bhij,bhjv->bhiv', att, vi) + jnp.einsum('bhld,bhdv->bhlv', qi, s) * q_dec[:, :, None]
        s = s * c_dec[:, None, None] + jnp.einsum('bhld,bhlv->bhdv', ki * k_dec[:, :, None], vi)
        return s, o

    s_fin, o = lax.scan(step, s0.astype(F32), (qc, kc, vc))
    return _from_chunks(o), s_fin


def _gdn_mixer(u, conv_buf, s0, w_in, conv_w, a_log, dt_bias, norm_w, w_out):
    B, T, D = u.shape
    proj = u @ w_in
    qkv, b_logit, a_logit, z = jnp.split(proj, [3 * D, 3 * D + H_G, 3 * D + 2 * H_G], axis=-1)
    conv_in = jnp.concatenate([conv_buf.astype(qkv.dtype), qkv], axis=1)
    y = conv_w[0] * conv_in[:, 0:T]
    for j in range(1, CONV_W):
        y = y + conv_w[j] * conv_in[:, j:j + T]
    q, k, v = jnp.split(jax.nn.silu(y), 3, axis=-1)
    q = _l2_normalize(q.reshape(B, T, H_G, DK_G)) * DK_G ** -0.5
    k = _l2_normalize(k.reshape(B, T, H_G, DK_G))
    beta = jax.nn.sigmoid(b_logit.astype(F32))
    g = -jnp.exp(a_log.astype(F32)) * jax.nn.softplus(a_logit.astype(F32) + dt_bias.astype(F32))
    o, s = _gated_delta_chunked(q, k, v.reshape(B, T, H_G, DV_G), g, beta, s0)
    o = _rms_norm(o, norm_w).astype(u.dtype) * jax.nn.silu(z.reshape(B, T, H_G, DV_G))
    return o.reshape(B, T, D) @ w_out, conv_in[:, T:], s.astype(u.dtype)


def _fox_mixer(u, past_k, past_v, past_logf, w_in, b_f, w_out):
    B, T, D = u.shape
    proj = u @ w_in
    q, k, v, z, f_logit = jnp.split(proj, [D, 2 * D, 3 * D, 4 * D], axis=-1)
    q = q.reshape(B, T, H_F, DH_F)
    k = k.reshape(B, T, H_F, DH_F)
    v = v.reshape(B, T, H_F, DH_F)
    logf = jax.nn.log_sigmoid(f_logit.astype(F32) + b_f.astype(F32))
    if past_k is None:
        start, k_all, v_all, logf_all = 0, k, v, logf
    else:
        start = past_k.shape[1]
        k_all = jnp.concatenate([past_k.astype(k.dtype), k], axis=1)
        v_all = jnp.concatenate([past_v.astype(v.dtype), v], axis=1)
        logf_all = jnp.concatenate([past_logf.astype(F32), logf], axis=1)
    cum = jnp.cumsum(logf_all, axis=1)
    cum_k = jnp.swapaxes(cum, 1, 2)
    pos_k = jnp.arange(k_all.shape[1])

    def block(pos_q, qb, cqb):
        logits = jnp.einsum('bqhd,bshd->bhqs', qb, k_all, preferred_element_type=F32) * DH_F ** -0.5
        logits = logits + jnp.swapaxes(cqb, 1, 2)[..., None] - cum_k[:, :, None, :]
        logits = jnp.where(pos_k[None, :] <= pos_q[:, None], logits, NEG_INF)
        p = jax.nn.softmax(logits, axis=-1).astype(v_all.dtype)
        return jnp.einsum('bhqs,bshd->bqhd', p, v_all)

    o = _sweep_query_blocks(block, start, q, cum[:, start:])
    o = o * jax.nn.silu(z.reshape(B, T, H_F, DH_F))
    return o.reshape(B, T, D) @ w_out, k, v, logf


def _diff_mixer(u, past_k, past_v, w_in, lam_q1, lam_k1, lam_q2, lam_k2, subln_w, rel_table, w_out):
    B, T, D = u.shape
    proj = u @ w_in
    q, k, v, z = jnp.split(proj, 4, axis=-1)
    q = q.reshape(B, T, H_D, 2, DH_D)
    k = k.reshape(B, T, H_D, 2, DH_D)
    v = v.reshape(B, T, H_D, 2 * DH_D)
    if past_k is None:
        start, k_all, v_all = 0, k, v
    else:
        start = past_k.shape[1]
        k_all = jnp.concatenate([past_k.astype(k.dtype), k], axis=1)
        v_all = jnp.concatenate([past_v.astype(v.dtype), v], axis=1)
    pos_k = jnp.arange(k_all.shape[1])
    lam = (jnp.exp(jnp.sum(lam_q1.astype(F32) * lam_k1.astype(F32)))
           - jnp.exp(jnp.sum(lam_q2.astype(F32) * lam_k2.astype(F32))) + LAMBDA_INIT)

    def block(pos_q, qb):
        bias = jnp.moveaxis(rel_table[_t5_bucket(pos_k[None, :] - pos_q[:, None])], -1, 0).astype(F32)
        logits = jnp.einsum('bqhmd,bshmd->bhmqs', qb, k_all, preferred_element_type=F32) * DH_D ** -0.5
        logits = logits + bias[None, :, None]
        mask = (pos_k[None, :] // CHUNK) <= (pos_q[:, None] // CHUNK)
        p = jax.nn.softmax(jnp.where(mask, logits, NEG_INF), axis=-1)
        w = (p[:, :, 0] - lam * p[:, :, 1]).astype(v_all.dtype)
        return jnp.einsum('bhqs,bshe->bqhe', w, v_all)

    o = _sweep_query_blocks(block, start, q)
    o = _rms_norm(o, subln_w) * (1.0 - LAMBDA_INIT)
    o = o * jax.nn.silu(z.reshape(B, T, H_D, 2 * DH_D))
    return o.reshape(B, T, D) @ w_out, k, v


def _retention_mixer(u, s0, start, w_in, gn_w, w_out):
    B, T, D = u.shape
    proj = u @ w_in
    q, k, v, z = jnp.split(proj, [D, 2 * D, 4 * D], axis=-1)
    pos = start + jnp.arange(T)
    q = _rotary(q.reshape(B, T, H_R, DK_R), pos) * DK_R ** -0.5
    k = _rotary(k.reshape(B, T, H_R, DK_R), pos)
    o, s = _retention_chunked(q, k, v.reshape(B, T, H_R, DV_R), s0)
    mu = jnp.mean(o, axis=-1, keepdims=True)
    var = jnp.mean(jnp.square(o - mu), axis=-1, keepdims=True)
    o = (o - mu) * lax.rsqrt(var + LN_EPS) * gn_w.astype(F32).reshape(H_R, DV_R)
    o = o.astype(u.dtype) * jax.nn.silu(z.reshape(B, T, H_R, DV_R))
    return o.reshape(B, T, H_R * DV_R) @ w_out, s.astype(u.dtype)


def setup_inputs(seed: int = 0) -> dict:
    key = jax.random.key(seed)
    ks = iter(jax.random.split(key, 64))
    D = D_MODEL
    fan = D ** -0.5

    def nrm(shape, scale):
        return jax.random.normal(next(ks), shape, F32) * scale

    def unif(shape, lo, hi):
        return jax.random.uniform(next(ks), shape, F32, lo, hi)

    dt = jnp.exp(unif((H_G,), math.log(1e-3), math.log(1e-1)))
    return {
        'x_prompt': nrm((BATCH, SEQ, D), 1.0),
        'x_sample': nrm((DEC_BATCH, DEC_SEQ, D), 1.0),
        'c_prompt': nrm((BATCH, D), 1.0),
        'c_sample': nrm((DEC_BATCH, D), 1.0),
        'state_gdn': nrm((DEC_BATCH, H_G, DK_G, DV_G), 0.1),
        'state_gdn_conv': nrm((DEC_BATCH, CONV_W - 1, 3 * D), 1.0),
        'cache_fox_k': nrm((DEC_BATCH, PAST_LEN, H_F, DH_F), 1.0),
        'cache_fox_v': nrm((DEC_BATCH, PAST_LEN, H_F, DH_F), 1.0),
        'cache_fox_logf': jax.nn.log_sigmoid(nrm((DEC_BATCH, PAST_LEN, H_F), 1.0) + 3.0),
        'cache_diff_k': nrm((DEC_BATCH, PAST_LEN, H_D, 2, DH_D), 1.0),
        'cache_diff_v': nrm((DEC_BATCH, PAST_LEN, H_D, 2 * DH_D), 1.0),
        'state_ret': nrm((DEC_BATCH, H_R, DK_R, DV_R), 1.0),
        'ada_w': nrm((DEPTH, D, 3 * D), 0.3 * fan),
        'ada_b': nrm((DEPTH, 3 * D), 0.01),
        'ln_g': 1.0 + nrm((DEPTH, D), 0.02),
        'ln_b': nrm((DEPTH, D), 0.02),
        'gdn_w_in': nrm((D, 4 * D + 2 * H_G), fan),
        'gdn_conv_w': nrm((CONV_W, 3 * D), CONV_W ** -0.5),
        'gdn_a_log': jnp.log(unif((H_G,), 1.0, 16.0)),
        'gdn_dt_bias': dt + jnp.log(-jnp.expm1(-dt)),
        'gdn_norm_w': 1.0 + nrm((DV_G,), 0.02),
        'gdn_w_out': nrm((D, D), fan * DEEPNORM_BETA),
        'fox_w_in': nrm((D, 4 * D + H_F), fan),
        'fox_b_f': unif((H_F,), 1.0, 4.0),
        'fox_w_out': nrm((D, D), fan * DEEPNORM_BETA),
        'rel_bias_table': nrm((N_BUCKETS, H_D), 0.5),
        'diff_w_in': nrm((D, 4 * D), fan),
        'diff_lam_q1': nrm((DH_D,), 0.1),
        'diff_lam_k1': nrm((DH_D,), 0.1),
        'diff_lam_q2': nrm((DH_D,), 0.1),
        'diff_lam_k2': nrm((DH_D,), 0.1),
        'diff_subln_w': 1.0 + nrm((2 * DH_D,), 0.02),
        'diff_w_out': nrm((D, D), fan * DEEPNORM_BETA),
        'ret_w_in': nrm((D, 6 * D), fan),
        'ret_gn_w': 1.0 + nrm((2 * D,), 0.02),
        'ret_w_out': nrm((2 * D, D), (2 * D) ** -0.5 * DEEPNORM_BETA),
    }


def reference(x_prompt, x_sample, c_prompt, c_sample, state_gdn, state_gdn_conv, cache_fox_k, cache_fox_v,
              cache_fox_logf, cache_diff_k, cache_diff_v, state_ret, ada_w, ada_b, ln_g, ln_b,
              gdn_w_in, gdn_conv_w, gdn_a_log, gdn_dt_bias, gdn_norm_w, gdn_w_out,
              fox_w_in, fox_b_f, fox_w_out, rel_bias_table,
              diff_w_in, diff_lam_q1, diff_lam_k1, diff_lam_q2, diff_lam_k2, diff_subln_w, diff_w_out,
              ret_w_in, ret_gn_w, ret_w_out):
    xp, xs = x_prompt, x_sample
    bp = x_prompt.shape[0]
    for i in range(DEPTH):
        kind = i % N_MIXERS
        up, gate_p = _modulate(xp, c_prompt, ada_w[i], ada_b[i])
        us, gate_s = _modulate(xs, c_sample, ada_w[i], ada_b[i])
        if kind == 0:
            zero_buf = jnp.zeros((bp, CONV_W - 1, 3 * D_MODEL), up.dtype)
            zero_s = jnp.zeros((bp, H_G, DK_G, DV_G), F32)
            hp, gdn_conv_p, gdn_state_p = _gdn_mixer(up, zero_buf, zero_s, gdn_w_in, gdn_conv_w, gdn_a_log,
                                                     gdn_dt_bias, gdn_norm_w, gdn_w_out)
            hs, gdn_conv_s, gdn_state_s = _gdn_mixer(us, state_gdn_conv, state_gdn, gdn_w_in, gdn_conv_w, gdn_a_log,
                                                     gdn_dt_bias, gdn_norm_w, gdn_w_out)
        elif kind == 1:
            hp, fox_k_p, fox_v_p, fox_logf_p = _fox_mixer(up, None, None, None, fox_w_in, fox_b_f, fox_w_out)
            hs, fox_k_s, fox_v_s, fox_logf_s = _fox_mixer(us, cache_fox_k, cache_fox_v, cache_fox_logf,
                                                          fox_w_in, fox_b_f, fox_w_out)
        elif kind == 2:
            hp, diff_k_p, diff_v_p = _diff_mixer(up, None, None, diff_w_in, diff_lam_q1, diff_lam_k1, diff_lam_q2,
                                                 diff_lam_k2, diff_subln_w, rel_bias_table, diff_w_out)
            hs, diff_k_s, diff_v_s = _diff_mixer(us, cache_diff_k, cache_diff_v, diff_w_in, diff_lam_q1, diff_lam_k1,
                                                 diff_lam_q2, diff_lam_k2, diff_subln_w, rel_bias_table, diff_w_out)
        else:
            zero_r = jnp.zeros((bp, H_R, DK_R, DV_R), F32)
            hp, ret_state_p = _retention_mixer(up, zero_r, 0, ret_w_in, ret_gn_w, ret_w_out)
            hs, ret_state_s = _retention_mixer(us, state_ret, PAST_LEN, ret_w_in, ret_gn_w, ret_w_out)
        xp = _post_norm(xp, hp, gate_p, ln_g[i], ln_b[i])
        xs = _post_norm(xs, hs, gate_s, ln_g[i], ln_b[i])
    return (xp, xs, gdn_state_p, gdn_conv_p, fox_k_p, fox_v_p, fox_logf_p, diff_k_p, diff_v_p, ret_state_p,
            gdn_state_s, gdn_conv_s, fox_k_s, fox_v_s, fox_logf_s, diff_k_s, diff_v_s, ret_state_s)
```

```python
import math
from contextlib import ExitStack

import numpy as np
import ml_dtypes
import concourse.bass as bass
import concourse.mybir as mybir
from concourse.bass_utils import run_bass_kernel_spmd

F32 = mybir.dt.float32
BF16 = mybir.dt.bfloat16
AF = mybir.ActivationFunctionType
ALU = mybir.AluOpType
AX = mybir.AxisListType

D = 1024
DEPTH = 4
ALPHA = (2.0 * DEPTH) ** 0.25
LN_EPS = 1e-5
NORM_EPS = 1e-6
NCORES = 8


class Buf:
    __slots__ = ("w", "r", "x")

    def __init__(self, x=False):
        self.w = None
        self.r = {}
        self.x = x


class Prog:
    ENGS = ("pe", "act", "dve", "pool", "sp")

    def __init__(self, nc, es, ndma=28):
        self.nc = nc
        self.sem = {e: es.enter_context(nc.semaphore("s_" + e)) for e in self.ENGS}
        self.dsem = [es.enter_context(nc.semaphore("d%d" % i)) for i in range(ndma)]
        self.cnt = {e: 0 for e in self.ENGS}
        self.dcnt = [0] * ndma
        self.drr = 0
        self.drr_sw = 0
        self.NSW_BASE = ndma - 8
        self.seen = {e: {} for e in self.ENGS}
        self.q = {e: [] for e in self.ENGS}

    def _collect(self, reads, writes, e=None):
        deps = {}
        for b in reads:
            if b.w is not None and deps.get(b.w[0], 0) < b.w[1]:
                deps[b.w[0]] = b.w[1]
            if b.x:
                for k, v in b.r.items():
                    if k != e and deps.get(k, 0) < v:
                        deps[k] = v
        for b in writes:
            if b.w is not None and deps.get(b.w[0], 0) < b.w[1]:
                deps[b.w[0]] = b.w[1]
            for k, v in b.r.items():
                if deps.get(k, 0) < v:
                    deps[k] = v
        return deps

    def _waits(self, e, deps):
        out = []
        seen = self.seen[e]
        for k, v in deps.items():
            if k == e and e == "pe":
                continue
            if seen.get(k, 0) < v:
                seen[k] = v
                out.append((k, v))
        return out

    def _mark(self, tok, reads, writes):
        k, v = tok
        for b in reads:
            if b.r.get(k, 0) < v:
                b.r[k] = v
        for b in writes:
            b.w = tok
            b.r = {}

    def op(self, e, fn, reads=(), writes=()):
        waits = self._waits(e, self._collect(reads, writes, e))
        self.cnt[e] += 1
        self.q[e].append((waits, fn, None))
        self._mark((e, self.cnt[e]), reads, writes)

    def dma(self, e, fn, reads=(), writes=()):
        if e == "pool":
            s = self.NSW_BASE + self.drr_sw
            self.drr_sw = (self.drr_sw + 1) % (len(self.dsem) - self.NSW_BASE)
        else:
            s = self.drr
            self.drr = (s + 1) % self.NSW_BASE
        deps = self._collect(reads, writes, e)
        k = ("d", s)
        if self.dcnt[s] and deps.get(k, 0) < self.dcnt[s]:
            deps[k] = self.dcnt[s]
        waits = self._waits(e, deps)
        self.dcnt[s] += 16
        self.q[e].append((waits, fn, s))
        self._mark((k, self.dcnt[s]), reads, writes)

    def barrier(self):
        deps = {e: c for e, c in self.cnt.items() if c}
        for s, c in enumerate(self.dcnt):
            if c:
                deps[("d", s)] = c
        for e in self.ENGS:
            waits = self._waits(e, dict(deps))
            if waits:
                self.q[e].append((waits, None, None))

    def _semof(self, k):
        return self.sem[k] if isinstance(k, str) else self.dsem[k[1]]

    def emit(self):
        nc = self.nc
        with nc.Block() as block:
            regs = dict(sp=block.sync, pe=block.tensor, act=block.scalar, dve=block.vector, pool=block.gpsimd)
            for e in self.ENGS:
                def body(eng, e=e):
                    for waits, fn, ds in self.q[e]:
                        ws = [(self._semof(k), v) for k, v in waits]
                        if fn is None:
                            for sm, v in ws:
                                eng.wait_ge(sm, v)
                            continue
                        if ds is None and ws:
                            for sm, v in ws[:-1]:
                                eng.wait_ge(sm, v)
                            ins = fn(eng)
                            ins._wait_ge(ws[-1][0], ws[-1][1])
                        else:
                            for sm, v in ws:
                                eng.wait_ge(sm, v)
                            ins = fn(eng)
                        if ds is None:
                            ins.then_inc(self.sem[e], 1)
                        else:
                            ins.then_inc(self.dsem[ds], 16)
                    if e == "sp":
                        for s, c in enumerate(self.dcnt):
                            if c:
                                eng.wait_ge(self.dsem[s], c)
                regs[e](body)


class Ring:
    def __init__(self, K, name, shape, dtype, n, psum=False, arena=None):
        self.t = []
        for i in range(n):
            if arena is not None:
                t = arena.alloc(shape, dtype)
            else:
                f = K.nc.psum_tensor if psum else K.nc.sbuf_tensor
                t = K.es.enter_context(f("%s%d" % (name, i), shape, dtype))
            self.t.append((t, Buf(x=psum)))
        self.i = 0

    def next(self):
        r = self.t[self.i % len(self.t)]
        self.i += 1
        return r


class Arena:
    def __init__(self, K, nbytes):
        self.t = K.es.enter_context(K.nc.sbuf_tensor("arena", [128, nbytes // 4], F32))
        self.nbytes = nbytes
        self.off = 0

    def reset(self):
        self.off = 0

    def alloc(self, shape, dtype):
        esz = 2 if dtype == BF16 else 4
        n = 1
        for s in shape[1:]:
            n *= s
        nb = (n * esz + 3) // 4 * 4
        assert self.off + nb <= self.nbytes, "arena overflow: need %d have %d" % (nb, self.nbytes - self.off)
        v = self.t[0:shape[0], self.off // 4:(self.off + nb) // 4]
        self.off += nb
        if dtype == BF16:
            v = v.bitcast(BF16)[:, 0:n]
        if len(shape) == 3:
            v = v.rearrange("p (a b) -> p a b", a=shape[1])
        elif len(shape) == 4:
            v = v.rearrange("p (a b c) -> p a b c", a=shape[1], b=shape[2])
        return v


class Cfg:
    def __init__(self, T=4096, TS=32, PAST=2048, layers=(0, 1, 2, 3)):
        self.T, self.TS, self.PAST, self.layers = T, TS, PAST, tuple(layers)


class Group:
    def __init__(self, gi, name, T, past):
        self.gi, self.name, self.T, self.past = gi, name, T, past
        self.S = past + T
        self.tiles = [(t0, min(128, T - t0)) for t0 in range(0, T, 128)]
        self.blocks = [self.tiles[i:i + 4] for i in range(0, len(self.tiles), 4)]
        self.ktiles = [(k0, 128) for k0 in range(0, past, 128)] + [(past + t0, n) for t0, n in self.tiles]


class K:
    def __init__(self, cfg):
        self.cfg = cfg
        self.nc = nc = bass.Bass("TRN2", target_bir_lowering=False)
        self.es = ExitStack()
        self.P = Prog(nc, self.es)
        self.bufs = {}
        self.dr = {}
        self.dh = {}
        self.groups = [Group(0, "p", cfg.T, 0), Group(1, "s", cfg.TS, cfg.PAST)]

    def buf(self, key):
        b = self.bufs.get(key)
        if b is None:
            b = self.bufs[key] = Buf()
        return b

    def din(self, name, shape, dtype=F32):
        t = self.nc.dram_tensor(name, list(shape), dtype, kind="ExternalInput")
        self.dh[name] = t
        self.dr[name] = t.ap()
        return self.dr[name]

    def dout(self, name, shape, dtype=F32):
        t = self.nc.dram_tensor(name, list(shape), dtype, kind="ExternalOutput")
        self.dr[name] = t.ap()
        return self.dr[name]

    def dscr(self, name, shape, dtype):
        t = self.nc.dram_tensor(name, list(shape), dtype)
        self.dh[name] = t
        self.dr[name] = t.ap()
        return self.dr[name]

    def sb(self, name, shape, dtype):
        t = self.es.enter_context(self.nc.sbuf_tensor(name, list(shape), dtype))
        return t, Buf()

    def dma(self, out, in_, rd, wr, eng="sp"):
        self.P.dma(eng, lambda e: e.dma_start(out=out, in_=in_), rd, wr)

    def mm(self, out, lhsT, rhs, start, stop, rd, wr, skip=False):
        self.P.op("pe", lambda e: e.matmul(out, lhsT, rhs, start=start, stop=stop, skip_group_check=skip), rd, wr)

    def tr(self, out, in_, ident, rd, wr):
        self.P.op("pe", lambda e: e.transpose(out, in_, ident), rd, wr)

    def act(self, out, in_, func, rd, wr, bias=0.0, scale=1.0, accum=None):
        if accum is None:
            self.P.op("act", lambda e: e.activation(out=out, in_=in_, func=func, bias=bias, scale=scale), rd, wr)
        else:
            self.P.op("act", lambda e: e.activation(out=out, in_=in_, func=func, bias=bias, scale=scale,
                                                     accum_out=accum), rd, wr)

    def tt(self, eng, out, in0, in1, op, rd, wr):
        o = self.nc.vector if eng == "dve" else self.nc.gpsimd
        self.P.op(eng, lambda e: e.tensor_tensor(out=out, in0=in0, in1=in1, op=op), rd, wr)

    def ts(self, eng, out, in0, s1, s2, op0, op1, rd, wr, accum=None):
        if op1 is None:
            self.P.op(eng, lambda e: e.tensor_scalar(out=out, in0=in0, scalar1=s1, scalar2=None, op0=op0), rd, wr)
        elif accum is None:
            self.P.op(eng, lambda e: e.tensor_scalar(out=out, in0=in0, scalar1=s1, scalar2=s2, op0=op0, op1=op1),
                      rd, wr)
        else:
            self.P.op(eng, lambda e: e.tensor_scalar(out=out, in0=in0, scalar1=s1, scalar2=s2, op0=op0, op1=op1,
                                                     accum_out=accum), rd, wr)

    def stt(self, out, in0, scalar, in1, op0, op1, rd, wr):
        self.P.op("dve", lambda e: e.scalar_tensor_tensor(out=out, in0=in0, scalar=scalar, in1=in1, op0=op0, op1=op1),
                  rd, wr)

    def cp(self, eng, out, in_, rd, wr):
        if eng == "act":
            self.P.op("act", lambda e: e.copy(out=out, in_=in_), rd, wr)
        else:
            self.P.op(eng, lambda e: e.tensor_copy(out=out, in_=in_), rd, wr)

    def rsqrt(self, out, in_, rd, wr, scale=1.0, eps=0.0):
        self.act(out, in_, AF.Ln, rd, wr, bias=eps, scale=scale)
        self.act(out, out, AF.Exp, wr, wr, scale=-0.5)

    def recip(self, out, in_, rd, wr):
        self.P.op("dve", lambda e: e.reciprocal(out=out, in_=in_), rd, wr)

    def memset(self, eng, ap, val, wr):
        self.P.op(eng, lambda e: e.memset(ap, val), (), wr)

    def setup(self):
        cfg = self.cfg
        T, TS, PAST = cfg.T, cfg.TS, cfg.PAST
        din, dout, dscr = self.din, self.dout, self.dscr
        din("xp", [T, D]); din("xs", [TS, D]); din("cT", [128, 8, 2])
        din("ada_w", [4, D, 3 * D]); din("ada_b", [4, 3 * D]); din("ln_g", [4, D]); din("ln_b", [4, D])
        din("ident", [128, 128]); din("cmask", [128, 128])
        din("fox_w_in", [D, 4112]); din("fox_b_f", [16, 1]); din("fox_w_out", [D, D])
        din("cache_fox_k", [PAST, D]); din("cache_fox_v", [PAST, D]); din("cache_fox_logf", [PAST, 16])
        din("diff_w_in", [D, 4096]); din("diff_w_out", [D, D]); din("rel_bias_table", [32, 8])
        for nmv in ("diff_lam_q1", "diff_lam_k1", "diff_lam_q2", "diff_lam_k2"):
            din(nmv, [1, 64])
        din("diff_subln_w", [1, 128]); din("t5_onehot", [32, 384]); din("antiident", [128, 128]); din("dmask", [128, 128])
        din("cache_diff_k", [PAST, D]); din("cache_diff_v", [PAST, D])
        dscr("lam_scr", [2, 8], F32); dscr("fr_scr", [16, 384], F32)
        din("ret_w_in", [D, 6144]); din("ret_w_out", [2 * D, D]); din("ret_gn_w", [1, 2 * D])
        din("state_ret", [4, 256, 512]); din("ret_dmaskT", [128, 4, 128]); din("ret_qdec", [128, 4, 128])
        din("ret_kdec", [128, 8])
        for g in self.groups:
            din("rot_cos_" + g.name, [128, g.T]); din("rot_sin_" + g.name, [128, g.T])
            dscr("QR_" + g.name, [8, 128, g.T], BF16); dscr("KR_" + g.name, [8, 128, g.T], BF16)
            dscr("OG_" + g.name + "v", [g.T, 2 * D], BF16)
            dout("ret_state_" + g.name, [4, 256, 512])
        din("gdn_w_in", [D, 4112]); din("gdn_w_out", [D, D]); din("gdn_convw", [128, 24, 4]); din("gdn_conv_in", [128, 24, 3])
        din("gdn_a_log", [8, 1]); din("gdn_dt_bias", [8, 1]); din("gdn_norm_w", [1, 128]); din("state_gdn", [8, 128, 128])
        din("gdn_masks", [128, 3, 128]); din("gdn_sel", [8, 8, 128])
        for g in self.groups:
            for nmv in ("QG_", "KG_", "VG_"):
                dscr(nmv + g.name, [8, 128, g.T], BF16)
            dscr("BTb_" + g.name, [8, g.T], BF16); dscr("GT_" + g.name, [8, g.T], F32); dscr("GB_" + g.name, [g.T, 16], F32)
            dout("gdn_state_" + g.name, [8, 128, 128]); dout("gdn_conv_" + g.name, [3, 3 * D])
        dout("yp", [T, D]); dout("ys", [TS, D])
        for g in self.groups:
            n = g.name
            dout("fox_k_" + n, [g.T, D]); dout("fox_v_" + n, [g.T, D]); dout("fox_logf_" + n, [g.T, 16])
            dout("diff_k_" + n, [g.T, D]); dout("diff_v_" + n, [g.T, D])
            dscr("OGF_" + n, [g.T, 2 * D], F32)
        dscr("xres_p", [T, D], F32); dscr("xres_s", [TS, D], F32)
        for g in self.groups:
            n = g.name
            dscr("QT_" + n, [16, 70, g.T], BF16); dscr("KT_" + n, [16, 70, g.S], BF16)
            dscr("VA_" + n, [g.S, 1040], BF16)
            dscr("ZS_" + n, [g.T, 2 * D], BF16); dscr("OG_" + n, [g.T, 2 * D], BF16)

        self.W, self.Wb = self.sb("W_in", [128, 8, 4112], BF16)
        self.WO, self.WOb = self.sb("W_out", [128, 8, D], BF16)
        self.identf, self.identf_b = self.sb("identf", [128, 128], F32)
        self.identb, self.identb_b = self.sb("identb", [128, 128], BF16)
        self.cmaskb, self.cmaskb_b = self.sb("cmaskb", [128, 128], BF16)
        self.cTs, self.cTs_b = self.sb("cTs", [128, 8, 2], F32)
        self.crep = [self.sb("crep%d" % g, [128, 8, 128], BF16) for g in range(2)]
        self.mod = [self.sb("mod_%d" % k, [128, D], F32) for k in range(3)]
        dscr("mods", [3, D], F32)
        self.lng, self.lng_b = self.sb("lng", [128, D], F32)
        self.lnb, self.lnb_b = self.sb("lnb", [128, D], F32)
        self.ps_mm = Ring(self, "psmm", [128, 512], F32, 4, psum=True)
        self.ps_acc = Ring(self, "psacc", [128, 512], F32, 2, psum=True)
        self.ps_tp = Ring(self, "pstp", [128, 512], F32, 2, psum=True)
        self.r_x = Ring(self, "rx", [128, D], F32, 2)
        self.r_f = Ring(self, "rf", [128, D], F32, 2)
        self.r_ub = Ring(self, "rub", [128, D], BF16, 2)
        self.r_uT = Ring(self, "ruT", [128, 8, 512], BF16, 2)
        self.r_h = Ring(self, "rh", [128, 1040], BF16, 3)
        self.r_small = Ring(self, "rsm", [128, 16], F32, 8)
        self.arena = Arena(self, 58 * 1024)

        self.dma(self.identf[:], self.dr["ident"][:, :], [], [self.identf_b])
        self.cp("dve", self.identb[:], self.identf[:], [self.identf_b], [self.identb_b])
        t, b = self.r_f.next()
        self.dma(t[:, 0:128], self.dr["cmask"][:, :], [], [b])
        self.cp("dve", self.cmaskb[:], t[:, 0:128], [b], [self.cmaskb_b])
        self.dma(self.cTs[:], self.dr["cT"][:, :, :], [], [self.cTs_b])
        self.act(self.cTs[:], self.cTs[:], AF.Silu, [self.cTs_b], [self.cTs_b])
        for g in range(2):
            t, b = self.crep[g]
            self.cp("dve", t[:], self.cTs[:, :, g:g + 1].to_broadcast([128, 8, 128]), [self.cTs_b], [b])

    def load_w(self, dst, dstb, src, ncols, nk=8, c_off=0):
        for k in range(nk):
            self.dma(dst[:, k, 0:ncols], src[k * 128:(k + 1) * 128, c_off:c_off + ncols], [], [dstb], eng="pool")

    def layer_mod(self, L):
        ada_w, ada_b = self.dr["ada_w"], self.dr["ada_b"]
        for cg in range(6):
            wt, wb = self.r_uT.next()
            self.load_w(wt, wb, ada_w[L], 512, c_off=cg * 512)
            bt_, bb = self.r_f.next()
            bt = bt_[:, 0:512]
            self.dma(bt[:, :], ada_b[L:L + 1, cg * 512:(cg + 1) * 512].to_broadcast([128, 512]), [], [bb])
            kind = cg // 2
            for g in range(2):
                ps, pb = self.ps_mm.next()
                ct, cb = self.crep[g]
                for k in range(8):
                    self.mm(ps[:, :], ct[:, k, :], wt[:, k, :], k == 0, k == 7, [cb, wb], [pb])
                if g == 0:
                    mt, mb = self.mod[kind]
                    dst = mt[:, (cg % 2) * 512:(cg % 2 + 1) * 512]
                else:
                    mt, mb = self.r_f.next()
                    dst = mt[:, 0:512]
                if kind == 0:
                    self.tt("dve", dst, ps[:, :], bt[:, :], ALU.add, [pb, bb], [mb])
                else:
                    self.stt(dst, ps[:, :], 1.0, bt[:, :], ALU.add, ALU.add, [pb, bb], [mb])
                if g == 1:
                    self.dma(self.dr["mods"][kind:kind + 1, (cg % 2) * 512:(cg % 2 + 1) * 512], mt[0:1, 0:512], [mb],
                             [self.buf(("mods", kind, cg % 2))])
        self.dma(self.lng[:], self.dr["ln_g"][L:L + 1, :].to_broadcast([128, D]), [], [self.lng_b])
        self.dma(self.lnb[:], self.dr["ln_b"][L:L + 1, :].to_broadcast([128, D]), [], [self.lnb_b])

    def modtile(self, g, kind, n):
        if g.gi == 0:
            return self.mod[kind]
        t, b = self.r_f.next()
        self.dma(t[:n, :], self.dr["mods"][kind:kind + 1, :].to_broadcast([n, D]),
                 [self.buf(("mods", kind, 0)), self.buf(("mods", kind, 1))], [b])
        return t, b

    def xsrc(self, g, first):
        if first:
            return self.dr["xp" if g.gi == 0 else "xs"], "xin%d" % g.gi
        return self.dr["xres_" + g.name], "xres%d" % g.gi

    def make_uT(self, g, blk, first):
        src, skey = self.xsrc(g, first)
        uT, uTb = self.r_uT.next()
        off = 0
        for (t0, n) in blk:
            sc, scb = self.modtile(g, 1, n)
            xt, xb = self.r_x.next()
            self.dma(xt[:n, :], src[t0:t0 + n, :], [self.buf((skey, t0))], [xb])
            ft, fb = self.r_f.next()
            self.tt("dve", ft[:n, :], xt[:n, :], sc[:n, :], ALU.mult, [xb, scb], [fb])
            sh, shb = self.modtile(g, 0, n)
            ub, ubb = self.r_ub.next()
            self.tt("dve", ub[:n, :], ft[:n, :], sh[:n, :], ALU.add, [fb, shb], [ubb])
            tp, tpb = self.ps_tp.next()
            tpv = tp[:, :].bitcast(BF16)
            for j in range(8):
                self.tr(tpv[:, j * 128:j * 128 + n], ub[:n, j * 128:(j + 1) * 128], self.identb[:n, :n],
                        [ubb, self.identb_b], [tpb])
            self.cp("act", uT[:, :, off:off + n],
                    tpv.rearrange("p (j t) -> p j t", j=8)[:, :, 0:n], [tpb], [uTb])
            off += n
        return uT, uTb, off

    def proj_tm(self, uT, uTb, off, n, c0, c1):
        ps, pb = self.ps_mm.next()
        for k in range(8):
            self.mm(ps[:n, 0:c1 - c0], uT[:, k, off:off + n], self.W[:, k, c0:c1], k == 0, k == 7,
                    [uTb, self.Wb], [pb])
        return ps, pb

    def proj_fm(self, uT, uTb, nb, c0, c1):
        ps, pb = self.ps_mm.next()
        for k in range(8):
            self.mm(ps[:c1 - c0, 0:nb], self.W[:, k, c0:c1], uT[:, k, 0:nb], k == 0, k == 7, [uTb, self.Wb], [pb])
        return ps, pb

    def pass_c_tile(self, g, t0, n, first, last, ogz, ogzb, nk, wo=None):
        src, skey = self.xsrc(g, first)
        dst, dkey = (self.dr["yp" if g.gi == 0 else "ys"], "y%d" % g.gi) if last else \
            (self.dr["xres_" + g.name], "xres%d" % g.gi)
        oT, oTb = self.r_uT.next()
        for half in range((nk + 7) // 8):
            tp, tpb = self.ps_tp.next()
            tpv = tp[:, :].bitcast(BF16)
            for j in range(8):
                jj = half * 8 + j
                self.tr(tpv[:, j * 128:j * 128 + n], ogz[:n, jj * 128:(jj + 1) * 128], self.identb[:n, :n],
                        [ogzb, self.identb_b], [tpb])
            self.cp("act", oT[:, :, half * 128:half * 128 + n],
                    tpv.rearrange("p (j t) -> p j t", j=8)[:, :, 0:n], [tpb], [oTb])
        hps = []
        for cg in range(2):
            ps, pb = self.ps_acc.next()
            for k in range(nk):
                wt_, wtb_ = (self.WO[:, k, :], self.WOb) if wo is None else wo(k)
                self.mm(ps[:n, :], oT[:, k % 8, (k // 8) * 128:(k // 8) * 128 + n],
                        wt_[:, cg * 512:(cg + 1) * 512], k == 0, k == nk - 1, [oTb, wtb_], [pb])
            hps.append((ps, pb))
        xt, xb = self.r_x.next()
        self.dma(xt[:n, :], src[t0:t0 + n, :], [self.buf((skey, t0))], [xb])
        gt, gb = self.modtile(g, 2, n)
        rt, rb = self.r_f.next()
        for cg in range(2):
            ps, pb = hps[cg]
            sl = slice(cg * 512, (cg + 1) * 512)
            self.tt("dve", rt[:n, sl], ps[:n, :], gt[:n, sl], ALU.mult, [pb, gb], [rb])
        self.stt(rt[:n, :], xt[:n, :], ALPHA, rt[:n, :], ALU.mult, ALU.add, [xb, rb], [rb])
        st, stb = self.r_small.next()
        for cg in range(2):
            self.P.op("dve", lambda e, cg=cg: e.bn_stats(out=st[:n, cg * 6:cg * 6 + 6],
                                                          in_=rt[:n, cg * 512:(cg + 1) * 512]), [rb], [stb])
        mv, mvb = self.r_small.next()
        self.P.op("dve", lambda e: e.bn_aggr(out=mv[:n, 0:2], in_=st[:n, 0:12]), [stb], [mvb])
        self.rsqrt(mv[:n, 4:5], mv[:n, 1:2], [mvb], [mvb], eps=LN_EPS)
        self.stt(mv[:n, 5:6], mv[:n, 0:1], -1.0, mv[:n, 4:5], ALU.mult, ALU.mult, [mvb], [mvb])
        yt, yb = self.r_f.next()
        self.act(yt[:n, :], rt[:n, :], AF.Identity, [rb, mvb], [yb], bias=mv[:n, 5:6], scale=mv[:n, 4:5])
        self.tt("dve", yt[:n, :], yt[:n, :], self.lng[:n, :], ALU.mult, [yb, self.lng_b], [yb])
        self.tt("dve", yt[:n, :], yt[:n, :], self.lnb[:n, :], ALU.add, [yb, self.lnb_b], [yb])
        self.dma(dst[t0:t0 + n, :], yt[:n, :], [yb], [self.buf((dkey, t0))])

    def layer(self, L, first, last):
        self.layer_mod(L)
        getattr(self, ("layer_gdn", "layer_fox", "layer_diff", "layer_ret")[L % 4])(L, first, last)

    def attn_rings(self, w, lf):
        self.P.barrier()
        A = self.arena
        A.reset()
        smax = max(g.S for g in self.groups)
        self.r_KT = Ring(self, "rKT", [70, smax], BF16, 2, arena=A)
        nkt = max(len(g.ktiles) for g in self.groups)
        self.r_VA = Ring(self, "rVA", [128, nkt, w], BF16, 2, arena=A)
        self.r_QT = Ring(self, "rQT", [70, 512], BF16, 2, arena=A)
        self.r_PT = Ring(self, "rPT", [128, 512], BF16, 3 if lf else 2, arena=A)
        if lf:
            self.r_og = Ring(self, "rog", [128, 4, 128], BF16, 2, arena=A)
        self.r_vb = self.r_h
        self.r_st = Ring(self, "rst", [128, 2048], BF16, 1, arena=A)
        if lf:
            self.r_lf = Ring(self, "rlf", [16, 512], F32, 4, arena=A)
            self.r_pt = Ring(self, "rpt", [16, 3, 512], BF16, 2, arena=A)
            self.ones3, self.ones3_b = A.alloc([16, 3, 512], BF16), Buf()
            self.memset("pool", self.ones3[:, :, :], 1.0, [self.ones3_b])
            self.carry, self.carry_b = A.alloc([16, 1], F32), Buf()
            self.fbf, self.fbf_b = A.alloc([16, 1], F32), Buf()

    def attn_head(self, g, KTd, QTd, VAd, vcol0, dv, kc, rd_bufs, extras, finish):
        KT, KTb = self.r_KT.next()
        self.dma(KT[:kc, 0:g.S], KTd[:, :], rd_bufs, [KTb])
        VA, VAb = self.r_VA.next()
        nfull = sum(1 for (k0, nk) in g.ktiles if nk == 128)
        for k4 in range(0, nfull, 4):
            k5 = min(nfull, k4 + 4)
            self.dma(VA[:, k4:k5, 0:dv + 1],
                     VAd[k4 * 128:k5 * 128, vcol0:vcol0 + dv + 1].rearrange("(kt p) c -> p kt c", p=128), rd_bufs, [VAb])
        for i, (k0, nk) in enumerate(g.ktiles):
            if nk != 128:
                self.dma(VA[:nk, i, 0:dv + 1], VAd[k0:k0 + nk, vcol0:vcol0 + dv + 1], rd_bufs, [VAb])
        w = dv + 1
        per_bank = 512 // w
        for bi, blk in enumerate(g.blocks):
            nq = sum(n for _, n in blk)
            q0b = blk[0][0]
            QT, QTb = self.r_QT.next()
            self.dma(QT[:kc, 0:nq], QTd[:, q0b:q0b + nq], rd_bufs, [QTb])
            nbank = (len(blk) + per_bank - 1) // per_bank
            accs = [self.ps_acc.next() for _ in range(nbank)]
            started = [False] * nbank
            coff = []
            o = 0
            for (_, n) in blk:
                coff.append(o)
                o += n
            for ki, (k0, nk) in enumerate(g.ktiles):
                vis = [qi for qi, (q0, n) in enumerate(blk) if g.past + q0 >= k0]
                if not vis or k0 > g.past + blk[-1][0]:
                    continue
                fi = vis[0]
                c0 = coff[fi]
                ST, STb = self.ps_mm.next()
                ex = []
                for qi in vis:
                    q0, n = blk[qi]
                    for (l, r, bl) in extras(ki, k0, nk, qi, g.past + q0, n):
                        ex.append((qi, n, l, r, bl))
                self.mm(ST[:nk, c0:nq], KT[:kc, k0:k0 + nk], QT[:kc, c0:nq], True, not ex, [KTb, QTb], [STb])
                for j, (qi, n, l, r, bl) in enumerate(ex):
                    self.mm(ST[:nk, coff[qi]:coff[qi] + n], l, r, False, j == len(ex) - 1, bl, [STb], skip=True)
                PT, PTb = self.r_PT.next()
                self.act(PT[:nk, c0:nq], ST[:nk, c0:nq], AF.Exp, [STb], [PTb])
                for qi in vis:
                    q0, n = blk[qi]
                    bk = qi // per_bank
                    acc, accb = accs[bk]
                    col = (qi % per_bank) * w
                    last = (k0 == g.past + q0)
                    self.mm(acc[:n, col:col + w], PT[:nk, coff[qi]:coff[qi] + n], VA[:nk, ki, 0:w],
                            not started[bk], last, [PTb, VAb], [accb], skip=True)
                    started[bk] = True
            finish(bi, blk, accs, per_bank, w)

    def layer_fox(self, L, first, last):
        dr = self.dr
        self.attn_rings(65, True)
        self.load_w(self.W, self.Wb, dr["fox_w_in"], 4112)
        self.load_w(self.WO, self.WOb, dr["fox_w_out"], D)
        bf, bfb = self.fbf, self.fbf_b
        self.dma(bf[:, :], dr["fox_b_f"][:, :], [], [bfb])
        self.ts("dve", bf[:, :], bf[:, :], -1.0, None, ALU.mult, None, [bfb], [bfb])
        for g in self.groups:
            nm = g.name
            KTd, QTd, VAd, ZSd, OGd = dr["KT_" + nm], dr["QT_" + nm], dr["VA_" + nm], dr["ZS_" + nm], dr["OG_" + nm]
            KTv = KTd.rearrange("(hp two) r s -> two r hp s", two=2)
            QTv = QTd.rearrange("(hp two) r s -> two r hp s", two=2)
            wb = []

            def nb_(key):
                b = self.buf(("fox", g.gi, L) + key)
                wb.append(b)
                return b
            self.memset("pool", self.carry[:, :], 0.0, [self.carry_b])

            def cum_block(lf, lfb, nb, kpos, qpos, tag):
                cum, cumb = self.r_lf.next()
                self.P.op("dve", lambda e: e.tensor_tensor_scan(out=cum[:, 0:nb], data0=self.ones3[:, 0, 0:nb],
                                                                data1=lf[:, 0:nb], initial=self.carry[:, 0:1],
                                                                op0=ALU.mult, op1=ALU.add),
                          [lfb, self.ones3_b, self.carry_b], [cumb])
                self.cp("act", self.carry[:, 0:1], cum[:, nb - 1:nb], [cumb], [self.carry_b])
                pt, ptb = self.r_pt.next()
                t32, t32b = self.r_lf.next()
                r1, r1b = self.r_lf.next()
                self.cp("dve", pt[:, 0, 0:nb], cum[:, 0:nb], [cumb], [ptb])
                self.cp("dve", t32[:, 0:nb], pt[:, 0, 0:nb], [ptb], [t32b])
                self.tt("dve", r1[:, 0:nb], cum[:, 0:nb], t32[:, 0:nb], ALU.subtract, [cumb, t32b], [r1b])
                self.cp("dve", pt[:, 1, 0:nb], r1[:, 0:nb], [r1b], [ptb])
                self.cp("dve", t32[:, 0:nb], pt[:, 1, 0:nb], [ptb], [t32b])
                self.tt("dve", r1[:, 0:nb], r1[:, 0:nb], t32[:, 0:nb], ALU.subtract, [r1b, t32b], [r1b])
                self.cp("dve", pt[:, 2, 0:nb], r1[:, 0:nb], [r1b], [ptb])
                npt, nptb = self.r_pt.next()
                self.ts("dve", npt[:, :, 0:nb], pt[:, :, 0:nb], -1.0, None, ALU.mult, None, [ptb], [nptb])
                self.dma(KTd[:, 64:67, kpos:kpos + nb], npt[:, :, 0:nb], [nptb], [nb_((tag, "kc"))])
                self.dma(KTd[:, 67:70, kpos:kpos + nb], self.ones3[:, :, 0:nb], [self.ones3_b], [nb_((tag, "k1"))])
                if qpos is not None:
                    self.dma(QTd[:, 64:67, qpos:qpos + nb], self.ones3[:, :, 0:nb], [self.ones3_b],
                             [nb_((tag, "q1"))])
                    self.dma(QTd[:, 67:70, qpos:qpos + nb], pt[:, :, 0:nb], [ptb], [nb_((tag, "qc"))])


            ptiles = [(k0, nk) for (k0, nk) in g.ktiles if k0 < g.past]
            for b0 in range(0, len(ptiles), 2):
                pblk = ptiles[b0:b0 + 2]
                lf, lfb = self.r_lf.next()
                st_, stb = self.r_st.next()
                st = st_[:, :].rearrange("p (c t) -> p c t", c=8)
                for j, (k0, nk) in enumerate(pblk):
                    xt, xb = self.r_x.next()
                    self.dma(xt[:, :], dr["cache_fox_k"][k0:k0 + 128, :], [], [xb])
                    ub, ubb = self.r_ub.next()
                    self.cp("dve", ub[:, :], xt[:, :], [xb], [ubb])
                    tp, tpb = self.ps_tp.next()
                    tpv = tp[:, :].bitcast(BF16)
                    for c in range(8):
                        self.tr(tpv[:, c * 128:(c + 1) * 128], ub[:, c * 128:(c + 1) * 128], self.identb[:, :],
                                [ubb, self.identb_b], [tpb])
                    self.cp("act", st[:, :, j * 128:(j + 1) * 128], tpv.rearrange("p (c t) -> p c t", c=8), [tpb], [stb])
                    vt, vtb = self.r_x.next()
                    self.dma(vt[:, :], dr["cache_fox_v"][k0:k0 + 128, :], [], [vtb])
                    vb, vbb = self.r_vb.next()
                    self.memset("pool", vb[:, :].rearrange("p (h c) -> p h c", h=16)[:, :, 64:65], 1.0, [vbb])
                    self.cp("dve", vb[:, :].rearrange("p (h c) -> p h c", h=16)[:, :, 0:64],
                            vt[:, :].rearrange("p (h c) -> p h c", h=16), [vtb], [vbb])
                    self.dma(VAd[k0:k0 + 128, :], vb[:, :], [vbb], [nb_(("pv", k0))])
                    lt, ltb = self.r_small.next()
                    self.dma(lt[:, 0:16], dr["cache_fox_logf"][k0:k0 + 128, :], [], [ltb])
                    tp2, tp2b = self.ps_tp.next()
                    self.tr(tp2[0:16, 0:128], lt[:, 0:16], self.identf[:, :], [ltb, self.identf_b], [tp2b])
                    self.cp("act", lf[:, j * 128:(j + 1) * 128], tp2[0:16, 0:128], [tp2b], [lfb])
                kp, nbk = pblk[0][0], 128 * len(pblk)
                for two in range(2):
                    self.dma(KTv[two, 0:64, :, kp:kp + nbk], st[two * 64:(two + 1) * 64, :, 0:nbk], [stb],
                             [nb_(("pk", kp, two))])
                cum_block(lf, lfb, nbk, kp, None, ("pc", kp))

            for blk in g.blocks:
                uT, uTb, nb = self.make_uT(g, blk, first)
                t0b = blk[0][0]
                kpos = g.past + t0b
                for which, dst in ((0, QTv), (1, KTv)):
                    pos = t0b if which == 0 else kpos
                    for hf in range(2):
                        st_, stb = self.r_st.next()
                        st = st_[:, :].rearrange("p (c t) -> p c t", c=4)
                        for c4 in range(4):
                            c = hf * 4 + c4
                            ps, pb = self.proj_fm(uT, uTb, nb, which * D + c * 128, which * D + (c + 1) * 128)
                            if which == 0:
                                self.act(st[:, c4, 0:nb], ps[:, 0:nb], AF.Copy, [pb], [stb], scale=0.125)
                            else:
                                self.cp("dve", st[:, c4, 0:nb], ps[:, 0:nb], [pb], [stb])
                        for two in range(2):
                            self.dma(dst[two, 0:64, hf * 4:hf * 4 + 4, pos:pos + nb], st[two * 64:(two + 1) * 64, :, 0:nb],
                                     [stb], [nb_(("qk", which, t0b, two, hf))])
                ps, pb = self.proj_fm(uT, uTb, nb, 4 * D, 4 * D + 16)
                e1, e1b = self.r_lf.next()
                self.act(e1[:, 0:nb], ps[0:16, 0:nb], AF.Exp, [pb, bfb], [e1b], bias=bf[:, 0:1], scale=-1.0)
                lf, lfb = self.r_lf.next()
                self.act(lf[:, 0:nb], e1[:, 0:nb], AF.Ln, [e1b], [lfb], bias=1.0)
                self.ts("dve", lf[:, 0:nb], lf[:, 0:nb], -1.0, None, ALU.mult, None, [lfb], [lfb])
                off = 0
                for (t0, n) in blk:
                    tp2, tp2b = self.ps_tp.next()
                    self.tr(tp2[0:n, 0:16], lf[:, off:off + n], self.identf[0:16, 0:16], [lfb, self.identf_b], [tp2b])
                    lo, lob = self.r_small.next()
                    self.cp("act", lo[:n, 0:16], tp2[0:n, 0:16], [tp2b], [lob])
                    self.dma(dr["fox_logf_" + nm][t0:t0 + n, :], lo[:n, 0:16], [lob], [self.buf(("flo", g.gi, t0))])
                    for which, oname in ((1, "fox_k_"), (2, "fox_v_")):
                        ft, fb = self.r_f.next()
                        for cg in range(2):
                            ps, pb = self.proj_tm(uT, uTb, off, n, which * D + cg * 512, which * D + (cg + 1) * 512)
                            self.cp("act" if cg == 0 else "dve", ft[:n, cg * 512:(cg + 1) * 512], ps[:n, :], [pb], [fb])
                        self.dma(dr[oname + nm][t0:t0 + n, :], ft[:n, :], [fb], [self.buf((oname, g.gi, t0))])
                        if which == 2:
                            vb, vbb = self.r_vb.next()
                            self.memset("pool", vb[:n, :].rearrange("p (h c) -> p h c", h=16)[:, :, 64:65], 1.0, [vbb])
                            self.cp("dve", vb[:n, :].rearrange("p (h c) -> p h c", h=16)[:, :, 0:64],
                                    ft[:n, :].rearrange("p (h c) -> p h c", h=16), [fb], [vbb])
                            self.dma(VAd[g.past + t0:g.past + t0 + n, :], vb[:n, :], [vbb], [nb_(("v", t0))])
                    zt, ztb = self.r_h.next()
                    for cg in range(2):
                        ps, pb = self.proj_tm(uT, uTb, off, n, 3 * D + cg * 512, 3 * D + (cg + 1) * 512)
                        self.act(zt[:n, cg * 512:(cg + 1) * 512], ps[:n, :], AF.Silu, [pb], [ztb])
                    self.dma(ZSd[t0:t0 + n, 0:D], zt[:n, 0:D], [ztb], [nb_(("z", t0))])
                    off += n
                cum_block(lf, lfb, nb, kpos, t0b, ("c", t0b))

            ogw = []

            def extras(ki, k0, nk, qi, q0a, n):
                if k0 == q0a:
                    return [(self.identb[:nk, :nk], self.cmaskb[:nk, :n], [self.identb_b, self.cmaskb_b])]
                return []

            for h in range(16):
                def finish(bi, blk, accs, per_bank, w, h=h):
                    og, ogb = self.r_og.next()
                    for qi, (q0, n) in enumerate(blk):
                        acc, accb = accs[qi // per_bank]
                        col = (qi % per_bank) * w
                        rd, rdb = self.r_small.next()
                        self.recip(rd[:n, 0:1], acc[:n, col + 64:col + 65], [accb], [rdb])
                        self.ts("dve", og[:n, qi, 0:64], acc[:n, col:col + 64], rd[:n, 0:1], None, ALU.mult, None,
                                [accb, rdb], [ogb])
                    q0b = blk[0][0]
                    kb = self.buf(("fox_og", g.gi, L, h, bi))
                    ogw.append(kb)
                    if len(blk) > 1 or blk[0][1] == 128:
                        nt = len(blk)
                        self.dma(OGd[q0b:q0b + nt * 128, h * 64:(h + 1) * 64].rearrange("(qi p) c -> p qi c", p=128),
                                 og[:, 0:nt, 0:64], [ogb], [kb])
                    else:
                        n = blk[0][1]
                        self.dma(OGd[q0b:q0b + n, h * 64:(h + 1) * 64], og[:n, 0, 0:64], [ogb], [kb])
                self.attn_head(g, KTd[h], QTd[h], VAd, h * 65, 64, 70, wb, extras, finish)

            for (t0, n) in g.tiles:
                og, ogb = self.r_h.next()
                self.dma(og[:n, 0:D], OGd[t0:t0 + n, 0:D], ogw, [ogb])
                zs, zsb = self.r_h.next()
                self.dma(zs[:n, 0:D], ZSd[t0:t0 + n, 0:D], wb, [zsb])
                self.tt("dve", og[:n, 0:D], og[:n, 0:D], zs[:n, 0:D], ALU.mult, [ogb, zsb], [ogb])
                self.pass_c_tile(g, t0, n, first, last, og, ogb, 8)

    def layer_diff(self, L, first, last):
        dr = self.dr
        LAM_INIT = 0.8 - 0.6 * math.exp(-0.3 * 2)
        self.attn_rings(129, False)
        A = self.arena
        self.load_w(self.W, self.Wb, dr["diff_w_in"], 4096)
        self.load_w(self.WO, self.WOb, dr["diff_w_out"], D)
        lv, lvb = A.alloc([1, 4, 64], F32), Buf()
        for i, nmv in enumerate(("diff_lam_q1", "diff_lam_k1", "diff_lam_q2", "diff_lam_k2")):
            self.dma(lv[:, i, :], dr[nmv][:, :], [], [lvb])
        l2, l2b = A.alloc([1, 8], F32), Buf()
        pr, prb = A.alloc([1, 2, 64], F32), Buf()
        lvv = lv[:, :, :].rearrange("p (a b) c -> p a b c", b=2)
        self.tt("dve", pr[:, :, :], lvv[:, :, 0, :], lvv[:, :, 1, :], ALU.mult, [lvb], [prb])
        self.P.op("dve", lambda e: e.tensor_reduce(out=l2[:, 0:2], in_=pr[:, :, :], axis=AX.X, op=ALU.add), [prb], [l2b])
        self.act(l2[:, 2:4], l2[:, 0:2], AF.Exp, [l2b], [l2b])
        self.tt("dve", l2[:, 4:5], l2[:, 3:4], l2[:, 2:3], ALU.subtract, [l2b], [l2b])
        self.ts("dve", l2[:, 5:6], l2[:, 4:5], -LAM_INIT, None, ALU.add, None, [l2b], [l2b])
        self.dma(dr["lam_scr"][0:1, 0:1], l2[:, 5:6], [l2b], [self.buf("lam_scr")])
        nlam, nlamb = A.alloc([128, 1], F32), Buf()
        self.dma(nlam[:, :], dr["lam_scr"][0:1, 0:1].to_broadcast([128, 1]), [self.buf("lam_scr")], [nlamb])
        tb, tbb = A.alloc([32, 2, 8], F32), Buf()
        for m_ in range(2):
            self.dma(tb[:, m_, :], dr["rel_bias_table"][:, :], [], [tbb])
        oh, ohb = self.r_x.next()
        self.dma(oh[0:32, 0:384], dr["t5_onehot"][:, :], [], [ohb])
        tbs, tbsb = A.alloc([32, 3, 16], BF16), Buf()
        ohb16, ohb16b = A.alloc([32, 384], BF16), Buf()
        self.cp("dve", ohb16[:, :], oh[0:32, 0:384], [ohb], [ohb16b])
        tbf = tb[:, :, :].rearrange("p a b -> p (a b)")
        tmp32, tmp32b = self.r_small.next()
        self.cp("dve", tbs[:, 0, :], tbf, [tbb], [tbsb])
        self.cp("dve", tmp32[0:32, 0:16], tbs[:, 0, :], [tbsb], [tmp32b])
        self.tt("dve", tmp32[0:32, 0:16], tbf, tmp32[0:32, 0:16], ALU.subtract, [tbb, tmp32b], [tmp32b])
        self.cp("dve", tbs[:, 1, :], tmp32[0:32, 0:16], [tmp32b], [tbsb])
        tmp33, tmp33b = self.r_small.next()
        self.cp("dve", tmp33[0:32, 0:16], tbs[:, 1, :], [tbsb], [tmp33b])
        self.tt("dve", tmp33[0:32, 0:16], tmp32[0:32, 0:16], tmp33[0:32, 0:16], ALU.subtract, [tmp32b, tmp33b], [tmp33b])
        self.cp("dve", tbs[:, 2, :], tmp33[0:32, 0:16], [tmp33b], [tbsb])
        ps, pb = self.ps_mm.next()
        for j_ in range(3):
            self.mm(ps[0:16, 0:384], tbs[:, j_, :], ohb16[:, :], j_ == 0, j_ == 2, [tbsb, ohb16b], [pb])
        fr, frb = self.r_f.next()
        c16, c16b = self.r_small.next()
        self.cp("dve", c16[0:16, 0:1], ps[0:16, 0:1], [pb], [c16b])
        self.ts("dve", fr[0:16, 0:384], ps[0:16, 0:384], c16[0:16, 0:1], None, ALU.subtract, None, [pb, c16b], [frb])
        self.dma(dr["fr_scr"][:, :], fr[0:16, 0:384], [frb], [self.buf("fr_scr")])
        crow, crowb = A.alloc([16, 2, 512], BF16), Buf()
        chi, chib = self.r_small.next()
        cbf = chi[0:16, 0:4].bitcast(BF16)
        self.cp("dve", cbf[:, 0:1], c16[0:16, 0:1], [c16b], [chib])
        self.cp("dve", chi[0:16, 4:5], cbf[:, 0:1], [chib], [chib])
        self.tt("dve", chi[0:16, 5:6], c16[0:16, 0:1], chi[0:16, 4:5], ALU.subtract, [c16b, chib], [chib])
        self.cp("dve", cbf[:, 1:2], chi[0:16, 5:6], [chib], [chib])
        for j in range(2):
            self.cp("dve", crow[:, j, :], cbf[:, j:j + 1].to_broadcast([16, 512]), [chib], [crowb])
        ones2, ones2b = A.alloc([16, 2, 512], BF16), Buf()
        self.memset("pool", ones2[:, :, :], 1.0, [ones2b])
        J, Jb = A.alloc([128, 128], BF16), Buf()
        dmk, dmkb = A.alloc([128, 128], BF16), Buf()
        t, b = self.r_f.next()
        self.dma(t[:, 0:128], dr["antiident"][:, :], [], [b])
        self.cp("dve", J[:, :], t[:, 0:128], [b], [Jb])
        t, b = self.r_f.next()
        self.dma(t[:, 0:128], dr["dmask"][:, :], [], [b])
        self.cp("dve", dmk[:, :], t[:, 0:128], [b], [dmkb])
        Hh = [[None, None] for _ in range(8)]
        for h in range(8):
            for ti, c in enumerate((128, 0)):
                t, b = self.r_f.next()
                src_ap = bass.AP(self.dh["fr_scr"], h * 384 + c, [[1, 128], [1, 128]])
                self.dma(t[:, 0:128], src_ap, [self.buf("fr_scr")], [b])
                hi, hib = A.alloc([128, 128], BF16), Buf()
                lo, lob = A.alloc([128, 128], BF16), Buf()
                self.cp("dve", hi[:, :], t[:, 0:128], [b], [hib])
                self.cp("dve", t[:, 128:256], hi[:, :], [hib], [b])
                self.tt("dve", t[:, 256:384], t[:, 0:128], t[:, 128:256], ALU.subtract, [b], [b])
                self.cp("dve", lo[:, :], t[:, 256:384], [b], [lob])
                Hh[h][ti] = (hi, hib, lo, lob)

        slw, slwb = A.alloc([128, 128], F32), Buf()
        self.dma(slw[:, :], dr["diff_subln_w"][0:1, :].to_broadcast([128, 128]), [], [slwb])
        self.P.op("act", lambda e: e.mul(out=slw[:, :], in_=slw[:, :], mul=1.0 - LAM_INIT), [slwb], [slwb])
        for g in self.groups:
            nm = g.name
            KTd, QTd, VAd, ZSd, OGd = dr["KT_" + nm], dr["QT_" + nm], dr["VA_" + nm], dr["ZS_" + nm], dr["OGF_" + nm]
            KTv = KTd.rearrange("(hp two) r s -> two r hp s", two=2)
            QTv = QTd.rearrange("(hp two) r s -> two r hp s", two=2)
            wb = []

            def nb_(key):
                b = self.buf(("diff", g.gi, L) + key)
                wb.append(b)
                return b
            ptiles = [(k0, nk) for (k0, nk) in g.ktiles if k0 < g.past]
            for b0 in range(0, len(ptiles), 2):
                pblk = ptiles[b0:b0 + 2]
                st_, stb = self.r_st.next()
                st = st_[:, :].rearrange("p (c t) -> p c t", c=8)
                for j, (k0, nk) in enumerate(pblk):
                    xt, xb = self.r_x.next()
                    self.dma(xt[:, :], dr["cache_diff_k"][k0:k0 + 128, :], [], [xb])
                    ub, ubb = self.r_ub.next()
                    self.cp("dve", ub[:, :], xt[:, :], [xb], [ubb])
                    tp, tpb = self.ps_tp.next()
                    tpv = tp[:, :].bitcast(BF16)
                    for c in range(8):
                        self.tr(tpv[:, c * 128:(c + 1) * 128], ub[:, c * 128:(c + 1) * 128], self.identb[:, :],
                                [ubb, self.identb_b], [tpb])
                    self.cp("act", st[:, :, j * 128:(j + 1) * 128], tpv.rearrange("p (c t) -> p c t", c=8), [tpb], [stb])
                    vt, vtb = self.r_x.next()
                    self.dma(vt[:, :], dr["cache_diff_v"][k0:k0 + 128, :], [], [vtb])
                    vb, vbb = self.r_vb.next()
                    vb3 = vb[:, 0:1032].rearrange("p (h c) -> p h c", h=8)
                    self.memset("pool", vb3[:, :, 128:129], 1.0, [vbb])
                    self.cp("dve", vb3[:, :, 0:128], vt[:, :].rearrange("p (h c) -> p h c", h=8), [vtb], [vbb])
                    self.dma(VAd[k0:k0 + 128, 0:1032], vb[:, 0:1032], [vbb], [nb_(("pv", k0))])
                kp, nbk = pblk[0][0], 128 * len(pblk)
                for two in range(2):
                    self.dma(KTv[two, 0:64, :, kp:kp + nbk], st[two * 64:(two + 1) * 64, :, 0:nbk], [stb],
                             [nb_(("pk", kp, two))])
                self.dma(KTd[:, 64:66, kp:kp + nbk], ones2[:, :, 0:nbk], [ones2b], [nb_(("p1", kp))])
            for blk in g.blocks:
                uT, uTb, nb = self.make_uT(g, blk, first)
                t0b = blk[0][0]
                kpos = g.past + t0b
                for which, dst in ((0, QTv), (1, KTv)):
                    pos = t0b if which == 0 else kpos
                    for hf in range(2):
                        st_, stb = self.r_st.next()
                        st = st_[:, :].rearrange("p (c t) -> p c t", c=4)
                        for c4 in range(4):
                            c = hf * 4 + c4
                            ps, pb = self.proj_fm(uT, uTb, nb, which * D + c * 128, which * D + (c + 1) * 128)
                            if which == 0:
                                self.act(st[:, c4, 0:nb], ps[:, 0:nb], AF.Copy, [pb], [stb], scale=0.125)
                            else:
                                self.cp("dve", st[:, c4, 0:nb], ps[:, 0:nb], [pb], [stb])
                        for two in range(2):
                            self.dma(dst[two, 0:64, hf * 4:hf * 4 + 4, pos:pos + nb], st[two * 64:(two + 1) * 64, :, 0:nb],
                                     [stb], [nb_(("qk", which, t0b, two, hf))])
                self.dma(KTd[:, 64:66, kpos:kpos + nb], ones2[:, :, 0:nb], [ones2b], [nb_(("k1", t0b))])
                QTm = QTd.rearrange("(h m) r s -> m h r s", m=2)
                for m_ in range(2):
                    self.dma(QTm[m_, :, 64:66, t0b:t0b + nb], crow[m_ * 8:(m_ + 1) * 8, :, 0:nb], [crowb],
                             [nb_(("qc", t0b, m_))])
                off = 0
                for (t0, n) in blk:
                    for which, oname in ((1, "diff_k_"), (2, "diff_v_")):
                        ft, fb = self.r_f.next()
                        for cg in range(2):
                            ps, pb = self.proj_tm(uT, uTb, off, n, which * D + cg * 512, which * D + (cg + 1) * 512)
                            self.cp("act" if cg == 0 else "dve", ft[:n, cg * 512:(cg + 1) * 512], ps[:n, :], [pb], [fb])
                        self.dma(dr[oname + nm][t0:t0 + n, :], ft[:n, :], [fb], [self.buf((oname, g.gi, t0))])
                        if which == 2:
                            vb, vbb = self.r_vb.next()
                            vb3 = vb[:n, 0:1032].rearrange("p (h c) -> p h c", h=8)
                            self.memset("pool", vb3[:, :, 128:129], 1.0, [vbb])
                            self.cp("dve", vb3[:, :, 0:128], ft[:n, :].rearrange("p (h c) -> p h c", h=8), [fb], [vbb])
                            self.dma(VAd[g.past + t0:g.past + t0 + n, 0:1032], vb[:n, 0:1032], [vbb], [nb_(("v", t0))])
                    zt, ztb = self.r_h.next()
                    for cg in range(2):
                        ps, pb = self.proj_tm(uT, uTb, off, n, 3 * D + cg * 512, 3 * D + (cg + 1) * 512)
                        self.act(zt[:n, cg * 512:(cg + 1) * 512], ps[:n, :], AF.Silu, [pb], [ztb])
                    self.dma(ZSd[t0:t0 + n, 0:D], zt[:n, 0:D], [ztb], [nb_(("z", t0))])
                    off += n
            ogw = []
            for vh in range(16):
                h, m = vh // 2, vh % 2

                def extras(ki, k0, nk, qi, q0a, n, h=h):
                    if k0 == q0a:
                        hi, hib, lo, lob = Hh[h][0]
                        return [(hi[:, 0:nk], J[:, 0:n], [hib, Jb]), (lo[:, 0:nk], J[:, 0:n], [lob, Jb]),
                                (self.identb[:nk, :nk], dmk[:nk, :n], [self.identb_b, dmkb])]
                    if k0 == q0a - 128:
                        hi, hib, lo, lob = Hh[h][1]
                        return [(hi[:, 0:nk], J[:, 0:n], [hib, Jb]), (lo[:, 0:nk], J[:, 0:n], [lob, Jb])]
                    return []

                def finish(bi, blk, accs, per_bank, w, h=h, m=m, vh=vh):
                    og, ogb = self.r_x.next()
                    for qi, (q0, n) in enumerate(blk):
                        acc, accb = accs[qi // per_bank]
                        col = (qi % per_bank) * w
                        rd, rdb = self.r_small.next()
                        self.recip(rd[:n, 0:1], acc[:n, col + 128:col + 129], [accb], [rdb])
                        self.ts("dve", og[:n, qi * 128:(qi + 1) * 128], acc[:n, col:col + 128], rd[:n, 0:1], None,
                                ALU.mult, None, [accb, rdb], [ogb])
                    q0b = blk[0][0]
                    kb = self.buf(("diff_og", g.gi, L, vh, bi))
                    ogw.append(kb)
                    c0 = m * D + h * 128
                    if blk[0][1] == 128:
                        nt = len(blk)
                        self.dma(OGd[q0b:q0b + nt * 128, c0:c0 + 128].rearrange("(qi p) c -> p qi c", p=128),
                                 og[:, 0:nt * 128].rearrange("p (qi c) -> p qi c", c=128), [ogb], [kb])
                    else:
                        n = blk[0][1]
                        self.dma(OGd[q0b:q0b + n, c0:c0 + 128], og[:n, 0:128], [ogb], [kb])
                self.attn_head(g, KTd[vh][0:66, :], QTd[vh][0:66, :], VAd, h * 129, 128, 66, wb, extras, finish)
            for (t0, n) in g.tiles:
                o1, o1b = self.r_x.next()
                self.dma(o1[:n, :], OGd[t0:t0 + n, 0:D], ogw, [o1b])
                o2, o2b = self.r_f.next()
                self.dma(o2[:n, :], OGd[t0:t0 + n, D:2 * D], ogw, [o2b])
                self.stt(o1[:n, :], o2[:n, :], nlam[:n, 0:1], o1[:n, :], ALU.mult, ALU.add, [o2b, nlamb, o1b], [o1b])
                self.tt("dve", o2[:n, :], o1[:n, :], o1[:n, :], ALU.mult, [o1b], [o2b])
                ss, ssb = self.r_small.next()
                self.P.op("dve", lambda e, ss=ss, o2=o2, n=n: e.tensor_reduce(
                    out=ss[:n, 0:8], in_=o2[:n, :].rearrange("p (h c) -> p h c", h=8), axis=AX.X, op=ALU.add),
                    [o2b], [ssb])
                self.rsqrt(ss[:n, 0:8], ss[:n, 0:8], [ssb], [ssb], scale=1.0 / 128, eps=NORM_EPS)
                self.tt("dve", o1[:n, :].rearrange("p (h c) -> p h c", h=8), o1[:n, :].rearrange("p (h c) -> p h c", h=8),
                        ss[:n, 0:8].unsqueeze(2).to_broadcast([n, 8, 128]), ALU.mult, [o1b, ssb], [o1b])
                self.tt("dve", o1[:n, :].rearrange("p (h c) -> p h c", h=8), o1[:n, :].rearrange("p (h c) -> p h c", h=8),
                        slw[:n, :].unsqueeze(1).to_broadcast([n, 8, 128]), ALU.mult, [o1b, slwb], [o1b])
                zs, zsb = self.r_h.next()
                self.dma(zs[:n, 0:D], ZSd[t0:t0 + n, 0:D], wb, [zsb])
                og, ogb = self.r_h.next()
                self.tt("dve", og[:n, 0:D], o1[:n, :], zs[:n, 0:D], ALU.mult, [o1b, zsb], [ogb])
                self.pass_c_tile(g, t0, n, first, last, og, ogb, 8)

    def layer_ret(self, L, first, last):
        dr = self.dr
        P = self.P
        A = self.arena
        LG = [math.log1p(-2.0 ** (-5.0 - h)) for h in range(4)]
        P.barrier(); A.reset()
        self.load_w(self.W, self.Wb, dr["ret_w_in"], 4096)
        r_st = Ring(self, "rst", [128, 8, 512], BF16, 1, arena=A)
        r_cs = Ring(self, "rcs", [128, 2, 512], F32, 1, arena=A)
        r_vt = Ring(self, "rvt", [128, 2048], BF16, 2, arena=A)
        wbs = {}
        for g in self.groups:
            nm = g.name
            wb = wbs[g.gi] = []

            def nb_(key, wb=wb, g=g):
                b = self.buf(("ret", g.gi) + key)
                wb.append(b)
                return b
            for blk in g.blocks:
                uT, uTb, nb = self.make_uT(g, blk, first)
                t0b = blk[0][0]
                cs, csb = r_cs.next()
                self.dma(cs[:, 0, 0:nb], dr["rot_cos_" + nm][:, t0b:t0b + nb], [], [csb])
                self.dma(cs[:, 1, 0:nb], dr["rot_sin_" + nm][:, t0b:t0b + nb], [], [csb])
                for which, dname in ((0, "QR_"), (1, "KR_")):
                    st, stb = r_st.next()
                    for h in range(4):
                        c0 = which * D + h * 256
                        pe_, peb = self.proj_fm(uT, uTb, nb, c0, c0 + 128)
                        po_, pob = self.proj_fm(uT, uTb, nb, c0 + 128, c0 + 256)
                        xe, xeb = self.r_f.next()
                        sc = 0.0625 if which == 0 else 1.0
                        self.act(xe[:, 0:nb], pe_[:, 0:nb], AF.Copy, [peb], [xeb], scale=sc)
                        self.act(xe[:, 512:512 + nb], po_[:, 0:nb], AF.Copy, [pob], [xeb], scale=sc)
                        tm, tmb = self.r_x.next()
                        self.tt("dve", tm[:, 0:nb], xe[:, 0:nb], cs[:, 0, 0:nb], ALU.mult, [xeb, csb], [tmb])
                        self.tt("dve", tm[:, 512:512 + nb], xe[:, 512:512 + nb], cs[:, 1, 0:nb], ALU.mult, [xeb, csb], [tmb])
                        self.tt("dve", st[:, 2 * h, 0:nb], tm[:, 0:nb], tm[:, 512:512 + nb], ALU.subtract, [tmb], [stb])
                        tm, tmb = self.r_x.next()
                        self.tt("dve", tm[:, 0:nb], xe[:, 0:nb], cs[:, 1, 0:nb], ALU.mult, [xeb, csb], [tmb])
                        self.tt("dve", tm[:, 512:512 + nb], xe[:, 512:512 + nb], cs[:, 0, 0:nb], ALU.mult, [xeb, csb], [tmb])
                        self.tt("dve", st[:, 2 * h + 1, 0:nb], tm[:, 0:nb], tm[:, 512:512 + nb], ALU.add, [tmb], [stb])
                    self.dma(dr[dname + nm].rearrange("he p t -> p he t")[:, :, t0b:t0b + nb], st[:, :, 0:nb], [stb],
                             [nb_((dname, t0b))])
                off = 0
                for (t0, n) in blk:
                    vt, vtb = r_vt.next()
                    for cg in range(4):
                        ps, pb = self.proj_tm(uT, uTb, off, n, 2 * D + cg * 512, 2 * D + (cg + 1) * 512)
                        self.cp("act" if cg % 2 == 0 else "dve", vt[:n, cg * 512:(cg + 1) * 512], ps[:n, :], [pb], [vtb])
                    self.dma(dr["OG_" + nm + "v"][t0:t0 + n, :], vt[:n, :], [vtb], [nb_(("v", t0))])
                    off += n
        self.load_w(self.W, self.Wb, dr["ret_w_in"], 2048, c_off=4096)
        for g in self.groups:
            nm = g.name
            wb = wbs[g.gi]
            for blk in g.blocks:
                uT, uTb, nb = self.make_uT(g, blk, first)
                off = 0
                for (t0, n) in blk:
                    zt, ztb = r_vt.next()
                    for cg in range(4):
                        ps, pb = self.proj_tm(uT, uTb, off, n, cg * 512, (cg + 1) * 512)
                        self.act(zt[:n, cg * 512:(cg + 1) * 512], ps[:n, :], AF.Silu, [pb], [ztb])
                    b = self.buf(("ret", g.gi, "z", t0))
                    wb.append(b)
                    self.dma(dr["ZS_" + nm][t0:t0 + n, :], zt[:n, :], [ztb], [b])
                    off += n
        P.barrier(); A.reset()
        self.load_w(self.WO, self.WOb, dr["ret_w_out"], D)
        self.load_w(self.W, self.Wb, dr["ret_w_out"][D:2 * D, :], D)
        dmk, dmkb = A.alloc([128, 4, 128], F32), Buf()
        self.dma(dmk[:, :, :], dr["ret_dmaskT"][:, :, :], [], [dmkb])
        qdc, qdcb = A.alloc([128, 4, 128], F32), Buf()
        self.dma(qdc[:, :, :], dr["ret_qdec"][:, :, :], [], [qdcb])
        kdc, kdcb = A.alloc([128, 8], F32), Buf()
        self.dma(kdc[:, :], dr["ret_kdec"][:, :], [], [kdcb])
        S, Sb = A.alloc([128, 2, 512], F32), Buf()
        Sbf, Sbfb = A.alloc([128, 2, 512], BF16), Buf()
        r_q = Ring(self, "rq", [128, 2, 512], BF16, 2, arena=A)
        r_k = Ring(self, "rk", [128, 2, 512], BF16, 2, arena=A)
        r_qd = Ring(self, "rqd", [128, 2, 128], BF16, 2, arena=A)
        r_v = Ring(self, "rv", [128, 4, 512], BF16, 2, arena=A)
        r_at = Ring(self, "rat", [128, 128], BF16, 2, arena=A)
        r_kd = Ring(self, "rkd", [128, 2, 128], BF16, 2, arena=A)
        r_o = Ring(self, "ro", [128, 512], BF16, 3, arena=A)
        ogws = {}
        for g in self.groups:
            nm = g.name
            wb = wbs[g.gi]
            ogw = ogws[g.gi] = []
            QRd = dr["QR_" + nm].rearrange("(h e) p t -> h p e t", e=2)
            KRd = dr["KR_" + nm].rearrange("(h e) p t -> h p e t", e=2)
            VRd = dr["OG_" + nm + "v"]
            for h in range(4):
                if g.gi == 0:
                    self.memset("pool", S[:, :, :], 0.0, [Sb])
                else:
                    self.dma(S[:, :, :], dr["state_ret"][h].rearrange("(e p) v -> p e v", e=2), [], [Sb])
                self.cp("act", Sbf[:, :, :], S[:, :, :], [Sb], [Sbfb])
                for blk in g.blocks:
                    t0b = blk[0][0]
                    nb = sum(n for _, n in blk)
                    qt, qtb = r_q.next()
                    self.dma(qt[:, :, 0:nb], QRd[h, :, :, t0b:t0b + nb], wb, [qtb])
                    kt, ktb = r_k.next()
                    self.dma(kt[:, :, 0:nb], KRd[h, :, :, t0b:t0b + nb], wb, [ktb])
                    vt, vtb = r_v.next()
                    if blk[0][1] == 128:
                        self.dma(vt[:, 0:len(blk), :], VRd[t0b:t0b + nb, h * 512:(h + 1) * 512]
                                 .rearrange("(c p) v -> p c v", p=128), wb, [vtb])
                    else:
                        self.dma(vt[:nb, 0, :], VRd[t0b:t0b + nb, h * 512:(h + 1) * 512], wb, [vtb])
                    off = 0
                    for ci, (t0, n) in enumerate(blk):
                        kcol = h if n == 128 else 4 + h
                        cdec = math.exp(LG[h] * n)
                        ps, pb = self.ps_mm.next()
                        for e in range(2):
                            self.mm(ps[:n, 0:n], kt[:, e, off:off + n], qt[:, e, off:off + n], e == 0, e == 1,
                                    [ktb, qtb], [pb])
                        at, atb = r_at.next()
                        self.tt("dve", at[:n, 0:n], ps[:n, 0:n], dmk[:n, h, 0:n], ALU.mult, [pb, dmkb], [atb])
                        qd, qdb = r_qd.next()
                        for e in range(2):
                            self.tt("dve", qd[:, e, 0:n], qt[:, e, off:off + n], qdc[:, h, 0:n], ALU.mult,
                                    [qtb, qdcb], [qdb])
                        po, pob = self.ps_acc.next()
                        self.mm(po[:n, :], at[:n, 0:n], vt[:n, ci, :], True, False, [atb, vtb], [pob])
                        for e in range(2):
                            self.mm(po[:n, :], qd[:, e, 0:n], Sbf[:, e, :], False, e == 1, [qdb, Sbfb], [pob])
                        st, stb = self.r_small.next()
                        P.op("dve", lambda e_, st=st, po=po, n=n: e_.bn_stats(out=st[:n, 0:6], in_=po[:n, :]), [pob], [stb])
                        P.op("dve", lambda e_, st=st, n=n: e_.bn_aggr(out=st[:n, 6:8], in_=st[:n, 0:6]), [stb], [stb])
                        self.rsqrt(st[:n, 10:11], st[:n, 7:8], [stb], [stb], eps=LN_EPS)
                        self.stt(st[:n, 11:12], st[:n, 6:7], -1.0, st[:n, 10:11], ALU.mult, ALU.mult, [stb], [stb])
                        ot, otb = r_o.next()
                        self.act(ot[:n, :], po[:n, :], AF.Identity, [pob, stb], [otb], bias=st[:n, 11:12], scale=st[:n, 10:11])
                        b = self.buf(("ret_og", g.gi, h, t0))
                        ogw.append(b)
                        self.dma(dr["OG_" + nm][t0:t0 + n, h * 512:(h + 1) * 512], ot[:n, :], [otb], [b])
                        kd, kdb = r_kd.next()
                        tp, tpb = self.ps_tp.next()
                        tpv = tp[:, :].bitcast(BF16)
                        for e in range(2):
                            self.tr(tpv[:n, e * 128:(e + 1) * 128], kt[:, e, off:off + n], self.identb[:, :],
                                    [ktb, self.identb_b], [tpb])
                        self.ts("dve", kd[:n, :, :], tpv[:n, 0:256].rearrange("p (e d) -> p e d", e=2),
                                kdc[:n, kcol:kcol + 1], None, ALU.mult, None, [tpb, kdcb], [kdb])
                        for e in range(2):
                            pu, pub = self.ps_mm.next()
                            self.mm(pu[:, :], kd[:n, e, :], vt[:n, ci, :], True, True, [kdb, vtb], [pub])
                            self.stt(S[:, e, :], S[:, e, :], cdec, pu[:, :], ALU.mult, ALU.add, [Sb, pub], [Sb])
                            self.cp("act", Sbf[:, e, :], S[:, e, :], [Sb], [Sbfb])
                        off += n
                self.dma(dr["ret_state_" + nm][h].rearrange("(e p) v -> p e v", e=2), S[:, :, :], [Sb],
                         [self.buf(("ret_state", g.gi, h))])
        P.barrier(); A.reset()
        gnw, gnwb = A.alloc([128, 2048], F32), Buf()
        self.dma(gnw[:, :], dr["ret_gn_w"][0:1, :].to_broadcast([128, 2048]), [], [gnwb])
        r_og = Ring(self, "rog2", [128, 2048], BF16, 2, arena=A)
        r_zs = Ring(self, "rzs2", [128, 2048], BF16, 2, arena=A)
        r_gz = Ring(self, "rgz2", [128, 2048], BF16, 2, arena=A)

        def wo(k):
            if k < 8:
                return self.WO[:, k, :], self.WOb
            return self.W[:, k - 8, 0:D], self.Wb
        for g in self.groups:
            nm = g.name
            for (t0, n) in g.tiles:
                og, ogb = r_og.next()
                self.dma(og[:n, :], dr["OG_" + nm][t0:t0 + n, :], ogws[g.gi], [ogb])
                zs, zsb = r_zs.next()
                self.dma(zs[:n, :], dr["ZS_" + nm][t0:t0 + n, :], wbs[g.gi], [zsb])
                gz, gzb = r_gz.next()
                self.tt("dve", gz[:n, :], og[:n, :], gnw[:n, :], ALU.mult, [ogb, gnwb], [gzb])
                self.tt("dve", gz[:n, :], gz[:n, :], zs[:n, :], ALU.mult, [gzb, zsb], [gzb])
                self.pass_c_tile(g, t0, n, first, last, gz, gzb, 16, wo=wo)

    def layer_gdn(self, L, first, last):
        dr = self.dr
        P = self.P
        A = self.arena
        P.barrier(); A.reset()
        self.load_w(self.W, self.Wb, dr["gdn_w_in"], 4112)
        self.load_w(self.WO, self.WOb, dr["gdn_w_out"], D)
        cw, cwb = A.alloc([128, 24, 4], F32), Buf()
        self.dma(cw[:, :, :], dr["gdn_convw"][:, :, :], [], [cwb])
        halo, halob = A.alloc([128, 24, 3], F32), Buf()
        onesb, onesbb = A.alloc([128, 128], BF16), Buf()
        self.memset("pool", onesb[:, :], 1.0, [onesbb])
        nhalf, nhalfb = A.alloc([128, 512], F32), Buf()
        self.memset("pool", nhalf[:, :], -0.5, [nhalfb])
        sc8, sc8b = A.alloc([8, 4], F32), Buf()
        self.dma(sc8[:, 0:1], dr["gdn_dt_bias"][:, :], [], [sc8b])
        self.dma(sc8[:, 2:3], dr["gdn_a_log"][:, :], [], [sc8b])
        self.act(sc8[:, 3:4], sc8[:, 2:3], AF.Exp, [sc8b], [sc8b])
        self.ts("dve", sc8[:, 1:2], sc8[:, 3:4], -1.0, None, ALU.mult, None, [sc8b], [sc8b])
        r_xr = Ring(self, "rxr", [128, 516], F32, 2, arena=A)
        r_y = Ring(self, "ry", [128, 512], F32, 3, arena=A)
        r_sq = Ring(self, "rsq", [128, 512], BF16, 2, arena=A)
        r_st = Ring(self, "rst", [128, 8, 512], BF16, 2, arena=A)
        r_g8 = Ring(self, "rg8", [8, 512], F32, 6, arena=A)
        wbs = {}
        for g in self.groups:
            nm = g.name
            wb = wbs[g.gi] = []

            def nb_(key, wb=wb, g=g):
                b = self.buf(("gdn", g.gi) + key)
                wb.append(b)
                return b
            if g.gi == 0:
                self.memset("pool", halo[:, :, :], 0.0, [halob])
            else:
                self.dma(halo[:, :, :], dr["gdn_conv_in"][:, :, :], [], [halob])
            for blk in g.blocks:
                uT, uTb, nb = self.make_uT(g, blk, first)
                t0b = blk[0][0]
                for which, dname in ((0, "QG_"), (1, "KG_"), (2, "VG_")):
                    st, stb = r_st.next()
                    for h in range(8):
                        c = which * 8 + h
                        ps, pb = self.proj_fm(uT, uTb, nb, c * 128, (c + 1) * 128)
                        xr, xrb = r_xr.next()
                        self.cp("act", xr[:, 0:3], halo[:, c, :], [halob], [xrb])
                        self.cp("act", xr[:, 3:3 + nb], ps[:, 0:nb], [pb], [xrb])
                        self.cp("act", halo[:, c, :], xr[:, nb:nb + 3], [xrb], [halob])
                        y, yb = r_y.next()
                        self.ts("dve", y[:, 0:nb], xr[:, 0:nb], cw[:, c, 0:1], None, ALU.mult, None, [xrb, cwb], [yb])
                        for j in range(1, 4):
                            self.stt(y[:, 0:nb], xr[:, j:j + nb], cw[:, c, j:j + 1], y[:, 0:nb], ALU.mult, ALU.add,
                                     [xrb, cwb, yb], [yb])
                        self.act(y[:, 0:nb], y[:, 0:nb], AF.Silu, [yb], [yb])
                        if which == 2:
                            self.cp("dve", st[:, h, 0:nb], y[:, 0:nb], [yb], [stb])
                        else:
                            sq, sqb = r_sq.next()
                            self.tt("dve", sq[:, 0:nb], y[:, 0:nb], y[:, 0:nb], ALU.mult, [yb], [sqb])
                            p2, p2b = self.ps_mm.next()
                            self.mm(p2[:, 0:nb], onesb[:, :], sq[:, 0:nb], True, True, [onesbb, sqb], [p2b])
                            rn, rnb = r_y.next()
                            self.rsqrt(rn[:, 0:nb], p2[:, 0:nb], [p2b], [rnb], eps=NORM_EPS)
                            if which == 0:
                                self.stt(st[:, h, 0:nb], y[:, 0:nb], 128.0 ** -0.5, rn[:, 0:nb], ALU.mult, ALU.mult,
                                         [yb, rnb], [stb])
                            else:
                                self.tt("dve", st[:, h, 0:nb], y[:, 0:nb], rn[:, 0:nb], ALU.mult, [yb, rnb], [stb])
                    self.dma(dr[dname + nm].rearrange("h p t -> p h t")[:, :, t0b:t0b + nb], st[:, :, 0:nb], [stb],
                             [nb_((dname, t0b))])
                ps, pb = self.proj_fm(uT, uTb, nb, 3072, 3080)
                bt, btb = r_g8.next()
                self.act(bt[:, 0:nb], ps[0:8, 0:nb], AF.Exp, [pb], [btb], scale=-1.0)
                self.ts("dve", bt[:, 0:nb], bt[:, 0:nb], 1.0, None, ALU.add, None, [btb], [btb])
                self.recip(bt[:, 0:nb], bt[:, 0:nb], [btb], [btb])
                ps, pb = self.proj_fm(uT, uTb, nb, 3080, 3088)
                gt, gtb = r_g8.next()
                self.act(gt[:, 0:nb], ps[0:8, 0:nb], AF.Exp, [pb, sc8b], [gtb], bias=sc8[:, 0:1])
                self.act(gt[:, 0:nb], gt[:, 0:nb], AF.Ln, [gtb], [gtb], bias=1.0)
                self.ts("dve", gt[:, 0:nb], gt[:, 0:nb], sc8[:, 1:2], None, ALU.mult, None, [gtb, sc8b], [gtb])
                Gt, Gtb = r_g8.next()
                on8, on8b = r_g8.next()
                self.memset("pool", on8[:, 0:128], 1.0, [on8b])
                for off in range(0, nb, 64):
                    n = min(64, nb - off)
                    P.op("dve", lambda e, Gt=Gt, gt=gt, on8=on8, off=off, n=n: e.tensor_tensor_scan(
                        out=Gt[:, off:off + n], data0=on8[:, 0:n], data1=gt[:, off:off + n], initial=0.0,
                        op0=ALU.mult, op1=ALU.add), [gtb, on8b], [Gtb])
                btb16, btb16b = r_g8.next()
                bt16v = btb16[:, 0:256].bitcast(BF16)
                self.cp("act", bt16v[:, 0:nb], bt[:, 0:nb], [btb], [btb16b])
                self.dma(dr["BTb_" + nm][:, t0b:t0b + nb], bt16v[:, 0:nb], [btb16b], [nb_(("btb", t0b))])
                self.dma(dr["GT_" + nm][:, t0b:t0b + nb], Gt[:, 0:nb], [Gtb], [nb_(("gt", t0b))])
                off = 0
                for ti, (t0, n) in enumerate(blk):
                    tp, tpb = self.ps_tp.next()
                    self.tr(tp[0:n, 0:8], bt[:, off:off + n], self.identf[0:8, 0:8], [btb, self.identf_b], [tpb])
                    tp2, tp2b = self.ps_tp.next()
                    self.tr(tp2[0:n, 0:8], Gt[:, off:off + n], self.identf[0:8, 0:8], [Gtb, self.identf_b], [tp2b])
                    gb, gbb = self.r_small.next()
                    self.cp("act", gb[:n, 0:8], tp[0:n, 0:8], [tpb], [gbb])
                    self.cp("act", gb[:n, 8:16], tp2[0:n, 0:8], [tp2b], [gbb])
                    self.dma(dr["GB_" + nm][t0:t0 + n, :], gb[:n, 0:16], [gbb], [nb_(("gb", t0))])
                    zt, ztb = self.r_h.next()
                    for cg in range(2):
                        ps, pb = self.proj_tm(uT, uTb, off, n, 3088 + cg * 512, 3088 + (cg + 1) * 512)
                        self.act(zt[:n, cg * 512:(cg + 1) * 512], ps[:n, :], AF.Silu, [pb], [ztb])
                    self.dma(dr["ZS_" + nm][t0:t0 + n, 0:D], zt[:n, 0:D], [ztb], [nb_(("z", t0))])
                    off += n
                if blk is g.blocks[-1]:
                    co, cob = self.r_f.next()
                    for cg in range(6):
                        ps, pb = self.ps_mm.next()
                        for k in range(8):
                            self.mm(ps[0:3, :], uT[:, k, nb - 3:nb], self.W[:, k, cg * 512:(cg + 1) * 512], k == 0, k == 7,
                                    [uTb, self.Wb], [pb])
                        self.cp("act", co[0:3, (cg % 2) * 512:(cg % 2 + 1) * 512], ps[0:3, :], [pb], [cob])
                        if cg % 2 == 1:
                            self.dma(dr["gdn_conv_" + nm][:, (cg - 1) * 512:(cg + 1) * 512], co[0:3, :], [cob],
                                     [self.buf(("gdn_conv", g.gi, cg))])
                            if cg < 5:
                                co, cob = self.r_f.next()
        import os
        if os.environ.get("GDN_STOP") == "A":
            return
        P.barrier(); A.reset()
        msk, mskb = A.alloc([128, 3, 128], F32), Buf()
        self.dma(msk[:, :, :], dr["gdn_masks"][:, :, :], [], [mskb])
        S, Sb = A.alloc([128, 8, 128], F32), Buf()
        Sbf, Sbfb = A.alloc([128, 8, 128], BF16), Buf()
        r_qkv = Ring(self, "rqkv", [128, 3, 8, 128], BF16, 1, arena=A)
        r_gb = Ring(self, "rgb", [128, 8, 128], F32, 1, arena=A)
        r_bb = Ring(self, "rbb", [128, 8, 128], BF16, 1, arena=A)
        r_gbk = Ring(self, "rgbk", [64, 2, 16], F32, 2, arena=A)
        r_F = Ring(self, "rF", [128, 8, 64], F32, 6, arena=A)
        r_B = Ring(self, "rB", [128, 8, 64], BF16, 8, arena=A)
        r_L = Ring(self, "rL", [128, 8, 64], BF16, 4, arena=A)
        r_B2 = Ring(self, "rB2", [64, 8, 128], BF16, 4, arena=A)
        r_sq = Ring(self, "rsq", [64, 8, 128], F32, 1, arena=A)
        r_sc = Ring(self, "rsc", [64, 48], F32, 2, arena=A)
        ogws = {}

        def bc(ap, shape):
            return ap.to_broadcast(shape)
        for g in self.groups:
            nm = g.name
            wb = wbs[g.gi]
            ogw = ogws[g.gi] = []
            if g.gi == 0:
                self.memset("pool", S[:, :, :], 0.0, [Sb])
            else:
                self.dma(S[:, :, :], dr["state_gdn"].rearrange("h d v -> d h v"), [], [Sb])
            self.cp("act", Sbf[:, :, :], S[:, :, :], [Sb], [Sbfb])
            for (tt0, tn) in g.tiles:
                qkv, qkvb = r_qkv.next()
                for i, dname in enumerate(("QG_", "KG_", "VG_")):
                    self.dma(qkv[:, i, :, 0:tn], dr[dname + nm].rearrange("h p t -> p h t")[:, :, tt0:tt0 + tn], wb, [qkvb])
                gb, gbb = r_gb.next()
                self.dma(gb[:, :, 0:tn], bass.AP(self.dh["GT_" + nm], tt0, [[0, 128], [g.T, 8], [1, tn]]), wb, [gbb])
                bb, bbb = r_bb.next()
                self.dma(bb[:, :, 0:tn], bass.AP(self.dh["BTb_" + nm], tt0, [[0, 128], [g.T, 8], [1, tn]]), wb, [bbb])
                gbk, gbkb = r_gbk.next()
                if tn % 64 == 0:
                    self.dma(gbk[0:64, 0:tn // 64, :], dr["GB_" + nm][tt0:tt0 + tn, :].rearrange("(c p) x -> p c x", p=64),
                             wb, [gbkb])
                else:
                    self.dma(gbk[:tn, 0, :], dr["GB_" + nm][tt0:tt0 + tn, :], wb, [gbkb])
                for ci, off in enumerate(range(0, tn, 64)):
                    n = min(64, tn - off)
                    t0 = tt0 + off
                    Qt = qkv[:, 0, :, off:off + n]
                    Kt = qkv[:, 1, :, off:off + n]
                    Vt = qkv[:, 2, :, off:off + n]
                    Gb = gb[:, :, off:off + n]
                    Gp = gbk[:n, ci, 8:16]
                    Bp = gbk[:n, ci, 0:8]
                    mk = lambda k_: msk[:n, k_, 0:n].unsqueeze(1).to_broadcast([n, 8, n])
                    eG, eGb = r_F.next()
                    self.act(eG[:, :, 0:n], Gb, AF.Exp, [gbb], [eGb])
                    KbT, KbTb = r_B.next()
                    self.tt("dve", KbT[:, :, 0:n], Kt, bb[:, :, off:off + n], ALU.mult, [qkvb, bbb], [KbTb])
                    qdT, qdTb = r_L.next()
                    self.tt("dve", qdT[:, :, 0:n], Qt, eG[:, :, 0:n], ALU.mult, [qkvb, eGb], [qdTb])
                    sc, scb = r_sc.next()
                    self.act(sc[:n, 16:24], Gp, AF.Exp, [gbkb], [scb])
                    self.tt("dve", sc[:n, 0:8], sc[:n, 16:24], Bp, ALU.mult, [scb, gbkb], [scb])
                    self.tt("dve", sc[:n, 24:32], Gp, gb[:n, :, off + n - 1], ALU.subtract, [gbkb, gbb], [scb])
                    self.act(sc[:n, 8:16], sc[:n, 24:32], AF.Exp, [scb], [scb], scale=-1.0)
                    tD, tDb = r_F.next()
                    self.tt("dve", tD[:n, :, 0:n], gb[:n, :, off:off + n], Gp.unsqueeze(2).to_broadcast([n, 8, n]),
                            ALU.subtract, [gbb, gbkb], [tDb])
                    D1, D1b = r_F.next()
                    self.ts("dve", D1[:n, :, 0:n], tD[:n, :, 0:n], 0.0, None, ALU.min, None, [tDb], [D1b])
                    self.act(D1[:n, :, 0:n], D1[:n, :, 0:n], AF.Exp, [D1b], [D1b])
                    D2, D2b = r_F.next()
                    self.ts("dve", D2[:n, :, 0:n], tD[:n, :, 0:n], 0.0, None, ALU.max, None, [tDb], [D2b])
                    self.act(D2[:n, :, 0:n], D2[:n, :, 0:n], AF.Exp, [D2b], [D2b], scale=-1.0)
                    D1i, D1ib = r_F.next()
                    self.tt("dve", D1i[:n, :, 0:n], D1[:n, :, 0:n], mk(2), ALU.mult, [D1b, mskb], [D1ib])
                    self.tt("dve", D1[:n, :, 0:n], D1[:n, :, 0:n], mk(0), ALU.mult, [D1b, mskb], [D1b])
                    self.tt("dve", D2[:n, :, 0:n], D2[:n, :, 0:n], mk(1), ALU.mult, [D2b, mskb], [D2b])
                    paA, paAb = self.ps_mm.next()
                    paB, paBb = self.ps_mm.next()
                    paQ, paQb = self.ps_mm.next()
                    for h in range(8):
                        self.mm(paA[:n, h * 64:h * 64 + n], Kt[:, h, :], KbT[:, h, 0:n], True, True, [qkvb, KbTb], [paAb], skip=True)
                    for h in range(8):
                        self.mm(paB[:n, h * 64:h * 64 + n], KbT[:, h, 0:n], Kt[:, h, :], True, True, [qkvb, KbTb], [paBb], skip=True)
                    for h in range(8):
                        self.mm(paQ[:n, h * 64:h * 64 + n], Kt[:, h, :], Qt[:, h, :], True, True, [qkvb], [paQb], skip=True)
                    v3 = lambda ps_: ps_[:n, :].rearrange("p (h c) -> p h c", h=8)[:, :, 0:n]
                    Pm, Pmb = r_F.next()
                    self.tt("dve", Pm[:n, :, 0:n], v3(paA), D1[:n, :, 0:n], ALU.mult, [paAb, D1b], [Pmb])
                    X, Xb = r_B.next()
                    self.cp("dve", X[:n, :, 0:n], Pm[:n, :, 0:n], [Pmb], [Xb])
                    self.tt("dve", Pm[:n, :, 0:n], Pm[:n, :, 0:n], self.identf[:n, 0:n].unsqueeze(1).to_broadcast([n, 8, n]),
                            ALU.add, [Pmb, self.identf_b], [Pmb])
                    Pbf, Pbfb = r_B.next()
                    self.cp("act", Pbf[:n, :, 0:n], Pm[:n, :, 0:n], [Pmb], [Pbfb])
                    Y, Yb = r_B.next()
                    self.tt("dve", Y[:n, :, 0:n], v3(paB), D2[:n, :, 0:n], ALU.mult, [paBb, D2b], [Yb])
                    QK, QKb = r_L.next()
                    self.tt("dve", QK[:n, :, 0:n], v3(paQ), D1i[:n, :, 0:n], ALU.mult, [paQb, D1ib], [QKb])
                    nsteps = max(1, int(math.ceil(math.log2(n))) - 1)
                    for s in range(nsteps):
                        lastst = s == nsteps - 1
                        pxY, pxYb = self.ps_mm.next()
                        for h in range(8):
                            self.mm(pxY[:n, h * 64:h * 64 + n], X[:n, h, 0:n], Y[:n, h, 0:n], True, True, [Xb, Yb], [pxYb], skip=True)
                        if not lastst:
                            pxX, pxXb = self.ps_mm.next()
                            for h in range(8):
                                self.mm(pxX[:n, h * 64:h * 64 + n], Y[:n, h, 0:n], X[:n, h, 0:n], True, True, [Xb, Yb], [pxXb], skip=True)
                        Y2, Y2b = r_B.next()
                        self.cp("act", Y2[:n, :, 0:n], v3(pxY), [pxYb], [Y2b])
                        if not lastst:
                            X2, X2b = r_B.next()
                            self.cp("dve", X2[:n, :, 0:n], v3(pxX), [pxXb], [X2b])
                        pp, ppb = self.ps_mm.next()
                        for h in range(8):
                            self.mm(pp[:n, h * 64:h * 64 + n], Y2[:n, h, 0:n], Pbf[:n, h, 0:n], True, True, [Y2b, Pbfb], [ppb], skip=True)
                        self.tt("dve", Pm[:n, :, 0:n], Pm[:n, :, 0:n], v3(pp), ALU.add, [Pmb, ppb], [Pmb])
                        Pbf, Pbfb = r_L.next() if lastst else r_B.next()
                        self.cp("act", Pbf[:n, :, 0:n], Pm[:n, :, 0:n], [Pmb], [Pbfb])
                        Y, Yb = Y2, Y2b
                        if not lastst:
                            X, Xb = X2, X2b
                    tpK, tpKb = self.ps_tp.next()
                    tpV, tpVb = self.ps_tp.next()
                    tKv = tpK[:, :].bitcast(BF16)
                    tVv = tpV[:, :].bitcast(BF16)
                    for h in range(8):
                        self.tr(tKv[:n, h * 128:(h + 1) * 128], Kt[:, h, :], self.identb[:, :], [qkvb, self.identb_b], [tpKb])
                    for h in range(8):
                        self.tr(tVv[:n, h * 128:(h + 1) * 128], Vt[:, h, :], self.identb[:, :], [qkvb, self.identb_b], [tpVb])
                    t3 = lambda v_: v_[:n, :].rearrange("p (h c) -> p h c", h=8)
                    bcs = lambda a_: a_.unsqueeze(2).to_broadcast([n, 8, 128])
                    kbg, kbgb = r_B2.next()
                    self.tt("dve", kbg[:n, :, :], t3(tKv), bcs(sc[:n, 0:8]), ALU.mult, [tpKb, scb], [kbgb])
                    kdc, kdcb = r_B2.next()
                    self.tt("dve", kdc[:n, :, :], t3(tKv), bcs(sc[:n, 8:16]), ALU.mult, [tpKb, scb], [kdcb])
                    vb_, vbb_ = r_B2.next()
                    self.tt("dve", vb_[:n, :, :], t3(tVv), bcs(Bp), ALU.mult, [tpVb, gbkb], [vbb_])
                    pw, pwb = self.ps_mm.next()
                    for h in range(8):
                        self.mm(pw[:, h * 64:h * 64 + n], kbg[:n, h, :], Pbf[:n, h, 0:n], True, True, [kbgb, Pbfb], [pwb], skip=True)
                    wdT, wdTb = r_B.next()
                    self.act(wdT[:, :, 0:n], pw[:, :].rearrange("p (h c) -> p h c", h=8)[:, :, 0:n], AF.Copy, [pwb], [wdTb],
                             scale=-1.0)
                    pv = [self.ps_acc.next(), self.ps_acc.next()]
                    for h in range(8):
                        pt_, ptb_ = pv[h // 4]
                        o_ = pt_[:n, (h % 4) * 128:(h % 4 + 1) * 128]
                        self.mm(o_, Pbf[:n, h, 0:n], vb_[:n, h, :], True, False, [Pbfb, vbb_], [ptb_], skip=True)
                        self.mm(o_, wdT[:, h, 0:n], Sbf[:, h, :], False, True, [wdTb, Sbfb], [ptb_], skip=True)
                    vr, vrb = r_B2.next()
                    for hb in range(2):
                        self.cp("act", vr[:n, hb * 4:(hb + 1) * 4, :], pv[hb][0][:n, :].rearrange("p (h c) -> p h c", h=4),
                                [pv[hb][1]], [vrb])
                    po = [self.ps_acc.next(), self.ps_acc.next()]
                    for h in range(8):
                        pt_, ptb_ = po[h // 4]
                        o_ = pt_[:n, (h % 4) * 128:(h % 4 + 1) * 128]
                        self.mm(o_, qdT[:, h, 0:n], Sbf[:, h, :], True, False, [qdTb, Sbfb], [ptb_], skip=True)
                        self.mm(o_, QK[:n, h, 0:n], vr[:n, h, :], False, True, [QKb, vrb], [ptb_], skip=True)
                    sq, sqb = r_sq.next()
                    for hb in range(2):
                        self.act(sq[:n, hb * 4:(hb + 1) * 4, :], po[hb][0][:n, :].rearrange("p (h c) -> p h c", h=4),
                                 AF.Square, [po[hb][1]], [sqb])
                    P.op("dve", lambda e, sc=sc, sq=sq, n=n: e.tensor_reduce(out=sc[:n, 32:40], in_=sq[:n, :, :], axis=AX.X,
                                                                         op=ALU.add), [sqb], [scb])
                    self.rsqrt(sc[:n, 32:40], sc[:n, 32:40], [scb], [scb], scale=1.0 / 128, eps=NORM_EPS)
                    ot, otb = r_B2.next()
                    for hb in range(2):
                        self.tt("dve", ot[:n, hb * 4:(hb + 1) * 4, :], po[hb][0][:n, :].rearrange("p (h c) -> p h c", h=4),
                                sc[:n, 32 + hb * 4:36 + hb * 4].unsqueeze(2).to_broadcast([n, 4, 128]), ALU.mult,
                                [po[hb][1], scb], [otb])
                    b = self.buf(("gdn_og", g.gi, t0))
                    ogw.append(b)
                    self.dma(dr["OG_" + nm][t0:t0 + n, 0:D], ot[:n, :, :].rearrange("p h c -> p (h c)"), [otb], [b])
                    self.tt("dve", S[:, :, :], S[:, :, :], eG[:, :, n - 1:n].to_broadcast([128, 8, 128]), ALU.mult,
                            [Sb, eGb], [Sb])
                    pu = [self.ps_mm.next(), self.ps_mm.next()]
                    for h in range(8):
                        pt_, ptb_ = pu[h // 4]
                        self.mm(pt_[:, (h % 4) * 128:(h % 4 + 1) * 128], kdc[:n, h, :], vr[:n, h, :], True, True,
                                [kdcb, vrb], [ptb_], skip=True)
                    for hb in range(2):
                        self.tt("dve", S[:, hb * 4:(hb + 1) * 4, :], S[:, hb * 4:(hb + 1) * 4, :],
                                pu[hb][0][:, :].rearrange("p (h c) -> p h c", h=4), ALU.add, [Sb, pu[hb][1]], [Sb])
                    self.cp("act", Sbf[:, :, :], S[:, :, :], [Sb], [Sbfb])
            self.dma(dr["gdn_state_" + nm].rearrange("h d v -> d h v"), S[:, :, :], [Sb], [self.buf(("gdn_state", g.gi))])
        if os.environ.get("GDN_STOP") == "B":
            return
        P.barrier(); A.reset()
        nw, nwb = A.alloc([128, D], F32), Buf()
        for h in range(8):
            self.dma(nw[:, h * 128:(h + 1) * 128], dr["gdn_norm_w"][0:1, :].to_broadcast([128, 128]), [], [nwb])
        for g in self.groups:
            nm = g.name
            for (t0, n) in g.tiles:
                og, ogb = self.r_h.next()
                self.dma(og[:n, 0:D], dr["OG_" + nm][t0:t0 + n, 0:D], ogws[g.gi], [ogb])
                zs, zsb = self.r_h.next()
                self.dma(zs[:n, 0:D], dr["ZS_" + nm][t0:t0 + n, 0:D], wbs[g.gi], [zsb])
                gz, gzb = self.r_f.next()
                self.tt("dve", gz[:n, :], og[:n, 0:D], nw[:n, :], ALU.mult, [ogb, nwb], [gzb])
                self.tt("dve", og[:n, 0:D], gz[:n, :], zs[:n, 0:D], ALU.mult, [gzb, zsb], [ogb])
                self.pass_c_tile(g, t0, n, first, last, og, ogb, 8)

    def layer_dummy(self, L, first, last):
        self.layer_mod(L)
        self.load_w(self.W, self.Wb, self.dr["fox_w_in"], 4112)
        self.load_w(self.WO, self.WOb, self.dr["fox_w_out"], D)
        for g in self.groups:
            for blk in g.blocks:
                uT, uTb, nb = self.make_uT(g, blk, first)
                off = 0
                for (t0, n) in blk:
                    og, ogb = self.r_h.next()
                    for cg in range(2):
                        ps, pb = self.proj_tm(uT, uTb, off, n, cg * 512, (cg + 1) * 512)
                        self.cp("act", og[:n, cg * 512:(cg + 1) * 512], ps[:n, :], [pb], [ogb])
                    self.pass_c_tile(g, t0, n, first, last, og, ogb, 8)
                    off += n


def build(cfg, mode="full"):
    k = K(cfg)
    k.setup()
    nl = len(cfg.layers)
    for i, L in enumerate(cfg.layers):
        if mode == "dummy":
            k.layer_dummy(L, i == 0, i == nl - 1)
        else:
            k.layer(L, i == 0, i == nl - 1)
    k.P.emit()
    return k


def t5_onehot():
    import jax
    import jax.numpy as jnp
    with jax.default_device(jax.devices("cpu")[0]):
        rel = jnp.arange(384, dtype=jnp.int32) - 255
        nb = 16
        max_exact = 8
        ret = jnp.where(rel > 0, nb, 0)
        n = jnp.abs(rel)
        large = max_exact + (jnp.log(jnp.maximum(n, 1).astype(jnp.float32) / max_exact)
                             / math.log(128 / max_exact) * (nb - max_exact)).astype(jnp.int32)
        large = jnp.minimum(large, nb - 1)
        bucket = np.asarray(ret + jnp.where(n < max_exact, n, large))
    oh = np.zeros((32, 384), np.float32)
    oh[bucket, np.arange(384)] = 1.0
    return oh


def ret_perm():
    one = np.concatenate([np.arange(0, 256, 2), np.arange(1, 256, 2)])
    return np.concatenate([h * 256 + one for h in range(4)])


_RC = {}


def ret_consts(cfg):
    key = (cfg.T, cfg.TS, cfg.PAST)
    if key in _RC:
        return _RC[key]
    import jax
    import jax.numpy as jnp
    out = {}
    with jax.default_device(jax.devices("cpu")[0]):
        inv = 10000.0 ** (-jnp.arange(0, 256, 2, dtype=jnp.float32) / 256)
        for nm, start, T in (("p", 0, cfg.T), ("s", cfg.PAST, cfg.TS)):
            pos = (start + jnp.arange(T)).astype(jnp.float32)
            ang = pos[:, None] * inv[None, :]
            out["rot_cos_" + nm] = np.ascontiguousarray(np.asarray(jnp.cos(ang)).T)
            out["rot_sin_" + nm] = np.ascontiguousarray(np.asarray(jnp.sin(ang)).T)
        lg = np.asarray(jnp.log1p(-jnp.power(2.0, -5.0 - jnp.arange(4, dtype=jnp.float32))), np.float64)
    i = np.arange(128)
    dm = np.zeros((128, 4, 128), np.float32)
    qd = np.zeros((128, 4, 128), np.float32)
    kd = np.zeros((128, 8), np.float32)
    for h in range(4):
        d = i[None, :] - i[:, None]
        dm[:, h, :] = np.where(d >= 0, np.exp(lg[h] * np.maximum(d, 0)), 0.0)
        qd[:, h, :] = np.exp(lg[h] * (i + 1.0))[None, :]
        kd[:, h] = np.exp(lg[h] * (127.0 - i))
        kd[:, 4 + h] = np.exp(lg[h] * (cfg.TS - 1.0 - i))
    out["ret_dmaskT"], out["ret_qdec"], out["ret_kdec"] = dm, qd, kd
    _RC[key] = out
    return out


def core_inputs(cfg, inp, b):
    T, PAST = cfg.T, cfg.PAST
    f = np.ascontiguousarray
    cT = np.stack([inp["c_prompt"][b], inp["c_sample"][b]], -1).reshape(8, 128, 2).transpose(1, 0, 2)
    m = {
        "xp": f(inp["x_prompt"][b, :T]), "xs": f(inp["x_sample"][b]), "cT": f(cT),
        "ada_w": inp["ada_w"], "ada_b": inp["ada_b"], "ln_g": inp["ln_g"], "ln_b": inp["ln_b"],
        "ident": np.eye(128, dtype=np.float32),
        "cmask": np.where(np.arange(128)[:, None] > np.arange(128)[None, :], -30000.0, 0.0).astype(np.float32),
        "fox_w_in": inp["fox_w_in"], "fox_b_f": f(inp["fox_b_f"].reshape(16, 1)), "fox_w_out": inp["fox_w_out"],
        "cache_fox_k": f(inp["cache_fox_k"][b, :PAST].reshape(PAST, -1)),
        "cache_fox_v": f(inp["cache_fox_v"][b, :PAST].reshape(PAST, -1)),
        "cache_fox_logf": f(inp["cache_fox_logf"][b, :PAST]),
        "diff_w_in": inp["diff_w_in"], "diff_w_out": inp["diff_w_out"], "rel_bias_table": inp["rel_bias_table"],
        "diff_lam_q1": f(inp["diff_lam_q1"].reshape(1, 64)), "diff_lam_k1": f(inp["diff_lam_k1"].reshape(1, 64)),
        "diff_lam_q2": f(inp["diff_lam_q2"].reshape(1, 64)), "diff_lam_k2": f(inp["diff_lam_k2"].reshape(1, 64)),
        "diff_subln_w": f(inp["diff_subln_w"].reshape(1, 128)), "t5_onehot": t5_onehot(),
        "antiident": np.ascontiguousarray(np.eye(128, dtype=np.float32)[::-1]),
        "dmask": np.where((np.arange(128)[:, None] >= 64) & (np.arange(128)[None, :] < 64), -30000.0, 0.0).astype(np.float32),
        "cache_diff_k": f(inp["cache_diff_k"][b, :PAST].reshape(PAST, -1)),
        "cache_diff_v": f(inp["cache_diff_v"][b, :PAST].reshape(PAST, -1)),
    }
    m.update(ret_consts(cfg))
    i = np.arange(128)
    msk = np.zeros((128, 3, 128), np.float32)
    msk[:, 0, :] = np.where(i[None, :] > i[:, None], -1.0, 0.0)
    msk[:, 1, :] = np.where(i[None, :] < i[:, None], -1.0, 0.0)
    msk[:, 2, :] = np.where(i[None, :] >= i[:, None], 1.0, 0.0)
    sel = np.zeros((8, 8, 128), np.float32)
    for h in range(8):
        sel[h, h, :] = 1.0
    m.update({
        "gdn_w_in": inp["gdn_w_in"], "gdn_w_out": inp["gdn_w_out"],
        "gdn_convw": f(inp["gdn_conv_w"].reshape(4, 24, 128).transpose(2, 1, 0)),
        "gdn_conv_in": f(inp["state_gdn_conv"][b].reshape(3, 24, 128).transpose(2, 1, 0)),
        "gdn_a_log": f(inp["gdn_a_log"].reshape(8, 1)), "gdn_dt_bias": f(inp["gdn_dt_bias"].reshape(8, 1)),
        "gdn_norm_w": f(inp["gdn_norm_w"].reshape(1, 128)), "state_gdn": f(inp["state_gdn"][b]),
        "gdn_masks": msk, "gdn_sel": sel,
    })
    perm = ret_perm()
    w = inp["ret_w_in"]
    m["ret_w_in"] = f(np.concatenate([w[:, 0:D][:, perm], w[:, D:2 * D][:, perm], w[:, 2 * D:]], axis=1))
    m["ret_w_out"] = inp["ret_w_out"]
    m["ret_gn_w"] = f(inp["ret_gn_w"].reshape(1, -1))
    m["state_ret"] = f(inp["state_ret"][b][:, perm[:256], :])
    return m


OUT_NAMES = ("y", "gdn_state_", "gdn_conv_", "fox_k_", "fox_v_", "fox_logf_", "diff_k_", "diff_v_", "ret_state_")


def gather(cfg, results):
    inv = np.argsort(ret_perm()[:256])
    outs = []
    for nm, T in (("p", cfg.T), ("s", cfg.TS)):
        g = {}
        for base in OUT_NAMES:
            key = base + nm
            g[base] = np.stack([np.asarray(r[key]) for r in results], 0)
        B = len(results)
        outs.append([
            g["y"],
            g["gdn_state_"],
            g["gdn_conv_"],
            g["fox_k_"].reshape(B, T, 16, 64),
            g["fox_v_"].reshape(B, T, 16, 64),
            g["fox_logf_"],
            g["diff_k_"].reshape(B, T, 8, 2, 64),
            g["diff_v_"].reshape(B, T, 8, 128),
            np.ascontiguousarray(g["ret_state_"][:, :, inv, :]),
        ])
    p, s = outs
    return (p[0], s[0]) + tuple(p[1:]) + tuple(s[1:])


def kernel(**inputs):
    inputs = {k: np.asarray(v) for k, v in inputs.items()}
    cfg = Cfg()
    k = build(cfg)
    in_maps = [core_inputs(cfg, inputs, b) for b in range(NCORES)]
    res = run_bass_kernel_spmd(k.nc, in_maps, core_ids=list(range(NCORES)))
    return gather(cfg, res.results)
```

```python
import math
from contextlib import ExitStack

import numpy as np
import ml_dtypes
import concourse.bass as bass
import concourse.mybir as mybir
from concourse.bass_utils import run_bass_kernel_spmd

F32 = mybir.dt.float32
BF16 = mybir.dt.bfloat16
AF = mybir.ActivationFunctionType
ALU = mybir.AluOpType
AX = mybir.AxisListType

D = 1024
DEPTH = 4
ALPHA = (2.0 * DEPTH) ** 0.25
LN_EPS = 1e-5
NORM_EPS = 1e-6
NCORES = 8


class Buf:
    __slots__ = ("w", "r", "x")

    def __init__(self, x=False):
        self.w = None
        self.r = {}
        self.x = x


class Prog:
    ENGS = ("pe", "act", "dve", "pool", "sp")

    def __init__(self, nc, es, ndma=28):
        self.nc = nc
        self.sem = {e: es.enter_context(nc.semaphore("s_" + e)) for e in self.ENGS}
        self.dsem = [es.enter_context(nc.semaphore("d%d" % i)) for i in range(ndma)]
        self.cnt = {e: 0 for e in self.ENGS}
        self.dcnt = [0] * ndma
        self.drr = 0
        self.drr_sw = 0
        self.NSW_BASE = ndma - 8
        self.seen = {e: {} for e in self.ENGS}
        self.q = {e: [] for e in self.ENGS}

    def _collect(self, reads, writes, e=None):
        deps = {}
        for b in reads:
            if b.w is not None and deps.get(b.w[0], 0) < b.w[1]:
                deps[b.w[0]] = b.w[1]
            if b.x:
                for k, v in b.r.items():
                    if k != e and deps.get(k, 0) < v:
                        deps[k] = v
        for b in writes:
            if b.w is not None and deps.get(b.w[0], 0) < b.w[1]:
                deps[b.w[0]] = b.w[1]
            for k, v in b.r.items():
                if deps.get(k, 0) < v:
                    deps[k] = v
        return deps

    def _waits(self, e, deps):
        out = []
        seen = self.seen[e]
        for k, v in deps.items():
            if k == e and e == "pe":
                continue
            if seen.get(k, 0) < v:
                seen[k] = v
                out.append((k, v))
        return out

    def _mark(self, tok, reads, writes):
        k, v = tok
        for b in reads:
            if b.r.get(k, 0) < v:
                b.r[k] = v
        for b in writes:
            b.w = tok
            b.r = {}

    def op(self, e, fn, reads=(), writes=()):
        waits = self._waits(e, self._collect(reads, writes, e))
        self.cnt[e] += 1
        self.q[e].append((waits, fn, None))
        self._mark((e, self.cnt[e]), reads, writes)

    def dma(self, e, fn, reads=(), writes=()):
        if e == "pool":
            s = self.NSW_BASE + self.drr_sw
            self.drr_sw = (self.drr_sw + 1) % (len(self.dsem) - self.NSW_BASE)
        else:
            s = self.drr
            self.drr = (s + 1) % self.NSW_BASE
        deps = self._collect(reads, writes, e)
        k = ("d", s)
        if self.dcnt[s] and deps.get(k, 0) < self.dcnt[s]:
            deps[k] = self.dcnt[s]
        waits = self._waits(e, deps)
        self.dcnt[s] += 16
        self.q[e].append((waits, fn, s))
        self._mark((k, self.dcnt[s]), reads, writes)

    def barrier(self):
        deps = {e: c for e, c in self.cnt.items() if c}
        for s, c in enumerate(self.dcnt):
            if c:
                deps[("d", s)] = c
        for e in self.ENGS:
            waits = self._waits(e, dict(deps))
            if waits:
                self.q[e].append((waits, None, None))

    def _semof(self, k):
        return self.sem[k] if isinstance(k, str) else self.dsem[k[1]]

    def emit(self):
        nc = self.nc
        with nc.Block() as block:
            regs = dict(sp=block.sync, pe=block.tensor, act=block.scalar, dve=block.vector, pool=block.gpsimd)
            for e in self.ENGS:
                def body(eng, e=e):
                    for waits, fn, ds in self.q[e]:
                        ws = [(self._semof(k), v) for k, v in waits]
                        if fn is None:
                            for sm, v in ws:
                                eng.wait_ge(sm, v)
                            continue
                        if ds is None and ws:
                            for sm, v in ws[:-1]:
                                eng.wait_ge(sm, v)
                            ins = fn(eng)
                            ins._wait_ge(ws[-1][0], ws[-1][1])
                        else:
                            for sm, v in ws:
                                eng.wait_ge(sm, v)
                            ins = fn(eng)
                        if ds is None:
                            ins.then_inc(self.sem[e], 1)
                        else:
                            ins.then_inc(self.dsem[ds], 16)
                    if e == "sp":
                        for s, c in enumerate(self.dcnt):
                            if c:
                                eng.wait_ge(self.dsem[s], c)
                regs[e](body)


class Ring:
    def __init__(self, K, name, shape, dtype, n, psum=False, arena=None):
        self.t = []
        for i in range(n):
            if arena is not None:
                t = arena.alloc(shape, dtype)
            else:
                f = K.nc.psum_tensor if psum else K.nc.sbuf_tensor
                t = K.es.enter_context(f("%s%d" % (name, i), shape, dtype))
            self.t.append((t, Buf(x=psum)))
        self.i = 0

    def next(self):
        r = self.t[self.i % len(self.t)]
        self.i += 1
        return r


class Arena:
    def __init__(self, K, nbytes, base=None):
        self.t = K.es.enter_context(K.nc.sbuf_tensor("arena", [128, nbytes // 4], F32)) if base is None else base
        self.nbytes = nbytes
        self.off = 0

    def reset(self):
        self.off = 0

    def alloc(self, shape, dtype):
        esz = 2 if dtype == BF16 else 4
        n = 1
        for s in shape[1:]:
            n *= s
        nb = (n * esz + 3) // 4 * 4
        assert self.off + nb <= self.nbytes, "arena overflow: need %d have %d" % (nb, self.nbytes - self.off)
        v = self.t[0:shape[0], self.off // 4:(self.off + nb) // 4]
        self.off += nb
        if dtype == BF16:
            v = v.bitcast(BF16)[:, 0:n]
        if len(shape) == 3:
            v = v.rearrange("p (a b) -> p a b", a=shape[1])
        elif len(shape) == 4:
            v = v.rearrange("p (a b c) -> p a b c", a=shape[1], b=shape[2])
        return v


class Cfg:
    def __init__(self, T=4096, TS=32, PAST=2048, layers=(0, 1, 2, 3)):
        self.T, self.TS, self.PAST, self.layers = T, TS, PAST, tuple(layers)


class Group:
    def __init__(self, gi, name, T, past):
        self.gi, self.name, self.T, self.past = gi, name, T, past
        self.S = past + T
        self.tiles = [(t0, min(128, T - t0)) for t0 in range(0, T, 128)]
        self.blocks = [self.tiles[i:i + 4] for i in range(0, len(self.tiles), 4)]
        self.ktiles = [(k0, 128) for k0 in range(0, past, 128)] + [(past + t0, n) for t0, n in self.tiles]


class K:
    def __init__(self, cfg):
        self.cfg = cfg
        self.nc = nc = bass.Bass("TRN2", target_bir_lowering=False)
        self.es = ExitStack()
        self.P = Prog(nc, self.es)
        self.bufs = {}
        self.dr = {}
        self.dh = {}
        self.all_groups = [Group(0, "p", cfg.T, 0), Group(1, "s", cfg.TS, cfg.PAST)]
        import os
        gs = os.environ.get("KGROUPS", "01")
        self.groups = [g for g in self.all_groups if str(g.gi) in gs]

    def buf(self, key):
        b = self.bufs.get(key)
        if b is None:
            b = self.bufs[key] = Buf()
        return b

    def din(self, name, shape, dtype=F32):
        t = self.nc.dram_tensor(name, list(shape), dtype, kind="ExternalInput")
        self.dh[name] = t
        self.dr[name] = t.ap()
        return self.dr[name]

    def dout(self, name, shape, dtype=F32):
        t = self.nc.dram_tensor(name, list(shape), dtype, kind="ExternalOutput")
        self.dr[name] = t.ap()
        return self.dr[name]

    def dscr(self, name, shape, dtype):
        t = self.nc.dram_tensor(name, list(shape), dtype)
        self.dh[name] = t
        self.dr[name] = t.ap()
        return self.dr[name]

    def sb(self, name, shape, dtype):
        t = self.es.enter_context(self.nc.sbuf_tensor(name, list(shape), dtype))
        return t, Buf()

    def dma(self, out, in_, rd, wr, eng="sp"):
        self.P.dma(eng, lambda e: e.dma_start(out=out, in_=in_), rd, wr)

    def mm(self, out, lhsT, rhs, start, stop, rd, wr, skip=False):
        self.P.op("pe", lambda e: e.matmul(out, lhsT, rhs, start=start, stop=stop, skip_group_check=skip), rd, wr)

    def tr(self, out, in_, ident, rd, wr):
        self.P.op("pe", lambda e: e.transpose(out, in_, ident), rd, wr)

    def act(self, out, in_, func, rd, wr, bias=0.0, scale=1.0, accum=None):
        if accum is None:
            self.P.op("act", lambda e: e.activation(out=out, in_=in_, func=func, bias=bias, scale=scale), rd, wr)
        else:
            self.P.op("act", lambda e: e.activation(out=out, in_=in_, func=func, bias=bias, scale=scale,
                                                     accum_out=accum), rd, wr)

    def tt(self, eng, out, in0, in1, op, rd, wr):
        o = self.nc.vector if eng == "dve" else self.nc.gpsimd
        self.P.op(eng, lambda e: e.tensor_tensor(out=out, in0=in0, in1=in1, op=op), rd, wr)

    def ts(self, eng, out, in0, s1, s2, op0, op1, rd, wr, accum=None):
        if op1 is None:
            self.P.op(eng, lambda e: e.tensor_scalar(out=out, in0=in0, scalar1=s1, scalar2=None, op0=op0), rd, wr)
        elif accum is None:
            self.P.op(eng, lambda e: e.tensor_scalar(out=out, in0=in0, scalar1=s1, scalar2=s2, op0=op0, op1=op1),
                      rd, wr)
        else:
            self.P.op(eng, lambda e: e.tensor_scalar(out=out, in0=in0, scalar1=s1, scalar2=s2, op0=op0, op1=op1,
                                                     accum_out=accum), rd, wr)

    def stt(self, out, in0, scalar, in1, op0, op1, rd, wr):
        self.P.op("dve", lambda e: e.scalar_tensor_tensor(out=out, in0=in0, scalar=scalar, in1=in1, op0=op0, op1=op1),
                  rd, wr)

    def cp(self, eng, out, in_, rd, wr):
        if eng == "act":
            self.P.op("act", lambda e: e.copy(out=out, in_=in_), rd, wr)
        else:
            self.P.op(eng, lambda e: e.tensor_copy(out=out, in_=in_), rd, wr)

    def rsqrt(self, out, in_, rd, wr, scale=1.0, eps=0.0):
        self.act(out, in_, AF.Ln, rd, wr, bias=eps, scale=scale)
        self.act(out, out, AF.Exp, wr, wr, scale=-0.5)

    def recip(self, out, in_, rd, wr):
        self.P.op("dve", lambda e: e.reciprocal(out=out, in_=in_), rd, wr)

    def memset(self, eng, ap, val, wr):
        self.P.op(eng, lambda e: e.memset(ap, val), (), wr)

    def set_psum(self, n_mm, n_acc, n_tp):
        def mk(lst):
            r = Ring.__new__(Ring)
            r.t = lst
            r.i = 0
            return r
        assert n_mm + n_acc + n_tp == 8
        self.ps_mm = mk(self.psb[0:n_mm])
        self.ps_acc = mk(self.psb[n_mm:n_mm + n_acc])
        self.ps_tp = mk(self.psb[n_mm + n_acc:8])

    def setup(self):
        cfg = self.cfg
        T, TS, PAST = cfg.T, cfg.TS, cfg.PAST
        din, dout, dscr = self.din, self.dout, self.dscr
        din("xp", [T, D]); din("xs", [TS, D]); din("cT", [128, 8, 2])
        din("ada_w", [4, D, 3 * D]); din("ada_b", [4, 3 * D]); din("ln_g", [4, D]); din("ln_b", [4, D])
        din("ident", [128, 128]); din("cmask", [128, 128])
        din("fox_w_in", [D, 4112]); din("fox_b_f", [16, 1]); din("fox_w_out", [D, D])
        din("cache_fox_k", [PAST, D]); din("cache_fox_v", [PAST, D]); din("cache_fox_logf", [PAST, 16])
        din("diff_w_in", [D, 4096]); din("diff_w_out", [D, D]); din("rel_bias_table", [32, 8])
        for nmv in ("diff_lam_q1", "diff_lam_k1", "diff_lam_q2", "diff_lam_k2"):
            din(nmv, [1, 64])
        din("diff_subln_w", [1, 128]); din("t5_onehot", [32, 384]); din("antiident", [128, 128]); din("dmask", [128, 128])
        din("cache_diff_k", [PAST, D]); din("cache_diff_v", [PAST, D])
        dscr("lam_scr", [2, 8], F32); dscr("fr_scr", [16, 384], F32)
        din("ret_w_in", [D, 6144]); din("ret_w_out", [2 * D, D]); din("ret_gn_w", [1, 2 * D])
        din("state_ret", [4, 256, 512]); din("ret_dmaskT", [128, 4, 128]); din("ret_qdec", [128, 4, 128])
        din("ret_kdec", [128, 8])
        for g in self.all_groups:
            din("rot_cos_" + g.name, [128, g.T]); din("rot_sin_" + g.name, [128, g.T])
            dscr("QR_" + g.name, [8, 128, g.T], BF16); dscr("KR_" + g.name, [8, 128, g.T], BF16)
            dscr("OG_" + g.name + "v", [g.T, 2 * D], BF16)
            dout("ret_state_" + g.name, [4, 256, 512])
        din("gdn_w_in", [D, 4112]); din("gdn_w_out", [D, D]); din("gdn_convw", [128, 24, 4]); din("gdn_conv_in", [128, 24, 3])
        din("gdn_a_log", [8, 1]); din("gdn_dt_bias", [8, 1]); din("gdn_norm_w", [1, 128]); din("state_gdn", [8, 128, 128])
        din("gdn_masks", [128, 3, 128]); din("gdn_sel", [8, 8, 128])
        for g in self.all_groups:
            for nmv in ("QG_", "KG_", "VG_"):
                dscr(nmv + g.name, [8, 128, g.T], BF16)
            dscr("BTb_" + g.name, [8, g.T], BF16); dscr("GT_" + g.name, [8, g.T], F32); dscr("GB_" + g.name, [g.T, 16], F32)
            dout("gdn_state_" + g.name, [8, 128, 128]); dout("gdn_conv_" + g.name, [3, 3 * D])
        dout("yp", [T, D]); dout("ys", [TS, D])
        for g in self.all_groups:
            n = g.name
            dout("fox_k_" + n, [g.T, D]); dout("fox_v_" + n, [g.T, D]); dout("fox_logf_" + n, [g.T, 16])
            dout("diff_k_" + n, [g.T, D]); dout("diff_v_" + n, [g.T, D])
            dscr("OGF_" + n, [g.T, 2 * D], F32)
        dscr("xres_p", [T, D], F32); dscr("xres_s", [TS, D], F32)
        for g in self.all_groups:
            n = g.name
            dscr("QT_" + n, [16, 70, g.T], BF16); dscr("KT_" + n, [16, 70, g.S], BF16)
            dscr("VA_" + n, [g.S, 1040], BF16)
            dscr("ZS_" + n, [g.T, 2 * D], BF16); dscr("OG_" + n, [g.T, 2 * D], BF16)

        self.W, self.Wb = self.sb("W_in", [128, 8, 4112], BF16)
        self.WO, self.WOb = self.sb("W_out", [128, 8, D], BF16)
        self.identf, self.identf_b = self.sb("identf", [128, 128], F32)
        self.identb, self.identb_b = self.sb("identb", [128, 128], BF16)
        self.cmaskb, self.cmaskb_b = self.sb("cmaskb", [128, 128], BF16)
        self.cTs, self.cTs_b = self.sb("cTs", [128, 8, 2], F32)
        self.crep = [self.sb("crep%d" % g, [128, 8, 128], BF16) for g in range(2)]
        self.mod = [self.sb("mod_%d" % k, [128, D], F32) for k in range(3)]
        dscr("mods", [3, D], F32)
        self.lng, self.lng_b = self.sb("lng", [128, D], F32)
        self.lnb, self.lnb_b = self.sb("lnb", [128, D], F32)
        self.psb = Ring(self, "psb", [128, 512], F32, 8, psum=True).t
        self.set_psum(4, 2, 2)
        self.r_x = Ring(self, "rx", [128, D], F32, 2)
        self.r_f = Ring(self, "rf", [128, D], F32, 2)
        self.r_ub = Ring(self, "rub", [128, D], BF16, 2)
        self.r_uT = Ring(self, "ruT", [128, 8, 512], BF16, 2)
        self.r_h = Ring(self, "rh", [128, 1040], BF16, 3)
        self.r_small = Ring(self, "rsm", [128, 16], F32, 8)
        self.arena = Arena(self, 58 * 1024)
        wbytes = 8 * 4112 * 2
        self.arenaW = Arena(self, wbytes, base=self.W[:, :, :].rearrange("p a b -> p (a b)").bitcast(F32))

        self.dma(self.identf[:], self.dr["ident"][:, :], [], [self.identf_b])
        self.cp("dve", self.identb[:], self.identf[:], [self.identf_b], [self.identb_b])
        t, b = self.r_f.next()
        self.dma(t[:, 0:128], self.dr["cmask"][:, :], [], [b])
        self.cp("dve", self.cmaskb[:], t[:, 0:128], [b], [self.cmaskb_b])
        self.dma(self.cTs[:], self.dr["cT"][:, :, :], [], [self.cTs_b])
        self.act(self.cTs[:], self.cTs[:], AF.Silu, [self.cTs_b], [self.cTs_b])
        for g in range(2):
            t, b = self.crep[g]
            self.cp("dve", t[:], self.cTs[:, :, g:g + 1].to_broadcast([128, 8, 128]), [self.cTs_b], [b])

    def load_w(self, dst, dstb, src, ncols, nk=8, c_off=0):
        for k in range(nk):
            self.dma(dst[:, k, 0:ncols], src[k * 128:(k + 1) * 128, c_off:c_off + ncols], [], [dstb], eng="pool")

    def layer_mod(self, L):
        ada_w, ada_b = self.dr["ada_w"], self.dr["ada_b"]
        for cg in range(6):
            wt, wb = self.r_uT.next()
            self.load_w(wt, wb, ada_w[L], 512, c_off=cg * 512)
            bt_, bb = self.r_f.next()
            bt = bt_[:, 0:512]
            self.dma(bt[:, :], ada_b[L:L + 1, cg * 512:(cg + 1) * 512].to_broadcast([128, 512]), [], [bb])
            kind = cg // 2
            for g in range(2):
                ps, pb = self.ps_mm.next()
                ct, cb = self.crep[g]
                for k in range(8):
                    self.mm(ps[:, :], ct[:, k, :], wt[:, k, :], k == 0, k == 7, [cb, wb], [pb])
                if g == 0:
                    mt, mb = self.mod[kind]
                    dst = mt[:, (cg % 2) * 512:(cg % 2 + 1) * 512]
                else:
                    mt, mb = self.r_f.next()
                    dst = mt[:, 0:512]
                if kind == 0:
                    self.tt("dve", dst, ps[:, :], bt[:, :], ALU.add, [pb, bb], [mb])
                else:
                    self.stt(dst, ps[:, :], 1.0, bt[:, :], ALU.add, ALU.add, [pb, bb], [mb])
                if g == 1:
                    self.dma(self.dr["mods"][kind:kind + 1, (cg % 2) * 512:(cg % 2 + 1) * 512], mt[0:1, 0:512], [mb],
                             [self.buf(("mods", kind, cg % 2))])
        self.dma(self.lng[:], self.dr["ln_g"][L:L + 1, :].to_broadcast([128, D]), [], [self.lng_b])
        self.dma(self.lnb[:], self.dr["ln_b"][L:L + 1, :].to_broadcast([128, D]), [], [self.lnb_b])

    def modtile(self, g, kind, n):
        if g.gi == 0:
            return self.mod[kind]
        t, b = self.r_f.next()
        self.dma(t[:n, :], self.dr["mods"][kind:kind + 1, :].to_broadcast([n, D]),
                 [self.buf(("mods", kind, 0)), self.buf(("mods", kind, 1))], [b])
        return t, b

    def xsrc(self, g, first):
        if first:
            return self.dr["xp" if g.gi == 0 else "xs"], "xin%d" % g.gi
        return self.dr["xres_" + g.name], "xres%d" % g.gi

    def make_uT(self, g, blk, first):
        src, skey = self.xsrc(g, first)
        uT, uTb = self.r_uT.next()
        off = 0
        for (t0, n) in blk:
            sc, scb = self.modtile(g, 1, n)
            xt, xb = self.r_x.next()
            self.dma(xt[:n, :], src[t0:t0 + n, :], [self.buf((skey, t0))], [xb])
            ft, fb = self.r_f.next()
            self.tt("dve", ft[:n, :], xt[:n, :], sc[:n, :], ALU.mult, [xb, scb], [fb])
            sh, shb = self.modtile(g, 0, n)
            ub, ubb = self.r_ub.next()
            self.tt("dve", ub[:n, :], ft[:n, :], sh[:n, :], ALU.add, [fb, shb], [ubb])
            tp, tpb = self.ps_tp.next()
            tpv = tp[:, :].bitcast(BF16)
            for j in range(8):
                self.tr(tpv[:, j * 128:j * 128 + n], ub[:n, j * 128:(j + 1) * 128], self.identb[:n, :n],
                        [ubb, self.identb_b], [tpb])
            self.cp("act", uT[:, :, off:off + n],
                    tpv.rearrange("p (j t) -> p j t", j=8)[:, :, 0:n], [tpb], [uTb])
            off += n
        return uT, uTb, off

    def proj_tm(self, uT, uTb, off, n, c0, c1):
        ps, pb = self.ps_mm.next()
        for k in range(8):
            self.mm(ps[:n, 0:c1 - c0], uT[:, k, off:off + n], self.W[:, k, c0:c1], k == 0, k == 7,
                    [uTb, self.Wb], [pb])
        return ps, pb

    def proj_fm(self, uT, uTb, nb, c0, c1):
        ps, pb = self.ps_mm.next()
        for k in range(8):
            self.mm(ps[:c1 - c0, 0:nb], self.W[:, k, c0:c1], uT[:, k, 0:nb], k == 0, k == 7, [uTb, self.Wb], [pb])
        return ps, pb

    def pass_c_tile(self, g, t0, n, first, last, ogz, ogzb, nk, wo=None):
        src, skey = self.xsrc(g, first)
        dst, dkey = (self.dr["yp" if g.gi == 0 else "ys"], "y%d" % g.gi) if last else \
            (self.dr["xres_" + g.name], "xres%d" % g.gi)
        oT, oTb = self.r_uT.next()
        for half in range((nk + 7) // 8):
            tp, tpb = self.ps_tp.next()
            tpv = tp[:, :].bitcast(BF16)
            for j in range(8):
                jj = half * 8 + j
                self.tr(tpv[:, j * 128:j * 128 + n], ogz[:n, jj * 128:(jj + 1) * 128], self.identb[:n, :n],
                        [ogzb, self.identb_b], [tpb])
            self.cp("act", oT[:, :, half * 128:half * 128 + n],
                    tpv.rearrange("p (j t) -> p j t", j=8)[:, :, 0:n], [tpb], [oTb])
        hps = []
        for cg in range(2):
            ps, pb = self.ps_acc.next()
            for k in range(nk):
                wt_, wtb_ = (self.WO[:, k, :], self.WOb) if wo is None else wo(k)
                self.mm(ps[:n, :], oT[:, k % 8, (k // 8) * 128:(k // 8) * 128 + n],
                        wt_[:, cg * 512:(cg + 1) * 512], k == 0, k == nk - 1, [oTb, wtb_], [pb])
            hps.append((ps, pb))
        xt, xb = self.r_x.next()
        self.dma(xt[:n, :], src[t0:t0 + n, :], [self.buf((skey, t0))], [xb])
        gt, gb = self.modtile(g, 2, n)
        rt, rb = self.r_f.next()
        for cg in range(2):
            ps, pb = hps[cg]
            sl = slice(cg * 512, (cg + 1) * 512)
            self.tt("dve", rt[:n, sl], ps[:n, :], gt[:n, sl], ALU.mult, [pb, gb], [rb])
        self.stt(rt[:n, :], xt[:n, :], ALPHA, rt[:n, :], ALU.mult, ALU.add, [xb, rb], [rb])
        st, stb = self.r_small.next()
        for cg in range(2):
            self.P.op("dve", lambda e, cg=cg: e.bn_stats(out=st[:n, cg * 6:cg * 6 + 6],
                                                          in_=rt[:n, cg * 512:(cg + 1) * 512]), [rb], [stb])
        mv, mvb = self.r_small.next()
        self.P.op("dve", lambda e: e.bn_aggr(out=mv[:n, 0:2], in_=st[:n, 0:12]), [stb], [mvb])
        self.rsqrt(mv[:n, 4:5], mv[:n, 1:2], [mvb], [mvb], eps=LN_EPS)
        self.stt(mv[:n, 5:6], mv[:n, 0:1], -1.0, mv[:n, 4:5], ALU.mult, ALU.mult, [mvb], [mvb])
        yt, yb = self.r_f.next()
        self.act(yt[:n, :], rt[:n, :], AF.Identity, [rb, mvb], [yb], bias=mv[:n, 5:6], scale=mv[:n, 4:5])
        self.tt("dve", yt[:n, :], yt[:n, :], self.lng[:n, :], ALU.mult, [yb, self.lng_b], [yb])
        self.tt("dve", yt[:n, :], yt[:n, :], self.lnb[:n, :], ALU.add, [yb, self.lnb_b], [yb])
        self.dma(dst[t0:t0 + n, :], yt[:n, :], [yb], [self.buf((dkey, t0))])

    def layer(self, L, first, last):
        self.layer_mod(L)
        getattr(self, ("layer_gdn", "layer_fox", "layer_diff", "layer_ret")[L % 4])(L, first, last)

    def attn_rings(self, w, lf):
        self.P.barrier()
        self.set_psum(2, 4, 2)
        A = self.arena
        A.reset()
        smax = max(g.S for g in self.groups)
        self.r_KT = Ring(self, "rKT", [70, smax], BF16, 2, arena=A)
        nkt = max(len(g.ktiles) for g in self.groups)
        self.r_VA = Ring(self, "rVA", [128, nkt, w], BF16, 2, arena=A)
        self.r_QT = Ring(self, "rQT", [70, 512], BF16, 2, arena=A)
        self.r_PT = Ring(self, "rPT", [128, 512], BF16, 3 if lf else 2, arena=A)
        if lf:
            self.r_og = Ring(self, "rog", [128, 4, 128], BF16, 2, arena=A)
        self.r_vb = self.r_h
        self.r_st = Ring(self, "rst", [128, 2048], BF16, 1, arena=A)
        if lf:
            self.r_lf = Ring(self, "rlf", [16, 512], F32, 4, arena=A)
            self.r_pt = Ring(self, "rpt", [16, 3, 512], BF16, 2, arena=A)
            self.ones3, self.ones3_b = A.alloc([16, 3, 512], BF16), Buf()
            self.memset("pool", self.ones3[:, :, :], 1.0, [self.ones3_b])
            self.carry, self.carry_b = A.alloc([16, 1], F32), Buf()
            self.fbf, self.fbf_b = A.alloc([16, 1], F32), Buf()

    def attn_head(self, g, KTd, QTd, VAd, vcol0, dv, kc, rd_bufs, extras, finish):
        KT, KTb = self.r_KT.next()
        self.dma(KT[:kc, 0:g.S], KTd[:, :], rd_bufs, [KTb])
        VA, VAb = self.r_VA.next()
        nfull = sum(1 for (k0, nk) in g.ktiles if nk == 128)
        for k4 in range(0, nfull, 4):
            k5 = min(nfull, k4 + 4)
            self.dma(VA[:, k4:k5, 0:dv + 1],
                     VAd[k4 * 128:k5 * 128, vcol0:vcol0 + dv + 1].rearrange("(kt p) c -> p kt c", p=128), rd_bufs, [VAb])
        for i, (k0, nk) in enumerate(g.ktiles):
            if nk != 128:
                self.dma(VA[:nk, i, 0:dv + 1], VAd[k0:k0 + nk, vcol0:vcol0 + dv + 1], rd_bufs, [VAb])
        w = dv + 1
        per_bank = 512 // w
        for bi, blk in enumerate(g.blocks):
            nq = sum(n for _, n in blk)
            q0b = blk[0][0]
            QT, QTb = self.r_QT.next()
            self.dma(QT[:kc, 0:nq], QTd[:, q0b:q0b + nq], rd_bufs, [QTb])
            nbank = (len(blk) + per_bank - 1) // per_bank
            accs = [self.ps_acc.next() for _ in range(nbank)]
            started = [False] * nbank
            coff = []
            o = 0
            for (_, n) in blk:
                coff.append(o)
                o += n
            pend = None

            def emit_pv(item):
                ki, k0, nk, vis, PT, PTb = item
                for qi in vis:
                    q0, n = blk[qi]
                    bk = qi // per_bank
                    acc, accb = accs[bk]
                    col = (qi % per_bank) * w
                    last = (k0 == g.past + q0)
                    self.mm(acc[:n, col:col + w], PT[:nk, coff[qi]:coff[qi] + n], VA[:nk, ki, 0:w],
                            not started[bk], last, [PTb, VAb], [accb], skip=True)
                    started[bk] = True
            for ki, (k0, nk) in enumerate(g.ktiles):
                vis = [qi for qi, (q0, n) in enumerate(blk) if g.past + q0 >= k0]
                if not vis or k0 > g.past + blk[-1][0]:
                    continue
                fi = vis[0]
                c0 = coff[fi]
                ST, STb = self.ps_mm.next()
                ex = []
                for qi in vis:
                    q0, n = blk[qi]
                    for (l, r, bl) in extras(ki, k0, nk, qi, g.past + q0, n):
                        ex.append((qi, n, l, r, bl))
                self.mm(ST[:nk, c0:nq], KT[:kc, k0:k0 + nk], QT[:kc, c0:nq], True, not ex, [KTb, QTb], [STb])
                for j, (qi, n, l, r, bl) in enumerate(ex):
                    self.mm(ST[:nk, coff[qi]:coff[qi] + n], l, r, False, j == len(ex) - 1, bl, [STb], skip=True)
                PT, PTb = self.r_PT.next()
                self.act(PT[:nk, c0:nq], ST[:nk, c0:nq], AF.Exp, [STb], [PTb])
                if pend is not None:
                    emit_pv(pend)
                pend = (ki, k0, nk, vis, PT, PTb)
            if pend is not None:
                emit_pv(pend)
            finish(bi, blk, accs, per_bank, w)

    def layer_fox(self, L, first, last):
        dr = self.dr
        self.attn_rings(65, True)
        self.load_w(self.W, self.Wb, dr["fox_w_in"], 4112)
        self.load_w(self.WO, self.WOb, dr["fox_w_out"], D)
        bf, bfb = self.fbf, self.fbf_b
        self.dma(bf[:, :], dr["fox_b_f"][:, :], [], [bfb])
        self.ts("dve", bf[:, :], bf[:, :], -1.0, None, ALU.mult, None, [bfb], [bfb])
        for g in self.groups:
            nm = g.name
            KTd, QTd, VAd, ZSd, OGd = dr["KT_" + nm], dr["QT_" + nm], dr["VA_" + nm], dr["ZS_" + nm], dr["OG_" + nm]
            KTv = KTd.rearrange("(hp two) r s -> two r hp s", two=2)
            QTv = QTd.rearrange("(hp two) r s -> two r hp s", two=2)
            wb = []

            def nb_(key):
                b = self.buf(("fox", g.gi, L) + key)
                wb.append(b)
                return b
            self.memset("pool", self.carry[:, :], 0.0, [self.carry_b])

            def cum_block(lf, lfb, nb, kpos, qpos, tag):
                cum, cumb = self.r_lf.next()
                self.P.op("dve", lambda e: e.tensor_tensor_scan(out=cum[:, 0:nb], data0=self.ones3[:, 0, 0:nb],
                                                                data1=lf[:, 0:nb], initial=self.carry[:, 0:1],
                                                                op0=ALU.mult, op1=ALU.add),
                          [lfb, self.ones3_b, self.carry_b], [cumb])
                self.cp("act", self.carry[:, 0:1], cum[:, nb - 1:nb], [cumb], [self.carry_b])
                pt, ptb = self.r_pt.next()
                t32, t32b = self.r_lf.next()
                r1, r1b = self.r_lf.next()
                self.cp("dve", pt[:, 0, 0:nb], cum[:, 0:nb], [cumb], [ptb])
                self.cp("dve", t32[:, 0:nb], pt[:, 0, 0:nb], [ptb], [t32b])
                self.tt("dve", r1[:, 0:nb], cum[:, 0:nb], t32[:, 0:nb], ALU.subtract, [cumb, t32b], [r1b])
                self.cp("dve", pt[:, 1, 0:nb], r1[:, 0:nb], [r1b], [ptb])
                self.cp("dve", t32[:, 0:nb], pt[:, 1, 0:nb], [ptb], [t32b])
                self.tt("dve", r1[:, 0:nb], r1[:, 0:nb], t32[:, 0:nb], ALU.subtract, [r1b, t32b], [r1b])
                self.cp("dve", pt[:, 2, 0:nb], r1[:, 0:nb], [r1b], [ptb])
                npt, nptb = self.r_pt.next()
                self.ts("dve", npt[:, :, 0:nb], pt[:, :, 0:nb], -1.0, None, ALU.mult, None, [ptb], [nptb])
                self.dma(KTd[:, 64:67, kpos:kpos + nb], npt[:, :, 0:nb], [nptb], [nb_((tag, "kc"))])
                self.dma(KTd[:, 67:70, kpos:kpos + nb], self.ones3[:, :, 0:nb], [self.ones3_b], [nb_((tag, "k1"))])
                if qpos is not None:
                    self.dma(QTd[:, 64:67, qpos:qpos + nb], self.ones3[:, :, 0:nb], [self.ones3_b],
                             [nb_((tag, "q1"))])
                    self.dma(QTd[:, 67:70, qpos:qpos + nb], pt[:, :, 0:nb], [ptb], [nb_((tag, "qc"))])


            ptiles = [(k0, nk) for (k0, nk) in g.ktiles if k0 < g.past]
            for b0 in range(0, len(ptiles), 2):
                pblk = ptiles[b0:b0 + 2]
                lf, lfb = self.r_lf.next()
                st_, stb = self.r_st.next()
                st = st_[:, :].rearrange("p (c t) -> p c t", c=8)
                for j, (k0, nk) in enumerate(pblk):
                    xt, xb = self.r_x.next()
                    self.dma(xt[:, :], dr["cache_fox_k"][k0:k0 + 128, :], [], [xb])
                    ub, ubb = self.r_ub.next()
                    self.cp("dve", ub[:, :], xt[:, :], [xb], [ubb])
                    tp, tpb = self.ps_tp.next()
                    tpv = tp[:, :].bitcast(BF16)
                    for c in range(8):
                        self.tr(tpv[:, c * 128:(c + 1) * 128], ub[:, c * 128:(c + 1) * 128], self.identb[:, :],
                                [ubb, self.identb_b], [tpb])
                    self.cp("act", st[:, :, j * 128:(j + 1) * 128], tpv.rearrange("p (c t) -> p c t", c=8), [tpb], [stb])
                    vt, vtb = self.r_x.next()
                    self.dma(vt[:, :], dr["cache_fox_v"][k0:k0 + 128, :], [], [vtb])
                    vb, vbb = self.r_vb.next()
                    self.memset("pool", vb[:, :].rearrange("p (h c) -> p h c", h=16)[:, :, 64:65], 1.0, [vbb])
                    self.cp("dve", vb[:, :].rearrange("p (h c) -> p h c", h=16)[:, :, 0:64],
                            vt[:, :].rearrange("p (h c) -> p h c", h=16), [vtb], [vbb])
                    self.dma(VAd[k0:k0 + 128, :], vb[:, :], [vbb], [nb_(("pv", k0))])
                    lt, ltb = self.r_small.next()
                    self.dma(lt[:, 0:16], dr["cache_fox_logf"][k0:k0 + 128, :], [], [ltb])
                    tp2, tp2b = self.ps_tp.next()
                    self.tr(tp2[0:16, 0:128], lt[:, 0:16], self.identf[:, :], [ltb, self.identf_b], [tp2b])
                    self.cp("act", lf[:, j * 128:(j + 1) * 128], tp2[0:16, 0:128], [tp2b], [lfb])
                kp, nbk = pblk[0][0], 128 * len(pblk)
                for two in range(2):
                    self.dma(KTv[two, 0:64, :, kp:kp + nbk], st[two * 64:(two + 1) * 64, :, 0:nbk], [stb],
                             [nb_(("pk", kp, two))])
                cum_block(lf, lfb, nbk, kp, None, ("pc", kp))

            for blk in g.blocks:
                uT, uTb, nb = self.make_uT(g, blk, first)
                t0b = blk[0][0]
                kpos = g.past + t0b
                for which, dst in ((0, QTv), (1, KTv)):
                    pos = t0b if which == 0 else kpos
                    for hf in range(2):
                        st_, stb = self.r_st.next()
                        st = st_[:, :].rearrange("p (c t) -> p c t", c=4)
                        for c4 in range(4):
                            c = hf * 4 + c4
                            ps, pb = self.proj_fm(uT, uTb, nb, which * D + c * 128, which * D + (c + 1) * 128)
                            if which == 0:
                                self.act(st[:, c4, 0:nb], ps[:, 0:nb], AF.Copy, [pb], [stb], scale=0.125)
                            else:
                                self.cp("dve", st[:, c4, 0:nb], ps[:, 0:nb], [pb], [stb])
                        for two in range(2):
                            self.dma(dst[two, 0:64, hf * 4:hf * 4 + 4, pos:pos + nb], st[two * 64:(two + 1) * 64, :, 0:nb],
                                     [stb], [nb_(("qk", which, t0b, two, hf))])
                ps, pb = self.proj_fm(uT, uTb, nb, 4 * D, 4 * D + 16)
                e1, e1b = self.r_lf.next()
                self.act(e1[:, 0:nb], ps[0:16, 0:nb], AF.Exp, [pb, bfb], [e1b], bias=bf[:, 0:1], scale=-1.0)
                lf, lfb = self.r_lf.next()
                self.act(lf[:, 0:nb], e1[:, 0:nb], AF.Ln, [e1b], [lfb], bias=1.0)
                self.ts("dve", lf[:, 0:nb], lf[:, 0:nb], -1.0, None, ALU.mult, None, [lfb], [lfb])
                off = 0
                for (t0, n) in blk:
                    tp2, tp2b = self.ps_tp.next()
                    self.tr(tp2[0:n, 0:16], lf[:, off:off + n], self.identf[0:16, 0:16], [lfb, self.identf_b], [tp2b])
                    lo, lob = self.r_small.next()
                    self.cp("act", lo[:n, 0:16], tp2[0:n, 0:16], [tp2b], [lob])
                    self.dma(dr["fox_logf_" + nm][t0:t0 + n, :], lo[:n, 0:16], [lob], [self.buf(("flo", g.gi, t0))])
                    for which, oname in ((1, "fox_k_"), (2, "fox_v_")):
                        ft, fb = self.r_f.next()
                        for cg in range(2):
                            ps, pb = self.proj_tm(uT, uTb, off, n, which * D + cg * 512, which * D + (cg + 1) * 512)
                            self.cp("act" if cg == 0 else "dve", ft[:n, cg * 512:(cg + 1) * 512], ps[:n, :], [pb], [fb])
                        self.dma(dr[oname + nm][t0:t0 + n, :], ft[:n, :], [fb], [self.buf((oname, g.gi, t0))])
                        if which == 2:
                            vb, vbb = self.r_vb.next()
                            self.memset("pool", vb[:n, :].rearrange("p (h c) -> p h c", h=16)[:, :, 64:65], 1.0, [vbb])
                            self.cp("dve", vb[:n, :].rearrange("p (h c) -> p h c", h=16)[:, :, 0:64],
                                    ft[:n, :].rearrange("p (h c) -> p h c", h=16), [fb], [vbb])
                            self.dma(VAd[g.past + t0:g.past + t0 + n, :], vb[:n, :], [vbb], [nb_(("v", t0))])
                    zt, ztb = self.r_h.next()
                    for cg in range(2):
                        ps, pb = self.proj_tm(uT, uTb, off, n, 3 * D + cg * 512, 3 * D + (cg + 1) * 512)
                        self.act(zt[:n, cg * 512:(cg + 1) * 512], ps[:n, :], AF.Silu, [pb], [ztb])
                    self.dma(ZSd[t0:t0 + n, 0:D], zt[:n, 0:D], [ztb], [nb_(("z", t0))])
                    off += n
                cum_block(lf, lfb, nb, kpos, t0b, ("c", t0b))

            ogw = []

            def extras(ki, k0, nk, qi, q0a, n):
                if k0 == q0a:
                    return [(self.identb[:nk, :nk], self.cmaskb[:nk, :n], [self.identb_b, self.cmaskb_b])]
                return []

            for h in range(16):
                def finish(bi, blk, accs, per_bank, w, h=h):
                    og, ogb = self.r_og.next()
                    for qi, (q0, n) in enumerate(blk):
                        acc, accb = accs[qi // per_bank]
                        col = (qi % per_bank) * w
                        rd, rdb = self.r_small.next()
                        self.recip(rd[:n, 0:1], acc[:n, col + 64:col + 65], [accb], [rdb])
                        self.ts("dve", og[:n, qi, 0:64], acc[:n, col:col + 64], rd[:n, 0:1], None, ALU.mult, None,
                                [accb, rdb], [ogb])
                    q0b = blk[0][0]
                    kb = self.buf(("fox_og", g.gi, L, h, bi))
                    ogw.append(kb)
                    if len(blk) > 1 or blk[0][1] == 128:
                        nt = len(blk)
                        self.dma(OGd[q0b:q0b + nt * 128, h * 64:(h + 1) * 64].rearrange("(qi p) c -> p qi c", p=128),
                                 og[:, 0:nt, 0:64], [ogb], [kb])
                    else:
                        n = blk[0][1]
                        self.dma(OGd[q0b:q0b + n, h * 64:(h + 1) * 64], og[:n, 0, 0:64], [ogb], [kb])
                self.attn_head(g, KTd[h], QTd[h], VAd, h * 65, 64, 70, wb, extras, finish)

            for (t0, n) in g.tiles:
                og, ogb = self.r_h.next()
                self.dma(og[:n, 0:D], OGd[t0:t0 + n, 0:D], ogw, [ogb])
                zs, zsb = self.r_h.next()
                self.dma(zs[:n, 0:D], ZSd[t0:t0 + n, 0:D], wb, [zsb])
                self.tt("dve", og[:n, 0:D], og[:n, 0:D], zs[:n, 0:D], ALU.mult, [ogb, zsb], [ogb])
                self.pass_c_tile(g, t0, n, first, last, og, ogb, 8)

    def layer_diff(self, L, first, last):
        dr = self.dr
        LAM_INIT = 0.8 - 0.6 * math.exp(-0.3 * 2)
        self.attn_rings(129, False)
        A = self.arena
        self.load_w(self.W, self.Wb, dr["diff_w_in"], 4096)
        self.load_w(self.WO, self.WOb, dr["diff_w_out"], D)
        lv, lvb = A.alloc([1, 4, 64], F32), Buf()
        for i, nmv in enumerate(("diff_lam_q1", "diff_lam_k1", "diff_lam_q2", "diff_lam_k2")):
            self.dma(lv[:, i, :], dr[nmv][:, :], [], [lvb])
        l2, l2b = A.alloc([1, 8], F32), Buf()
        pr, prb = A.alloc([1, 2, 64], F32), Buf()
        lvv = lv[:, :, :].rearrange("p (a b) c -> p a b c", b=2)
        self.tt("dve", pr[:, :, :], lvv[:, :, 0, :], lvv[:, :, 1, :], ALU.mult, [lvb], [prb])
        self.P.op("dve", lambda e: e.tensor_reduce(out=l2[:, 0:2], in_=pr[:, :, :], axis=AX.X, op=ALU.add), [prb], [l2b])
        self.act(l2[:, 2:4], l2[:, 0:2], AF.Exp, [l2b], [l2b])
        self.tt("dve", l2[:, 4:5], l2[:, 3:4], l2[:, 2:3], ALU.subtract, [l2b], [l2b])
        self.ts("dve", l2[:, 5:6], l2[:, 4:5], -LAM_INIT, None, ALU.add, None, [l2b], [l2b])
        self.dma(dr["lam_scr"][0:1, 0:1], l2[:, 5:6], [l2b], [self.buf("lam_scr")])
        nlam, nlamb = A.alloc([128, 1], F32), Buf()
        self.dma(nlam[:, :], dr["lam_scr"][0:1, 0:1].to_broadcast([128, 1]), [self.buf("lam_scr")], [nlamb])
        tb, tbb = A.alloc([32, 2, 8], F32), Buf()
        for m_ in range(2):
            self.dma(tb[:, m_, :], dr["rel_bias_table"][:, :], [], [tbb])
        oh, ohb = self.r_x.next()
        self.dma(oh[0:32, 0:384], dr["t5_onehot"][:, :], [], [ohb])
        tbs, tbsb = A.alloc([32, 3, 16], BF16), Buf()
        ohb16, ohb16b = A.alloc([32, 384], BF16), Buf()
        self.cp("dve", ohb16[:, :], oh[0:32, 0:384], [ohb], [ohb16b])
        tbf = tb[:, :, :].rearrange("p a b -> p (a b)")
        tmp32, tmp32b = self.r_small.next()
        self.cp("dve", tbs[:, 0, :], tbf, [tbb], [tbsb])
        self.cp("dve", tmp32[0:32, 0:16], tbs[:, 0, :], [tbsb], [tmp32b])
        self.tt("dve", tmp32[0:32, 0:16], tbf, tmp32[0:32, 0:16], ALU.subtract, [tbb, tmp32b], [tmp32b])
        self.cp("dve", tbs[:, 1, :], tmp32[0:32, 0:16], [tmp32b], [tbsb])
        tmp33, tmp33b = self.r_small.next()
        self.cp("dve", tmp33[0:32, 0:16], tbs[:, 1, :], [tbsb], [tmp33b])
        self.tt("dve", tmp33[0:32, 0:16], tmp32[0:32, 0:16], tmp33[0:32, 0:16], ALU.subtract, [tmp32b, tmp33b], [tmp33b])
        self.cp("dve", tbs[:, 2, :], tmp33[0:32, 0:16], [tmp33b], [tbsb])
        ps, pb = self.ps_mm.next()
        for j_ in range(3):
            self.mm(ps[0:16, 0:384], tbs[:, j_, :], ohb16[:, :], j_ == 0, j_ == 2, [tbsb, ohb16b], [pb])
        fr, frb = self.r_f.next()
        c16, c16b = self.r_small.next()
        self.cp("dve", c16[0:16, 0:1], ps[0:16, 0:1], [pb], [c16b])
        self.ts("dve", fr[0:16, 0:384], ps[0:16, 0:384], c16[0:16, 0:1], None, ALU.subtract, None, [pb, c16b], [frb])
        self.dma(dr["fr_scr"][:, :], fr[0:16, 0:384], [frb], [self.buf("fr_scr")])
        crow, crowb = A.alloc([16, 2, 512], BF16), Buf()
        chi, chib = self.r_small.next()
        cbf = chi[0:16, 0:4].bitcast(BF16)
        self.cp("dve", cbf[:, 0:1], c16[0:16, 0:1], [c16b], [chib])
        self.cp("dve", chi[0:16, 4:5], cbf[:, 0:1], [chib], [chib])
        self.tt("dve", chi[0:16, 5:6], c16[0:16, 0:1], chi[0:16, 4:5], ALU.subtract, [c16b, chib], [chib])
        self.cp("dve", cbf[:, 1:2], chi[0:16, 5:6], [chib], [chib])
        for j in range(2):
            self.cp("dve", crow[:, j, :], cbf[:, j:j + 1].to_broadcast([16, 512]), [chib], [crowb])
        ones2, ones2b = A.alloc([16, 2, 512], BF16), Buf()
        self.memset("pool", ones2[:, :, :], 1.0, [ones2b])
        J, Jb = A.alloc([128, 128], BF16), Buf()
        dmk, dmkb = A.alloc([128, 128], BF16), Buf()
        t, b = self.r_f.next()
        self.dma(t[:, 0:128], dr["antiident"][:, :], [], [b])
        self.cp("dve", J[:, :], t[:, 0:128], [b], [Jb])
        t, b = self.r_f.next()
        self.dma(t[:, 0:128], dr["dmask"][:, :], [], [b])
        self.cp("dve", dmk[:, :], t[:, 0:128], [b], [dmkb])
        Hh = [[None, None] for _ in range(8)]
        for h in range(8):
            for ti, c in enumerate((128, 0)):
                t, b = self.r_f.next()
                src_ap = bass.AP(self.dh["fr_scr"], h * 384 + c, [[1, 128], [1, 128]])
                self.dma(t[:, 0:128], src_ap, [self.buf("fr_scr")], [b])
                hi, hib = A.alloc([128, 128], BF16), Buf()
                lo, lob = A.alloc([128, 128], BF16), Buf()
                self.cp("dve", hi[:, :], t[:, 0:128], [b], [hib])
                self.cp("dve", t[:, 128:256], hi[:, :], [hib], [b])
                self.tt("dve", t[:, 256:384], t[:, 0:128], t[:, 128:256], ALU.subtract, [b], [b])
                self.cp("dve", lo[:, :], t[:, 256:384], [b], [lob])
                Hh[h][ti] = (hi, hib, lo, lob)

        slw, slwb = A.alloc([128, 128], F32), Buf()
        self.dma(slw[:, :], dr["diff_subln_w"][0:1, :].to_broadcast([128, 128]), [], [slwb])
        self.P.op("act", lambda e: e.mul(out=slw[:, :], in_=slw[:, :], mul=1.0 - LAM_INIT), [slwb], [slwb])
        for g in self.groups:
            nm = g.name
            KTd, QTd, VAd, ZSd, OGd = dr["KT_" + nm], dr["QT_" + nm], dr["VA_" + nm], dr["ZS_" + nm], dr["OGF_" + nm]
            KTv = KTd.rearrange("(hp two) r s -> two r hp s", two=2)
            QTv = QTd.rearrange("(hp two) r s -> two r hp s", two=2)
            wb = []

            def nb_(key):
                b = self.buf(("diff", g.gi, L) + key)
                wb.append(b)
                return b
            ptiles = [(k0, nk) for (k0, nk) in g.ktiles if k0 < g.past]
            for b0 in range(0, len(ptiles), 2):
                pblk = ptiles[b0:b0 + 2]
                st_, stb = self.r_st.next()
                st = st_[:, :].rearrange("p (c t) -> p c t", c=8)
                for j, (k0, nk) in enumerate(pblk):
                    xt, xb = self.r_x.next()
                    self.dma(xt[:, :], dr["cache_diff_k"][k0:k0 + 128, :], [], [xb])
                    ub, ubb = self.r_ub.next()
                    self.cp("dve", ub[:, :], xt[:, :], [xb], [ubb])
                    tp, tpb = self.ps_tp.next()
                    tpv = tp[:, :].bitcast(BF16)
                    for c in range(8):
                        self.tr(tpv[:, c * 128:(c + 1) * 128], ub[:, c * 128:(c + 1) * 128], self.identb[:, :],
                                [ubb, self.identb_b], [tpb])
                    self.cp("act", st[:, :, j * 128:(j + 1) * 128], tpv.rearrange("p (c t) -> p c t", c=8), [tpb], [stb])
                    vt, vtb = self.r_x.next()
                    self.dma(vt[:, :], dr["cache_diff_v"][k0:k0 + 128, :], [], [vtb])
                    vb, vbb = self.r_vb.next()
                    vb3 = vb[:, 0:1032].rearrange("p (h c) -> p h c", h=8)
                    self.memset("pool", vb3[:, :, 128:129], 1.0, [vbb])
                    self.cp("dve", vb3[:, :, 0:128], vt[:, :].rearrange("p (h c) -> p h c", h=8), [vtb], [vbb])
                    self.dma(VAd[k0:k0 + 128, 0:1032], vb[:, 0:1032], [vbb], [nb_(("pv", k0))])
                kp, nbk = pblk[0][0], 128 * len(pblk)
                for two in range(2):
                    self.dma(KTv[two, 0:64, :, kp:kp + nbk], st[two * 64:(two + 1) * 64, :, 0:nbk], [stb],
                             [nb_(("pk", kp, two))])
                self.dma(KTd[:, 64:66, kp:kp + nbk], ones2[:, :, 0:nbk], [ones2b], [nb_(("p1", kp))])
            for blk in g.blocks:
                uT, uTb, nb = self.make_uT(g, blk, first)
                t0b = blk[0][0]
                kpos = g.past + t0b
                for which, dst in ((0, QTv), (1, KTv)):
                    pos = t0b if which == 0 else kpos
                    for hf in range(2):
                        st_, stb = self.r_st.next()
                        st = st_[:, :].rearrange("p (c t) -> p c t", c=4)
                        for c4 in range(4):
                            c = hf * 4 + c4
                            ps, pb = self.proj_fm(uT, uTb, nb, which * D + c * 128, which * D + (c + 1) * 128)
                            if which == 0:
                                self.act(st[:, c4, 0:nb], ps[:, 0:nb], AF.Copy, [pb], [stb], scale=0.125)
                            else:
                                self.cp("dve", st[:, c4, 0:nb], ps[:, 0:nb], [pb], [stb])
                        for two in range(2):
                            self.dma(dst[two, 0:64, hf * 4:hf * 4 + 4, pos:pos + nb], st[two * 64:(two + 1) * 64, :, 0:nb],
                                     [stb], [nb_(("qk", which, t0b, two, hf))])
                self.dma(KTd[:, 64:66, kpos:kpos + nb], ones2[:, :, 0:nb], [ones2b], [nb_(("k1", t0b))])
                QTm = QTd.rearrange("(h m) r s -> m h r s", m=2)
                for m_ in range(2):
                    self.dma(QTm[m_, :, 64:66, t0b:t0b + nb], crow[m_ * 8:(m_ + 1) * 8, :, 0:nb], [crowb],
                             [nb_(("qc", t0b, m_))])
                off = 0
                for (t0, n) in blk:
                    for which, oname in ((1, "diff_k_"), (2, "diff_v_")):
                        ft, fb = self.r_f.next()
                        for cg in range(2):
                            ps, pb = self.proj_tm(uT, uTb, off, n, which * D + cg * 512, which * D + (cg + 1) * 512)
                            self.cp("act" if cg == 0 else "dve", ft[:n, cg * 512:(cg + 1) * 512], ps[:n, :], [pb], [fb])
                        self.dma(dr[oname + nm][t0:t0 + n, :], ft[:n, :], [fb], [self.buf((oname, g.gi, t0))])
                        if which == 2:
                            vb, vbb = self.r_vb.next()
                            vb3 = vb[:n, 0:1032].rearrange("p (h c) -> p h c", h=8)
                            self.memset("pool", vb3[:, :, 128:129], 1.0, [vbb])
                            self.cp("dve", vb3[:, :, 0:128], ft[:n, :].rearrange("p (h c) -> p h c", h=8), [fb], [vbb])
                            self.dma(VAd[g.past + t0:g.past + t0 + n, 0:1032], vb[:n, 0:1032], [vbb], [nb_(("v", t0))])
                    zt, ztb = self.r_h.next()
                    for cg in range(2):
                        ps, pb = self.proj_tm(uT, uTb, off, n, 3 * D + cg * 512, 3 * D + (cg + 1) * 512)
                        self.act(zt[:n, cg * 512:(cg + 1) * 512], ps[:n, :], AF.Silu, [pb], [ztb])
                    self.dma(ZSd[t0:t0 + n, 0:D], zt[:n, 0:D], [ztb], [nb_(("z", t0))])
                    off += n
            ogw = []
            for vh in range(16):
                h, m = vh // 2, vh % 2

                def extras(ki, k0, nk, qi, q0a, n, h=h):
                    if k0 == q0a:
                        hi, hib, lo, lob = Hh[h][0]
                        return [(hi[:, 0:nk], J[:, 0:n], [hib, Jb]), (lo[:, 0:nk], J[:, 0:n], [lob, Jb]),
                                (self.identb[:nk, :nk], dmk[:nk, :n], [self.identb_b, dmkb])]
                    if k0 == q0a - 128:
                        hi, hib, lo, lob = Hh[h][1]
                        return [(hi[:, 0:nk], J[:, 0:n], [hib, Jb]), (lo[:, 0:nk], J[:, 0:n], [lob, Jb])]
                    return []

                def finish(bi, blk, accs, per_bank, w, h=h, m=m, vh=vh):
                    og, ogb = self.r_x.next()
                    for qi, (q0, n) in enumerate(blk):
                        acc, accb = accs[qi // per_bank]
                        col = (qi % per_bank) * w
                        rd, rdb = self.r_small.next()
                        self.recip(rd[:n, 0:1], acc[:n, col + 128:col + 129], [accb], [rdb])
                        self.ts("dve", og[:n, qi * 128:(qi + 1) * 128], acc[:n, col:col + 128], rd[:n, 0:1], None,
                                ALU.mult, None, [accb, rdb], [ogb])
                    q0b = blk[0][0]
                    kb = self.buf(("diff_og", g.gi, L, vh, bi))
                    ogw.append(kb)
                    c0 = m * D + h * 128
                    if blk[0][1] == 128:
                        nt = len(blk)
                        self.dma(OGd[q0b:q0b + nt * 128, c0:c0 + 128].rearrange("(qi p) c -> p qi c", p=128),
                                 og[:, 0:nt * 128].rearrange("p (qi c) -> p qi c", c=128), [ogb], [kb])
                    else:
                        n = blk[0][1]
                        self.dma(OGd[q0b:q0b + n, c0:c0 + 128], og[:n, 0:128], [ogb], [kb])
                self.attn_head(g, KTd[vh][0:66, :], QTd[vh][0:66, :], VAd, h * 129, 128, 66, wb, extras, finish)
            for (t0, n) in g.tiles:
                o1, o1b = self.r_x.next()
                self.dma(o1[:n, :], OGd[t0:t0 + n, 0:D], ogw, [o1b])
                o2, o2b = self.r_f.next()
                self.dma(o2[:n, :], OGd[t0:t0 + n, D:2 * D], ogw, [o2b])
                self.stt(o1[:n, :], o2[:n, :], nlam[:n, 0:1], o1[:n, :], ALU.mult, ALU.add, [o2b, nlamb, o1b], [o1b])
                self.tt("dve", o2[:n, :], o1[:n, :], o1[:n, :], ALU.mult, [o1b], [o2b])
                ss, ssb = self.r_small.next()
                self.P.op("dve", lambda e, ss=ss, o2=o2, n=n: e.tensor_reduce(
                    out=ss[:n, 0:8], in_=o2[:n, :].rearrange("p (h c) -> p h c", h=8), axis=AX.X, op=ALU.add),
                    [o2b], [ssb])
                self.rsqrt(ss[:n, 0:8], ss[:n, 0:8], [ssb], [ssb], scale=1.0 / 128, eps=NORM_EPS)
                self.tt("dve", o1[:n, :].rearrange("p (h c) -> p h c", h=8), o1[:n, :].rearrange("p (h c) -> p h c", h=8),
                        ss[:n, 0:8].unsqueeze(2).to_broadcast([n, 8, 128]), ALU.mult, [o1b, ssb], [o1b])
                self.tt("dve", o1[:n, :].rearrange("p (h c) -> p h c", h=8), o1[:n, :].rearrange("p (h c) -> p h c", h=8),
                        slw[:n, :].unsqueeze(1).to_broadcast([n, 8, 128]), ALU.mult, [o1b, slwb], [o1b])
                zs, zsb = self.r_h.next()
                self.dma(zs[:n, 0:D], ZSd[t0:t0 + n, 0:D], wb, [zsb])
                og, ogb = self.r_h.next()
                self.tt("dve", og[:n, 0:D], o1[:n, :], zs[:n, 0:D], ALU.mult, [o1b, zsb], [ogb])
                self.pass_c_tile(g, t0, n, first, last, og, ogb, 8)

    def layer_ret(self, L, first, last):
        dr = self.dr
        P = self.P
        A = self.arena
        LG = [math.log1p(-2.0 ** (-5.0 - h)) for h in range(4)]
        P.barrier(); A.reset()
        self.set_psum(4, 2, 2)
        self.load_w(self.W, self.Wb, dr["ret_w_in"], 4096)
        r_st = Ring(self, "rst", [128, 8, 512], BF16, 1, arena=A)
        r_cs = Ring(self, "rcs", [128, 2, 512], F32, 1, arena=A)
        r_vt = Ring(self, "rvt", [128, 2048], BF16, 2, arena=A)
        wbs = {}
        for g in self.groups:
            nm = g.name
            wb = wbs[g.gi] = []

            def nb_(key, wb=wb, g=g):
                b = self.buf(("ret", g.gi) + key)
                wb.append(b)
                return b
            for blk in g.blocks:
                uT, uTb, nb = self.make_uT(g, blk, first)
                t0b = blk[0][0]
                cs, csb = r_cs.next()
                self.dma(cs[:, 0, 0:nb], dr["rot_cos_" + nm][:, t0b:t0b + nb], [], [csb])
                self.dma(cs[:, 1, 0:nb], dr["rot_sin_" + nm][:, t0b:t0b + nb], [], [csb])
                for which, dname in ((0, "QR_"), (1, "KR_")):
                    st, stb = r_st.next()
                    for h in range(4):
                        c0 = which * D + h * 256
                        pe_, peb = self.proj_fm(uT, uTb, nb, c0, c0 + 128)
                        po_, pob = self.proj_fm(uT, uTb, nb, c0 + 128, c0 + 256)
                        xe, xeb = self.r_f.next()
                        sc = 0.0625 if which == 0 else 1.0
                        self.act(xe[:, 0:nb], pe_[:, 0:nb], AF.Copy, [peb], [xeb], scale=sc)
                        self.act(xe[:, 512:512 + nb], po_[:, 0:nb], AF.Copy, [pob], [xeb], scale=sc)
                        tm, tmb = self.r_x.next()
                        self.tt("dve", tm[:, 0:nb], xe[:, 0:nb], cs[:, 0, 0:nb], ALU.mult, [xeb, csb], [tmb])
                        self.tt("dve", tm[:, 512:512 + nb], xe[:, 512:512 + nb], cs[:, 1, 0:nb], ALU.mult, [xeb, csb], [tmb])
                        self.tt("dve", st[:, 2 * h, 0:nb], tm[:, 0:nb], tm[:, 512:512 + nb], ALU.subtract, [tmb], [stb])
                        tm, tmb = self.r_x.next()
                        self.tt("dve", tm[:, 0:nb], xe[:, 0:nb], cs[:, 1, 0:nb], ALU.mult, [xeb, csb], [tmb])
                        self.tt("dve", tm[:, 512:512 + nb], xe[:, 512:512 + nb], cs[:, 0, 0:nb], ALU.mult, [xeb, csb], [tmb])
                        self.tt("dve", st[:, 2 * h + 1, 0:nb], tm[:, 0:nb], tm[:, 512:512 + nb], ALU.add, [tmb], [stb])
                    self.dma(dr[dname + nm].rearrange("he p t -> p he t")[:, :, t0b:t0b + nb], st[:, :, 0:nb], [stb],
                             [nb_((dname, t0b))])
                off = 0
                for (t0, n) in blk:
                    vt, vtb = r_vt.next()
                    for cg in range(4):
                        ps, pb = self.proj_tm(uT, uTb, off, n, 2 * D + cg * 512, 2 * D + (cg + 1) * 512)
                        self.cp("act" if cg % 2 == 0 else "dve", vt[:n, cg * 512:(cg + 1) * 512], ps[:n, :], [pb], [vtb])
                    self.dma(dr["OG_" + nm + "v"][t0:t0 + n, :], vt[:n, :], [vtb], [nb_(("v", t0))])
                    off += n
        self.load_w(self.W, self.Wb, dr["ret_w_in"], 2048, c_off=4096)
        for g in self.groups:
            nm = g.name
            wb = wbs[g.gi]
            for blk in g.blocks:
                uT, uTb, nb = self.make_uT(g, blk, first)
                off = 0
                for (t0, n) in blk:
                    zt, ztb = r_vt.next()
                    for cg in range(4):
                        ps, pb = self.proj_tm(uT, uTb, off, n, cg * 512, (cg + 1) * 512)
                        self.act(zt[:n, cg * 512:(cg + 1) * 512], ps[:n, :], AF.Silu, [pb], [ztb])
                    b = self.buf(("ret", g.gi, "z", t0))
                    wb.append(b)
                    self.dma(dr["ZS_" + nm][t0:t0 + n, :], zt[:n, :], [ztb], [b])
                    off += n
        P.barrier(); A.reset()
        self.load_w(self.WO, self.WOb, dr["ret_w_out"], D)
        self.load_w(self.W, self.Wb, dr["ret_w_out"][D:2 * D, :], D)
        dmk, dmkb = A.alloc([128, 4, 128], F32), Buf()
        self.dma(dmk[:, :, :], dr["ret_dmaskT"][:, :, :], [], [dmkb])
        qdc, qdcb = A.alloc([128, 4, 128], F32), Buf()
        self.dma(qdc[:, :, :], dr["ret_qdec"][:, :, :], [], [qdcb])
        kdc, kdcb = A.alloc([128, 8], F32), Buf()
        self.dma(kdc[:, :], dr["ret_kdec"][:, :], [], [kdcb])
        S, Sb = A.alloc([128, 2, 512], F32), Buf()
        Sbf, Sbfb = A.alloc([128, 2, 512], BF16), Buf()
        r_q = Ring(self, "rq", [128, 2, 512], BF16, 2, arena=A)
        r_k = Ring(self, "rk", [128, 2, 512], BF16, 2, arena=A)
        r_qd = Ring(self, "rqd", [128, 2, 128], BF16, 2, arena=A)
        r_v = Ring(self, "rv", [128, 4, 512], BF16, 2, arena=A)
        r_at = Ring(self, "rat", [128, 128], BF16, 2, arena=A)
        r_kd = Ring(self, "rkd", [128, 2, 128], BF16, 2, arena=A)
        r_o = Ring(self, "ro", [128, 512], BF16, 3, arena=A)
        ogws = {}
        for g in self.groups:
            nm = g.name
            wb = wbs[g.gi]
            ogw = ogws[g.gi] = []
            QRd = dr["QR_" + nm].rearrange("(h e) p t -> h p e t", e=2)
            KRd = dr["KR_" + nm].rearrange("(h e) p t -> h p e t", e=2)
            VRd = dr["OG_" + nm + "v"]
            for h in range(4):
                if g.gi == 0:
                    self.memset("pool", S[:, :, :], 0.0, [Sb])
                else:
                    self.dma(S[:, :, :], dr["state_ret"][h].rearrange("(e p) v -> p e v", e=2), [], [Sb])
                self.cp("act", Sbf[:, :, :], S[:, :, :], [Sb], [Sbfb])
                for blk in g.blocks:
                    t0b = blk[0][0]
                    nb = sum(n for _, n in blk)
                    qt, qtb = r_q.next()
                    self.dma(qt[:, :, 0:nb], QRd[h, :, :, t0b:t0b + nb], wb, [qtb])
                    kt, ktb = r_k.next()
                    self.dma(kt[:, :, 0:nb], KRd[h, :, :, t0b:t0b + nb], wb, [ktb])
                    vt, vtb = r_v.next()
                    if blk[0][1] == 128:
                        self.dma(vt[:, 0:len(blk), :], VRd[t0b:t0b + nb, h * 512:(h + 1) * 512]
                                 .rearrange("(c p) v -> p c v", p=128), wb, [vtb])
                    else:
                        self.dma(vt[:nb, 0, :], VRd[t0b:t0b + nb, h * 512:(h + 1) * 512], wb, [vtb])
                    off = 0
                    for ci, (t0, n) in enumerate(blk):
                        kcol = h if n == 128 else 4 + h
                        cdec = math.exp(LG[h] * n)
                        ps, pb = self.ps_mm.next()
                        for e in range(2):
                            self.mm(ps[:n, 0:n], kt[:, e, off:off + n], qt[:, e, off:off + n], e == 0, e == 1,
                                    [ktb, qtb], [pb])
                        at, atb = r_at.next()
                        self.tt("dve", at[:n, 0:n], ps[:n, 0:n], dmk[:n, h, 0:n], ALU.mult, [pb, dmkb], [atb])
                        qd, qdb = r_qd.next()
                        for e in range(2):
                            self.tt("dve", qd[:, e, 0:n], qt[:, e, off:off + n], qdc[:, h, 0:n], ALU.mult,
                                    [qtb, qdcb], [qdb])
                        po, pob = self.ps_acc.next()
                        self.mm(po[:n, :], at[:n, 0:n], vt[:n, ci, :], True, False, [atb, vtb], [pob])
                        for e in range(2):
                            self.mm(po[:n, :], qd[:, e, 0:n], Sbf[:, e, :], False, e == 1, [qdb, Sbfb], [pob])
                        st, stb = self.r_small.next()
                        P.op("dve", lambda e_, st=st, po=po, n=n: e_.bn_stats(out=st[:n, 0:6], in_=po[:n, :]), [pob], [stb])
                        P.op("dve", lambda e_, st=st, n=n: e_.bn_aggr(out=st[:n, 6:8], in_=st[:n, 0:6]), [stb], [stb])
                        self.rsqrt(st[:n, 10:11], st[:n, 7:8], [stb], [stb], eps=LN_EPS)
                        self.stt(st[:n, 11:12], st[:n, 6:7], -1.0, st[:n, 10:11], ALU.mult, ALU.mult, [stb], [stb])
                        ot, otb = r_o.next()
                        self.act(ot[:n, :], po[:n, :], AF.Identity, [pob, stb], [otb], bias=st[:n, 11:12], scale=st[:n, 10:11])
                        b = self.buf(("ret_og", g.gi, h, t0))
                        ogw.append(b)
                        self.dma(dr["OG_" + nm][t0:t0 + n, h * 512:(h + 1) * 512], ot[:n, :], [otb], [b])
                        kd, kdb = r_kd.next()
                        tp, tpb = self.ps_tp.next()
                        tpv = tp[:, :].bitcast(BF16)
                        for e in range(2):
                            self.tr(tpv[:n, e * 128:(e + 1) * 128], kt[:, e, off:off + n], self.identb[:, :],
                                    [ktb, self.identb_b], [tpb])
                        self.ts("dve", kd[:n, :, :], tpv[:n, 0:256].rearrange("p (e d) -> p e d", e=2),
                                kdc[:n, kcol:kcol + 1], None, ALU.mult, None, [tpb, kdcb], [kdb])
                        for e in range(2):
                            pu, pub = self.ps_mm.next()
                            self.mm(pu[:, :], kd[:n, e, :], vt[:n, ci, :], True, True, [kdb, vtb], [pub])
                            self.stt(S[:, e, :], S[:, e, :], cdec, pu[:, :], ALU.mult, ALU.add, [Sb, pub], [Sb])
                            self.cp("act", Sbf[:, e, :], S[:, e, :], [Sb], [Sbfb])
                        off += n
                self.dma(dr["ret_state_" + nm][h].rearrange("(e p) v -> p e v", e=2), S[:, :, :], [Sb],
                         [self.buf(("ret_state", g.gi, h))])
        P.barrier(); A.reset()
        gnw, gnwb = A.alloc([128, 2048], F32), Buf()
        self.dma(gnw[:, :], dr["ret_gn_w"][0:1, :].to_broadcast([128, 2048]), [], [gnwb])
        r_og = Ring(self, "rog2", [128, 2048], BF16, 2, arena=A)
        r_zs = Ring(self, "rzs2", [128, 2048], BF16, 2, arena=A)
        r_gz = Ring(self, "rgz2", [128, 2048], BF16, 2, arena=A)

        def wo(k):
            if k < 8:
                return self.WO[:, k, :], self.WOb
            return self.W[:, k - 8, 0:D], self.Wb
        for g in self.groups:
            nm = g.name
            for (t0, n) in g.tiles:
                og, ogb = r_og.next()
                self.dma(og[:n, :], dr["OG_" + nm][t0:t0 + n, :], ogws[g.gi], [ogb])
                zs, zsb = r_zs.next()
                self.dma(zs[:n, :], dr["ZS_" + nm][t0:t0 + n, :], wbs[g.gi], [zsb])
                gz, gzb = r_gz.next()
                self.tt("dve", gz[:n, :], og[:n, :], gnw[:n, :], ALU.mult, [ogb, gnwb], [gzb])
                self.tt("dve", gz[:n, :], gz[:n, :], zs[:n, :], ALU.mult, [gzb, zsb], [gzb])
                self.pass_c_tile(g, t0, n, first, last, gz, gzb, 16, wo=wo)

    def layer_gdn(self, L, first, last):
        dr = self.dr
        P = self.P
        A = self.arena
        P.barrier(); A.reset()
        self.set_psum(4, 2, 2)
        self.load_w(self.W, self.Wb, dr["gdn_w_in"], 4112)
        self.load_w(self.WO, self.WOb, dr["gdn_w_out"], D)
        cw, cwb = A.alloc([128, 24, 4], F32), Buf()
        self.dma(cw[:, :, :], dr["gdn_convw"][:, :, :], [], [cwb])
        halo, halob = A.alloc([128, 24, 3], F32), Buf()
        onesb, onesbb = A.alloc([128, 128], BF16), Buf()
        self.memset("pool", onesb[:, :], 1.0, [onesbb])
        sc8, sc8b = A.alloc([8, 4], F32), Buf()
        self.dma(sc8[:, 0:1], dr["gdn_dt_bias"][:, :], [], [sc8b])
        self.dma(sc8[:, 2:3], dr["gdn_a_log"][:, :], [], [sc8b])
        self.act(sc8[:, 3:4], sc8[:, 2:3], AF.Exp, [sc8b], [sc8b])
        self.ts("dve", sc8[:, 1:2], sc8[:, 3:4], -1.0, None, ALU.mult, None, [sc8b], [sc8b])
        r_xr = Ring(self, "rxr", [128, 516], F32, 4, arena=A)
        r_y = Ring(self, "ry", [128, 512], F32, 4, arena=A)
        r_rn = Ring(self, "rrn", [128, 512], F32, 4, arena=A)
        r_sq = Ring(self, "rsq", [128, 512], BF16, 4, arena=A)
        r_st = Ring(self, "rst", [128, 8, 512], BF16, 1, arena=A)
        r_g8 = Ring(self, "rg8", [8, 512], F32, 6, arena=A)
        wbs = {}
        for g in self.groups:
            nm = g.name
            wb = wbs[g.gi] = []

            def nb_(key, wb=wb, g=g):
                b = self.buf(("gdn", g.gi) + key)
                wb.append(b)
                return b
            if g.gi == 0:
                self.memset("pool", halo[:, :, :], 0.0, [halob])
            else:
                self.dma(halo[:, :, :], dr["gdn_conv_in"][:, :, :], [], [halob])
            for blk in g.blocks:
                uT, uTb, nb = self.make_uT(g, blk, first)
                t0b = blk[0][0]
                for which, dname in ((0, "QG_"), (1, "KG_"), (2, "VG_")):
                    st, stb = r_st.next()
                    for hg in (0, 4):
                        hs = list(range(hg, hg + 4))
                        xrs, ys, rns, sqs, p2s = {}, {}, {}, {}, {}
                        for h in hs:
                            c = which * 8 + h
                            ps, pb = self.proj_fm(uT, uTb, nb, c * 128, (c + 1) * 128)
                            xr, xrb = r_xr.next()
                            self.cp("act", xr[:, 0:3], halo[:, c, :], [halob], [xrb])
                            self.cp("act", xr[:, 3:3 + nb], ps[:, 0:nb], [pb], [xrb])
                            self.cp("act", halo[:, c, :], xr[:, nb:nb + 3], [xrb], [halob])
                            xrs[h] = (xr, xrb)
                        for h in hs:
                            c = which * 8 + h
                            xr, xrb = xrs[h]
                            y, yb = r_y.next()
                            self.ts("dve", y[:, 0:nb], xr[:, 0:nb], cw[:, c, 0:1], None, ALU.mult, None, [xrb, cwb], [yb])
                            for j in range(1, 4):
                                self.stt(y[:, 0:nb], xr[:, j:j + nb], cw[:, c, j:j + 1], y[:, 0:nb], ALU.mult, ALU.add,
                                         [xrb, cwb, yb], [yb])
                            ys[h] = (y, yb)
                        for h in hs:
                            y, yb = ys[h]
                            self.act(y[:, 0:nb], y[:, 0:nb], AF.Silu, [yb], [yb])
                        if which == 2:
                            for h in hs:
                                y, yb = ys[h]
                                self.cp("dve", st[:, h, 0:nb], y[:, 0:nb], [yb], [stb])
                            continue
                        for h in hs:
                            y, yb = ys[h]
                            sq, sqb = r_sq.next()
                            self.tt("dve", sq[:, 0:nb], y[:, 0:nb], y[:, 0:nb], ALU.mult, [yb], [sqb])
                            p2, p2b = self.ps_mm.next()
                            self.mm(p2[:, 0:nb], onesb[:, :], sq[:, 0:nb], True, True, [onesbb, sqb], [p2b])
                            p2s[h] = (p2, p2b)
                        for h in hs:
                            p2, p2b = p2s[h]
                            rn, rnb = r_rn.next()
                            self.rsqrt(rn[:, 0:nb], p2[:, 0:nb], [p2b], [rnb], eps=NORM_EPS)
                            rns[h] = (rn, rnb)
                        for h in hs:
                            y, yb = ys[h]
                            rn, rnb = rns[h]
                            if which == 0:
                                self.stt(st[:, h, 0:nb], y[:, 0:nb], 128.0 ** -0.5, rn[:, 0:nb], ALU.mult, ALU.mult,
                                         [yb, rnb], [stb])
                            else:
                                self.tt("dve", st[:, h, 0:nb], y[:, 0:nb], rn[:, 0:nb], ALU.mult, [yb, rnb], [stb])
                    self.dma(dr[dname + nm].rearrange("h p t -> p h t")[:, :, t0b:t0b + nb], st[:, :, 0:nb], [stb],
                             [nb_((dname, t0b))])
                ps, pb = self.proj_fm(uT, uTb, nb, 3072, 3080)
                bt, btb = r_g8.next()
                self.act(bt[:, 0:nb], ps[0:8, 0:nb], AF.Exp, [pb], [btb], scale=-1.0)
                self.ts("dve", bt[:, 0:nb], bt[:, 0:nb], 1.0, None, ALU.add, None, [btb], [btb])
                self.recip(bt[:, 0:nb], bt[:, 0:nb], [btb], [btb])
                ps, pb = self.proj_fm(uT, uTb, nb, 3080, 3088)
                gt, gtb = r_g8.next()
                self.act(gt[:, 0:nb], ps[0:8, 0:nb], AF.Exp, [pb, sc8b], [gtb], bias=sc8[:, 0:1])
                self.act(gt[:, 0:nb], gt[:, 0:nb], AF.Ln, [gtb], [gtb], bias=1.0)
                self.ts("dve", gt[:, 0:nb], gt[:, 0:nb], sc8[:, 1:2], None, ALU.mult, None, [gtb, sc8b], [gtb])
                Gt, Gtb = r_g8.next()
                on8, on8b = r_g8.next()
                self.memset("pool", on8[:, 0:128], 1.0, [on8b])
                for off in range(0, nb, 64):
                    n = min(64, nb - off)
                    P.op("dve", lambda e, Gt=Gt, gt=gt, on8=on8, off=off, n=n: e.tensor_tensor_scan(
                        out=Gt[:, off:off + n], data0=on8[:, 0:n], data1=gt[:, off:off + n], initial=0.0,
                        op0=ALU.mult, op1=ALU.add), [gtb, on8b], [Gtb])
                btb16, btb16b = r_g8.next()
                bt16v = btb16[:, 0:256].bitcast(BF16)
                self.cp("act", bt16v[:, 0:nb], bt[:, 0:nb], [btb], [btb16b])
                self.dma(dr["BTb_" + nm][:, t0b:t0b + nb], bt16v[:, 0:nb], [btb16b], [nb_(("btb", t0b))])
                self.dma(dr["GT_" + nm][:, t0b:t0b + nb], Gt[:, 0:nb], [Gtb], [nb_(("gt", t0b))])
                off = 0
                for ti, (t0, n) in enumerate(blk):
                    tp, tpb = self.ps_tp.next()
                    self.tr(tp[0:n, 0:8], bt[:, off:off + n], self.identf[0:8, 0:8], [btb, self.identf_b], [tpb])
                    tp2, tp2b = self.ps_tp.next()
                    self.tr(tp2[0:n, 0:8], Gt[:, off:off + n], self.identf[0:8, 0:8], [Gtb, self.identf_b], [tp2b])
                    gb, gbb = self.r_small.next()
                    self.cp("act", gb[:n, 0:8], tp[0:n, 0:8], [tpb], [gbb])
                    self.cp("act", gb[:n, 8:16], tp2[0:n, 0:8], [tp2b], [gbb])
                    self.dma(dr["GB_" + nm][t0:t0 + n, :], gb[:n, 0:16], [gbb], [nb_(("gb", t0))])
                    zt, ztb = self.r_h.next()
                    for cg in range(2):
                        ps, pb = self.proj_tm(uT, uTb, off, n, 3088 + cg * 512, 3088 + (cg + 1) * 512)
                        self.act(zt[:n, cg * 512:(cg + 1) * 512], ps[:n, :], AF.Silu, [pb], [ztb])
                    self.dma(dr["ZS_" + nm][t0:t0 + n, 0:D], zt[:n, 0:D], [ztb], [nb_(("z", t0))])
                    off += n
                if blk is g.blocks[-1]:
                    co, cob = self.r_f.next()
                    for cg in range(6):
                        ps, pb = self.ps_mm.next()
                        for k in range(8):
                            self.mm(ps[0:3, :], uT[:, k, nb - 3:nb], self.W[:, k, cg * 512:(cg + 1) * 512], k == 0, k == 7,
                                    [uTb, self.Wb], [pb])
                        self.cp("act", co[0:3, (cg % 2) * 512:(cg % 2 + 1) * 512], ps[0:3, :], [pb], [cob])
                        if cg % 2 == 1:
                            self.dma(dr["gdn_conv_" + nm][:, (cg - 1) * 512:(cg + 1) * 512], co[0:3, :], [cob],
                                     [self.buf(("gdn_conv", g.gi, cg))])
                            if cg < 5:
                                co, cob = self.r_f.next()
        import os
        if os.environ.get("GDN_STOP") == "A":
            return
        P.barrier(); A.reset()
        A2 = self.arenaW
        A2.reset()
        msk, mskb = A.alloc([128, 3, 128], F32), Buf()
        self.dma(msk[:, :, :], dr["gdn_masks"][:, :, :], [], [mskb])
        S, Sb = A.alloc([128, 8, 128], F32), Buf()
        Sbf, Sbfb = A.alloc([128, 8, 128], BF16), Buf()
        r_qkv = Ring(self, "rqkv", [128, 3, 8, 128], BF16, 2, arena=A)
        r_gb = Ring(self, "rgb", [128, 8, 128], F32, 2, arena=A)
        r_bb = Ring(self, "rbb", [128, 8, 128], BF16, 2, arena=A)
        r_gbk = Ring(self, "rgbk", [64, 2, 16], F32, 2, arena=A)
        r_sq = Ring(self, "rsq", [64, 8, 128], F32, 1, arena=A)
        r_sc = Ring(self, "rsc", [64, 48], F32, 6, arena=A)
        r_F = Ring(self, "rF", [128, 8, 64], F32, 14, arena=A2)
        r_B = Ring(self, "rB", [128, 8, 64], BF16, 16, arena=A2)
        r_L = Ring(self, "rL", [128, 8, 64], BF16, 10, arena=A2)
        r_B2 = Ring(self, "rB2", [64, 8, 128], BF16, 8, arena=A)
        ogws = {}

        def bc(ap, shape):
            return ap.to_broadcast(shape)
        for g in self.groups:
            nm = g.name
            wb = wbs[g.gi]
            ogw = ogws[g.gi] = []
            if g.gi == 0:
                self.memset("pool", S[:, :, :], 0.0, [Sb])
            else:
                self.dma(S[:, :, :], dr["state_gdn"].rearrange("h d v -> d h v"), [], [Sb])
            self.cp("act", Sbf[:, :, :], S[:, :, :], [Sb], [Sbfb])
            for (tt0, tn) in g.tiles:
                qkv, qkvb = r_qkv.next()
                for i, dname in enumerate(("QG_", "KG_", "VG_")):
                    self.dma(qkv[:, i, :, 0:tn], dr[dname + nm].rearrange("h p t -> p h t")[:, :, tt0:tt0 + tn], wb, [qkvb])
                gb, gbb = r_gb.next()
                self.dma(gb[:, :, 0:tn], bass.AP(self.dh["GT_" + nm], tt0, [[0, 128], [g.T, 8], [1, tn]]), wb, [gbb])
                bb, bbb = r_bb.next()
                self.dma(bb[:, :, 0:tn], bass.AP(self.dh["BTb_" + nm], tt0, [[0, 128], [g.T, 8], [1, tn]]), wb, [bbb])
                gbk, gbkb = r_gbk.next()
                if tn % 64 == 0:
                    self.dma(gbk[0:64, 0:tn // 64, :], dr["GB_" + nm][tt0:tt0 + tn, :].rearrange("(c p) x -> p c x", p=64),
                             wb, [gbkb])
                else:
                    self.dma(gbk[:tn, 0, :], dr["GB_" + nm][tt0:tt0 + tn, :], wb, [gbkb])
                from types import SimpleNamespace as NS
                cks = []
                for ci, off in enumerate(range(0, tn, 64)):
                    c = NS(ci=ci, off=off, n=min(64, tn - off), t0=tt0 + off)
                    c.Qt = qkv[:, 0, :, off:off + c.n]
                    c.Kt = qkv[:, 1, :, off:off + c.n]
                    c.Vt = qkv[:, 2, :, off:off + c.n]
                    c.Gp = gbk[:c.n, ci, 8:16]
                    c.Bp = gbk[:c.n, ci, 0:8]
                    cks.append(c)
                mk = lambda c, k_: msk[:c.n, k_, 0:c.n].unsqueeze(1).to_broadcast([c.n, 8, c.n])
                v3 = lambda c, ps_: ps_[:c.n, :].rearrange("p (h c) -> p h c", h=8)[:, :, 0:c.n]
                t3 = lambda c, v_: v_[:c.n, :].rearrange("p (h c) -> p h c", h=8)
                bcs = lambda c, a_: a_.unsqueeze(2).to_broadcast([c.n, 8, 128])

                def s_bcast(c):
                    n, off = c.n, c.off
                    c.eG, c.eGb = r_F.next()
                    self.act(c.eG[:, :, 0:n], gb[:, :, off:off + n], AF.Exp, [gbb], [c.eGb])
                    c.KbT, c.KbTb = r_B.next()
                    self.tt("dve", c.KbT[:, :, 0:n], c.Kt, bb[:, :, off:off + n], ALU.mult, [qkvb, bbb], [c.KbTb])
                    c.sc, c.scb = r_sc.next()
                    self.act(c.sc[:n, 16:24], c.Gp, AF.Exp, [gbkb], [c.scb])
                    c.tD, c.tDb = r_F.next()
                    self.tt("dve", c.tD[:n, :, 0:n], gb[:n, :, off:off + n], c.Gp.unsqueeze(2).to_broadcast([n, 8, n]),
                            ALU.subtract, [gbb, gbkb], [c.tDb])

                def s_gram(c):
                    n = c.n
                    c.paA, c.paAb = self.ps_mm.next()
                    for h in range(8):
                        self.mm(c.paA[:n, h * 64:h * 64 + n], c.Kt[:, h, :], c.KbT[:, h, 0:n], True, True, [qkvb, c.KbTb], [c.paAb], skip=True)
                    c.paB, c.paBb = self.ps_mm.next()
                    for h in range(8):
                        self.mm(c.paB[:n, h * 64:h * 64 + n], c.KbT[:, h, 0:n], c.Kt[:, h, :], True, True, [qkvb, c.KbTb], [c.paBb], skip=True)
                    c.D1, c.D1b = r_F.next()
                    self.ts("dve", c.D1[:n, :, 0:n], c.tD[:n, :, 0:n], 0.0, None, ALU.min, None, [c.tDb], [c.D1b])
                    self.act(c.D1[:n, :, 0:n], c.D1[:n, :, 0:n], AF.Exp, [c.D1b], [c.D1b])
                    c.D2, c.D2b = r_F.next()
                    self.ts("dve", c.D2[:n, :, 0:n], c.tD[:n, :, 0:n], 0.0, None, ALU.max, None, [c.tDb], [c.D2b])
                    self.act(c.D2[:n, :, 0:n], c.D2[:n, :, 0:n], AF.Exp, [c.D2b], [c.D2b], scale=-1.0)

                def s_mask(c):
                    n = c.n
                    c.D1i, c.D1ib = r_F.next()
                    self.tt("dve", c.D1i[:n, :, 0:n], c.D1[:n, :, 0:n], mk(c, 2), ALU.mult, [c.D1b, mskb], [c.D1ib])
                    self.tt("dve", c.D1[:n, :, 0:n], c.D1[:n, :, 0:n], mk(c, 0), ALU.mult, [c.D1b, mskb], [c.D1b])
                    self.tt("dve", c.D2[:n, :, 0:n], c.D2[:n, :, 0:n], mk(c, 1), ALU.mult, [c.D2b, mskb], [c.D2b])
                    c.Pm, c.Pmb = r_F.next()
                    self.tt("dve", c.Pm[:n, :, 0:n], v3(c, c.paA), c.D1[:n, :, 0:n], ALU.mult, [c.paAb, c.D1b], [c.Pmb])
                    c.X, c.Xb = r_B.next()
                    self.cp("act", c.X[:n, :, 0:n], c.Pm[:n, :, 0:n], [c.Pmb], [c.Xb])
                    c.Y, c.Yb = r_B.next()
                    self.tt("dve", c.Y[:n, :, 0:n], v3(c, c.paB), c.D2[:n, :, 0:n], ALU.mult, [c.paBb, c.D2b], [c.Yb])
                    self.tt("dve", c.Pm[:n, :, 0:n], c.Pm[:n, :, 0:n],
                            self.identf[:n, 0:n].unsqueeze(1).to_broadcast([n, 8, n]), ALU.add, [c.Pmb, self.identf_b], [c.Pmb])
                    c.Pbf, c.Pbfb = r_B.next()
                    self.cp("act", c.Pbf[:n, :, 0:n], c.Pm[:n, :, 0:n], [c.Pmb], [c.Pbfb])
                    c.nsteps = max(1, int(math.ceil(math.log2(n))) - 1)

                def mk_step(s):
                    def s_sq(c):
                        n = c.n
                        if s >= c.nsteps:
                            return
                        c.lastst = s == c.nsteps - 1
                        c.pxY, c.pxYb = self.ps_mm.next()
                        for h in range(8):
                            self.mm(c.pxY[:n, h * 64:h * 64 + n], c.X[:n, h, 0:n], c.Y[:n, h, 0:n], True, True, [c.Xb, c.Yb], [c.pxYb], skip=True)
                        if not c.lastst:
                            c.pxX, c.pxXb = self.ps_mm.next()
                            for h in range(8):
                                self.mm(c.pxX[:n, h * 64:h * 64 + n], c.Y[:n, h, 0:n], c.X[:n, h, 0:n], True, True, [c.Xb, c.Yb], [c.pxXb], skip=True)
                        c.Y2, c.Y2b = r_B.next()
                        self.cp("act", c.Y2[:n, :, 0:n], v3(c, c.pxY), [c.pxYb], [c.Y2b])
                        if not c.lastst:
                            c.X2, c.X2b = r_B.next()
                            self.cp("dve", c.X2[:n, :, 0:n], v3(c, c.pxX), [c.pxXb], [c.X2b])

                    def s_acc(c):
                        n = c.n
                        if s >= c.nsteps:
                            return
                        pp, ppb = self.ps_mm.next()
                        for h in range(8):
                            self.mm(pp[:n, h * 64:h * 64 + n], c.Y2[:n, h, 0:n], c.Pbf[:n, h, 0:n], True, True, [c.Y2b, c.Pbfb], [ppb], skip=True)
                        self.tt("dve", c.Pm[:n, :, 0:n], c.Pm[:n, :, 0:n], v3(c, pp), ALU.add, [c.Pmb, ppb], [c.Pmb])
                        c.Pbf, c.Pbfb = r_L.next() if c.lastst else r_B.next()
                        self.cp("act", c.Pbf[:n, :, 0:n], c.Pm[:n, :, 0:n], [c.Pmb], [c.Pbfb])
                        c.Y, c.Yb = c.Y2, c.Y2b
                        if not c.lastst:
                            c.X, c.Xb = c.X2, c.X2b
                    return [s_sq, s_acc]

                def s_tok(c):
                    n, off = c.n, c.off
                    self.tt("dve", c.sc[:n, 0:8], c.sc[:n, 16:24], c.Bp, ALU.mult, [c.scb, gbkb], [c.scb])
                    self.tt("dve", c.sc[:n, 24:32], c.Gp, gb[:n, :, off + n - 1], ALU.subtract, [gbkb, gbb], [c.scb])
                    self.act(c.sc[:n, 8:16], c.sc[:n, 24:32], AF.Exp, [c.scb], [c.scb], scale=-1.0)
                    tpK, tpKb = self.ps_tp.next()
                    tpV, tpVb = self.ps_tp.next()
                    tKv = tpK[:, :].bitcast(BF16)
                    tVv = tpV[:, :].bitcast(BF16)
                    for h in range(8):
                        self.tr(tKv[:n, h * 128:(h + 1) * 128], c.Kt[:, h, :], self.identb[:, :], [qkvb, self.identb_b], [tpKb])
                    for h in range(8):
                        self.tr(tVv[:n, h * 128:(h + 1) * 128], c.Vt[:, h, :], self.identb[:, :], [qkvb, self.identb_b], [tpVb])
                    c.kbg, c.kbgb = r_B2.next()
                    self.tt("dve", c.kbg[:n, :, :], t3(c, tKv), bcs(c, c.sc[:n, 0:8]), ALU.mult, [tpKb, c.scb], [c.kbgb])
                    c.kdc, c.kdcb = r_B2.next()
                    self.tt("dve", c.kdc[:n, :, :], t3(c, tKv), bcs(c, c.sc[:n, 8:16]), ALU.mult, [tpKb, c.scb], [c.kdcb])
                    c.vb_, c.vbb_ = r_B2.next()
                    self.tt("dve", c.vb_[:n, :, :], t3(c, tVv), bcs(c, c.Bp), ALU.mult, [tpVb, gbkb], [c.vbb_])
                    c.qdT, c.qdTb = r_L.next()
                    self.tt("dve", c.qdT[:, :, 0:n], c.Qt, c.eG[:, :, 0:n], ALU.mult, [qkvb, c.eGb], [c.qdTb])
                    c.paQ, c.paQb = self.ps_mm.next()
                    for h in range(8):
                        self.mm(c.paQ[:n, h * 64:h * 64 + n], c.Kt[:, h, :], c.Qt[:, h, :], True, True, [qkvb], [c.paQb], skip=True)
                    c.QK, c.QKb = r_L.next()
                    self.tt("dve", c.QK[:n, :, 0:n], v3(c, c.paQ), c.D1i[:n, :, 0:n], ALU.mult, [c.paQb, c.D1ib], [c.QKb])

                def s_wd(c):
                    n = c.n
                    pw, pwb = self.ps_mm.next()
                    for h in range(8):
                        self.mm(pw[:, h * 64:h * 64 + n], c.kbg[:n, h, :], c.Pbf[:n, h, 0:n], True, True, [c.kbgb, c.Pbfb], [pwb], skip=True)
                    c.wdT, c.wdTb = r_L.next()
                    self.act(c.wdT[:, :, 0:n], pw[:, :].rearrange("p (h c) -> p h c", h=8)[:, :, 0:n], AF.Copy, [pwb], [c.wdTb],
                             scale=-1.0)

                stages = [s_bcast, s_gram, s_mask]
                for s in range(5):
                    stages += mk_step(s)
                stages += [s_tok, s_wd]
                for f in stages:
                    for c in cks:
                        f(c)

                for c in cks:
                    n, off, t0 = c.n, c.off, c.t0
                    pv = [self.ps_acc.next(), self.ps_acc.next()]
                    for h in range(8):
                        pt_, ptb_ = pv[h // 4]
                        o_ = pt_[:n, (h % 4) * 128:(h % 4 + 1) * 128]
                        self.mm(o_, c.Pbf[:n, h, 0:n], c.vb_[:n, h, :], True, False, [c.Pbfb, c.vbb_], [ptb_], skip=True)
                        self.mm(o_, c.wdT[:, h, 0:n], Sbf[:, h, :], False, True, [c.wdTb, Sbfb], [ptb_], skip=True)
                    vr, vrb = r_B2.next()
                    for hb in range(2):
                        self.cp("act", vr[:n, hb * 4:(hb + 1) * 4, :], pv[hb][0][:n, :].rearrange("p (h c) -> p h c", h=4),
                                [pv[hb][1]], [vrb])
                    self.tt("dve", S[:, :, :], S[:, :, :], c.eG[:, :, n - 1:n].to_broadcast([128, 8, 128]), ALU.mult,
                            [Sb, c.eGb], [Sb])
                    po = [self.ps_acc.next(), self.ps_acc.next()] if False else [self.ps_mm.next(), self.ps_mm.next()]
                    for h in range(8):
                        pt_, ptb_ = po[h // 4]
                        o_ = pt_[:n, (h % 4) * 128:(h % 4 + 1) * 128]
                        self.mm(o_, c.qdT[:, h, 0:n], Sbf[:, h, :], True, False, [c.qdTb, Sbfb], [ptb_], skip=True)
                        self.mm(o_, c.QK[:n, h, 0:n], vr[:n, h, :], False, True, [c.QKb, vrb], [ptb_], skip=True)
                    pu = [self.ps_mm.next(), self.ps_mm.next()]
                    for h in range(8):
                        pt_, ptb_ = pu[h // 4]
                        self.mm(pt_[:, (h % 4) * 128:(h % 4 + 1) * 128], c.kdc[:n, h, :], vr[:n, h, :], True, True,
                                [c.kdcb, vrb], [ptb_], skip=True)
                    for hb in range(2):
                        self.tt("dve", S[:, hb * 4:(hb + 1) * 4, :], S[:, hb * 4:(hb + 1) * 4, :],
                                pu[hb][0][:, :].rearrange("p (h c) -> p h c", h=4), ALU.add, [Sb, pu[hb][1]], [Sb])
                    self.cp("act", Sbf[:, :, :], S[:, :, :], [Sb], [Sbfb])
                    sq, sqb = r_sq.next()
                    for hb in range(2):
                        self.act(sq[:n, hb * 4:(hb + 1) * 4, :], po[hb][0][:n, :].rearrange("p (h c) -> p h c", h=4),
                                 AF.Square, [po[hb][1]], [sqb])
                    sc, scb = c.sc, c.scb
                    P.op("dve", lambda e, sc=sc, sq=sq, n=n: e.tensor_reduce(out=sc[:n, 32:40], in_=sq[:n, :, :], axis=AX.X,
                                                                         op=ALU.add), [sqb], [scb])
                    self.rsqrt(sc[:n, 32:40], sc[:n, 32:40], [scb], [scb], scale=1.0 / 128, eps=NORM_EPS)
                    ot, otb = r_B2.next()
                    for hb in range(2):
                        self.tt("dve", ot[:n, hb * 4:(hb + 1) * 4, :], po[hb][0][:n, :].rearrange("p (h c) -> p h c", h=4),
                                sc[:n, 32 + hb * 4:36 + hb * 4].unsqueeze(2).to_broadcast([n, 4, 128]), ALU.mult,
                                [po[hb][1], scb], [otb])
                    b = self.buf(("gdn_og", g.gi, t0))
                    ogw.append(b)
                    self.dma(dr["OG_" + nm][t0:t0 + n, 0:D], ot[:n, :, :].rearrange("p h c -> p (h c)"), [otb], [b])
            self.dma(dr["gdn_state_" + nm].rearrange("h d v -> d h v"), S[:, :, :], [Sb], [self.buf(("gdn_state", g.gi))])
        if os.environ.get("GDN_STOP") == "B":
            return
        P.barrier(); A.reset()
        nw, nwb = A.alloc([128, D], F32), Buf()
        for h in range(8):
            self.dma(nw[:, h * 128:(h + 1) * 128], dr["gdn_norm_w"][0:1, :].to_broadcast([128, 128]), [], [nwb])
        for g in self.groups:
            nm = g.name
            for (t0, n) in g.tiles:
                og, ogb = self.r_h.next()
                self.dma(og[:n, 0:D], dr["OG_" + nm][t0:t0 + n, 0:D], ogws[g.gi], [ogb])
                zs, zsb = self.r_h.next()
                self.dma(zs[:n, 0:D], dr["ZS_" + nm][t0:t0 + n, 0:D], wbs[g.gi], [zsb])
                gz, gzb = self.r_f.next()
                self.tt("dve", gz[:n, :], og[:n, 0:D], nw[:n, :], ALU.mult, [ogb, nwb], [gzb])
                self.tt("dve", og[:n, 0:D], gz[:n, :], zs[:n, 0:D], ALU.mult, [gzb, zsb], [ogb])
                self.pass_c_tile(g, t0, n, first, last, og, ogb, 8)

    def layer_dummy(self, L, first, last):
        self.layer_mod(L)
        self.load_w(self.W, self.Wb, self.dr["fox_w_in"], 4112)
        self.load_w(self.WO, self.WOb, self.dr["fox_w_out"], D)
        for g in self.groups:
            for blk in g.blocks:
                uT, uTb, nb = self.make_uT(g, blk, first)
                off = 0
                for (t0, n) in blk:
                    og, ogb = self.r_h.next()
                    for cg in range(2):
                        ps, pb = self.proj_tm(uT, uTb, off, n, cg * 512, (cg + 1) * 512)
                        self.cp("act", og[:n, cg * 512:(cg + 1) * 512], ps[:n, :], [pb], [ogb])
                    self.pass_c_tile(g, t0, n, first, last, og, ogb, 8)
                    off += n


def build(cfg, mode="full"):
    k = K(cfg)
    k.setup()
    nl = len(cfg.layers)
    for i, L in enumerate(cfg.layers):
        if mode == "dummy":
            k.layer_dummy(L, i == 0, i == nl - 1)
        else:
            k.layer(L, i == 0, i == nl - 1)
    k.P.emit()
    return k


def t5_onehot():
    import jax
    import jax.numpy as jnp
    with jax.default_device(jax.devices("cpu")[0]):
        rel = jnp.arange(384, dtype=jnp.int32) - 255
        nb = 16
        max_exact = 8
        ret = jnp.where(rel > 0, nb, 0)
        n = jnp.abs(rel)
        large = max_exact + (jnp.log(jnp.maximum(n, 1).astype(jnp.float32) / max_exact)
                             / math.log(128 / max_exact) * (nb - max_exact)).astype(jnp.int32)
        large = jnp.minimum(large, nb - 1)
        bucket = np.asarray(ret + jnp.where(n < max_exact, n, large))
    oh = np.zeros((32, 384), np.float32)
    oh[bucket, np.arange(384)] = 1.0
    return oh


def ret_perm():
    one = np.concatenate([np.arange(0, 256, 2), np.arange(1, 256, 2)])
    return np.concatenate([h * 256 + one for h in range(4)])


_RC = {}


def ret_consts(cfg):
    key = (cfg.T, cfg.TS, cfg.PAST)
    if key in _RC:
        return _RC[key]
    import jax
    import jax.numpy as jnp
    out = {}
    with jax.default_device(jax.devices("cpu")[0]):
        inv = 10000.0 ** (-jnp.arange(0, 256, 2, dtype=jnp.float32) / 256)
        for nm, start, T in (("p", 0, cfg.T), ("s", cfg.PAST, cfg.TS)):
            pos = (start + jnp.arange(T)).astype(jnp.float32)
            ang = pos[:, None] * inv[None, :]
            out["rot_cos_" + nm] = np.ascontiguousarray(np.asarray(jnp.cos(ang)).T)
            out["rot_sin_" + nm] = np.ascontiguousarray(np.asarray(jnp.sin(ang)).T)
        lg = np.asarray(jnp.log1p(-jnp.power(2.0, -5.0 - jnp.arange(4, dtype=jnp.float32))), np.float64)
    i = np.arange(128)
    dm = np.zeros((128, 4, 128), np.float32)
    qd = np.zeros((128, 4, 128), np.float32)
    kd = np.zeros((128, 8), np.float32)
    for h in range(4):
        d = i[None, :] - i[:, None]
        dm[:, h, :] = np.where(d >= 0, np.exp(lg[h] * np.maximum(d, 0)), 0.0)
        qd[:, h, :] = np.exp(lg[h] * (i + 1.0))[None, :]
        kd[:, h] = np.exp(lg[h] * (127.0 - i))
        kd[:, 4 + h] = np.exp(lg[h] * (cfg.TS - 1.0 - i))
    out["ret_dmaskT"], out["ret_qdec"], out["ret_kdec"] = dm, qd, kd
    _RC[key] = out
    return out


def core_inputs(cfg, inp, b):
    T, PAST = cfg.T, cfg.PAST
    f = np.ascontiguousarray
    cT = np.stack([inp["c_prompt"][b], inp["c_sample"][b]], -1).reshape(8, 128, 2).transpose(1, 0, 2)
    m = {
        "xp": f(inp["x_prompt"][b, :T]), "xs": f(inp["x_sample"][b]), "cT": f(cT),
        "ada_w": inp["ada_w"], "ada_b": inp["ada_b"], "ln_g": inp["ln_g"], "ln_b": inp["ln_b"],
        "ident": np.eye(128, dtype=np.float32),
        "cmask": np.where(np.arange(128)[:, None] > np.arange(128)[None, :], -30000.0, 0.0).astype(np.float32),
        "fox_w_in": inp["fox_w_in"], "fox_b_f": f(inp["fox_b_f"].reshape(16, 1)), "fox_w_out": inp["fox_w_out"],
        "cache_fox_k": f(inp["cache_fox_k"][b, :PAST].reshape(PAST, -1)),
        "cache_fox_v": f(inp["cache_fox_v"][b, :PAST].reshape(PAST, -1)),
        "cache_fox_logf": f(inp["cache_fox_logf"][b, :PAST]),
        "diff_w_in": inp["diff_w_in"], "diff_w_out": inp["diff_w_out"], "rel_bias_table": inp["rel_bias_table"],
        "diff_lam_q1": f(inp["diff_lam_q1"].reshape(1, 64)), "diff_lam_k1": f(inp["diff_lam_k1"].reshape(1, 64)),
        "diff_lam_q2": f(inp["diff_lam_q2"].reshape(1, 64)), "diff_lam_k2": f(inp["diff_lam_k2"].reshape(1, 64)),
        "diff_subln_w": f(inp["diff_subln_w"].reshape(1, 128)), "t5_onehot": t5_onehot(),
        "antiident": np.ascontiguousarray(np.eye(128, dtype=np.float32)[::-1]),
        "dmask": np.where((np.arange(128)[:, None] >= 64) & (np.arange(128)[None, :] < 64), -30000.0, 0.0).astype(np.float32),
        "cache_diff_k": f(inp["cache_diff_k"][b, :PAST].reshape(PAST, -1)),
        "cache_diff_v": f(inp["cache_diff_v"][b, :PAST].reshape(PAST, -1)),
    }
    m.update(ret_consts(cfg))
    i = np.arange(128)
    msk = np.zeros((128, 3, 128), np.float32)
    msk[:, 0, :] = np.where(i[None, :] > i[:, None], -1.0, 0.0)
    msk[:, 1, :] = np.where(i[None, :] < i[:, None], -1.0, 0.0)
    msk[:, 2, :] = np.where(i[None, :] >= i[:, None], 1.0, 0.0)
    sel = np.zeros((8, 8, 128), np.float32)
    for h in range(8):
        sel[h, h, :] = 1.0
    m.update({
        "gdn_w_in": inp["gdn_w_in"], "gdn_w_out": inp["gdn_w_out"],
        "gdn_convw": f(inp["gdn_conv_w"].reshape(4, 24, 128).transpose(2, 1, 0)),
        "gdn_conv_in": f(inp["state_gdn_conv"][b].reshape(3, 24, 128).transpose(2, 1, 0)),
        "gdn_a_log": f(inp["gdn_a_log"].reshape(8, 1)), "gdn_dt_bias": f(inp["gdn_dt_bias"].reshape(8, 1)),
        "gdn_norm_w": f(inp["gdn_norm_w"].reshape(1, 128)), "state_gdn": f(inp["state_gdn"][b]),
        "gdn_masks": msk, "gdn_sel": sel,
    })
    perm = ret_perm()
    w = inp["ret_w_in"]
    m["ret_w_in"] = f(np.concatenate([w[:, 0:D][:, perm], w[:, D:2 * D][:, perm], w[:, 2 * D:]], axis=1))
    m["ret_w_out"] = inp["ret_w_out"]
    m["ret_gn_w"] = f(inp["ret_gn_w"].reshape(1, -1))
    m["state_ret"] = f(inp["state_ret"][b][:, perm[:256], :])
    return m


OUT_NAMES = ("y", "gdn_state_", "gdn_conv_", "fox_k_", "fox_v_", "fox_logf_", "diff_k_", "diff_v_", "ret_state_")


def gather(cfg, results):
    inv = np.argsort(ret_perm()[:256])
    outs = []
    for nm, T in (("p", cfg.T), ("s", cfg.TS)):
        g = {}
        for base in OUT_NAMES:
            key = base + nm
            g[base] = np.stack([np.asarray(r[key]) for r in results], 0)
        B = len(results)
        outs.append([
            g["y"],
            g["gdn_state_"],
            g["gdn_conv_"],
            g["fox_k_"].reshape(B, T, 16, 64),
            g["fox_v_"].reshape(B, T, 16, 64),
            g["fox_logf_"],
            g["diff_k_"].reshape(B, T, 8, 2, 64),
            g["diff_v_"].reshape(B, T, 8, 128),
            np.ascontiguousarray(g["ret_state_"][:, :, inv, :]),
        ])
    p, s = outs
    return (p[0], s[0]) + tuple(p[1:]) + tuple(s[1:])


def kernel(**inputs):
    inputs = {k: np.asarray(v) for k, v in inputs.items()}
    cfg = Cfg()
    k = build(cfg)
    in_maps = [core_inputs(cfg, inputs, b) for b in range(NCORES)]
    res = run_bass_kernel_spmd(k.nc, in_maps, core_ids=list(range(NCORES)))
    return gather(cfg, res.results)
```

```python
import math
from contextlib import ExitStack

import numpy as np
import ml_dtypes
import concourse.bass as bass
import concourse.mybir as mybir
from concourse.bass_utils import run_bass_kernel_spmd

F32 = mybir.dt.float32
BF16 = mybir.dt.bfloat16
AF = mybir.ActivationFunctionType
ALU = mybir.AluOpType
AX = mybir.AxisListType

D = 1024
DEPTH = 4
ALPHA = (2.0 * DEPTH) ** 0.25
LN_EPS = 1e-5
NORM_EPS = 1e-6
NCORES = 8


class Buf:
    __slots__ = ("w", "r", "x")

    def __init__(self, x=False):
        self.w = None
        self.r = {}
        self.x = x


class Prog:
    ENGS = ("pe", "act", "dve", "pool", "sp")
    NO_SELF_SYNC = ("pe",)

    def __init__(self, nc, es, ndma=28):
        self.nc = nc
        self.sem = {e: es.enter_context(nc.semaphore("s_" + e)) for e in self.ENGS}
        self.dsem = [es.enter_context(nc.semaphore("d%d" % i)) for i in range(ndma)]
        self.cnt = {e: 0 for e in self.ENGS}
        self.dcnt = [0] * ndma
        self.drr = 0
        self.drr_sw = 0
        self.NSW_BASE = ndma - 8
        self.seen = {e: {} for e in self.ENGS}
        self.q = {e: [] for e in self.ENGS}

    def _collect(self, reads, writes, e=None):
        deps = {}
        for b in reads:
            if b.w is not None and deps.get(b.w[0], 0) < b.w[1]:
                deps[b.w[0]] = b.w[1]
            if b.x:
                for k, v in b.r.items():
                    if k != e and deps.get(k, 0) < v:
                        deps[k] = v
        for b in writes:
            if b.w is not None and deps.get(b.w[0], 0) < b.w[1]:
                deps[b.w[0]] = b.w[1]
            for k, v in b.r.items():
                if deps.get(k, 0) < v:
                    deps[k] = v
        return deps

    def _waits(self, e, deps):
        out = []
        seen = self.seen[e]
        for k, v in deps.items():
            if k == e and e in self.NO_SELF_SYNC:
                continue
            if seen.get(k, 0) < v:
                seen[k] = v
                out.append((k, v))
        return out

    def _mark(self, tok, reads, writes):
        k, v = tok
        for b in reads:
            if b.r.get(k, 0) < v:
                b.r[k] = v
        for b in writes:
            b.w = tok
            b.r = {}

    def op(self, e, fn, reads=(), writes=()):
        waits = self._waits(e, self._collect(reads, writes, e))
        self.cnt[e] += 1
        self.q[e].append((waits, fn, None))
        self._mark((e, self.cnt[e]), reads, writes)

    def dma(self, e, fn, reads=(), writes=()):
        if e == "pool":
            s = self.NSW_BASE + self.drr_sw
            self.drr_sw = (self.drr_sw + 1) % (len(self.dsem) - self.NSW_BASE)
        else:
            s = self.drr
            self.drr = (s + 1) % self.NSW_BASE
        deps = self._collect(reads, writes, e)
        k = ("d", s)
        if self.dcnt[s] and deps.get(k, 0) < self.dcnt[s]:
            deps[k] = self.dcnt[s]
        waits = self._waits(e, deps)
        self.dcnt[s] += 16
        self.q[e].append((waits, fn, s))
        self._mark((k, self.dcnt[s]), reads, writes)

    def barrier(self):
        deps = {e: c for e, c in self.cnt.items() if c}
        for s, c in enumerate(self.dcnt):
            if c:
                deps[("d", s)] = c
        for e in self.ENGS:
            waits = self._waits(e, dict(deps))
            if waits:
                self.q[e].append((waits, None, None))

    def _semof(self, k):
        return self.sem[k] if isinstance(k, str) else self.dsem[k[1]]

    def emit(self):
        nc = self.nc
        with nc.Block() as block:
            regs = dict(sp=block.sync, pe=block.tensor, act=block.scalar, dve=block.vector, pool=block.gpsimd)
            for e in self.ENGS:
                def body(eng, e=e):
                    for waits, fn, ds in self.q[e]:
                        ws = [(self._semof(k), v) for k, v in waits]
                        if fn is None:
                            for sm, v in ws:
                                eng.wait_ge(sm, v)
                            continue
                        if ds is None and ws:
                            for sm, v in ws[:-1]:
                                eng.wait_ge(sm, v)
                            ins = fn(eng)
                            ins._wait_ge(ws[-1][0], ws[-1][1])
                        else:
                            for sm, v in ws:
                                eng.wait_ge(sm, v)
                            ins = fn(eng)
                        if ds is None:
                            ins.then_inc(self.sem[e], 1)
                        else:
                            ins.then_inc(self.dsem[ds], 16)
                    if e == "sp":
                        for s, c in enumerate(self.dcnt):
                            if c:
                                eng.wait_ge(self.dsem[s], c)
                regs[e](body)


class Ring:
    def __init__(self, K, name, shape, dtype, n, psum=False, arena=None):
        self.t = []
        for i in range(n):
            if arena is not None:
                t = arena.alloc(shape, dtype)
            else:
                f = K.nc.psum_tensor if psum else K.nc.sbuf_tensor
                t = K.es.enter_context(f("%s%d" % (name, i), shape, dtype))
            self.t.append((t, Buf(x=psum)))
        self.i = 0

    def next(self):
        r = self.t[self.i % len(self.t)]
        self.i += 1
        return r


class Arena:
    def __init__(self, K, nbytes, base=None):
        self.t = K.es.enter_context(K.nc.sbuf_tensor("arena", [128, nbytes // 4], F32)) if base is None else base
        self.nbytes = nbytes
        self.off = 0

    def reset(self):
        self.off = 0

    def alloc(self, shape, dtype):
        esz = 2 if dtype == BF16 else 4
        n = 1
        for s in shape[1:]:
            n *= s
        nb = (n * esz + 3) // 4 * 4
        assert self.off + nb <= self.nbytes, "arena overflow: need %d have %d" % (nb, self.nbytes - self.off)
        v = self.t[0:shape[0], self.off // 4:(self.off + nb) // 4]
        self.off += nb
        if dtype == BF16:
            v = v.bitcast(BF16)[:, 0:n]
        if len(shape) == 3:
            v = v.rearrange("p (a b) -> p a b", a=shape[1])
        elif len(shape) == 4:
            v = v.rearrange("p (a b c) -> p a b c", a=shape[1], b=shape[2])
        return v


class Cfg:
    def __init__(self, T=4096, TS=32, PAST=2048, layers=(0, 1, 2, 3)):
        self.T, self.TS, self.PAST, self.layers = T, TS, PAST, tuple(layers)


class Group:
    def __init__(self, gi, name, T, past):
        self.gi, self.name, self.T, self.past = gi, name, T, past
        self.S = past + T
        self.tiles = [(t0, min(128, T - t0)) for t0 in range(0, T, 128)]
        self.blocks = [self.tiles[i:i + 4] for i in range(0, len(self.tiles), 4)]
        self.ktiles = [(k0, 128) for k0 in range(0, past, 128)] + [(past + t0, n) for t0, n in self.tiles]


class K:
    def __init__(self, cfg):
        self.cfg = cfg
        self.nc = nc = bass.Bass("TRN2", target_bir_lowering=False)
        self.es = ExitStack()
        self.P = Prog(nc, self.es)
        self.bufs = {}
        self.dr = {}
        self.dh = {}
        self.all_groups = [Group(0, "p", cfg.T, 0), Group(1, "s", cfg.TS, cfg.PAST)]
        import os
        gs = os.environ.get("KGROUPS", "01")
        self.groups = [g for g in self.all_groups if str(g.gi) in gs]

    def buf(self, key):
        b = self.bufs.get(key)
        if b is None:
            b = self.bufs[key] = Buf()
        return b

    def din(self, name, shape, dtype=F32):
        t = self.nc.dram_tensor(name, list(shape), dtype, kind="ExternalInput")
        self.dh[name] = t
        self.dr[name] = t.ap()
        return self.dr[name]

    def dout(self, name, shape, dtype=F32):
        t = self.nc.dram_tensor(name, list(shape), dtype, kind="ExternalOutput")
        self.dr[name] = t.ap()
        return self.dr[name]

    def dscr(self, name, shape, dtype):
        t = self.nc.dram_tensor(name, list(shape), dtype)
        self.dh[name] = t
        self.dr[name] = t.ap()
        return self.dr[name]

    def sb(self, name, shape, dtype):
        t = self.es.enter_context(self.nc.sbuf_tensor(name, list(shape), dtype))
        return t, Buf()

    def dma(self, out, in_, rd, wr, eng="sp"):
        self.P.dma(eng, lambda e: e.dma_start(out=out, in_=in_), rd, wr)

    def mm(self, out, lhsT, rhs, start, stop, rd, wr, skip=False):
        self.P.op("pe", lambda e: e.matmul(out, lhsT, rhs, start=start, stop=stop, skip_group_check=skip), rd, wr)

    def tr(self, out, in_, ident, rd, wr):
        self.P.op("pe", lambda e: e.transpose(out, in_, ident), rd, wr)

    def act(self, out, in_, func, rd, wr, bias=0.0, scale=1.0, accum=None):
        if accum is None:
            self.P.op("act", lambda e: e.activation(out=out, in_=in_, func=func, bias=bias, scale=scale), rd, wr)
        else:
            self.P.op("act", lambda e: e.activation(out=out, in_=in_, func=func, bias=bias, scale=scale,
                                                     accum_out=accum), rd, wr)

    def tt(self, eng, out, in0, in1, op, rd, wr):
        o = self.nc.vector if eng == "dve" else self.nc.gpsimd
        self.P.op(eng, lambda e: e.tensor_tensor(out=out, in0=in0, in1=in1, op=op), rd, wr)

    def ts(self, eng, out, in0, s1, s2, op0, op1, rd, wr, accum=None):
        if op1 is None:
            self.P.op(eng, lambda e: e.tensor_scalar(out=out, in0=in0, scalar1=s1, scalar2=None, op0=op0), rd, wr)
        elif accum is None:
            self.P.op(eng, lambda e: e.tensor_scalar(out=out, in0=in0, scalar1=s1, scalar2=s2, op0=op0, op1=op1),
                      rd, wr)
        else:
            self.P.op(eng, lambda e: e.tensor_scalar(out=out, in0=in0, scalar1=s1, scalar2=s2, op0=op0, op1=op1,
                                                     accum_out=accum), rd, wr)

    def stt(self, out, in0, scalar, in1, op0, op1, rd, wr):
        self.P.op("dve", lambda e: e.scalar_tensor_tensor(out=out, in0=in0, scalar=scalar, in1=in1, op0=op0, op1=op1),
                  rd, wr)

    def cp(self, eng, out, in_, rd, wr):
        if eng == "act":
            self.P.op("act", lambda e: e.copy(out=out, in_=in_), rd, wr)
        else:
            self.P.op(eng, lambda e: e.tensor_copy(out=out, in_=in_), rd, wr)

    def rsqrt(self, out, in_, rd, wr, scale=1.0, eps=0.0):
        self.act(out, in_, AF.Ln, rd, wr, bias=eps, scale=scale)
        self.act(out, out, AF.Exp, wr, wr, scale=-0.5)

    def recip(self, out, in_, rd, wr):
        self.P.op("dve", lambda e: e.reciprocal(out=out, in_=in_), rd, wr)

    def memset(self, eng, ap, val, wr):
        self.P.op(eng, lambda e: e.memset(ap, val), (), wr)

    def set_psum(self, n_mm, n_acc, n_tp):
        def mk(lst):
            r = Ring.__new__(Ring)
            r.t = lst
            r.i = 0
            return r
        assert n_mm + n_acc + n_tp == 8
        self.ps_mm = mk(self.psb[0:n_mm])
        self.ps_acc = mk(self.psb[n_mm:n_mm + n_acc])
        self.ps_tp = mk(self.psb[n_mm + n_acc:8])

    def setup(self):
        cfg = self.cfg
        T, TS, PAST = cfg.T, cfg.TS, cfg.PAST
        din, dout, dscr = self.din, self.dout, self.dscr
        din("xp", [T, D]); din("xs", [TS, D]); din("cT", [128, 8, 2])
        din("ada_w", [4, D, 3 * D]); din("ada_b", [4, 3 * D]); din("ln_g", [4, D]); din("ln_b", [4, D])
        din("ident", [128, 128]); din("cmask", [128, 128])
        din("fox_w_in", [D, 4112]); din("fox_b_f", [16, 1]); din("fox_w_out", [D, D])
        din("cache_fox_k", [PAST, D]); din("cache_fox_v", [PAST, D]); din("cache_fox_logf", [PAST, 16])
        din("diff_w_in", [D, 4096]); din("diff_w_out", [D, D]); din("rel_bias_table", [32, 8])
        for nmv in ("diff_lam_q1", "diff_lam_k1", "diff_lam_q2", "diff_lam_k2"):
            din(nmv, [1, 64])
        din("diff_subln_w", [1, 128]); din("t5_onehot", [32, 384]); din("antiident", [128, 128]); din("dmask", [128, 128])
        din("cache_diff_k", [PAST, D]); din("cache_diff_v", [PAST, D])
        dscr("lam_scr", [2, 8], F32); dscr("fr_scr", [16, 384], F32)
        din("ret_w_in", [D, 6144]); din("ret_w_out", [2 * D, D]); din("ret_gn_w", [1, 2 * D])
        din("state_ret", [4, 256, 512]); din("ret_dmaskT", [128, 4, 128]); din("ret_qdec", [128, 4, 128])
        din("ret_kdec", [128, 8])
        for g in self.all_groups:
            din("rot_cos_" + g.name, [128, g.T]); din("rot_sin_" + g.name, [128, g.T])
            dscr("QR_" + g.name, [8, 128, g.T], BF16); dscr("KR_" + g.name, [8, 128, g.T], BF16)
            dscr("OG_" + g.name + "v", [g.T, 2 * D], BF16)
            dout("ret_state_" + g.name, [4, 256, 512])
        din("gdn_w_in", [D, 4112]); din("gdn_w_out", [D, D]); din("gdn_convw", [128, 24, 4]); din("gdn_conv_in", [128, 24, 3])
        din("gdn_a_log", [8, 1]); din("gdn_dt_bias", [8, 1]); din("gdn_norm_w", [1, 128]); din("state_gdn", [8, 128, 128])
        din("gdn_masks", [128, 3, 128]); din("gdn_sel", [8, 8, 128])
        for g in self.all_groups:
            for nmv in ("QG_", "KG_", "VG_"):
                dscr(nmv + g.name, [8, 128, g.T], BF16)
            dscr("BTb_" + g.name, [8, g.T], BF16); dscr("GT_" + g.name, [8, g.T], F32); dscr("GB_" + g.name, [g.T, 16], F32)
            dout("gdn_state_" + g.name, [8, 128, 128]); dout("gdn_conv_" + g.name, [3, 3 * D])
        dout("yp", [T, D]); dout("ys", [TS, D])
        for g in self.all_groups:
            n = g.name
            dout("fox_k_" + n, [g.T, D]); dout("fox_v_" + n, [g.T, D]); dout("fox_logf_" + n, [g.T, 16])
            dout("diff_k_" + n, [g.T, D]); dout("diff_v_" + n, [g.T, D])
            dscr("OGF_" + n, [g.T, 2 * D], F32)
        dscr("xres_p", [T, D], F32); dscr("xres_s", [TS, D], F32)
        for g in self.all_groups:
            n = g.name
            dscr("QT_" + n, [16, 70, g.T], BF16); dscr("KT_" + n, [16, 70, g.S], BF16)
            dscr("VA_" + n, [g.S, 1040], BF16)
            dscr("ZS_" + n, [g.T, 2 * D], BF16); dscr("OG_" + n, [g.T, 2 * D], BF16)

        self.W, self.Wb = self.sb("W_in", [128, 8, 4112], BF16)
        self.WO, self.WOb = self.sb("W_out", [128, 8, D], BF16)
        self.identf, self.identf_b = self.sb("identf", [128, 128], F32)
        self.identb, self.identb_b = self.sb("identb", [128, 128], BF16)
        self.cmaskb, self.cmaskb_b = self.sb("cmaskb", [128, 128], BF16)
        self.cTs, self.cTs_b = self.sb("cTs", [128, 8, 2], F32)
        self.crep = [self.sb("crep%d" % g, [128, 8, 128], BF16) for g in range(2)]
        self.mod = [self.sb("mod_%d" % k, [128, D], F32) for k in range(3)]
        dscr("mods", [3, D], F32)
        self.lng, self.lng_b = self.sb("lng", [128, D], F32)
        self.lnb, self.lnb_b = self.sb("lnb", [128, D], F32)
        self.psb = Ring(self, "psb", [128, 512], F32, 8, psum=True).t
        self.set_psum(4, 2, 2)
        self.r_x = Ring(self, "rx", [128, D], F32, 2)
        self.r_f = Ring(self, "rf", [128, D], F32, 2)
        self.r_ub = Ring(self, "rub", [128, D], BF16, 2)
        self.r_uT = Ring(self, "ruT", [128, 8, 512], BF16, 2)
        self.r_h = Ring(self, "rh", [128, 1040], BF16, 3)
        self.r_small = Ring(self, "rsm", [128, 16], F32, 8)
        self.arena = Arena(self, 58 * 1024)
        wbytes = 8 * 4112 * 2
        self.arenaW = Arena(self, wbytes, base=self.W[:, :, :].rearrange("p a b -> p (a b)").bitcast(F32))

        self.dma(self.identf[:], self.dr["ident"][:, :], [], [self.identf_b])
        self.cp("dve", self.identb[:], self.identf[:], [self.identf_b], [self.identb_b])
        t, b = self.r_f.next()
        self.dma(t[:, 0:128], self.dr["cmask"][:, :], [], [b])
        self.cp("dve", self.cmaskb[:], t[:, 0:128], [b], [self.cmaskb_b])
        self.dma(self.cTs[:], self.dr["cT"][:, :, :], [], [self.cTs_b])
        self.act(self.cTs[:], self.cTs[:], AF.Silu, [self.cTs_b], [self.cTs_b])
        for g in range(2):
            t, b = self.crep[g]
            self.cp("dve", t[:], self.cTs[:, :, g:g + 1].to_broadcast([128, 8, 128]), [self.cTs_b], [b])

    def load_w(self, dst, dstb, src, ncols, nk=8, c_off=0):
        for k in range(nk):
            self.dma(dst[:, k, 0:ncols], src[k * 128:(k + 1) * 128, c_off:c_off + ncols], [], [dstb], eng="pool")

    def layer_mod(self, L):
        ada_w, ada_b = self.dr["ada_w"], self.dr["ada_b"]
        for cg in range(6):
            wt, wb = self.r_uT.next()
            self.load_w(wt, wb, ada_w[L], 512, c_off=cg * 512)
            bt_, bb = self.r_f.next()
            bt = bt_[:, 0:512]
            self.dma(bt[:, :], ada_b[L:L + 1, cg * 512:(cg + 1) * 512].to_broadcast([128, 512]), [], [bb])
            kind = cg // 2
            for g in range(2):
                ps, pb = self.ps_mm.next()
                ct, cb = self.crep[g]
                for k in range(8):
                    self.mm(ps[:, :], ct[:, k, :], wt[:, k, :], k == 0, k == 7, [cb, wb], [pb])
                if g == 0:
                    mt, mb = self.mod[kind]
                    dst = mt[:, (cg % 2) * 512:(cg % 2 + 1) * 512]
                else:
                    mt, mb = self.r_f.next()
                    dst = mt[:, 0:512]
                if kind == 0:
                    self.tt("dve", dst, ps[:, :], bt[:, :], ALU.add, [pb, bb], [mb])
                else:
                    self.stt(dst, ps[:, :], 1.0, bt[:, :], ALU.add, ALU.add, [pb, bb], [mb])
                if g == 1:
                    self.dma(self.dr["mods"][kind:kind + 1, (cg % 2) * 512:(cg % 2 + 1) * 512], mt[0:1, 0:512], [mb],
                             [self.buf(("mods", kind, cg % 2))])
        self.dma(self.lng[:], self.dr["ln_g"][L:L + 1, :].to_broadcast([128, D]), [], [self.lng_b])
        self.dma(self.lnb[:], self.dr["ln_b"][L:L + 1, :].to_broadcast([128, D]), [], [self.lnb_b])

    def modtile(self, g, kind, n):
        if g.gi == 0:
            return self.mod[kind]
        t, b = self.r_f.next()
        self.dma(t[:n, :], self.dr["mods"][kind:kind + 1, :].to_broadcast([n, D]),
                 [self.buf(("mods", kind, 0)), self.buf(("mods", kind, 1))], [b])
        return t, b

    def xsrc(self, g, first):
        if first:
            return self.dr["xp" if g.gi == 0 else "xs"], "xin%d" % g.gi
        return self.dr["xres_" + g.name], "xres%d" % g.gi

    def make_uT(self, g, blk, first):
        src, skey = self.xsrc(g, first)
        uT, uTb = self.r_uT.next()
        off = 0
        for (t0, n) in blk:
            sc, scb = self.modtile(g, 1, n)
            xt, xb = self.r_x.next()
            self.dma(xt[:n, :], src[t0:t0 + n, :], [self.buf((skey, t0))], [xb])
            ft, fb = self.r_f.next()
            self.tt("dve", ft[:n, :], xt[:n, :], sc[:n, :], ALU.mult, [xb, scb], [fb])
            sh, shb = self.modtile(g, 0, n)
            ub, ubb = self.r_ub.next()
            self.tt("dve", ub[:n, :], ft[:n, :], sh[:n, :], ALU.add, [fb, shb], [ubb])
            tp, tpb = self.ps_tp.next()
            tpv = tp[:, :].bitcast(BF16)
            for j in range(8):
                self.tr(tpv[:, j * 128:j * 128 + n], ub[:n, j * 128:(j + 1) * 128], self.identb[:n, :n],
                        [ubb, self.identb_b], [tpb])
            self.cp("act", uT[:, :, off:off + n],
                    tpv.rearrange("p (j t) -> p j t", j=8)[:, :, 0:n], [tpb], [uTb])
            off += n
        return uT, uTb, off

    def proj_tm(self, uT, uTb, off, n, c0, c1):
        ps, pb = self.ps_mm.next()
        for k in range(8):
            self.mm(ps[:n, 0:c1 - c0], uT[:, k, off:off + n], self.W[:, k, c0:c1], k == 0, k == 7,
                    [uTb, self.Wb], [pb])
        return ps, pb

    def proj_fm(self, uT, uTb, nb, c0, c1):
        ps, pb = self.ps_mm.next()
        for k in range(8):
            self.mm(ps[:c1 - c0, 0:nb], self.W[:, k, c0:c1], uT[:, k, 0:nb], k == 0, k == 7, [uTb, self.Wb], [pb])
        return ps, pb

    def pass_c_run(self, items, first, last, nk, wo=None):
        from types import SimpleNamespace as NS
        ctxs = [NS(g=g, t0=t0, n=n, prep=prep) for (g, t0, n, prep) in items]

        def stA(c):
            n = c.n
            ogz, ogzb = c.prep()
            c.oT, c.oTb = self.r_uT.next()
            for half in range((nk + 7) // 8):
                tp, tpb = self.ps_tp.next()
                tpv = tp[:, :].bitcast(BF16)
                for j in range(8):
                    jj = half * 8 + j
                    self.tr(tpv[:, j * 128:j * 128 + n], ogz[:n, jj * 128:(jj + 1) * 128], self.identb[:n, :n],
                            [ogzb, self.identb_b], [tpb])
                self.cp("act", c.oT[:, :, half * 128:half * 128 + n],
                        tpv.rearrange("p (j t) -> p j t", j=8)[:, :, 0:n], [tpb], [c.oTb])

        def stB(c):
            g, t0, n = c.g, c.t0, c.n
            src, skey = self.xsrc(g, first)
            hps = []
            for cg in range(2):
                ps, pb = self.ps_acc.next()
                for k in range(nk):
                    wt_, wtb_ = (self.WO[:, k, :], self.WOb) if wo is None else wo(k)
                    self.mm(ps[:n, :], c.oT[:, k % 8, (k // 8) * 128:(k // 8) * 128 + n],
                            wt_[:, cg * 512:(cg + 1) * 512], k == 0, k == nk - 1, [c.oTb, wtb_], [pb])
                hps.append((ps, pb))
            xt, xb = self.r_x.next()
            self.dma(xt[:n, :], src[t0:t0 + n, :], [self.buf((skey, t0))], [xb])
            gt, gb = self.modtile(g, 2, n)
            c.rt, c.rb = self.r_f.next()
            rt, rb = c.rt, c.rb
            for cg in range(2):
                ps, pb = hps[cg]
                sl = slice(cg * 512, (cg + 1) * 512)
                self.tt("dve", rt[:n, sl], ps[:n, :], gt[:n, sl], ALU.mult, [pb, gb], [rb])
            self.stt(rt[:n, :], xt[:n, :], ALPHA, rt[:n, :], ALU.mult, ALU.add, [xb, rb], [rb])
            st, stb = self.r_small.next()
            for cg in range(2):
                self.P.op("dve", lambda e, cg=cg: e.bn_stats(out=st[:n, cg * 6:cg * 6 + 6],
                                                              in_=rt[:n, cg * 512:(cg + 1) * 512]), [rb], [stb])
            c.mv, c.mvb = self.r_small.next()
            mv, mvb = c.mv, c.mvb
            self.P.op("dve", lambda e: e.bn_aggr(out=mv[:n, 0:2], in_=st[:n, 0:12]), [stb], [mvb])
            self.rsqrt(mv[:n, 4:5], mv[:n, 1:2], [mvb], [mvb], eps=LN_EPS)
            self.stt(mv[:n, 5:6], mv[:n, 0:1], -1.0, mv[:n, 4:5], ALU.mult, ALU.mult, [mvb], [mvb])

        def stC(c):
            g, t0, n = c.g, c.t0, c.n
            dst, dkey = (self.dr["yp" if g.gi == 0 else "ys"], "y%d" % g.gi) if last else \
                (self.dr["xres_" + g.name], "xres%d" % g.gi)
            yt, yb = self.r_x.next()
            self.act(yt[:n, :], c.rt[:n, :], AF.Identity, [c.rb, c.mvb], [yb], bias=c.mv[:n, 5:6], scale=c.mv[:n, 4:5])
            self.tt("dve", yt[:n, :], yt[:n, :], self.lng[:n, :], ALU.mult, [yb, self.lng_b], [yb])
            self.tt("dve", yt[:n, :], yt[:n, :], self.lnb[:n, :], ALU.add, [yb, self.lnb_b], [yb])
            self.dma(dst[t0:t0 + n, :], yt[:n, :], [yb], [self.buf((dkey, t0))])

        stages = [stA, stB, stC]
        N, S_ = len(ctxs), len(stages)
        for slot in range(N + S_ - 1):
            for s in range(S_ - 1, -1, -1):
                i = slot - s
                if 0 <= i < N:
                    stages[s](ctxs[i])

    def pass_c_tile(self, g, t0, n, first, last, ogz, ogzb, nk, wo=None):
        self.pass_c_run([(g, t0, n, lambda: (ogz, ogzb))], first, last, nk, wo)

    def layer(self, L, first, last):
        self.layer_mod(L)
        getattr(self, ("layer_gdn", "layer_fox", "layer_diff", "layer_ret")[L % 4])(L, first, last)

    def attn_rings(self, w, lf):
        self.P.barrier()
        self.set_psum(2, 4, 2)
        A = self.arena
        A.reset()
        smax = max(g.S for g in self.groups)
        self.r_KT = Ring(self, "rKT", [70, smax], BF16, 2, arena=A)
        nkt = max(len(g.ktiles) for g in self.groups)
        self.r_VA = Ring(self, "rVA", [128, nkt, w], BF16, 2, arena=A)
        self.r_QT = Ring(self, "rQT", [70, 512], BF16, 2, arena=A)
        self.r_PT = Ring(self, "rPT", [128, 512], BF16, 3 if lf else 2, arena=A)
        if lf:
            self.r_og = Ring(self, "rog", [128, 4, 128], BF16, 2, arena=A)
        self.r_vb = self.r_h
        self.r_st = Ring(self, "rst", [128, 2048], BF16, 1, arena=A)
        if lf:
            self.r_lf = Ring(self, "rlf", [16, 512], F32, 4, arena=A)
            self.r_pt = Ring(self, "rpt", [16, 3, 512], BF16, 2, arena=A)
            self.ones3, self.ones3_b = A.alloc([16, 3, 512], BF16), Buf()
            self.memset("pool", self.ones3[:, :, :], 1.0, [self.ones3_b])
            self.carry, self.carry_b = A.alloc([16, 1], F32), Buf()
            self.fbf, self.fbf_b = A.alloc([16, 1], F32), Buf()

    def attn_head(self, g, KTd, QTd, VAd, vcol0, dv, kc, rd_bufs, extras, finish):
        KT, KTb = self.r_KT.next()
        self.dma(KT[:kc, 0:g.S], KTd[:, :], rd_bufs, [KTb])
        VA, VAb = self.r_VA.next()
        nfull = sum(1 for (k0, nk) in g.ktiles if nk == 128)
        for k4 in range(0, nfull, 4):
            k5 = min(nfull, k4 + 4)
            self.dma(VA[:, k4:k5, 0:dv + 1],
                     VAd[k4 * 128:k5 * 128, vcol0:vcol0 + dv + 1].rearrange("(kt p) c -> p kt c", p=128), rd_bufs, [VAb])
        for i, (k0, nk) in enumerate(g.ktiles):
            if nk != 128:
                self.dma(VA[:nk, i, 0:dv + 1], VAd[k0:k0 + nk, vcol0:vcol0 + dv + 1], rd_bufs, [VAb])
        w = dv + 1
        per_bank = 512 // w
        for bi, blk in enumerate(g.blocks):
            nq = sum(n for _, n in blk)
            q0b = blk[0][0]
            QT, QTb = self.r_QT.next()
            self.dma(QT[:kc, 0:nq], QTd[:, q0b:q0b + nq], rd_bufs, [QTb])
            nbank = (len(blk) + per_bank - 1) // per_bank
            accs = [self.ps_acc.next() for _ in range(nbank)]
            started = [False] * nbank
            coff = []
            o = 0
            for (_, n) in blk:
                coff.append(o)
                o += n
            pend = None

            def emit_pv(item):
                ki, k0, nk, vis, PT, PTb = item
                for qi in vis:
                    q0, n = blk[qi]
                    bk = qi // per_bank
                    acc, accb = accs[bk]
                    col = (qi % per_bank) * w
                    last = (k0 == g.past + q0)
                    self.mm(acc[:n, col:col + w], PT[:nk, coff[qi]:coff[qi] + n], VA[:nk, ki, 0:w],
                            not started[bk], last, [PTb, VAb], [accb], skip=True)
                    started[bk] = True
            for ki, (k0, nk) in enumerate(g.ktiles):
                vis = [qi for qi, (q0, n) in enumerate(blk) if g.past + q0 >= k0]
                if not vis or k0 > g.past + blk[-1][0]:
                    continue
                fi = vis[0]
                c0 = coff[fi]
                ST, STb = self.ps_mm.next()
                ex = []
                for qi in vis:
                    q0, n = blk[qi]
                    for (l, r, bl) in extras(ki, k0, nk, qi, g.past + q0, n):
                        ex.append((qi, n, l, r, bl))
                self.mm(ST[:nk, c0:nq], KT[:kc, k0:k0 + nk], QT[:kc, c0:nq], True, not ex, [KTb, QTb], [STb])
                for j, (qi, n, l, r, bl) in enumerate(ex):
                    self.mm(ST[:nk, coff[qi]:coff[qi] + n], l, r, False, j == len(ex) - 1, bl, [STb], skip=True)
                PT, PTb = self.r_PT.next()
                self.act(PT[:nk, c0:nq], ST[:nk, c0:nq], AF.Exp, [STb], [PTb])
                if pend is not None:
                    emit_pv(pend)
                pend = (ki, k0, nk, vis, PT, PTb)
            if pend is not None:
                emit_pv(pend)
            finish(bi, blk, accs, per_bank, w)

    def layer_fox(self, L, first, last):
        dr = self.dr
        self.attn_rings(65, True)
        self.load_w(self.W, self.Wb, dr["fox_w_in"], 4112)
        self.load_w(self.WO, self.WOb, dr["fox_w_out"], D)
        bf, bfb = self.fbf, self.fbf_b
        self.dma(bf[:, :], dr["fox_b_f"][:, :], [], [bfb])
        self.ts("dve", bf[:, :], bf[:, :], -1.0, None, ALU.mult, None, [bfb], [bfb])
        for g in self.groups:
            nm = g.name
            KTd, QTd, VAd, ZSd, OGd = dr["KT_" + nm], dr["QT_" + nm], dr["VA_" + nm], dr["ZS_" + nm], dr["OG_" + nm]
            KTv = KTd.rearrange("(hp two) r s -> two r hp s", two=2)
            QTv = QTd.rearrange("(hp two) r s -> two r hp s", two=2)
            wb = []

            def nb_(key):
                b = self.buf(("fox", g.gi, L) + key)
                wb.append(b)
                return b
            self.memset("pool", self.carry[:, :], 0.0, [self.carry_b])

            def cum_block(lf, lfb, nb, kpos, qpos, tag):
                cum, cumb = self.r_lf.next()
                self.P.op("dve", lambda e: e.tensor_tensor_scan(out=cum[:, 0:nb], data0=self.ones3[:, 0, 0:nb],
                                                                data1=lf[:, 0:nb], initial=self.carry[:, 0:1],
                                                                op0=ALU.mult, op1=ALU.add),
                          [lfb, self.ones3_b, self.carry_b], [cumb])
                self.cp("act", self.carry[:, 0:1], cum[:, nb - 1:nb], [cumb], [self.carry_b])
                pt, ptb = self.r_pt.next()
                t32, t32b = self.r_lf.next()
                r1, r1b = self.r_lf.next()
                self.cp("dve", pt[:, 0, 0:nb], cum[:, 0:nb], [cumb], [ptb])
                self.cp("dve", t32[:, 0:nb], pt[:, 0, 0:nb], [ptb], [t32b])
                self.tt("dve", r1[:, 0:nb], cum[:, 0:nb], t32[:, 0:nb], ALU.subtract, [cumb, t32b], [r1b])
                self.cp("dve", pt[:, 1, 0:nb], r1[:, 0:nb], [r1b], [ptb])
                self.cp("dve", t32[:, 0:nb], pt[:, 1, 0:nb], [ptb], [t32b])
                self.tt("dve", r1[:, 0:nb], r1[:, 0:nb], t32[:, 0:nb], ALU.subtract, [r1b, t32b], [r1b])
                self.cp("dve", pt[:, 2, 0:nb], r1[:, 0:nb], [r1b], [ptb])
                npt, nptb = self.r_pt.next()
                self.ts("dve", npt[:, :, 0:nb], pt[:, :, 0:nb], -1.0, None, ALU.mult, None, [ptb], [nptb])
                self.dma(KTd[:, 64:67, kpos:kpos + nb], npt[:, :, 0:nb], [nptb], [nb_((tag, "kc"))])
                self.dma(KTd[:, 67:70, kpos:kpos + nb], self.ones3[:, :, 0:nb], [self.ones3_b], [nb_((tag, "k1"))])
                if qpos is not None:
                    self.dma(QTd[:, 64:67, qpos:qpos + nb], self.ones3[:, :, 0:nb], [self.ones3_b],
                             [nb_((tag, "q1"))])
                    self.dma(QTd[:, 67:70, qpos:qpos + nb], pt[:, :, 0:nb], [ptb], [nb_((tag, "qc"))])


            ptiles = [(k0, nk) for (k0, nk) in g.ktiles if k0 < g.past]
            for b0 in range(0, len(ptiles), 2):
                pblk = ptiles[b0:b0 + 2]
                lf, lfb = self.r_lf.next()
                st_, stb = self.r_st.next()
                st = st_[:, :].rearrange("p (c t) -> p c t", c=8)
                for j, (k0, nk) in enumerate(pblk):
                    xt, xb = self.r_x.next()
                    self.dma(xt[:, :], dr["cache_fox_k"][k0:k0 + 128, :], [], [xb])
                    ub, ubb = self.r_ub.next()
                    self.cp("dve", ub[:, :], xt[:, :], [xb], [ubb])
                    tp, tpb = self.ps_tp.next()
                    tpv = tp[:, :].bitcast(BF16)
                    for c in range(8):
                        self.tr(tpv[:, c * 128:(c + 1) * 128], ub[:, c * 128:(c + 1) * 128], self.identb[:, :],
                                [ubb, self.identb_b], [tpb])
                    self.cp("act", st[:, :, j * 128:(j + 1) * 128], tpv.rearrange("p (c t) -> p c t", c=8), [tpb], [stb])
                    vt, vtb = self.r_x.next()
                    self.dma(vt[:, :], dr["cache_fox_v"][k0:k0 + 128, :], [], [vtb])
                    vb, vbb = self.r_vb.next()
                    self.memset("pool", vb[:, :].rearrange("p (h c) -> p h c", h=16)[:, :, 64:65], 1.0, [vbb])
                    self.cp("dve", vb[:, :].rearrange("p (h c) -> p h c", h=16)[:, :, 0:64],
                            vt[:, :].rearrange("p (h c) -> p h c", h=16), [vtb], [vbb])
                    self.dma(VAd[k0:k0 + 128, :], vb[:, :], [vbb], [nb_(("pv", k0))])
                    lt, ltb = self.r_small.next()
                    self.dma(lt[:, 0:16], dr["cache_fox_logf"][k0:k0 + 128, :], [], [ltb])
                    tp2, tp2b = self.ps_tp.next()
                    self.tr(tp2[0:16, 0:128], lt[:, 0:16], self.identf[:, :], [ltb, self.identf_b], [tp2b])
                    self.cp("act", lf[:, j * 128:(j + 1) * 128], tp2[0:16, 0:128], [tp2b], [lfb])
                kp, nbk = pblk[0][0], 128 * len(pblk)
                for two in range(2):
                    self.dma(KTv[two, 0:64, :, kp:kp + nbk], st[two * 64:(two + 1) * 64, :, 0:nbk], [stb],
                             [nb_(("pk", kp, two))])
                cum_block(lf, lfb, nbk, kp, None, ("pc", kp))

            for blk in g.blocks:
                uT, uTb, nb = self.make_uT(g, blk, first)
                t0b = blk[0][0]
                kpos = g.past + t0b
                for which, dst in ((0, QTv), (1, KTv)):
                    pos = t0b if which == 0 else kpos
                    for hf in range(2):
                        st_, stb = self.r_st.next()
                        st = st_[:, :].rearrange("p (c t) -> p c t", c=4)
                        for c4 in range(4):
                            c = hf * 4 + c4
                            ps, pb = self.proj_fm(uT, uTb, nb, which * D + c * 128, which * D + (c + 1) * 128)
                            if which == 0:
                                self.act(st[:, c4, 0:nb], ps[:, 0:nb], AF.Copy, [pb], [stb], scale=0.125)
                            else:
                                self.cp("dve", st[:, c4, 0:nb], ps[:, 0:nb], [pb], [stb])
                        for two in range(2):
                            self.dma(dst[two, 0:64, hf * 4:hf * 4 + 4, pos:pos + nb], st[two * 64:(two + 1) * 64, :, 0:nb],
                                     [stb], [nb_(("qk", which, t0b, two, hf))])
                ps, pb = self.proj_fm(uT, uTb, nb, 4 * D, 4 * D + 16)
                e1, e1b = self.r_lf.next()
                self.act(e1[:, 0:nb], ps[0:16, 0:nb], AF.Exp, [pb, bfb], [e1b], bias=bf[:, 0:1], scale=-1.0)
                lf, lfb = self.r_lf.next()
                self.act(lf[:, 0:nb], e1[:, 0:nb], AF.Ln, [e1b], [lfb], bias=1.0)
                self.ts("dve", lf[:, 0:nb], lf[:, 0:nb], -1.0, None, ALU.mult, None, [lfb], [lfb])
                off = 0
                for (t0, n) in blk:
                    tp2, tp2b = self.ps_tp.next()
                    self.tr(tp2[0:n, 0:16], lf[:, off:off + n], self.identf[0:16, 0:16], [lfb, self.identf_b], [tp2b])
                    lo, lob = self.r_small.next()
                    self.cp("act", lo[:n, 0:16], tp2[0:n, 0:16], [tp2b], [lob])
                    self.dma(dr["fox_logf_" + nm][t0:t0 + n, :], lo[:n, 0:16], [lob], [self.buf(("flo", g.gi, t0))])
                    for which, oname in ((1, "fox_k_"), (2, "fox_v_")):
                        ft, fb = self.r_f.next()
                        for cg in range(2):
                            ps, pb = self.proj_tm(uT, uTb, off, n, which * D + cg * 512, which * D + (cg + 1) * 512)
                            self.cp("act" if cg == 0 else "dve", ft[:n, cg * 512:(cg + 1) * 512], ps[:n, :], [pb], [fb])
                        self.dma(dr[oname + nm][t0:t0 + n, :], ft[:n, :], [fb], [self.buf((oname, g.gi, t0))])
                        if which == 2:
                            vb, vbb = self.r_vb.next()
                            self.memset("pool", vb[:n, :].rearrange("p (h c) -> p h c", h=16)[:, :, 64:65], 1.0, [vbb])
                            self.cp("dve", vb[:n, :].rearrange("p (h c) -> p h c", h=16)[:, :, 0:64],
                                    ft[:n, :].rearrange("p (h c) -> p h c", h=16), [fb], [vbb])
                            self.dma(VAd[g.past + t0:g.past + t0 + n, :], vb[:n, :], [vbb], [nb_(("v", t0))])
                    zt, ztb = self.r_h.next()
                    for cg in range(2):
                        ps, pb = self.proj_tm(uT, uTb, off, n, 3 * D + cg * 512, 3 * D + (cg + 1) * 512)
                        self.act(zt[:n, cg * 512:(cg + 1) * 512], ps[:n, :], AF.Silu, [pb], [ztb])
                    self.dma(ZSd[t0:t0 + n, 0:D], zt[:n, 0:D], [ztb], [nb_(("z", t0))])
                    off += n
                cum_block(lf, lfb, nb, kpos, t0b, ("c", t0b))

            ogw = []

            def extras(ki, k0, nk, qi, q0a, n):
                if k0 == q0a:
                    return [(self.identb[:nk, :nk], self.cmaskb[:nk, :n], [self.identb_b, self.cmaskb_b])]
                return []

            for h in range(16):
                def finish(bi, blk, accs, per_bank, w, h=h):
                    og, ogb = self.r_og.next()
                    for qi, (q0, n) in enumerate(blk):
                        acc, accb = accs[qi // per_bank]
                        col = (qi % per_bank) * w
                        rd, rdb = self.r_small.next()
                        self.recip(rd[:n, 0:1], acc[:n, col + 64:col + 65], [accb], [rdb])
                        self.ts("dve", og[:n, qi, 0:64], acc[:n, col:col + 64], rd[:n, 0:1], None, ALU.mult, None,
                                [accb, rdb], [ogb])
                    q0b = blk[0][0]
                    kb = self.buf(("fox_og", g.gi, L, h, bi))
                    ogw.append(kb)
                    if len(blk) > 1 or blk[0][1] == 128:
                        nt = len(blk)
                        self.dma(OGd[q0b:q0b + nt * 128, h * 64:(h + 1) * 64].rearrange("(qi p) c -> p qi c", p=128),
                                 og[:, 0:nt, 0:64], [ogb], [kb])
                    else:
                        n = blk[0][1]
                        self.dma(OGd[q0b:q0b + n, h * 64:(h + 1) * 64], og[:n, 0, 0:64], [ogb], [kb])
                self.attn_head(g, KTd[h], QTd[h], VAd, h * 65, 64, 70, wb, extras, finish)

            def mkprep(t0, n):
                def prep():
                    og, ogb = self.r_h.next()
                    self.dma(og[:n, 0:D], OGd[t0:t0 + n, 0:D], ogw, [ogb])
                    zs, zsb = self.r_h.next()
                    self.dma(zs[:n, 0:D], ZSd[t0:t0 + n, 0:D], wb, [zsb])
                    self.tt("dve", og[:n, 0:D], og[:n, 0:D], zs[:n, 0:D], ALU.mult, [ogb, zsb], [ogb])
                    return og, ogb
                return prep
            self.pass_c_run([(g, t0, n, mkprep(t0, n)) for (t0, n) in g.tiles], first, last, 8)

    def layer_diff(self, L, first, last):
        dr = self.dr
        LAM_INIT = 0.8 - 0.6 * math.exp(-0.3 * 2)
        self.attn_rings(129, False)
        A = self.arena
        self.load_w(self.W, self.Wb, dr["diff_w_in"], 4096)
        self.load_w(self.WO, self.WOb, dr["diff_w_out"], D)
        lv, lvb = A.alloc([1, 4, 64], F32), Buf()
        for i, nmv in enumerate(("diff_lam_q1", "diff_lam_k1", "diff_lam_q2", "diff_lam_k2")):
            self.dma(lv[:, i, :], dr[nmv][:, :], [], [lvb])
        l2, l2b = A.alloc([1, 8], F32), Buf()
        pr, prb = A.alloc([1, 2, 64], F32), Buf()
        lvv = lv[:, :, :].rearrange("p (a b) c -> p a b c", b=2)
        self.tt("dve", pr[:, :, :], lvv[:, :, 0, :], lvv[:, :, 1, :], ALU.mult, [lvb], [prb])
        self.P.op("dve", lambda e: e.tensor_reduce(out=l2[:, 0:2], in_=pr[:, :, :], axis=AX.X, op=ALU.add), [prb], [l2b])
        self.act(l2[:, 2:4], l2[:, 0:2], AF.Exp, [l2b], [l2b])
        self.tt("dve", l2[:, 4:5], l2[:, 3:4], l2[:, 2:3], ALU.subtract, [l2b], [l2b])
        self.ts("dve", l2[:, 5:6], l2[:, 4:5], -LAM_INIT, None, ALU.add, None, [l2b], [l2b])
        self.dma(dr["lam_scr"][0:1, 0:1], l2[:, 5:6], [l2b], [self.buf("lam_scr")])
        nlam, nlamb = A.alloc([128, 1], F32), Buf()
        self.dma(nlam[:, :], dr["lam_scr"][0:1, 0:1].to_broadcast([128, 1]), [self.buf("lam_scr")], [nlamb])
        tb, tbb = A.alloc([32, 2, 8], F32), Buf()
        for m_ in range(2):
            self.dma(tb[:, m_, :], dr["rel_bias_table"][:, :], [], [tbb])
        oh, ohb = self.r_x.next()
        self.dma(oh[0:32, 0:384], dr["t5_onehot"][:, :], [], [ohb])
        tbs, tbsb = A.alloc([32, 3, 16], BF16), Buf()
        ohb16, ohb16b = A.alloc([32, 384], BF16), Buf()
        self.cp("dve", ohb16[:, :], oh[0:32, 0:384], [ohb], [ohb16b])
        tbf = tb[:, :, :].rearrange("p a b -> p (a b)")
        tmp32, tmp32b = self.r_small.next()
        self.cp("dve", tbs[:, 0, :], tbf, [tbb], [tbsb])
        self.cp("dve", tmp32[0:32, 0:16], tbs[:, 0, :], [tbsb], [tmp32b])
        self.tt("dve", tmp32[0:32, 0:16], tbf, tmp32[0:32, 0:16], ALU.subtract, [tbb, tmp32b], [tmp32b])
        self.cp("dve", tbs[:, 1, :], tmp32[0:32, 0:16], [tmp32b], [tbsb])
        tmp33, tmp33b = self.r_small.next()
        self.cp("dve", tmp33[0:32, 0:16], tbs[:, 1, :], [tbsb], [tmp33b])
        self.tt("dve", tmp33[0:32, 0:16], tmp32[0:32, 0:16], tmp33[0:32, 0:16], ALU.subtract, [tmp32b, tmp33b], [tmp33b])
        self.cp("dve", tbs[:, 2, :], tmp33[0:32, 0:16], [tmp33b], [tbsb])
        ps, pb = self.ps_mm.next()
        for j_ in range(3):
            self.mm(ps[0:16, 0:384], tbs[:, j_, :], ohb16[:, :], j_ == 0, j_ == 2, [tbsb, ohb16b], [pb])
        fr, frb = self.r_f.next()
        c16, c16b = self.r_small.next()
        self.cp("dve", c16[0:16, 0:1], ps[0:16, 0:1], [pb], [c16b])
        self.ts("dve", fr[0:16, 0:384], ps[0:16, 0:384], c16[0:16, 0:1], None, ALU.subtract, None, [pb, c16b], [frb])
        self.dma(dr["fr_scr"][:, :], fr[0:16, 0:384], [frb], [self.buf("fr_scr")])
        crow, crowb = A.alloc([16, 2, 512], BF16), Buf()
        chi, chib = self.r_small.next()
        cbf = chi[0:16, 0:4].bitcast(BF16)
        self.cp("dve", cbf[:, 0:1], c16[0:16, 0:1], [c16b], [chib])
        self.cp("dve", chi[0:16, 4:5], cbf[:, 0:1], [chib], [chib])
        self.tt("dve", chi[0:16, 5:6], c16[0:16, 0:1], chi[0:16, 4:5], ALU.subtract, [c16b, chib], [chib])
        self.cp("dve", cbf[:, 1:2], chi[0:16, 5:6], [chib], [chib])
        for j in range(2):
            self.cp("dve", crow[:, j, :], cbf[:, j:j + 1].to_broadcast([16, 512]), [chib], [crowb])
        ones2, ones2b = A.alloc([16, 2, 512], BF16), Buf()
        self.memset("pool", ones2[:, :, :], 1.0, [ones2b])
        J, Jb = A.alloc([128, 128], BF16), Buf()
        dmk, dmkb = A.alloc([128, 128], BF16), Buf()
        t, b = self.r_f.next()
        self.dma(t[:, 0:128], dr["antiident"][:, :], [], [b])
        self.cp("dve", J[:, :], t[:, 0:128], [b], [Jb])
        t, b = self.r_f.next()
        self.dma(t[:, 0:128], dr["dmask"][:, :], [], [b])
        self.cp("dve", dmk[:, :], t[:, 0:128], [b], [dmkb])
        Hh = [[None, None] for _ in range(8)]
        for h in range(8):
            for ti, c in enumerate((128, 0)):
                t, b = self.r_f.next()
                src_ap = bass.AP(self.dh["fr_scr"], h * 384 + c, [[1, 128], [1, 128]])
                self.dma(t[:, 0:128], src_ap, [self.buf("fr_scr")], [b])
                hi, hib = A.alloc([128, 128], BF16), Buf()
                lo, lob = A.alloc([128, 128], BF16), Buf()
                self.cp("dve", hi[:, :], t[:, 0:128], [b], [hib])
                self.cp("dve", t[:, 128:256], hi[:, :], [hib], [b])
                self.tt("dve", t[:, 256:384], t[:, 0:128], t[:, 128:256], ALU.subtract, [b], [b])
                self.cp("dve", lo[:, :], t[:, 256:384], [b], [lob])
                Hh[h][ti] = (hi, hib, lo, lob)

        slw, slwb = A.alloc([128, 128], F32), Buf()
        self.dma(slw[:, :], dr["diff_subln_w"][0:1, :].to_broadcast([128, 128]), [], [slwb])
        self.P.op("act", lambda e: e.mul(out=slw[:, :], in_=slw[:, :], mul=1.0 - LAM_INIT), [slwb], [slwb])
        for g in self.groups:
            nm = g.name
            KTd, QTd, VAd, ZSd, OGd = dr["KT_" + nm], dr["QT_" + nm], dr["VA_" + nm], dr["ZS_" + nm], dr["OGF_" + nm]
            KTv = KTd.rearrange("(hp two) r s -> two r hp s", two=2)
            QTv = QTd.rearrange("(hp two) r s -> two r hp s", two=2)
            wb = []

            def nb_(key):
                b = self.buf(("diff", g.gi, L) + key)
                wb.append(b)
                return b
            ptiles = [(k0, nk) for (k0, nk) in g.ktiles if k0 < g.past]
            for b0 in range(0, len(ptiles), 2):
                pblk = ptiles[b0:b0 + 2]
                st_, stb = self.r_st.next()
                st = st_[:, :].rearrange("p (c t) -> p c t", c=8)
                for j, (k0, nk) in enumerate(pblk):
                    xt, xb = self.r_x.next()
                    self.dma(xt[:, :], dr["cache_diff_k"][k0:k0 + 128, :], [], [xb])
                    ub, ubb = self.r_ub.next()
                    self.cp("dve", ub[:, :], xt[:, :], [xb], [ubb])
                    tp, tpb = self.ps_tp.next()
                    tpv = tp[:, :].bitcast(BF16)
                    for c in range(8):
                        self.tr(tpv[:, c * 128:(c + 1) * 128], ub[:, c * 128:(c + 1) * 128], self.identb[:, :],
                                [ubb, self.identb_b], [tpb])
                    self.cp("act", st[:, :, j * 128:(j + 1) * 128], tpv.rearrange("p (c t) -> p c t", c=8), [tpb], [stb])
                    vt, vtb = self.r_x.next()
                    self.dma(vt[:, :], dr["cache_diff_v"][k0:k0 + 128, :], [], [vtb])
                    vb, vbb = self.r_vb.next()
                    vb3 = vb[:, 0:1032].rearrange("p (h c) -> p h c", h=8)
                    self.memset("pool", vb3[:, :, 128:129], 1.0, [vbb])
                    self.cp("dve", vb3[:, :, 0:128], vt[:, :].rearrange("p (h c) -> p h c", h=8), [vtb], [vbb])
                    self.dma(VAd[k0:k0 + 128, 0:1032], vb[:, 0:1032], [vbb], [nb_(("pv", k0))])
                kp, nbk = pblk[0][0], 128 * len(pblk)
                for two in range(2):
                    self.dma(KTv[two, 0:64, :, kp:kp + nbk], st[two * 64:(two + 1) * 64, :, 0:nbk], [stb],
                             [nb_(("pk", kp, two))])
                self.dma(KTd[:, 64:66, kp:kp + nbk], ones2[:, :, 0:nbk], [ones2b], [nb_(("p1", kp))])
            for blk in g.blocks:
                uT, uTb, nb = self.make_uT(g, blk, first)
                t0b = blk[0][0]
                kpos = g.past + t0b
                for which, dst in ((0, QTv), (1, KTv)):
                    pos = t0b if which == 0 else kpos
                    for hf in range(2):
                        st_, stb = self.r_st.next()
                        st = st_[:, :].rearrange("p (c t) -> p c t", c=4)
                        for c4 in range(4):
                            c = hf * 4 + c4
                            ps, pb = self.proj_fm(uT, uTb, nb, which * D + c * 128, which * D + (c + 1) * 128)
                            if which == 0:
                                self.act(st[:, c4, 0:nb], ps[:, 0:nb], AF.Copy, [pb], [stb], scale=0.125)
                            else:
                                self.cp("dve", st[:, c4, 0:nb], ps[:, 0:nb], [pb], [stb])
                        for two in range(2):
                            self.dma(dst[two, 0:64, hf * 4:hf * 4 + 4, pos:pos + nb], st[two * 64:(two + 1) * 64, :, 0:nb],
                                     [stb], [nb_(("qk", which, t0b, two, hf))])
                self.dma(KTd[:, 64:66, kpos:kpos + nb], ones2[:, :, 0:nb], [ones2b], [nb_(("k1", t0b))])
                QTm = QTd.rearrange("(h m) r s -> m h r s", m=2)
                for m_ in range(2):
                    self.dma(QTm[m_, :, 64:66, t0b:t0b + nb], crow[m_ * 8:(m_ + 1) * 8, :, 0:nb], [crowb],
                             [nb_(("qc", t0b, m_))])
                off = 0
                for (t0, n) in blk:
                    for which, oname in ((1, "diff_k_"), (2, "diff_v_")):
                        ft, fb = self.r_f.next()
                        for cg in range(2):
                            ps, pb = self.proj_tm(uT, uTb, off, n, which * D + cg * 512, which * D + (cg + 1) * 512)
                            self.cp("act" if cg == 0 else "dve", ft[:n, cg * 512:(cg + 1) * 512], ps[:n, :], [pb], [fb])
                        self.dma(dr[oname + nm][t0:t0 + n, :], ft[:n, :], [fb], [self.buf((oname, g.gi, t0))])
                        if which == 2:
                            vb, vbb = self.r_vb.next()
                            vb3 = vb[:n, 0:1032].rearrange("p (h c) -> p h c", h=8)
                            self.memset("pool", vb3[:, :, 128:129], 1.0, [vbb])
                            self.cp("dve", vb3[:, :, 0:128], ft[:n, :].rearrange("p (h c) -> p h c", h=8), [fb], [vbb])
                            self.dma(VAd[g.past + t0:g.past + t0 + n, 0:1032], vb[:n, 0:1032], [vbb], [nb_(("v", t0))])
                    zt, ztb = self.r_h.next()
                    for cg in range(2):
                        ps, pb = self.proj_tm(uT, uTb, off, n, 3 * D + cg * 512, 3 * D + (cg + 1) * 512)
                        self.act(zt[:n, cg * 512:(cg + 1) * 512], ps[:n, :], AF.Silu, [pb], [ztb])
                    self.dma(ZSd[t0:t0 + n, 0:D], zt[:n, 0:D], [ztb], [nb_(("z", t0))])
                    off += n
            ogw = []
            for vh in range(16):
                h, m = vh // 2, vh % 2

                def extras(ki, k0, nk, qi, q0a, n, h=h):
                    if k0 == q0a:
                        hi, hib, lo, lob = Hh[h][0]
                        return [(hi[:, 0:nk], J[:, 0:n], [hib, Jb]), (lo[:, 0:nk], J[:, 0:n], [lob, Jb]),
                                (self.identb[:nk, :nk], dmk[:nk, :n], [self.identb_b, dmkb])]
                    if k0 == q0a - 128:
                        hi, hib, lo, lob = Hh[h][1]
                        return [(hi[:, 0:nk], J[:, 0:n], [hib, Jb]), (lo[:, 0:nk], J[:, 0:n], [lob, Jb])]
                    return []

                def finish(bi, blk, accs, per_bank, w, h=h, m=m, vh=vh):
                    og, ogb = self.r_x.next()
                    for qi, (q0, n) in enumerate(blk):
                        acc, accb = accs[qi // per_bank]
                        col = (qi % per_bank) * w
                        rd, rdb = self.r_small.next()
                        self.recip(rd[:n, 0:1], acc[:n, col + 128:col + 129], [accb], [rdb])
                        self.ts("dve", og[:n, qi * 128:(qi + 1) * 128], acc[:n, col:col + 128], rd[:n, 0:1], None,
                                ALU.mult, None, [accb, rdb], [ogb])
                    q0b = blk[0][0]
                    kb = self.buf(("diff_og", g.gi, L, vh, bi))
                    ogw.append(kb)
                    c0 = m * D + h * 128
                    if blk[0][1] == 128:
                        nt = len(blk)
                        self.dma(OGd[q0b:q0b + nt * 128, c0:c0 + 128].rearrange("(qi p) c -> p qi c", p=128),
                                 og[:, 0:nt * 128].rearrange("p (qi c) -> p qi c", c=128), [ogb], [kb])
                    else:
                        n = blk[0][1]
                        self.dma(OGd[q0b:q0b + n, c0:c0 + 128], og[:n, 0:128], [ogb], [kb])
                self.attn_head(g, KTd[vh][0:66, :], QTd[vh][0:66, :], VAd, h * 129, 128, 66, wb, extras, finish)
            def mkprep(t0, n):
                def prep():
                        o1, o1b = self.r_x.next()
                        self.dma(o1[:n, :], OGd[t0:t0 + n, 0:D], ogw, [o1b])
                        o2, o2b = self.r_f.next()
                        self.dma(o2[:n, :], OGd[t0:t0 + n, D:2 * D], ogw, [o2b])
                        self.stt(o1[:n, :], o2[:n, :], nlam[:n, 0:1], o1[:n, :], ALU.mult, ALU.add, [o2b, nlamb, o1b], [o1b])
                        self.tt("dve", o2[:n, :], o1[:n, :], o1[:n, :], ALU.mult, [o1b], [o2b])
                        ss, ssb = self.r_small.next()
                        self.P.op("dve", lambda e, ss=ss, o2=o2, n=n: e.tensor_reduce(
                            out=ss[:n, 0:8], in_=o2[:n, :].rearrange("p (h c) -> p h c", h=8), axis=AX.X, op=ALU.add),
                            [o2b], [ssb])
                        self.rsqrt(ss[:n, 0:8], ss[:n, 0:8], [ssb], [ssb], scale=1.0 / 128, eps=NORM_EPS)
                        self.tt("dve", o1[:n, :].rearrange("p (h c) -> p h c", h=8), o1[:n, :].rearrange("p (h c) -> p h c", h=8),
                                ss[:n, 0:8].unsqueeze(2).to_broadcast([n, 8, 128]), ALU.mult, [o1b, ssb], [o1b])
                        self.tt("dve", o1[:n, :].rearrange("p (h c) -> p h c", h=8), o1[:n, :].rearrange("p (h c) -> p h c", h=8),
                                slw[:n, :].unsqueeze(1).to_broadcast([n, 8, 128]), ALU.mult, [o1b, slwb], [o1b])
                        zs, zsb = self.r_h.next()
                        self.dma(zs[:n, 0:D], ZSd[t0:t0 + n, 0:D], wb, [zsb])
                        og, ogb = self.r_h.next()
                        self.tt("dve", og[:n, 0:D], o1[:n, :], zs[:n, 0:D], ALU.mult, [o1b, zsb], [ogb])
                        return og, ogb
                return prep
            self.pass_c_run([(g, t0, n, mkprep(t0, n)) for (t0, n) in g.tiles], first, last, 8)

    def layer_ret(self, L, first, last):
        dr = self.dr
        P = self.P
        A = self.arena
        LG = [math.log1p(-2.0 ** (-5.0 - h)) for h in range(4)]
        P.barrier(); A.reset()
        self.set_psum(4, 2, 2)
        self.load_w(self.W, self.Wb, dr["ret_w_in"], 4096)
        r_st = Ring(self, "rst", [128, 8, 512], BF16, 1, arena=A)
        r_cs = Ring(self, "rcs", [128, 2, 512], F32, 1, arena=A)
        r_vt = Ring(self, "rvt", [128, 2048], BF16, 2, arena=A)
        wbs = {}
        for g in self.groups:
            nm = g.name
            wb = wbs[g.gi] = []

            def nb_(key, wb=wb, g=g):
                b = self.buf(("ret", g.gi) + key)
                wb.append(b)
                return b
            for blk in g.blocks:
                uT, uTb, nb = self.make_uT(g, blk, first)
                t0b = blk[0][0]
                cs, csb = r_cs.next()
                self.dma(cs[:, 0, 0:nb], dr["rot_cos_" + nm][:, t0b:t0b + nb], [], [csb])
                self.dma(cs[:, 1, 0:nb], dr["rot_sin_" + nm][:, t0b:t0b + nb], [], [csb])
                for which, dname in ((0, "QR_"), (1, "KR_")):
                    st, stb = r_st.next()
                    for h in range(4):
                        c0 = which * D + h * 256
                        pe_, peb = self.proj_fm(uT, uTb, nb, c0, c0 + 128)
                        po_, pob = self.proj_fm(uT, uTb, nb, c0 + 128, c0 + 256)
                        xe, xeb = self.r_f.next()
                        sc = 0.0625 if which == 0 else 1.0
                        self.act(xe[:, 0:nb], pe_[:, 0:nb], AF.Copy, [peb], [xeb], scale=sc)
                        self.act(xe[:, 512:512 + nb], po_[:, 0:nb], AF.Copy, [pob], [xeb], scale=sc)
                        tm, tmb = self.r_x.next()
                        self.tt("dve", tm[:, 0:nb], xe[:, 0:nb], cs[:, 0, 0:nb], ALU.mult, [xeb, csb], [tmb])
                        self.tt("dve", tm[:, 512:512 + nb], xe[:, 512:512 + nb], cs[:, 1, 0:nb], ALU.mult, [xeb, csb], [tmb])
                        self.tt("dve", st[:, 2 * h, 0:nb], tm[:, 0:nb], tm[:, 512:512 + nb], ALU.subtract, [tmb], [stb])
                        tm, tmb = self.r_x.next()
                        self.tt("dve", tm[:, 0:nb], xe[:, 0:nb], cs[:, 1, 0:nb], ALU.mult, [xeb, csb], [tmb])
                        self.tt("dve", tm[:, 512:512 + nb], xe[:, 512:512 + nb], cs[:, 0, 0:nb], ALU.mult, [xeb, csb], [tmb])
                        self.tt("dve", st[:, 2 * h + 1, 0:nb], tm[:, 0:nb], tm[:, 512:512 + nb], ALU.add, [tmb], [stb])
                    self.dma(dr[dname + nm].rearrange("he p t -> p he t")[:, :, t0b:t0b + nb], st[:, :, 0:nb], [stb],
                             [nb_((dname, t0b))])
                off = 0
                for (t0, n) in blk:
                    vt, vtb = r_vt.next()
                    for cg in range(4):
                        ps, pb = self.proj_tm(uT, uTb, off, n, 2 * D + cg * 512, 2 * D + (cg + 1) * 512)
                        self.cp("act" if cg % 2 == 0 else "dve", vt[:n, cg * 512:(cg + 1) * 512], ps[:n, :], [pb], [vtb])
                    self.dma(dr["OG_" + nm + "v"][t0:t0 + n, :], vt[:n, :], [vtb], [nb_(("v", t0))])
                    off += n
        self.load_w(self.W, self.Wb, dr["ret_w_in"], 2048, c_off=4096)
        for g in self.groups:
            nm = g.name
            wb = wbs[g.gi]
            for blk in g.blocks:
                uT, uTb, nb = self.make_uT(g, blk, first)
                off = 0
                for (t0, n) in blk:
                    zt, ztb = r_vt.next()
                    for cg in range(4):
                        ps, pb = self.proj_tm(uT, uTb, off, n, cg * 512, (cg + 1) * 512)
                        self.act(zt[:n, cg * 512:(cg + 1) * 512], ps[:n, :], AF.Silu, [pb], [ztb])
                    b = self.buf(("ret", g.gi, "z", t0))
                    wb.append(b)
                    self.dma(dr["ZS_" + nm][t0:t0 + n, :], zt[:n, :], [ztb], [b])
                    off += n
        P.barrier(); A.reset()
        A2 = self.arenaW
        A2.reset()
        self.load_w(self.WO, self.WOb, dr["ret_w_out"], D)
        dmk, dmkb = A.alloc([128, 4, 128], F32), Buf()
        self.dma(dmk[:, :, :], dr["ret_dmaskT"][:, :, :], [], [dmkb])
        qdc, qdcb = A.alloc([128, 4, 128], F32), Buf()
        self.dma(qdc[:, :, :], dr["ret_qdec"][:, :, :], [], [qdcb])
        kdc, kdcb = A.alloc([128, 8], F32), Buf()
        self.dma(kdc[:, :], dr["ret_kdec"][:, :], [], [kdcb])
        S, Sb = A.alloc([128, 8, 512], F32), Buf()
        Sbf, Sbfb = A.alloc([128, 8, 512], BF16), Buf()
        r_q = Ring(self, "rq", [128, 8, 128], BF16, 2, arena=A)
        r_k = Ring(self, "rk", [128, 8, 128], BF16, 2, arena=A)
        r_v = Ring(self, "rv", [128, 2048], BF16, 2, arena=A)
        r_qd = Ring(self, "rqd", [128, 8, 128], BF16, 2, arena=A2)
        r_at = Ring(self, "rat", [128, 4, 128], BF16, 2, arena=A2)
        r_kd = Ring(self, "rkd", [128, 8, 128], BF16, 2, arena=A2)
        r_o = Ring(self, "ro", [128, 2048], BF16, 2, arena=A2)
        r_s = Ring(self, "rs", [128, 32], F32, 3, arena=A2)
        ogws = {}
        for g in self.groups:
            nm = g.name
            wb = wbs[g.gi]
            ogw = ogws[g.gi] = []
            QRd = dr["QR_" + nm].rearrange("he p t -> p he t")
            KRd = dr["KR_" + nm].rearrange("he p t -> p he t")
            VRd = dr["OG_" + nm + "v"]
            if g.gi == 0:
                self.memset("pool", S[:, :, :], 0.0, [Sb])
            else:
                self.dma(S[:, :, :], dr["state_ret"].rearrange("h (e p) v -> p (h e) v", e=2), [], [Sb])
            self.cp("act", Sbf[:, 0:4, :], S[:, 0:4, :], [Sb], [Sbfb])
            self.cp("dve", Sbf[:, 4:8, :], S[:, 4:8, :], [Sb], [Sbfb])
            for (t0, n) in g.tiles:
                qt, qtb = r_q.next()
                self.dma(qt[:, :, 0:n], QRd[:, :, t0:t0 + n], wb, [qtb])
                kt, ktb = r_k.next()
                self.dma(kt[:, :, 0:n], KRd[:, :, t0:t0 + n], wb, [ktb])
                vt, vtb = r_v.next()
                self.dma(vt[:n, :], VRd[t0:t0 + n, :], wb, [vtb])
                kc0 = 0 if n == 128 else 4
                ps, pb = self.psb[0]
                for h in range(4):
                    for e in range(2):
                        self.mm(ps[:n, h * 128:h * 128 + n], kt[:, 2 * h + e, 0:n], qt[:, 2 * h + e, 0:n], e == 0, e == 1,
                                [ktb, qtb], [pb], skip=True)
                at, atb = r_at.next()
                self.tt("dve", at[:n, :, 0:n], ps[:n, :].rearrange("p (h i) -> p h i", h=4)[:, :, 0:n], dmk[:n, :, 0:n],
                        ALU.mult, [pb, dmkb], [atb])
                qd, qdb = r_qd.next()
                self.tt("dve", qd[:, :, 0:n].rearrange("p (h e) i -> p h e i", e=2),
                        qt[:, :, 0:n].rearrange("p (h e) i -> p h e i", e=2),
                        qdc[:, :, 0:n].unsqueeze(2).to_broadcast([128, 4, 2, n]), ALU.mult, [qtb, qdcb], [qdb])
                tp, tpb = self.psb[5]
                tpv = tp[:, :].bitcast(BF16)
                for he in range(8):
                    self.tr(tpv[:n, he * 128:(he + 1) * 128], kt[:, he, 0:n], self.identb[:, :], [ktb, self.identb_b], [tpb])
                kd, kdb = r_kd.next()
                self.tt("dve", kd[:n, :, :].rearrange("p (h e) d -> p h (e d)", e=2),
                        tpv[:n, :].rearrange("p (h x) -> p h x", h=4),
                        kdc[:n, kc0:kc0 + 4].unsqueeze(2).to_broadcast([n, 4, 256]), ALU.mult, [tpb, kdcb], [kdb])
                pos = []
                for h in range(4):
                    po, pob = self.psb[1 + h]
                    self.mm(po[:n, :], at[:n, h, 0:n], vt[:n, h * 512:(h + 1) * 512], True, False, [atb, vtb], [pob])
                    for e in range(2):
                        self.mm(po[:n, :], qd[:, 2 * h + e, 0:n], Sbf[:, 2 * h + e, :], False, e == 1, [qdb, Sbfb], [pob])
                    pos.append((po, pob))
                for h in range(4):
                    cdec = math.exp(LG[h] * n)
                    for e in range(2):
                        pu, pub = self.psb[6 + e]
                        self.mm(pu[:, :], kd[:n, 2 * h + e, :], vt[:n, h * 512:(h + 1) * 512], True, True, [kdb, vtb], [pub])
                        self.stt(S[:, 2 * h + e, :], S[:, 2 * h + e, :], cdec, pu[:, :], ALU.mult, ALU.add, [Sb, pub], [Sb])
                    self.cp("act", Sbf[:, 2 * h:2 * h + 2, :], S[:, 2 * h:2 * h + 2, :], [Sb], [Sbfb])
                st, stb = r_s.next()
                for h in range(4):
                    po, pob = pos[h]
                    P.op("dve", lambda e_, st=st, po=po, n=n, h=h: e_.bn_stats(out=st[:n, 8 + 6 * h:14 + 6 * h], in_=po[:n, :]),
                         [pob], [stb])
                    P.op("dve", lambda e_, st=st, n=n, h=h: e_.bn_aggr(out=st[:n, 2 * h:2 * h + 2], in_=st[:n, 8 + 6 * h:14 + 6 * h]),
                         [stb], [stb])
                mv = st[:n, 0:8].rearrange("p (h t) -> p h t", t=2)
                rs, rsb = r_s.next()
                self.rsqrt(rs[:n, 0:4], mv[:, :, 1], [stb], [rsb], eps=LN_EPS)
                self.stt(rs[:n, 4:8], mv[:, :, 0], -1.0, rs[:n, 0:4], ALU.mult, ALU.mult, [stb, rsb], [rsb])
                ot, otb = r_o.next()
                for h in range(4):
                    po, pob = pos[h]
                    self.act(ot[:n, h * 512:(h + 1) * 512], po[:n, :], AF.Identity, [pob, rsb], [otb],
                             bias=rs[:n, 4 + h:5 + h], scale=rs[:n, h:h + 1])
                b = self.buf(("ret_og", g.gi, t0))
                ogw.append(b)
                self.dma(dr["OG_" + nm][t0:t0 + n, :], ot[:n, :], [otb], [b])
            self.dma(dr["ret_state_" + nm].rearrange("h (e p) v -> p (h e) v", e=2), S[:, :, :], [Sb],
                     [self.buf(("ret_state", g.gi))])
        P.barrier()
        self.load_w(self.W, self.Wb, dr["ret_w_out"][D:2 * D, :], D)
        P.barrier(); A.reset()
        gnw, gnwb = A.alloc([128, 2048], F32), Buf()
        self.dma(gnw[:, :], dr["ret_gn_w"][0:1, :].to_broadcast([128, 2048]), [], [gnwb])
        r_og = Ring(self, "rog2", [128, 2048], BF16, 2, arena=A)
        r_zs = Ring(self, "rzs2", [128, 2048], BF16, 2, arena=A)
        r_gz = Ring(self, "rgz2", [128, 2048], BF16, 2, arena=A)

        def wo(k):
            if k < 8:
                return self.WO[:, k, :], self.WOb
            return self.W[:, k - 8, 0:D], self.Wb
        items = []
        for g in self.groups:
            def mkprep(g, t0, n):
                def prep():
                    nm = g.name
                    og, ogb = r_og.next()
                    self.dma(og[:n, :], dr["OG_" + nm][t0:t0 + n, :], ogws[g.gi], [ogb])
                    zs, zsb = r_zs.next()
                    self.dma(zs[:n, :], dr["ZS_" + nm][t0:t0 + n, :], wbs[g.gi], [zsb])
                    gz, gzb = r_gz.next()
                    self.tt("dve", gz[:n, :], og[:n, :], gnw[:n, :], ALU.mult, [ogb, gnwb], [gzb])
                    self.tt("dve", gz[:n, :], gz[:n, :], zs[:n, :], ALU.mult, [gzb, zsb], [gzb])
                    return gz, gzb
                return prep
            items += [(g, t0, n, mkprep(g, t0, n)) for (t0, n) in g.tiles]
        self.pass_c_run(items, first, last, 16, wo=wo)

    def layer_gdn(self, L, first, last):
        dr = self.dr
        P = self.P
        A = self.arena
        P.barrier(); A.reset()
        self.set_psum(4, 2, 2)
        self.load_w(self.W, self.Wb, dr["gdn_w_in"], 4112)
        self.load_w(self.WO, self.WOb, dr["gdn_w_out"], D)
        cw, cwb = A.alloc([128, 24, 4], F32), Buf()
        self.dma(cw[:, :, :], dr["gdn_convw"][:, :, :], [], [cwb])
        halo, halob = A.alloc([128, 24, 3], F32), Buf()
        onesb, onesbb = A.alloc([128, 128], BF16), Buf()
        self.memset("pool", onesb[:, :], 1.0, [onesbb])
        sc8, sc8b = A.alloc([8, 4], F32), Buf()
        self.dma(sc8[:, 0:1], dr["gdn_dt_bias"][:, :], [], [sc8b])
        self.dma(sc8[:, 2:3], dr["gdn_a_log"][:, :], [], [sc8b])
        self.act(sc8[:, 3:4], sc8[:, 2:3], AF.Exp, [sc8b], [sc8b])
        self.ts("dve", sc8[:, 1:2], sc8[:, 3:4], -1.0, None, ALU.mult, None, [sc8b], [sc8b])
        r_xr = Ring(self, "rxr", [128, 516], F32, 4, arena=A)
        r_y = Ring(self, "ry", [128, 512], F32, 4, arena=A)
        r_rn = Ring(self, "rrn", [128, 512], F32, 4, arena=A)
        r_sq = Ring(self, "rsq", [128, 512], BF16, 4, arena=A)
        r_st = Ring(self, "rst", [128, 8, 512], BF16, 1, arena=A)
        r_g8 = Ring(self, "rg8", [8, 512], F32, 6, arena=A)
        wbs = {}
        for g in self.groups:
            nm = g.name
            wb = wbs[g.gi] = []

            def nb_(key, wb=wb, g=g):
                b = self.buf(("gdn", g.gi) + key)
                wb.append(b)
                return b
            if g.gi == 0:
                self.memset("pool", halo[:, :, :], 0.0, [halob])
            else:
                self.dma(halo[:, :, :], dr["gdn_conv_in"][:, :, :], [], [halob])
            for blk in g.blocks:
                uT, uTb, nb = self.make_uT(g, blk, first)
                t0b = blk[0][0]
                for which, dname in ((0, "QG_"), (1, "KG_"), (2, "VG_")):
                    st, stb = r_st.next()
                    for hg in (0, 4):
                        hs = list(range(hg, hg + 4))
                        xrs, ys, rns, sqs, p2s = {}, {}, {}, {}, {}
                        for h in hs:
                            c = which * 8 + h
                            ps, pb = self.proj_fm(uT, uTb, nb, c * 128, (c + 1) * 128)
                            xr, xrb = r_xr.next()
                            self.cp("act", xr[:, 0:3], halo[:, c, :], [halob], [xrb])
                            self.cp("act", xr[:, 3:3 + nb], ps[:, 0:nb], [pb], [xrb])
                            self.cp("act", halo[:, c, :], xr[:, nb:nb + 3], [xrb], [halob])
                            xrs[h] = (xr, xrb)
                        for h in hs:
                            c = which * 8 + h
                            xr, xrb = xrs[h]
                            y, yb = r_y.next()
                            self.ts("dve", y[:, 0:nb], xr[:, 0:nb], cw[:, c, 0:1], None, ALU.mult, None, [xrb, cwb], [yb])
                            for j in range(1, 4):
                                self.stt(y[:, 0:nb], xr[:, j:j + nb], cw[:, c, j:j + 1], y[:, 0:nb], ALU.mult, ALU.add,
                                         [xrb, cwb, yb], [yb])
                            ys[h] = (y, yb)
                        for h in hs:
                            y, yb = ys[h]
                            self.act(y[:, 0:nb], y[:, 0:nb], AF.Silu, [yb], [yb])
                        if which == 2:
                            for h in hs:
                                y, yb = ys[h]
                                self.cp("dve", st[:, h, 0:nb], y[:, 0:nb], [yb], [stb])
                            continue
                        for h in hs:
                            y, yb = ys[h]
                            sq, sqb = r_sq.next()
                            self.tt("dve", sq[:, 0:nb], y[:, 0:nb], y[:, 0:nb], ALU.mult, [yb], [sqb])
                            p2, p2b = self.ps_mm.next()
                            self.mm(p2[:, 0:nb], onesb[:, :], sq[:, 0:nb], True, True, [onesbb, sqb], [p2b])
                            p2s[h] = (p2, p2b)
                        for h in hs:
                            p2, p2b = p2s[h]
                            rn, rnb = r_rn.next()
                            self.rsqrt(rn[:, 0:nb], p2[:, 0:nb], [p2b], [rnb], eps=NORM_EPS)
                            rns[h] = (rn, rnb)
                        for h in hs:
                            y, yb = ys[h]
                            rn, rnb = rns[h]
                            if which == 0:
                                self.stt(st[:, h, 0:nb], y[:, 0:nb], 128.0 ** -0.5, rn[:, 0:nb], ALU.mult, ALU.mult,
                                         [yb, rnb], [stb])
                            else:
                                self.tt("dve", st[:, h, 0:nb], y[:, 0:nb], rn[:, 0:nb], ALU.mult, [yb, rnb], [stb])
                    self.dma(dr[dname + nm].rearrange("h p t -> p h t")[:, :, t0b:t0b + nb], st[:, :, 0:nb], [stb],
                             [nb_((dname, t0b))])
                ps, pb = self.proj_fm(uT, uTb, nb, 3072, 3080)
                bt, btb = r_g8.next()
                self.act(bt[:, 0:nb], ps[0:8, 0:nb], AF.Exp, [pb], [btb], scale=-1.0)
                self.ts("dve", bt[:, 0:nb], bt[:, 0:nb], 1.0, None, ALU.add, None, [btb], [btb])
                self.recip(bt[:, 0:nb], bt[:, 0:nb], [btb], [btb])
                ps, pb = self.proj_fm(uT, uTb, nb, 3080, 3088)
                gt, gtb = r_g8.next()
                self.act(gt[:, 0:nb], ps[0:8, 0:nb], AF.Exp, [pb, sc8b], [gtb], bias=sc8[:, 0:1])
                self.act(gt[:, 0:nb], gt[:, 0:nb], AF.Ln, [gtb], [gtb], bias=1.0)
                self.ts("dve", gt[:, 0:nb], gt[:, 0:nb], sc8[:, 1:2], None, ALU.mult, None, [gtb, sc8b], [gtb])
                Gt, Gtb = r_g8.next()
                on8, on8b = r_g8.next()
                self.memset("pool", on8[:, 0:128], 1.0, [on8b])
                for off in range(0, nb, 64):
                    n = min(64, nb - off)
                    P.op("dve", lambda e, Gt=Gt, gt=gt, on8=on8, off=off, n=n: e.tensor_tensor_scan(
                        out=Gt[:, off:off + n], data0=on8[:, 0:n], data1=gt[:, off:off + n], initial=0.0,
                        op0=ALU.mult, op1=ALU.add), [gtb, on8b], [Gtb])
                btb16, btb16b = r_g8.next()
                bt16v = btb16[:, 0:256].bitcast(BF16)
                self.cp("act", bt16v[:, 0:nb], bt[:, 0:nb], [btb], [btb16b])
                self.dma(dr["BTb_" + nm][:, t0b:t0b + nb], bt16v[:, 0:nb], [btb16b], [nb_(("btb", t0b))])
                self.dma(dr["GT_" + nm][:, t0b:t0b + nb], Gt[:, 0:nb], [Gtb], [nb_(("gt", t0b))])
                off = 0
                for ti, (t0, n) in enumerate(blk):
                    tp, tpb = self.ps_tp.next()
                    self.tr(tp[0:n, 0:8], bt[:, off:off + n], self.identf[0:8, 0:8], [btb, self.identf_b], [tpb])
                    tp2, tp2b = self.ps_tp.next()
                    self.tr(tp2[0:n, 0:8], Gt[:, off:off + n], self.identf[0:8, 0:8], [Gtb, self.identf_b], [tp2b])
                    gb, gbb = self.r_small.next()
                    self.cp("act", gb[:n, 0:8], tp[0:n, 0:8], [tpb], [gbb])
                    self.cp("act", gb[:n, 8:16], tp2[0:n, 0:8], [tp2b], [gbb])
                    self.dma(dr["GB_" + nm][t0:t0 + n, :], gb[:n, 0:16], [gbb], [nb_(("gb", t0))])
                    zt, ztb = self.r_h.next()
                    for cg in range(2):
                        ps, pb = self.proj_tm(uT, uTb, off, n, 3088 + cg * 512, 3088 + (cg + 1) * 512)
                        self.act(zt[:n, cg * 512:(cg + 1) * 512], ps[:n, :], AF.Silu, [pb], [ztb])
                    self.dma(dr["ZS_" + nm][t0:t0 + n, 0:D], zt[:n, 0:D], [ztb], [nb_(("z", t0))])
                    off += n
                if blk is g.blocks[-1]:
                    co, cob = self.r_f.next()
                    for cg in range(6):
                        ps, pb = self.ps_mm.next()
                        for k in range(8):
                            self.mm(ps[0:3, :], uT[:, k, nb - 3:nb], self.W[:, k, cg * 512:(cg + 1) * 512], k == 0, k == 7,
                                    [uTb, self.Wb], [pb])
                        self.cp("act", co[0:3, (cg % 2) * 512:(cg % 2 + 1) * 512], ps[0:3, :], [pb], [cob])
                        if cg % 2 == 1:
                            self.dma(dr["gdn_conv_" + nm][:, (cg - 1) * 512:(cg + 1) * 512], co[0:3, :], [cob],
                                     [self.buf(("gdn_conv", g.gi, cg))])
                            if cg < 5:
                                co, cob = self.r_f.next()
        import os
        if os.environ.get("GDN_STOP") == "A":
            return
        P.barrier(); A.reset()
        A2 = self.arenaW
        A2.reset()
        msk, mskb = A.alloc([128, 3, 128], F32), Buf()
        self.dma(msk[:, :, :], dr["gdn_masks"][:, :, :], [], [mskb])
        S, Sb = A.alloc([128, 8, 128], F32), Buf()
        Sbf, Sbfb = A.alloc([128, 8, 128], BF16), Buf()
        r_qkv = Ring(self, "rqkv", [128, 3, 8, 128], BF16, 2, arena=A)
        r_gb = Ring(self, "rgb", [128, 8, 128], F32, 2, arena=A)
        r_bb = Ring(self, "rbb", [128, 8, 128], BF16, 2, arena=A)
        r_gbk = Ring(self, "rgbk", [64, 2, 16], F32, 2, arena=A)
        r_sq = Ring(self, "rsq", [64, 8, 128], F32, 1, arena=A)
        r_sc = Ring(self, "rsc", [64, 48], F32, 6, arena=A)
        r_F = Ring(self, "rF", [128, 8, 64], F32, 14, arena=A2)
        r_B = Ring(self, "rB", [128, 8, 64], BF16, 16, arena=A2)
        r_L = Ring(self, "rL", [128, 8, 64], BF16, 10, arena=A2)
        r_B2 = Ring(self, "rB2", [64, 8, 128], BF16, 8, arena=A)
        ogws = {}

        def bc(ap, shape):
            return ap.to_broadcast(shape)
        for g in self.groups:
            nm = g.name
            wb = wbs[g.gi]
            ogw = ogws[g.gi] = []
            if g.gi == 0:
                self.memset("pool", S[:, :, :], 0.0, [Sb])
            else:
                self.dma(S[:, :, :], dr["state_gdn"].rearrange("h d v -> d h v"), [], [Sb])
            self.cp("act", Sbf[:, :, :], S[:, :, :], [Sb], [Sbfb])
            for (tt0, tn) in g.tiles:
                qkv, qkvb = r_qkv.next()
                for i, dname in enumerate(("QG_", "KG_", "VG_")):
                    self.dma(qkv[:, i, :, 0:tn], dr[dname + nm].rearrange("h p t -> p h t")[:, :, tt0:tt0 + tn], wb, [qkvb])
                gb, gbb = r_gb.next()
                self.dma(gb[:, :, 0:tn], bass.AP(self.dh["GT_" + nm], tt0, [[0, 128], [g.T, 8], [1, tn]]), wb, [gbb])
                bb, bbb = r_bb.next()
                self.dma(bb[:, :, 0:tn], bass.AP(self.dh["BTb_" + nm], tt0, [[0, 128], [g.T, 8], [1, tn]]), wb, [bbb])
                gbk, gbkb = r_gbk.next()
                if tn % 64 == 0:
                    self.dma(gbk[0:64, 0:tn // 64, :], dr["GB_" + nm][tt0:tt0 + tn, :].rearrange("(c p) x -> p c x", p=64),
                             wb, [gbkb])
                else:
                    self.dma(gbk[:tn, 0, :], dr["GB_" + nm][tt0:tt0 + tn, :], wb, [gbkb])
                from types import SimpleNamespace as NS
                cks = []
                for ci, off in enumerate(range(0, tn, 64)):
                    c = NS(ci=ci, off=off, n=min(64, tn - off), t0=tt0 + off)
                    c.Qt = qkv[:, 0, :, off:off + c.n]
                    c.Kt = qkv[:, 1, :, off:off + c.n]
                    c.Vt = qkv[:, 2, :, off:off + c.n]
                    c.Gp = gbk[:c.n, ci, 8:16]
                    c.Bp = gbk[:c.n, ci, 0:8]
                    cks.append(c)
                mk = lambda c, k_: msk[:c.n, k_, 0:c.n].unsqueeze(1).to_broadcast([c.n, 8, c.n])
                v3 = lambda c, ps_: ps_[:c.n, :].rearrange("p (h c) -> p h c", h=8)[:, :, 0:c.n]
                t3 = lambda c, v_: v_[:c.n, :].rearrange("p (h c) -> p h c", h=8)
                bcs = lambda c, a_: a_.unsqueeze(2).to_broadcast([c.n, 8, 128])

                def s_bcast(c):
                    n, off = c.n, c.off
                    c.eG, c.eGb = r_F.next()
                    self.act(c.eG[:, :, 0:n], gb[:, :, off:off + n], AF.Exp, [gbb], [c.eGb])
                    c.KbT, c.KbTb = r_B.next()
                    self.tt("dve", c.KbT[:, :, 0:n], c.Kt, bb[:, :, off:off + n], ALU.mult, [qkvb, bbb], [c.KbTb])
                    c.sc, c.scb = r_sc.next()
                    self.act(c.sc[:n, 16:24], c.Gp, AF.Exp, [gbkb], [c.scb])
                    c.tD, c.tDb = r_F.next()
                    self.tt("dve", c.tD[:n, :, 0:n], gb[:n, :, off:off + n], c.Gp.unsqueeze(2).to_broadcast([n, 8, n]),
                            ALU.subtract, [gbb, gbkb], [c.tDb])

                def s_gram(c):
                    n = c.n
                    c.paA, c.paAb = self.ps_mm.next()
                    for h in range(8):
                        self.mm(c.paA[:n, h * 64:h * 64 + n], c.Kt[:, h, :], c.KbT[:, h, 0:n], True, True, [qkvb, c.KbTb], [c.paAb], skip=True)
                    c.paB, c.paBb = self.ps_mm.next()
                    for h in range(8):
                        self.mm(c.paB[:n, h * 64:h * 64 + n], c.KbT[:, h, 0:n], c.Kt[:, h, :], True, True, [qkvb, c.KbTb], [c.paBb], skip=True)
                    c.D1, c.D1b = r_F.next()
                    self.ts("dve", c.D1[:n, :, 0:n], c.tD[:n, :, 0:n], 0.0, None, ALU.min, None, [c.tDb], [c.D1b])
                    self.act(c.D1[:n, :, 0:n], c.D1[:n, :, 0:n], AF.Exp, [c.D1b], [c.D1b])
                    c.D2, c.D2b = r_F.next()
                    self.ts("dve", c.D2[:n, :, 0:n], c.tD[:n, :, 0:n], 0.0, None, ALU.max, None, [c.tDb], [c.D2b])
                    self.act(c.D2[:n, :, 0:n], c.D2[:n, :, 0:n], AF.Exp, [c.D2b], [c.D2b], scale=-1.0)

                def s_mask(c):
                    n = c.n
                    c.D1i, c.D1ib = r_F.next()
                    self.tt("dve", c.D1i[:n, :, 0:n], c.D1[:n, :, 0:n], mk(c, 2), ALU.mult, [c.D1b, mskb], [c.D1ib])
                    self.tt("dve", c.D1[:n, :, 0:n], c.D1[:n, :, 0:n], mk(c, 0), ALU.mult, [c.D1b, mskb], [c.D1b])
                    self.tt("dve", c.D2[:n, :, 0:n], c.D2[:n, :, 0:n], mk(c, 1), ALU.mult, [c.D2b, mskb], [c.D2b])
                    c.Pm, c.Pmb = r_F.next()
                    self.tt("dve", c.Pm[:n, :, 0:n], v3(c, c.paA), c.D1[:n, :, 0:n], ALU.mult, [c.paAb, c.D1b], [c.Pmb])
                    c.X, c.Xb = r_B.next()
                    self.cp("act", c.X[:n, :, 0:n], c.Pm[:n, :, 0:n], [c.Pmb], [c.Xb])
                    c.Y, c.Yb = r_B.next()
                    self.tt("dve", c.Y[:n, :, 0:n], v3(c, c.paB), c.D2[:n, :, 0:n], ALU.mult, [c.paBb, c.D2b], [c.Yb])
                    self.tt("dve", c.Pm[:n, :, 0:n], c.Pm[:n, :, 0:n],
                            self.identf[:n, 0:n].unsqueeze(1).to_broadcast([n, 8, n]), ALU.add, [c.Pmb, self.identf_b], [c.Pmb])
                    c.Pbf, c.Pbfb = r_B.next()
                    self.cp("act", c.Pbf[:n, :, 0:n], c.Pm[:n, :, 0:n], [c.Pmb], [c.Pbfb])
                    c.nsteps = max(1, int(math.ceil(math.log2(n))) - 1)

                def mk_step(s):
                    def s_sq(c):
                        n = c.n
                        if s >= c.nsteps:
                            return
                        c.lastst = s == c.nsteps - 1
                        c.pxY, c.pxYb = self.ps_mm.next()
                        for h in range(8):
                            self.mm(c.pxY[:n, h * 64:h * 64 + n], c.X[:n, h, 0:n], c.Y[:n, h, 0:n], True, True, [c.Xb, c.Yb], [c.pxYb], skip=True)
                        if not c.lastst:
                            c.pxX, c.pxXb = self.ps_mm.next()
                            for h in range(8):
                                self.mm(c.pxX[:n, h * 64:h * 64 + n], c.Y[:n, h, 0:n], c.X[:n, h, 0:n], True, True, [c.Xb, c.Yb], [c.pxXb], skip=True)
                        c.Y2, c.Y2b = r_B.next()
                        self.cp("act", c.Y2[:n, :, 0:n], v3(c, c.pxY), [c.pxYb], [c.Y2b])
                        if not c.lastst:
                            c.X2, c.X2b = r_B.next()
                            self.cp("dve", c.X2[:n, :, 0:n], v3(c, c.pxX), [c.pxXb], [c.X2b])

                    def s_acc(c):
                        n = c.n
                        if s >= c.nsteps:
                            return
                        pp, ppb = self.ps_mm.next()
                        for h in range(8):
                            self.mm(pp[:n, h * 64:h * 64 + n], c.Y2[:n, h, 0:n], c.Pbf[:n, h, 0:n], True, True, [c.Y2b, c.Pbfb], [ppb], skip=True)
                        self.tt("dve", c.Pm[:n, :, 0:n], c.Pm[:n, :, 0:n], v3(c, pp), ALU.add, [c.Pmb, ppb], [c.Pmb])
                        c.Pbf, c.Pbfb = r_L.next() if c.lastst else r_B.next()
                        self.cp("act", c.Pbf[:n, :, 0:n], c.Pm[:n, :, 0:n], [c.Pmb], [c.Pbfb])
                        c.Y, c.Yb = c.Y2, c.Y2b
                        if not c.lastst:
                            c.X, c.Xb = c.X2, c.X2b
                    return [s_sq, s_acc]

                def s_tok(c):
                    n, off = c.n, c.off
                    self.tt("dve", c.sc[:n, 0:8], c.sc[:n, 16:24], c.Bp, ALU.mult, [c.scb, gbkb], [c.scb])
                    self.tt("dve", c.sc[:n, 24:32], c.Gp, gb[:n, :, off + n - 1], ALU.subtract, [gbkb, gbb], [c.scb])
                    self.act(c.sc[:n, 8:16], c.sc[:n, 24:32], AF.Exp, [c.scb], [c.scb], scale=-1.0)
                    tpK, tpKb = self.ps_tp.next()
                    tpV, tpVb = self.ps_tp.next()
                    tKv = tpK[:, :].bitcast(BF16)
                    tVv = tpV[:, :].bitcast(BF16)
                    for h in range(8):
                        self.tr(tKv[:n, h * 128:(h + 1) * 128], c.Kt[:, h, :], self.identb[:, :], [qkvb, self.identb_b], [tpKb])
                    for h in range(8):
                        self.tr(tVv[:n, h * 128:(h + 1) * 128], c.Vt[:, h, :], self.identb[:, :], [qkvb, self.identb_b], [tpVb])
                    c.kbg, c.kbgb = r_B2.next()
                    self.tt("dve", c.kbg[:n, :, :], t3(c, tKv), bcs(c, c.sc[:n, 0:8]), ALU.mult, [tpKb, c.scb], [c.kbgb])
                    c.kdc, c.kdcb = r_B2.next()
                    self.tt("dve", c.kdc[:n, :, :], t3(c, tKv), bcs(c, c.sc[:n, 8:16]), ALU.mult, [tpKb, c.scb], [c.kdcb])
                    c.vb_, c.vbb_ = r_B2.next()
                    self.tt("dve", c.vb_[:n, :, :], t3(c, tVv), bcs(c, c.Bp), ALU.mult, [tpVb, gbkb], [c.vbb_])
                    c.qdT, c.qdTb = r_L.next()
                    self.tt("dve", c.qdT[:, :, 0:n], c.Qt, c.eG[:, :, 0:n], ALU.mult, [qkvb, c.eGb], [c.qdTb])
                    c.paQ, c.paQb = self.ps_mm.next()
                    for h in range(8):
                        self.mm(c.paQ[:n, h * 64:h * 64 + n], c.Kt[:, h, :], c.Qt[:, h, :], True, True, [qkvb], [c.paQb], skip=True)
                    c.QK, c.QKb = r_L.next()
                    self.tt("dve", c.QK[:n, :, 0:n], v3(c, c.paQ), c.D1i[:n, :, 0:n], ALU.mult, [c.paQb, c.D1ib], [c.QKb])

                def s_wd(c):
                    n = c.n
                    pw, pwb = self.ps_mm.next()
                    for h in range(8):
                        self.mm(pw[:, h * 64:h * 64 + n], c.kbg[:n, h, :], c.Pbf[:n, h, 0:n], True, True, [c.kbgb, c.Pbfb], [pwb], skip=True)
                    c.wdT, c.wdTb = r_L.next()
                    self.act(c.wdT[:, :, 0:n], pw[:, :].rearrange("p (h c) -> p h c", h=8)[:, :, 0:n], AF.Copy, [pwb], [c.wdTb],
                             scale=-1.0)

                stages = [s_bcast, s_gram, s_mask]
                for s in range(5):
                    stages += mk_step(s)
                stages += [s_tok, s_wd]
                for f in stages:
                    for c in cks:
                        f(c)

                for c in cks:
                    n, off, t0 = c.n, c.off, c.t0
                    pv = [self.ps_acc.next(), self.ps_acc.next()]
                    for h in range(8):
                        pt_, ptb_ = pv[h // 4]
                        o_ = pt_[:n, (h % 4) * 128:(h % 4 + 1) * 128]
                        self.mm(o_, c.Pbf[:n, h, 0:n], c.vb_[:n, h, :], True, False, [c.Pbfb, c.vbb_], [ptb_], skip=True)
                        self.mm(o_, c.wdT[:, h, 0:n], Sbf[:, h, :], False, True, [c.wdTb, Sbfb], [ptb_], skip=True)
                    vr, vrb = r_B2.next()
                    for hb in range(2):
                        self.cp("act", vr[:n, hb * 4:(hb + 1) * 4, :], pv[hb][0][:n, :].rearrange("p (h c) -> p h c", h=4),
                                [pv[hb][1]], [vrb])
                    self.tt("dve", S[:, :, :], S[:, :, :], c.eG[:, :, n - 1:n].to_broadcast([128, 8, 128]), ALU.mult,
                            [Sb, c.eGb], [Sb])
                    po = [self.ps_acc.next(), self.ps_acc.next()] if False else [self.ps_mm.next(), self.ps_mm.next()]
                    for h in range(8):
                        pt_, ptb_ = po[h // 4]
                        o_ = pt_[:n, (h % 4) * 128:(h % 4 + 1) * 128]
                        self.mm(o_, c.qdT[:, h, 0:n], Sbf[:, h, :], True, False, [c.qdTb, Sbfb], [ptb_], skip=True)
                        self.mm(o_, c.QK[:n, h, 0:n], vr[:n, h, :], False, True, [c.QKb, vrb], [ptb_], skip=True)
                    pu = [self.ps_mm.next(), self.ps_mm.next()]
                    for h in range(8):
                        pt_, ptb_ = pu[h // 4]
                        self.mm(pt_[:, (h % 4) * 128:(h % 4 + 1) * 128], c.kdc[:n, h, :], vr[:n, h, :], True, True,
                                [c.kdcb, vrb], [ptb_], skip=True)
                    for hb in range(2):
                        self.tt("dve", S[:, hb * 4:(hb + 1) * 4, :], S[:, hb * 4:(hb + 1) * 4, :],
                                pu[hb][0][:, :].rearrange("p (h c) -> p h c", h=4), ALU.add, [Sb, pu[hb][1]], [Sb])
                    self.cp("act", Sbf[:, :, :], S[:, :, :], [Sb], [Sbfb])
                    sq, sqb = r_sq.next()
                    for hb in range(2):
                        self.act(sq[:n, hb * 4:(hb + 1) * 4, :], po[hb][0][:n, :].rearrange("p (h c) -> p h c", h=4),
                                 AF.Square, [po[hb][1]], [sqb])
                    sc, scb = c.sc, c.scb
                    P.op("dve", lambda e, sc=sc, sq=sq, n=n: e.tensor_reduce(out=sc[:n, 32:40], in_=sq[:n, :, :], axis=AX.X,
                                                                         op=ALU.add), [sqb], [scb])
                    self.rsqrt(sc[:n, 32:40], sc[:n, 32:40], [scb], [scb], scale=1.0 / 128, eps=NORM_EPS)
                    ot, otb = r_B2.next()
                    for hb in range(2):
                        self.tt("dve", ot[:n, hb * 4:(hb + 1) * 4, :], po[hb][0][:n, :].rearrange("p (h c) -> p h c", h=4),
                                sc[:n, 32 + hb * 4:36 + hb * 4].unsqueeze(2).to_broadcast([n, 4, 128]), ALU.mult,
                                [po[hb][1], scb], [otb])
                    b = self.buf(("gdn_og", g.gi, t0))
                    ogw.append(b)
                    self.dma(dr["OG_" + nm][t0:t0 + n, 0:D], ot[:n, :, :].rearrange("p h c -> p (h c)"), [otb], [b])
            self.dma(dr["gdn_state_" + nm].rearrange("h d v -> d h v"), S[:, :, :], [Sb], [self.buf(("gdn_state", g.gi))])
        if os.environ.get("GDN_STOP") == "B":
            return
        P.barrier(); A.reset()
        r_gz = Ring(self, "rgz", [128, D], F32, 2, arena=A)
        nw, nwb = A.alloc([128, D], F32), Buf()
        for h in range(8):
            self.dma(nw[:, h * 128:(h + 1) * 128], dr["gdn_norm_w"][0:1, :].to_broadcast([128, 128]), [], [nwb])
        items = []
        for g in self.groups:
            def mkprep(g, t0, n):
                def prep():
                    nm = g.name
                    og, ogb = self.r_h.next()
                    self.dma(og[:n, 0:D], dr["OG_" + nm][t0:t0 + n, 0:D], ogws[g.gi], [ogb])
                    zs, zsb = self.r_h.next()
                    self.dma(zs[:n, 0:D], dr["ZS_" + nm][t0:t0 + n, 0:D], wbs[g.gi], [zsb])
                    gz, gzb = r_gz.next()
                    self.tt("dve", gz[:n, :], og[:n, 0:D], nw[:n, :], ALU.mult, [ogb, nwb], [gzb])
                    self.tt("dve", og[:n, 0:D], gz[:n, :], zs[:n, 0:D], ALU.mult, [gzb, zsb], [ogb])
                    return og, ogb
                return prep
            items += [(g, t0, n, mkprep(g, t0, n)) for (t0, n) in g.tiles]
        self.pass_c_run(items, first, last, 8)

    def layer_dummy(self, L, first, last):
        self.layer_mod(L)
        self.load_w(self.W, self.Wb, self.dr["fox_w_in"], 4112)
        self.load_w(self.WO, self.WOb, self.dr["fox_w_out"], D)
        for g in self.groups:
            for blk in g.blocks:
                uT, uTb, nb = self.make_uT(g, blk, first)
                off = 0
                for (t0, n) in blk:
                    og, ogb = self.r_h.next()
                    for cg in range(2):
                        ps, pb = self.proj_tm(uT, uTb, off, n, cg * 512, (cg + 1) * 512)
                        self.cp("act", og[:n, cg * 512:(cg + 1) * 512], ps[:n, :], [pb], [ogb])
                    self.pass_c_tile(g, t0, n, first, last, og, ogb, 8)
                    off += n


def build(cfg, mode="full"):
    k = K(cfg)
    k.setup()
    nl = len(cfg.layers)
    for i, L in enumerate(cfg.layers):
        if mode == "dummy":
            k.layer_dummy(L, i == 0, i == nl - 1)
        else:
            k.layer(L, i == 0, i == nl - 1)
    k.P.emit()
    return k


def t5_onehot():
    import jax
    import jax.numpy as jnp
    with jax.default_device(jax.devices("cpu")[0]):
        rel = jnp.arange(384, dtype=jnp.int32) - 255
        nb = 16
        max_exact = 8
        ret = jnp.where(rel > 0, nb, 0)
        n = jnp.abs(rel)
        large = max_exact + (jnp.log(jnp.maximum(n, 1).astype(jnp.float32) / max_exact)
                             / math.log(128 / max_exact) * (nb - max_exact)).astype(jnp.int32)
        large = jnp.minimum(large, nb - 1)
        bucket = np.asarray(ret + jnp.where(n < max_exact, n, large))
    oh = np.zeros((32, 384), np.float32)
    oh[bucket, np.arange(384)] = 1.0
    return oh


def ret_perm():
    one = np.concatenate([np.arange(0, 256, 2), np.arange(1, 256, 2)])
    return np.concatenate([h * 256 + one for h in range(4)])


_RC = {}


def ret_consts(cfg):
    key = (cfg.T, cfg.TS, cfg.PAST)
    if key in _RC:
        return _RC[key]
    import jax
    import jax.numpy as jnp
    out = {}
    with jax.default_device(jax.devices("cpu")[0]):
        inv = 10000.0 ** (-jnp.arange(0, 256, 2, dtype=jnp.float32) / 256)
        for nm, start, T in (("p", 0, cfg.T), ("s", cfg.PAST, cfg.TS)):
            pos = (start + jnp.arange(T)).astype(jnp.float32)
            ang = pos[:, None] * inv[None, :]
            out["rot_cos_" + nm] = np.ascontiguousarray(np.asarray(jnp.cos(ang)).T)
            out["rot_sin_" + nm] = np.ascontiguousarray(np.asarray(jnp.sin(ang)).T)
        lg = np.asarray(jnp.log1p(-jnp.power(2.0, -5.0 - jnp.arange(4, dtype=jnp.float32))), np.float64)
    i = np.arange(128)
    dm = np.zeros((128, 4, 128), np.float32)
    qd = np.zeros((128, 4, 128), np.float32)
    kd = np.zeros((128, 8), np.float32)
    for h in range(4):
        d = i[None, :] - i[:, None]
        dm[:, h, :] = np.where(d >= 0, np.exp(lg[h] * np.maximum(d, 0)), 0.0)
        qd[:, h, :] = np.exp(lg[h] * (i + 1.0))[None, :]
        kd[:, h] = np.exp(lg[h] * (127.0 - i))
        kd[:, 4 + h] = np.exp(lg[h] * (cfg.TS - 1.0 - i))
    out["ret_dmaskT"], out["ret_qdec"], out["ret_kdec"] = dm, qd, kd
    _RC[key] = out
    return out


def core_inputs(cfg, inp, b):
    T, PAST = cfg.T, cfg.PAST
    f = np.ascontiguousarray
    cT = np.stack([inp["c_prompt"][b], inp["c_sample"][b]], -1).reshape(8, 128, 2).transpose(1, 0, 2)
    m = {
        "xp": f(inp["x_prompt"][b, :T]), "xs": f(inp["x_sample"][b]), "cT": f(cT),
        "ada_w": inp["ada_w"], "ada_b": inp["ada_b"], "ln_g": inp["ln_g"], "ln_b": inp["ln_b"],
        "ident": np.eye(128, dtype=np.float32),
        "cmask": np.where(np.arange(128)[:, None] > np.arange(128)[None, :], -30000.0, 0.0).astype(np.float32),
        "fox_w_in": inp["fox_w_in"], "fox_b_f": f(inp["fox_b_f"].reshape(16, 1)), "fox_w_out": inp["fox_w_out"],
        "cache_fox_k": f(inp["cache_fox_k"][b, :PAST].reshape(PAST, -1)),
        "cache_fox_v": f(inp["cache_fox_v"][b, :PAST].reshape(PAST, -1)),
        "cache_fox_logf": f(inp["cache_fox_logf"][b, :PAST]),
        "diff_w_in": inp["diff_w_in"], "diff_w_out": inp["diff_w_out"], "rel_bias_table": inp["rel_bias_table"],
        "diff_lam_q1": f(inp["diff_lam_q1"].reshape(1, 64)), "diff_lam_k1": f(inp["diff_lam_k1"].reshape(1, 64)),
        "diff_lam_q2": f(inp["diff_lam_q2"].reshape(1, 64)), "diff_lam_k2": f(inp["diff_lam_k2"].reshape(1, 64)),
        "diff_subln_w": f(inp["diff_subln_w"].reshape(1, 128)), "t5_onehot": t5_onehot(),
        "antiident": np.ascontiguousarray(np.eye(128, dtype=np.float32)[::-1]),
        "dmask": np.where((np.arange(128)[:, None] >= 64) & (np.arange(128)[None, :] < 64), -30000.0, 0.0).astype(np.float32),
        "cache_diff_k": f(inp["cache_diff_k"][b, :PAST].reshape(PAST, -1)),
        "cache_diff_v": f(inp["cache_diff_v"][b, :PAST].reshape(PAST, -1)),
    }
    m.update(ret_consts(cfg))
    i = np.arange(128)
    msk = np.zeros((128, 3, 128), np.float32)
    msk[:, 0, :] = np.where(i[None, :] > i[:, None], -1.0, 0.0)
    msk[:, 1, :] = np.where(i[None, :] < i[:, None], -1.0, 0.0)
    msk[:, 2, :] = np.where(i[None, :] >= i[:, None], 1.0, 0.0)
    sel = np.zeros((8, 8, 128), np.float32)
    for h in range(8):
        sel[h, h, :] = 1.0
    m.update({
        "gdn_w_in": inp["gdn_w_in"], "gdn_w_out": inp["gdn_w_out"],
        "gdn_convw": f(inp["gdn_conv_w"].reshape(4, 24, 128).transpose(2, 1, 0)),
        "gdn_conv_in": f(inp["state_gdn_conv"][b].reshape(3, 24, 128).transpose(2, 1, 0)),
        "gdn_a_log": f(inp["gdn_a_log"].reshape(8, 1)), "gdn_dt_bias": f(inp["gdn_dt_bias"].reshape(8, 1)),
        "gdn_norm_w": f(inp["gdn_norm_w"].reshape(1, 128)), "state_gdn": f(inp["state_gdn"][b]),
        "gdn_masks": msk, "gdn_sel": sel,
    })
    perm = ret_perm()
    w = inp["ret_w_in"]
    m["ret_w_in"] = f(np.concatenate([w[:, 0:D][:, perm], w[:, D:2 * D][:, perm], w[:, 2 * D:]], axis=1))
    m["ret_w_out"] = inp["ret_w_out"]
    m["ret_gn_w"] = f(inp["ret_gn_w"].reshape(1, -1))
    m["state_ret"] = f(inp["state_ret"][b][:, perm[:256], :])
    return m


OUT_NAMES = ("y", "gdn_state_", "gdn_conv_", "fox_k_", "fox_v_", "fox_logf_", "diff_k_", "diff_v_", "ret_state_")


def gather(cfg, results):
    inv = np.argsort(ret_perm()[:256])
    outs = []
    for nm, T in (("p", cfg.T), ("s", cfg.TS)):
        g = {}
        for base in OUT_NAMES:
            key = base + nm
            g[base] = np.stack([np.asarray(r[key]) for r in results], 0)
        B = len(results)
        outs.append([
            g["y"],
            g["gdn_state_"],
            g["gdn_conv_"],
            g["fox_k_"].reshape(B, T, 16, 64),
            g["fox_v_"].reshape(B, T, 16, 64),
            g["fox_logf_"],
            g["diff_k_"].reshape(B, T, 8, 2, 64),
            g["diff_v_"].reshape(B, T, 8, 128),
            np.ascontiguousarray(g["ret_state_"][:, :, inv, :]),
        ])
    p, s = outs
    return (p[0], s[0]) + tuple(p[1:]) + tuple(s[1:])


def kernel(**inputs):
    inputs = {k: np.asarray(v) for k, v in inputs.items()}
    cfg = Cfg()
    k = build(cfg)
    in_maps = [core_inputs(cfg, inputs, b) for b in range(NCORES)]
    res = run_bass_kernel_spmd(k.nc, in_maps, core_ids=list(range(NCORES)))
    return gather(cfg, res.results)
```

```python
import math
from contextlib import ExitStack

import numpy as np
import ml_dtypes
import concourse.bass as bass
import concourse.mybir as mybir
from concourse.bass_utils import run_bass_kernel_spmd

F32 = mybir.dt.float32
BF16 = mybir.dt.bfloat16
AF = mybir.ActivationFunctionType
ALU = mybir.AluOpType
AX = mybir.AxisListType

D = 1024
DEPTH = 4
ALPHA = (2.0 * DEPTH) ** 0.25
LN_EPS = 1e-5
NORM_EPS = 1e-6
NCORES = 8


class Buf:
    __slots__ = ("w", "r", "x")

    def __init__(self, x=False):
        self.w = None
        self.r = {}
        self.x = x


class Prog:
    ENGS = ("pe", "act", "dve", "pool", "sp")
    NO_SELF_SYNC = ("pe",)

    def __init__(self, nc, es, ndma=28):
        self.nc = nc
        self.sem = {e: es.enter_context(nc.semaphore("s_" + e)) for e in self.ENGS}
        self.dsem = [es.enter_context(nc.semaphore("d%d" % i)) for i in range(ndma)]
        self.cnt = {e: 0 for e in self.ENGS}
        self.dcnt = [0] * ndma
        self.drr = 0
        self.drr_sw = 0
        self.NSW_BASE = ndma - 8
        self.seen = {e: {} for e in self.ENGS}
        self.q = {e: [] for e in self.ENGS}

    def _collect(self, reads, writes, e=None):
        deps = {}
        for b in reads:
            if b.w is not None and deps.get(b.w[0], 0) < b.w[1]:
                deps[b.w[0]] = b.w[1]
            if b.x:
                for k, v in b.r.items():
                    if k != e and deps.get(k, 0) < v:
                        deps[k] = v
        for b in writes:
            if b.w is not None and deps.get(b.w[0], 0) < b.w[1]:
                deps[b.w[0]] = b.w[1]
            for k, v in b.r.items():
                if deps.get(k, 0) < v:
                    deps[k] = v
        return deps

    def _waits(self, e, deps):
        out = []
        seen = self.seen[e]
        for k, v in deps.items():
            if k == e and e in self.NO_SELF_SYNC:
                continue
            if seen.get(k, 0) < v:
                seen[k] = v
                out.append((k, v))
        return out

    def _mark(self, tok, reads, writes):
        k, v = tok
        for b in reads:
            if b.r.get(k, 0) < v:
                b.r[k] = v
        for b in writes:
            b.w = tok
            b.r = {}

    def op(self, e, fn, reads=(), writes=()):
        waits = self._waits(e, self._collect(reads, writes, e))
        self.cnt[e] += 1
        self.q[e].append((waits, fn, None))
        self._mark((e, self.cnt[e]), reads, writes)

    def dma(self, e, fn, reads=(), writes=()):
        if e == "pool":
            s = self.NSW_BASE + self.drr_sw
            self.drr_sw = (self.drr_sw + 1) % (len(self.dsem) - self.NSW_BASE)
        else:
            s = self.drr
            self.drr = (s + 1) % self.NSW_BASE
        deps = self._collect(reads, writes, e)
        k = ("d", s)
        if self.dcnt[s] and deps.get(k, 0) < self.dcnt[s]:
            deps[k] = self.dcnt[s]
        waits = self._waits(e, deps)
        self.dcnt[s] += 16
        self.q[e].append((waits, fn, s))
        self._mark((k, self.dcnt[s]), reads, writes)

    def barrier(self):
        deps = {e: c for e, c in self.cnt.items() if c}
        for s, c in enumerate(self.dcnt):
            if c:
                deps[("d", s)] = c
        for e in self.ENGS:
            waits = self._waits(e, dict(deps))
            if waits:
                self.q[e].append((waits, None, None))

    def _semof(self, k):
        return self.sem[k] if isinstance(k, str) else self.dsem[k[1]]

    def emit(self):
        nc = self.nc
        with nc.Block() as block:
            regs = dict(sp=block.sync, pe=block.tensor, act=block.scalar, dve=block.vector, pool=block.gpsimd)
            for e in self.ENGS:
                def body(eng, e=e):
                    for waits, fn, ds in self.q[e]:
                        ws = [(self._semof(k), v) for k, v in waits]
                        if fn is None:
                            for sm, v in ws:
                                eng.wait_ge(sm, v)
                            continue
                        if ds is None and ws:
                            for sm, v in ws[:-1]:
                                eng.wait_ge(sm, v)
                            ins = fn(eng)
                            ins._wait_ge(ws[-1][0], ws[-1][1])
                        else:
                            for sm, v in ws:
                                eng.wait_ge(sm, v)
                            ins = fn(eng)
                        if ds is None:
                            ins.then_inc(self.sem[e], 1)
                        else:
                            ins.then_inc(self.dsem[ds], 16)
                    if e == "sp":
                        for s, c in enumerate(self.dcnt):
                            if c:
                                eng.wait_ge(self.dsem[s], c)
                regs[e](body)


class Ring:
    def __init__(self, K, name, shape, dtype, n, psum=False, arena=None):
        self.t = []
        for i in range(n):
            if arena is not None:
                t = arena.alloc(shape, dtype)
            else:
                f = K.nc.psum_tensor if psum else K.nc.sbuf_tensor
                t = K.es.enter_context(f("%s%d" % (name, i), shape, dtype))
            self.t.append((t, Buf(x=psum)))
        self.i = 0

    def next(self):
        r = self.t[self.i % len(self.t)]
        self.i += 1
        return r


class Arena:
    def __init__(self, K, nbytes, base=None):
        self.t = K.es.enter_context(K.nc.sbuf_tensor("arena", [128, nbytes // 4], F32)) if base is None else base
        self.nbytes = nbytes
        self.off = 0

    def reset(self):
        self.off = 0

    def alloc(self, shape, dtype):
        esz = 2 if dtype == BF16 else 4
        n = 1
        for s in shape[1:]:
            n *= s
        nb = (n * esz + 3) // 4 * 4
        assert self.off + nb <= self.nbytes, "arena overflow: need %d have %d" % (nb, self.nbytes - self.off)
        v = self.t[0:shape[0], self.off // 4:(self.off + nb) // 4]
        self.off += nb
        if dtype == BF16:
            v = v.bitcast(BF16)[:, 0:n]
        if len(shape) == 3:
            v = v.rearrange("p (a b) -> p a b", a=shape[1])
        elif len(shape) == 4:
            v = v.rearrange("p (a b c) -> p a b c", a=shape[1], b=shape[2])
        return v


class Cfg:
    def __init__(self, T=4096, TS=32, PAST=2048, layers=(0, 1, 2, 3)):
        self.T, self.TS, self.PAST, self.layers = T, TS, PAST, tuple(layers)


class Group:
    def __init__(self, gi, name, T, past):
        self.gi, self.name, self.T, self.past = gi, name, T, past
        self.S = past + T
        self.tiles = [(t0, min(128, T - t0)) for t0 in range(0, T, 128)]
        self.blocks = [self.tiles[i:i + 4] for i in range(0, len(self.tiles), 4)]
        self.ktiles = [(k0, 128) for k0 in range(0, past, 128)] + [(past + t0, n) for t0, n in self.tiles]


class K:
    def __init__(self, cfg):
        self.cfg = cfg
        self.nc = nc = bass.Bass("TRN2", target_bir_lowering=False)
        self.es = ExitStack()
        self.P = Prog(nc, self.es)
        self.bufs = {}
        self.dr = {}
        self.dh = {}
        self.all_groups = [Group(0, "p", cfg.T, 0), Group(1, "s", cfg.TS, cfg.PAST)]
        import os
        gs = os.environ.get("KGROUPS", "01")
        self.groups = [g for g in self.all_groups if str(g.gi) in gs]

    def buf(self, key):
        b = self.bufs.get(key)
        if b is None:
            b = self.bufs[key] = Buf()
        return b

    def din(self, name, shape, dtype=F32):
        t = self.nc.dram_tensor(name, list(shape), dtype, kind="ExternalInput")
        self.dh[name] = t
        self.dr[name] = t.ap()
        return self.dr[name]

    def dout(self, name, shape, dtype=F32):
        t = self.nc.dram_tensor(name, list(shape), dtype, kind="ExternalOutput")
        self.dr[name] = t.ap()
        return self.dr[name]

    def dscr(self, name, shape, dtype):
        t = self.nc.dram_tensor(name, list(shape), dtype)
        self.dh[name] = t
        self.dr[name] = t.ap()
        return self.dr[name]

    def sb(self, name, shape, dtype):
        t = self.es.enter_context(self.nc.sbuf_tensor(name, list(shape), dtype))
        return t, Buf()

    def dma(self, out, in_, rd, wr, eng="sp"):
        self.P.dma(eng, lambda e: e.dma_start(out=out, in_=in_), rd, wr)

    def mm(self, out, lhsT, rhs, start, stop, rd, wr, skip=False):
        self.P.op("pe", lambda e: e.matmul(out, lhsT, rhs, start=start, stop=stop, skip_group_check=skip), rd, wr)

    def tr(self, out, in_, ident, rd, wr):
        self.P.op("pe", lambda e: e.transpose(out, in_, ident), rd, wr)

    def act(self, out, in_, func, rd, wr, bias=0.0, scale=1.0, accum=None):
        if accum is None:
            self.P.op("act", lambda e: e.activation(out=out, in_=in_, func=func, bias=bias, scale=scale), rd, wr)
        else:
            self.P.op("act", lambda e: e.activation(out=out, in_=in_, func=func, bias=bias, scale=scale,
                                                     accum_out=accum), rd, wr)

    def tt(self, eng, out, in0, in1, op, rd, wr):
        o = self.nc.vector if eng == "dve" else self.nc.gpsimd
        self.P.op(eng, lambda e: e.tensor_tensor(out=out, in0=in0, in1=in1, op=op), rd, wr)

    def ts(self, eng, out, in0, s1, s2, op0, op1, rd, wr, accum=None):
        if op1 is None:
            self.P.op(eng, lambda e: e.tensor_scalar(out=out, in0=in0, scalar1=s1, scalar2=None, op0=op0), rd, wr)
        elif accum is None:
            self.P.op(eng, lambda e: e.tensor_scalar(out=out, in0=in0, scalar1=s1, scalar2=s2, op0=op0, op1=op1),
                      rd, wr)
        else:
            self.P.op(eng, lambda e: e.tensor_scalar(out=out, in0=in0, scalar1=s1, scalar2=s2, op0=op0, op1=op1,
                                                     accum_out=accum), rd, wr)

    def stt(self, out, in0, scalar, in1, op0, op1, rd, wr):
        self.P.op("dve", lambda e: e.scalar_tensor_tensor(out=out, in0=in0, scalar=scalar, in1=in1, op0=op0, op1=op1),
                  rd, wr)

    def cp(self, eng, out, in_, rd, wr):
        if eng == "act":
            self.P.op("act", lambda e: e.copy(out=out, in_=in_), rd, wr)
        else:
            self.P.op(eng, lambda e: e.tensor_copy(out=out, in_=in_), rd, wr)

    def rsqrt(self, out, in_, rd, wr, scale=1.0, eps=0.0):
        self.act(out, in_, AF.Ln, rd, wr, bias=eps, scale=scale)
        self.act(out, out, AF.Exp, wr, wr, scale=-0.5)

    def recip(self, out, in_, rd, wr):
        self.P.op("dve", lambda e: e.reciprocal(out=out, in_=in_), rd, wr)

    def memset(self, eng, ap, val, wr):
        self.P.op(eng, lambda e: e.memset(ap, val), (), wr)

    def set_psum(self, n_mm, n_acc, n_tp):
        def mk(lst):
            r = Ring.__new__(Ring)
            r.t = lst
            r.i = 0
            return r
        assert n_mm + n_acc + n_tp == 8
        self.ps_mm = mk(self.psb[0:n_mm])
        self.ps_acc = mk(self.psb[n_mm:n_mm + n_acc])
        self.ps_tp = mk(self.psb[n_mm + n_acc:8])

    def setup(self):
        cfg = self.cfg
        T, TS, PAST = cfg.T, cfg.TS, cfg.PAST
        din, dout, dscr = self.din, self.dout, self.dscr
        din("xp", [T, D]); din("xs", [TS, D]); din("cT", [128, 8, 2])
        din("ada_w", [4, D, 3 * D]); din("ada_b", [4, 3 * D]); din("ln_g", [4, D]); din("ln_b", [4, D])
        din("ident", [128, 128]); din("cmask", [128, 128])
        din("fox_w_in", [D, 4112]); din("fox_b_f", [16, 1]); din("fox_w_out", [D, D])
        din("cache_fox_k", [PAST, D]); din("cache_fox_v", [PAST, D]); din("cache_fox_logf", [PAST, 16])
        din("diff_w_in", [D, 4096]); din("diff_w_out", [D, D]); din("rel_bias_table", [32, 8])
        for nmv in ("diff_lam_q1", "diff_lam_k1", "diff_lam_q2", "diff_lam_k2"):
            din(nmv, [1, 64])
        din("diff_subln_w", [1, 128]); din("t5_onehot", [32, 384]); din("antiident", [128, 128]); din("dmask", [128, 128])
        din("cache_diff_k", [PAST, D]); din("cache_diff_v", [PAST, D])
        dscr("lam_scr", [2, 8], F32); dscr("fr_scr", [16, 384], F32)
        din("ret_w_in", [D, 6144]); din("ret_w_out", [2 * D, D]); din("ret_gn_w", [1, 2 * D])
        din("state_ret", [4, 256, 512]); din("ret_dmaskT", [128, 4, 128]); din("ret_qdec", [128, 4, 128])
        din("ret_kdec", [128, 8])
        for g in self.all_groups:
            din("rot_cos_" + g.name, [128, g.T]); din("rot_sin_" + g.name, [128, g.T])
            dscr("QR_" + g.name, [8, 128, g.T], BF16); dscr("KR_" + g.name, [8, 128, g.T], BF16)
            dscr("OG_" + g.name + "v", [g.T, 2 * D], BF16)
            dout("ret_state_" + g.name, [4, 256, 512])
        din("gdn_w_in", [D, 4112]); din("gdn_w_out", [D, D]); din("gdn_convw", [128, 24, 4]); din("gdn_conv_in", [128, 24, 3])
        din("gdn_a_log", [8, 1]); din("gdn_dt_bias", [8, 1]); din("gdn_norm_w", [1, 128]); din("state_gdn", [8, 128, 128])
        din("gdn_masks", [128, 3, 128]); din("gdn_sel", [8, 8, 128])
        for g in self.all_groups:
            for nmv in ("QG_", "KG_", "VG_"):
                dscr(nmv + g.name, [8, 128, g.T], BF16)
            dscr("BTb_" + g.name, [8, g.T], BF16); dscr("GT_" + g.name, [8, g.T], F32); dscr("GB_" + g.name, [g.T, 16], F32)
            dout("gdn_state_" + g.name, [8, 128, 128]); dout("gdn_conv_" + g.name, [3, 3 * D])
        dout("yp", [T, D]); dout("ys", [TS, D])
        for g in self.all_groups:
            n = g.name
            dout("fox_k_" + n, [g.T, D]); dout("fox_v_" + n, [g.T, D]); dout("fox_logf_" + n, [g.T, 16])
            dout("diff_k_" + n, [g.T, D]); dout("diff_v_" + n, [g.T, D])
            dscr("OGF_" + n, [g.T, 2 * D], F32)
        dscr("xres_p", [T, D], F32); dscr("xres_s", [TS, D], F32)
        for g in self.all_groups:
            n = g.name
            dscr("QT_" + n, [16, 70, g.T], BF16); dscr("KT_" + n, [16, 70, g.S], BF16)
            dscr("VA_" + n, [g.S, 1040], BF16)
            dscr("ZS_" + n, [g.T, 2 * D], BF16); dscr("OG_" + n, [g.T, 2 * D], BF16)

        self.W, self.Wb = self.sb("W_in", [128, 8, 4112], BF16)
        self.WO, self.WOb = self.sb("W_out", [128, 8, D], BF16)
        self.identf, self.identf_b = self.sb("identf", [128, 128], F32)
        self.identb, self.identb_b = self.sb("identb", [128, 128], BF16)
        self.cmaskb, self.cmaskb_b = self.sb("cmaskb", [128, 128], BF16)
        self.cTs, self.cTs_b = self.sb("cTs", [128, 8, 2], F32)
        self.crep = [self.sb("crep%d" % g, [128, 8, 128], BF16) for g in range(2)]
        self.mod = [self.sb("mod_%d" % k, [128, D], F32) for k in range(3)]
        dscr("mods", [3, D], F32)
        self.lng, self.lng_b = self.sb("lng", [128, D], F32)
        self.lnb, self.lnb_b = self.sb("lnb", [128, D], F32)
        self.psb = Ring(self, "psb", [128, 512], F32, 8, psum=True).t
        self.set_psum(4, 2, 2)
        self.r_x = Ring(self, "rx", [128, D], F32, 2)
        self.r_f = Ring(self, "rf", [128, D], F32, 2)
        self.r_ub = Ring(self, "rub", [128, D], BF16, 2)
        self.r_uT = Ring(self, "ruT", [128, 8, 512], BF16, 2)
        self.r_h = Ring(self, "rh", [128, 1040], BF16, 3)
        self.r_small = Ring(self, "rsm", [128, 16], F32, 8)
        self.arena = Arena(self, 58 * 1024)
        wbytes = 8 * 4112 * 2
        self.arenaW = Arena(self, wbytes, base=self.W[:, :, :].rearrange("p a b -> p (a b)").bitcast(F32))

        self.dma(self.identf[:], self.dr["ident"][:, :], [], [self.identf_b])
        self.cp("dve", self.identb[:], self.identf[:], [self.identf_b], [self.identb_b])
        t, b = self.r_f.next()
        self.dma(t[:, 0:128], self.dr["cmask"][:, :], [], [b])
        self.cp("dve", self.cmaskb[:], t[:, 0:128], [b], [self.cmaskb_b])
        self.dma(self.cTs[:], self.dr["cT"][:, :, :], [], [self.cTs_b])
        self.act(self.cTs[:], self.cTs[:], AF.Silu, [self.cTs_b], [self.cTs_b])
        for g in range(2):
            t, b = self.crep[g]
            self.cp("dve", t[:], self.cTs[:, :, g:g + 1].to_broadcast([128, 8, 128]), [self.cTs_b], [b])

    def load_w(self, dst, dstb, src, ncols, nk=8, c_off=0):
        for k in range(nk):
            self.dma(dst[:, k, 0:ncols], src[k * 128:(k + 1) * 128, c_off:c_off + ncols], [], [dstb], eng="pool")

    def layer_mod(self, L):
        ada_w, ada_b = self.dr["ada_w"], self.dr["ada_b"]
        for cg in range(6):
            wt, wb = self.r_uT.next()
            self.load_w(wt, wb, ada_w[L], 512, c_off=cg * 512)
            bt_, bb = self.r_f.next()
            bt = bt_[:, 0:512]
            self.dma(bt[:, :], ada_b[L:L + 1, cg * 512:(cg + 1) * 512].to_broadcast([128, 512]), [], [bb])
            kind = cg // 2
            for g in range(2):
                ps, pb = self.ps_mm.next()
                ct, cb = self.crep[g]
                for k in range(8):
                    self.mm(ps[:, :], ct[:, k, :], wt[:, k, :], k == 0, k == 7, [cb, wb], [pb])
                if g == 0:
                    mt, mb = self.mod[kind]
                    dst = mt[:, (cg % 2) * 512:(cg % 2 + 1) * 512]
                else:
                    mt, mb = self.r_f.next()
                    dst = mt[:, 0:512]
                if kind == 0:
                    self.tt("dve", dst, ps[:, :], bt[:, :], ALU.add, [pb, bb], [mb])
                else:
                    self.stt(dst, ps[:, :], 1.0, bt[:, :], ALU.add, ALU.add, [pb, bb], [mb])
                if g == 1:
                    self.dma(self.dr["mods"][kind:kind + 1, (cg % 2) * 512:(cg % 2 + 1) * 512], mt[0:1, 0:512], [mb],
                             [self.buf(("mods", kind, cg % 2))])
        self.dma(self.lng[:], self.dr["ln_g"][L:L + 1, :].to_broadcast([128, D]), [], [self.lng_b])
        self.dma(self.lnb[:], self.dr["ln_b"][L:L + 1, :].to_broadcast([128, D]), [], [self.lnb_b])

    def modtile(self, g, kind, n):
        if g.gi == 0:
            return self.mod[kind]
        t, b = self.r_f.next()
        self.dma(t[:n, :], self.dr["mods"][kind:kind + 1, :].to_broadcast([n, D]),
                 [self.buf(("mods", kind, 0)), self.buf(("mods", kind, 1))], [b])
        return t, b

    def xsrc(self, g, first):
        if first:
            return self.dr["xp" if g.gi == 0 else "xs"], "xin%d" % g.gi
        return self.dr["xres_" + g.name], "xres%d" % g.gi

    def make_uT(self, g, blk, first):
        src, skey = self.xsrc(g, first)
        uT, uTb = self.r_uT.next()
        off = 0
        for (t0, n) in blk:
            sc, scb = self.modtile(g, 1, n)
            xt, xb = self.r_x.next()
            self.dma(xt[:n, :], src[t0:t0 + n, :], [self.buf((skey, t0))], [xb])
            ft, fb = self.r_f.next()
            self.tt("dve", ft[:n, :], xt[:n, :], sc[:n, :], ALU.mult, [xb, scb], [fb])
            sh, shb = self.modtile(g, 0, n)
            ub, ubb = self.r_ub.next()
            self.tt("dve", ub[:n, :], ft[:n, :], sh[:n, :], ALU.add, [fb, shb], [ubb])
            tp, tpb = self.ps_tp.next()
            tpv = tp[:, :].bitcast(BF16)
            for j in range(8):
                self.tr(tpv[:, j * 128:j * 128 + n], ub[:n, j * 128:(j + 1) * 128], self.identb[:n, :n],
                        [ubb, self.identb_b], [tpb])
            self.cp("act", uT[:, :, off:off + n],
                    tpv.rearrange("p (j t) -> p j t", j=8)[:, :, 0:n], [tpb], [uTb])
            off += n
        return uT, uTb, off

    def proj_tm(self, uT, uTb, off, n, c0, c1):
        ps, pb = self.ps_mm.next()
        for k in range(8):
            self.mm(ps[:n, 0:c1 - c0], uT[:, k, off:off + n], self.W[:, k, c0:c1], k == 0, k == 7,
                    [uTb, self.Wb], [pb])
        return ps, pb

    def proj_fm(self, uT, uTb, nb, c0, c1):
        ps, pb = self.ps_mm.next()
        for k in range(8):
            self.mm(ps[:c1 - c0, 0:nb], self.W[:, k, c0:c1], uT[:, k, 0:nb], k == 0, k == 7, [uTb, self.Wb], [pb])
        return ps, pb

    def pass_c_run(self, items, first, last, nk, wo=None):
        from types import SimpleNamespace as NS
        ctxs = [NS(g=g, t0=t0, n=n, prep=prep) for (g, t0, n, prep) in items]

        def stA(c):
            n = c.n
            ogz, ogzb = c.prep()
            c.oT, c.oTb = self.r_uT.next()
            for half in range((nk + 7) // 8):
                tp, tpb = self.ps_tp.next()
                tpv = tp[:, :].bitcast(BF16)
                for j in range(8):
                    jj = half * 8 + j
                    self.tr(tpv[:, j * 128:j * 128 + n], ogz[:n, jj * 128:(jj + 1) * 128], self.identb[:n, :n],
                            [ogzb, self.identb_b], [tpb])
                self.cp("act", c.oT[:, :, half * 128:half * 128 + n],
                        tpv.rearrange("p (j t) -> p j t", j=8)[:, :, 0:n], [tpb], [c.oTb])

        def stB(c):
            g, t0, n = c.g, c.t0, c.n
            src, skey = self.xsrc(g, first)
            hps = []
            for cg in range(2):
                ps, pb = self.ps_acc.next()
                for k in range(nk):
                    wt_, wtb_ = (self.WO[:, k, :], self.WOb) if wo is None else wo(k)
                    self.mm(ps[:n, :], c.oT[:, k % 8, (k // 8) * 128:(k // 8) * 128 + n],
                            wt_[:, cg * 512:(cg + 1) * 512], k == 0, k == nk - 1, [c.oTb, wtb_], [pb])
                hps.append((ps, pb))
            xt, xb = self.r_x.next()
            self.dma(xt[:n, :], src[t0:t0 + n, :], [self.buf((skey, t0))], [xb])
            gt, gb = self.modtile(g, 2, n)
            c.rt, c.rb = self.r_f.next()
            rt, rb = c.rt, c.rb
            for cg in range(2):
                ps, pb = hps[cg]
                sl = slice(cg * 512, (cg + 1) * 512)
                self.tt("dve", rt[:n, sl], ps[:n, :], gt[:n, sl], ALU.mult, [pb, gb], [rb])
            self.stt(rt[:n, :], xt[:n, :], ALPHA, rt[:n, :], ALU.mult, ALU.add, [xb, rb], [rb])
            st, stb = self.r_small.next()
            for cg in range(2):
                self.P.op("dve", lambda e, cg=cg: e.bn_stats(out=st[:n, cg * 6:cg * 6 + 6],
                                                              in_=rt[:n, cg * 512:(cg + 1) * 512]), [rb], [stb])
            c.mv, c.mvb = self.r_small.next()
            mv, mvb = c.mv, c.mvb
            self.P.op("dve", lambda e: e.bn_aggr(out=mv[:n, 0:2], in_=st[:n, 0:12]), [stb], [mvb])
            self.rsqrt(mv[:n, 4:5], mv[:n, 1:2], [mvb], [mvb], eps=LN_EPS)
            self.stt(mv[:n, 5:6], mv[:n, 0:1], -1.0, mv[:n, 4:5], ALU.mult, ALU.mult, [mvb], [mvb])

        def stC(c):
            g, t0, n = c.g, c.t0, c.n
            dst, dkey = (self.dr["yp" if g.gi == 0 else "ys"], "y%d" % g.gi) if last else \
                (self.dr["xres_" + g.name], "xres%d" % g.gi)
            yt, yb = self.r_x.next()
            self.act(yt[:n, :], c.rt[:n, :], AF.Identity, [c.rb, c.mvb], [yb], bias=c.mv[:n, 5:6], scale=c.mv[:n, 4:5])
            self.tt("dve", yt[:n, :], yt[:n, :], self.lng[:n, :], ALU.mult, [yb, self.lng_b], [yb])
            self.tt("dve", yt[:n, :], yt[:n, :], self.lnb[:n, :], ALU.add, [yb, self.lnb_b], [yb])
            self.dma(dst[t0:t0 + n, :], yt[:n, :], [yb], [self.buf((dkey, t0))])

        stages = [stA, stB, stC]
        N, S_ = len(ctxs), len(stages)
        for slot in range(N + S_ - 1):
            for s in range(S_ - 1, -1, -1):
                i = slot - s
                if 0 <= i < N:
                    stages[s](ctxs[i])

    def pass_c_tile(self, g, t0, n, first, last, ogz, ogzb, nk, wo=None):
        self.pass_c_run([(g, t0, n, lambda: (ogz, ogzb))], first, last, nk, wo)

    def layer(self, L, first, last):
        self.layer_mod(L)
        getattr(self, ("layer_gdn", "layer_fox", "layer_diff", "layer_ret")[L % 4])(L, first, last)

    def attn_rings(self, w, lf):
        self.P.barrier()
        self.attn_skew = 2
        if lf:
            self.set_psum(4, 2, 2)
        else:
            self.set_psum(3, 4, 1)
        A = self.arena
        A.reset()
        smax = max(g.S for g in self.groups)
        self.r_KT = Ring(self, "rKT", [70, smax], BF16, 2, arena=A)
        nkt = max(len(g.ktiles) for g in self.groups)
        self.r_VA = Ring(self, "rVA", [128, nkt, w], BF16, 2, arena=A)
        self.r_QT = Ring(self, "rQT", [70, 512], BF16, 2, arena=A)
        self.r_PT = Ring(self, "rPT", [128, 512], BF16, 4 if lf else 3, arena=A)
        if lf:
            self.r_og = Ring(self, "rog", [128, 4, 128], BF16, 2, arena=A)
        self.r_vb = self.r_h
        self.r_st = Ring(self, "rst", [128, 2048], BF16, 1, arena=A)
        if lf:
            self.r_lf = Ring(self, "rlf", [16, 512], F32, 4, arena=A)
            self.r_pt = Ring(self, "rpt", [16, 3, 512], BF16, 2, arena=A)
            self.ones3, self.ones3_b = A.alloc([16, 3, 512], BF16), Buf()
            self.memset("pool", self.ones3[:, :, :], 1.0, [self.ones3_b])
            self.carry, self.carry_b = A.alloc([16, 1], F32), Buf()
            self.fbf, self.fbf_b = A.alloc([16, 1], F32), Buf()

    def attn_head(self, g, KTd, QTd, VAd, vcol0, dv, kc, rd_bufs, extras, finish):
        KT, KTb = self.r_KT.next()
        self.dma(KT[:kc, 0:g.S], KTd[:, :], rd_bufs, [KTb])
        VA, VAb = self.r_VA.next()
        nfull = sum(1 for (k0, nk) in g.ktiles if nk == 128)
        for k4 in range(0, nfull, 4):
            k5 = min(nfull, k4 + 4)
            self.dma(VA[:, k4:k5, 0:dv + 1],
                     VAd[k4 * 128:k5 * 128, vcol0:vcol0 + dv + 1].rearrange("(kt p) c -> p kt c", p=128), rd_bufs, [VAb])
        for i, (k0, nk) in enumerate(g.ktiles):
            if nk != 128:
                self.dma(VA[:nk, i, 0:dv + 1], VAd[k0:k0 + nk, vcol0:vcol0 + dv + 1], rd_bufs, [VAb])
        w = dv + 1
        per_bank = 512 // w
        for bi, blk in enumerate(g.blocks):
            nq = sum(n for _, n in blk)
            q0b = blk[0][0]
            QT, QTb = self.r_QT.next()
            self.dma(QT[:kc, 0:nq], QTd[:, q0b:q0b + nq], rd_bufs, [QTb])
            nbank = (len(blk) + per_bank - 1) // per_bank
            accs = [self.ps_acc.next() for _ in range(nbank)]
            started = [False] * nbank
            coff = []
            o = 0
            for (_, n) in blk:
                coff.append(o)
                o += n
            pend = []

            def emit_pv(item):
                ki, k0, nk, vis, PT, PTb = item
                for qi in vis:
                    q0, n = blk[qi]
                    bk = qi // per_bank
                    acc, accb = accs[bk]
                    col = (qi % per_bank) * w
                    last = (k0 == g.past + q0)
                    self.mm(acc[:n, col:col + w], PT[:nk, coff[qi]:coff[qi] + n], VA[:nk, ki, 0:w],
                            not started[bk], last, [PTb, VAb], [accb], skip=True)
                    started[bk] = True
            for ki, (k0, nk) in enumerate(g.ktiles):
                vis = [qi for qi, (q0, n) in enumerate(blk) if g.past + q0 >= k0]
                if not vis or k0 > g.past + blk[-1][0]:
                    continue
                fi = vis[0]
                c0 = coff[fi]
                ST, STb = self.ps_mm.next()
                ex = []
                for qi in vis:
                    q0, n = blk[qi]
                    for (l, r, bl) in extras(ki, k0, nk, qi, g.past + q0, n):
                        ex.append((qi, n, l, r, bl))
                self.mm(ST[:nk, c0:nq], KT[:kc, k0:k0 + nk], QT[:kc, c0:nq], True, not ex, [KTb, QTb], [STb])
                for j, (qi, n, l, r, bl) in enumerate(ex):
                    self.mm(ST[:nk, coff[qi]:coff[qi] + n], l, r, False, j == len(ex) - 1, bl, [STb], skip=True)
                PT, PTb = self.r_PT.next()
                self.act(PT[:nk, c0:nq], ST[:nk, c0:nq], AF.Exp, [STb], [PTb])
                pend.append((ki, k0, nk, vis, PT, PTb))
                if len(pend) > self.attn_skew:
                    emit_pv(pend.pop(0))
            while pend:
                emit_pv(pend.pop(0))
            finish(bi, blk, accs, per_bank, w)

    def layer_fox(self, L, first, last):
        dr = self.dr
        self.attn_rings(65, True)
        self.load_w(self.W, self.Wb, dr["fox_w_in"], 4112)
        self.load_w(self.WO, self.WOb, dr["fox_w_out"], D)
        bf, bfb = self.fbf, self.fbf_b
        self.dma(bf[:, :], dr["fox_b_f"][:, :], [], [bfb])
        self.ts("dve", bf[:, :], bf[:, :], -1.0, None, ALU.mult, None, [bfb], [bfb])
        for g in self.groups:
            nm = g.name
            KTd, QTd, VAd, ZSd, OGd = dr["KT_" + nm], dr["QT_" + nm], dr["VA_" + nm], dr["ZS_" + nm], dr["OG_" + nm]
            KTv = KTd.rearrange("(hp two) r s -> two r hp s", two=2)
            QTv = QTd.rearrange("(hp two) r s -> two r hp s", two=2)
            wb = []

            def nb_(key):
                b = self.buf(("fox", g.gi, L) + key)
                wb.append(b)
                return b
            self.memset("pool", self.carry[:, :], 0.0, [self.carry_b])

            def cum_block(lf, lfb, nb, kpos, qpos, tag):
                cum, cumb = self.r_lf.next()
                self.P.op("dve", lambda e: e.tensor_tensor_scan(out=cum[:, 0:nb], data0=self.ones3[:, 0, 0:nb],
                                                                data1=lf[:, 0:nb], initial=self.carry[:, 0:1],
                                                                op0=ALU.mult, op1=ALU.add),
                          [lfb, self.ones3_b, self.carry_b], [cumb])
                self.cp("act", self.carry[:, 0:1], cum[:, nb - 1:nb], [cumb], [self.carry_b])
                pt, ptb = self.r_pt.next()
                t32, t32b = self.r_lf.next()
                r1, r1b = self.r_lf.next()
                self.cp("dve", pt[:, 0, 0:nb], cum[:, 0:nb], [cumb], [ptb])
                self.cp("dve", t32[:, 0:nb], pt[:, 0, 0:nb], [ptb], [t32b])
                self.tt("dve", r1[:, 0:nb], cum[:, 0:nb], t32[:, 0:nb], ALU.subtract, [cumb, t32b], [r1b])
                self.cp("dve", pt[:, 1, 0:nb], r1[:, 0:nb], [r1b], [ptb])
                self.cp("dve", t32[:, 0:nb], pt[:, 1, 0:nb], [ptb], [t32b])
                self.tt("dve", r1[:, 0:nb], r1[:, 0:nb], t32[:, 0:nb], ALU.subtract, [r1b, t32b], [r1b])
                self.cp("dve", pt[:, 2, 0:nb], r1[:, 0:nb], [r1b], [ptb])
                npt, nptb = self.r_pt.next()
                self.ts("dve", npt[:, :, 0:nb], pt[:, :, 0:nb], -1.0, None, ALU.mult, None, [ptb], [nptb])
                self.dma(KTd[:, 64:67, kpos:kpos + nb], npt[:, :, 0:nb], [nptb], [nb_((tag, "kc"))])
                self.dma(KTd[:, 67:70, kpos:kpos + nb], self.ones3[:, :, 0:nb], [self.ones3_b], [nb_((tag, "k1"))])
                if qpos is not None:
                    self.dma(QTd[:, 64:67, qpos:qpos + nb], self.ones3[:, :, 0:nb], [self.ones3_b],
                             [nb_((tag, "q1"))])
                    self.dma(QTd[:, 67:70, qpos:qpos + nb], pt[:, :, 0:nb], [ptb], [nb_((tag, "qc"))])


            ptiles = [(k0, nk) for (k0, nk) in g.ktiles if k0 < g.past]
            for b0 in range(0, len(ptiles), 2):
                pblk = ptiles[b0:b0 + 2]
                lf, lfb = self.r_lf.next()
                st_, stb = self.r_st.next()
                st = st_[:, :].rearrange("p (c t) -> p c t", c=8)
                for j, (k0, nk) in enumerate(pblk):
                    xt, xb = self.r_x.next()
                    self.dma(xt[:, :], dr["cache_fox_k"][k0:k0 + 128, :], [], [xb])
                    ub, ubb = self.r_ub.next()
                    self.cp("dve", ub[:, :], xt[:, :], [xb], [ubb])
                    tp, tpb = self.ps_tp.next()
                    tpv = tp[:, :].bitcast(BF16)
                    for c in range(8):
                        self.tr(tpv[:, c * 128:(c + 1) * 128], ub[:, c * 128:(c + 1) * 128], self.identb[:, :],
                                [ubb, self.identb_b], [tpb])
                    self.cp("act", st[:, :, j * 128:(j + 1) * 128], tpv.rearrange("p (c t) -> p c t", c=8), [tpb], [stb])
                    vt, vtb = self.r_x.next()
                    self.dma(vt[:, :], dr["cache_fox_v"][k0:k0 + 128, :], [], [vtb])
                    vb, vbb = self.r_vb.next()
                    self.memset("pool", vb[:, :].rearrange("p (h c) -> p h c", h=16)[:, :, 64:65], 1.0, [vbb])
                    self.cp("dve", vb[:, :].rearrange("p (h c) -> p h c", h=16)[:, :, 0:64],
                            vt[:, :].rearrange("p (h c) -> p h c", h=16), [vtb], [vbb])
                    self.dma(VAd[k0:k0 + 128, :], vb[:, :], [vbb], [nb_(("pv", k0))])
                    lt, ltb = self.r_small.next()
                    self.dma(lt[:, 0:16], dr["cache_fox_logf"][k0:k0 + 128, :], [], [ltb])
                    tp2, tp2b = self.ps_tp.next()
                    self.tr(tp2[0:16, 0:128], lt[:, 0:16], self.identf[:, :], [ltb, self.identf_b], [tp2b])
                    self.cp("act", lf[:, j * 128:(j + 1) * 128], tp2[0:16, 0:128], [tp2b], [lfb])
                kp, nbk = pblk[0][0], 128 * len(pblk)
                for two in range(2):
                    self.dma(KTv[two, 0:64, :, kp:kp + nbk], st[two * 64:(two + 1) * 64, :, 0:nbk], [stb],
                             [nb_(("pk", kp, two))])
                cum_block(lf, lfb, nbk, kp, None, ("pc", kp))

            for blk in g.blocks:
                uT, uTb, nb = self.make_uT(g, blk, first)
                t0b = blk[0][0]
                kpos = g.past + t0b
                for which, dst in ((0, QTv), (1, KTv)):
                    pos = t0b if which == 0 else kpos
                    for hf in range(2):
                        st_, stb = self.r_st.next()
                        st = st_[:, :].rearrange("p (c t) -> p c t", c=4)
                        for c4 in range(4):
                            c = hf * 4 + c4
                            ps, pb = self.proj_fm(uT, uTb, nb, which * D + c * 128, which * D + (c + 1) * 128)
                            if which == 0:
                                self.act(st[:, c4, 0:nb], ps[:, 0:nb], AF.Copy, [pb], [stb], scale=0.125)
                            else:
                                self.cp("dve", st[:, c4, 0:nb], ps[:, 0:nb], [pb], [stb])
                        for two in range(2):
                            self.dma(dst[two, 0:64, hf * 4:hf * 4 + 4, pos:pos + nb], st[two * 64:(two + 1) * 64, :, 0:nb],
                                     [stb], [nb_(("qk", which, t0b, two, hf))])
                ps, pb = self.proj_fm(uT, uTb, nb, 4 * D, 4 * D + 16)
                e1, e1b = self.r_lf.next()
                self.act(e1[:, 0:nb], ps[0:16, 0:nb], AF.Exp, [pb, bfb], [e1b], bias=bf[:, 0:1], scale=-1.0)
                lf, lfb = self.r_lf.next()
                self.act(lf[:, 0:nb], e1[:, 0:nb], AF.Ln, [e1b], [lfb], bias=1.0)
                self.ts("dve", lf[:, 0:nb], lf[:, 0:nb], -1.0, None, ALU.mult, None, [lfb], [lfb])
                off = 0
                for (t0, n) in blk:
                    tp2, tp2b = self.ps_tp.next()
                    self.tr(tp2[0:n, 0:16], lf[:, off:off + n], self.identf[0:16, 0:16], [lfb, self.identf_b], [tp2b])
                    lo, lob = self.r_small.next()
                    self.cp("act", lo[:n, 0:16], tp2[0:n, 0:16], [tp2b], [lob])
                    self.dma(dr["fox_logf_" + nm][t0:t0 + n, :], lo[:n, 0:16], [lob], [self.buf(("flo", g.gi, t0))])
                    for which, oname in ((1, "fox_k_"), (2, "fox_v_")):
                        ft, fb = self.r_f.next()
                        for cg in range(2):
                            ps, pb = self.proj_tm(uT, uTb, off, n, which * D + cg * 512, which * D + (cg + 1) * 512)
                            self.cp("act" if cg == 0 else "dve", ft[:n, cg * 512:(cg + 1) * 512], ps[:n, :], [pb], [fb])
                        self.dma(dr[oname + nm][t0:t0 + n, :], ft[:n, :], [fb], [self.buf((oname, g.gi, t0))])
                        if which == 2:
                            vb, vbb = self.r_vb.next()
                            self.memset("pool", vb[:n, :].rearrange("p (h c) -> p h c", h=16)[:, :, 64:65], 1.0, [vbb])
                            self.cp("dve", vb[:n, :].rearrange("p (h c) -> p h c", h=16)[:, :, 0:64],
                                    ft[:n, :].rearrange("p (h c) -> p h c", h=16), [fb], [vbb])
                            self.dma(VAd[g.past + t0:g.past + t0 + n, :], vb[:n, :], [vbb], [nb_(("v", t0))])
                    zt, ztb = self.r_h.next()
                    for cg in range(2):
                        ps, pb = self.proj_tm(uT, uTb, off, n, 3 * D + cg * 512, 3 * D + (cg + 1) * 512)
                        self.act(zt[:n, cg * 512:(cg + 1) * 512], ps[:n, :], AF.Silu, [pb], [ztb])
                    self.dma(ZSd[t0:t0 + n, 0:D], zt[:n, 0:D], [ztb], [nb_(("z", t0))])
                    off += n
                cum_block(lf, lfb, nb, kpos, t0b, ("c", t0b))

            ogw = []

            def extras(ki, k0, nk, qi, q0a, n):
                if k0 == q0a:
                    return [(self.identb[:nk, :nk], self.cmaskb[:nk, :n], [self.identb_b, self.cmaskb_b])]
                return []

            for h in range(16):
                def finish(bi, blk, accs, per_bank, w, h=h):
                    og, ogb = self.r_og.next()
                    for qi, (q0, n) in enumerate(blk):
                        acc, accb = accs[qi // per_bank]
                        col = (qi % per_bank) * w
                        rd, rdb = self.r_small.next()
                        self.recip(rd[:n, 0:1], acc[:n, col + 64:col + 65], [accb], [rdb])
                        self.ts("dve", og[:n, qi, 0:64], acc[:n, col:col + 64], rd[:n, 0:1], None, ALU.mult, None,
                                [accb, rdb], [ogb])
                    q0b = blk[0][0]
                    kb = self.buf(("fox_og", g.gi, L, h, bi))
                    ogw.append(kb)
                    if len(blk) > 1 or blk[0][1] == 128:
                        nt = len(blk)
                        self.dma(OGd[q0b:q0b + nt * 128, h * 64:(h + 1) * 64].rearrange("(qi p) c -> p qi c", p=128),
                                 og[:, 0:nt, 0:64], [ogb], [kb])
                    else:
                        n = blk[0][1]
                        self.dma(OGd[q0b:q0b + n, h * 64:(h + 1) * 64], og[:n, 0, 0:64], [ogb], [kb])
                self.attn_head(g, KTd[h], QTd[h], VAd, h * 65, 64, 70, wb, extras, finish)

            def mkprep(t0, n):
                def prep():
                    og, ogb = self.r_h.next()
                    self.dma(og[:n, 0:D], OGd[t0:t0 + n, 0:D], ogw, [ogb])
                    zs, zsb = self.r_h.next()
                    self.dma(zs[:n, 0:D], ZSd[t0:t0 + n, 0:D], wb, [zsb])
                    self.tt("dve", og[:n, 0:D], og[:n, 0:D], zs[:n, 0:D], ALU.mult, [ogb, zsb], [ogb])
                    return og, ogb
                return prep
            self.pass_c_run([(g, t0, n, mkprep(t0, n)) for (t0, n) in g.tiles], first, last, 8)

    def layer_diff(self, L, first, last):
        dr = self.dr
        LAM_INIT = 0.8 - 0.6 * math.exp(-0.3 * 2)
        self.attn_rings(129, False)
        A = self.arena
        self.load_w(self.W, self.Wb, dr["diff_w_in"], 4096)
        self.load_w(self.WO, self.WOb, dr["diff_w_out"], D)
        lv, lvb = A.alloc([1, 4, 64], F32), Buf()
        for i, nmv in enumerate(("diff_lam_q1", "diff_lam_k1", "diff_lam_q2", "diff_lam_k2")):
            self.dma(lv[:, i, :], dr[nmv][:, :], [], [lvb])
        l2, l2b = A.alloc([1, 8], F32), Buf()
        pr, prb = A.alloc([1, 2, 64], F32), Buf()
        lvv = lv[:, :, :].rearrange("p (a b) c -> p a b c", b=2)
        self.tt("dve", pr[:, :, :], lvv[:, :, 0, :], lvv[:, :, 1, :], ALU.mult, [lvb], [prb])
        self.P.op("dve", lambda e: e.tensor_reduce(out=l2[:, 0:2], in_=pr[:, :, :], axis=AX.X, op=ALU.add), [prb], [l2b])
        self.act(l2[:, 2:4], l2[:, 0:2], AF.Exp, [l2b], [l2b])
        self.tt("dve", l2[:, 4:5], l2[:, 3:4], l2[:, 2:3], ALU.subtract, [l2b], [l2b])
        self.ts("dve", l2[:, 5:6], l2[:, 4:5], -LAM_INIT, None, ALU.add, None, [l2b], [l2b])
        self.dma(dr["lam_scr"][0:1, 0:1], l2[:, 5:6], [l2b], [self.buf("lam_scr")])
        nlam, nlamb = A.alloc([128, 1], F32), Buf()
        self.dma(nlam[:, :], dr["lam_scr"][0:1, 0:1].to_broadcast([128, 1]), [self.buf("lam_scr")], [nlamb])
        tb, tbb = A.alloc([32, 2, 8], F32), Buf()
        for m_ in range(2):
            self.dma(tb[:, m_, :], dr["rel_bias_table"][:, :], [], [tbb])
        oh, ohb = self.r_x.next()
        self.dma(oh[0:32, 0:384], dr["t5_onehot"][:, :], [], [ohb])
        tbs, tbsb = A.alloc([32, 3, 16], BF16), Buf()
        ohb16, ohb16b = A.alloc([32, 384], BF16), Buf()
        self.cp("dve", ohb16[:, :], oh[0:32, 0:384], [ohb], [ohb16b])
        tbf = tb[:, :, :].rearrange("p a b -> p (a b)")
        tmp32, tmp32b = self.r_small.next()
        self.cp("dve", tbs[:, 0, :], tbf, [tbb], [tbsb])
        self.cp("dve", tmp32[0:32, 0:16], tbs[:, 0, :], [tbsb], [tmp32b])
        self.tt("dve", tmp32[0:32, 0:16], tbf, tmp32[0:32, 0:16], ALU.subtract, [tbb, tmp32b], [tmp32b])
        self.cp("dve", tbs[:, 1, :], tmp32[0:32, 0:16], [tmp32b], [tbsb])
        tmp33, tmp33b = self.r_small.next()
        self.cp("dve", tmp33[0:32, 0:16], tbs[:, 1, :], [tbsb], [tmp33b])
        self.tt("dve", tmp33[0:32, 0:16], tmp32[0:32, 0:16], tmp33[0:32, 0:16], ALU.subtract, [tmp32b, tmp33b], [tmp33b])
        self.cp("dve", tbs[:, 2, :], tmp33[0:32, 0:16], [tmp33b], [tbsb])
        ps, pb = self.ps_mm.next()
        for j_ in range(3):
            self.mm(ps[0:16, 0:384], tbs[:, j_, :], ohb16[:, :], j_ == 0, j_ == 2, [tbsb, ohb16b], [pb])
        fr, frb = self.r_f.next()
        c16, c16b = self.r_small.next()
        self.cp("dve", c16[0:16, 0:1], ps[0:16, 0:1], [pb], [c16b])
        self.ts("dve", fr[0:16, 0:384], ps[0:16, 0:384], c16[0:16, 0:1], None, ALU.subtract, None, [pb, c16b], [frb])
        self.dma(dr["fr_scr"][:, :], fr[0:16, 0:384], [frb], [self.buf("fr_scr")])
        crow, crowb = A.alloc([16, 2, 512], BF16), Buf()
        chi, chib = self.r_small.next()
        cbf = chi[0:16, 0:4].bitcast(BF16)
        self.cp("dve", cbf[:, 0:1], c16[0:16, 0:1], [c16b], [chib])
        self.cp("dve", chi[0:16, 4:5], cbf[:, 0:1], [chib], [chib])
        self.tt("dve", chi[0:16, 5:6], c16[0:16, 0:1], chi[0:16, 4:5], ALU.subtract, [c16b, chib], [chib])
        self.cp("dve", cbf[:, 1:2], chi[0:16, 5:6], [chib], [chib])
        for j in range(2):
            self.cp("dve", crow[:, j, :], cbf[:, j:j + 1].to_broadcast([16, 512]), [chib], [crowb])
        ones2, ones2b = A.alloc([16, 2, 512], BF16), Buf()
        self.memset("pool", ones2[:, :, :], 1.0, [ones2b])
        J, Jb = A.alloc([128, 128], BF16), Buf()
        dmk, dmkb = A.alloc([128, 128], BF16), Buf()
        t, b = self.r_f.next()
        self.dma(t[:, 0:128], dr["antiident"][:, :], [], [b])
        self.cp("dve", J[:, :], t[:, 0:128], [b], [Jb])
        t, b = self.r_f.next()
        self.dma(t[:, 0:128], dr["dmask"][:, :], [], [b])
        self.cp("dve", dmk[:, :], t[:, 0:128], [b], [dmkb])
        Hh = [[None, None] for _ in range(8)]
        for h in range(8):
            for ti, c in enumerate((128, 0)):
                t, b = self.r_f.next()
                src_ap = bass.AP(self.dh["fr_scr"], h * 384 + c, [[1, 128], [1, 128]])
                self.dma(t[:, 0:128], src_ap, [self.buf("fr_scr")], [b])
                hi, hib = A.alloc([128, 128], BF16), Buf()
                lo, lob = A.alloc([128, 128], BF16), Buf()
                self.cp("dve", hi[:, :], t[:, 0:128], [b], [hib])
                self.cp("dve", t[:, 128:256], hi[:, :], [hib], [b])
                self.tt("dve", t[:, 256:384], t[:, 0:128], t[:, 128:256], ALU.subtract, [b], [b])
                self.cp("dve", lo[:, :], t[:, 256:384], [b], [lob])
                Hh[h][ti] = (hi, hib, lo, lob)

        slw, slwb = A.alloc([128, 128], F32), Buf()
        self.dma(slw[:, :], dr["diff_subln_w"][0:1, :].to_broadcast([128, 128]), [], [slwb])
        self.P.op("act", lambda e: e.mul(out=slw[:, :], in_=slw[:, :], mul=1.0 - LAM_INIT), [slwb], [slwb])
        for g in self.groups:
            nm = g.name
            KTd, QTd, VAd, ZSd, OGd = dr["KT_" + nm], dr["QT_" + nm], dr["VA_" + nm], dr["ZS_" + nm], dr["OGF_" + nm]
            KTv = KTd.rearrange("(hp two) r s -> two r hp s", two=2)
            QTv = QTd.rearrange("(hp two) r s -> two r hp s", two=2)
            wb = []

            def nb_(key):
                b = self.buf(("diff", g.gi, L) + key)
                wb.append(b)
                return b
            ptiles = [(k0, nk) for (k0, nk) in g.ktiles if k0 < g.past]
            for b0 in range(0, len(ptiles), 2):
                pblk = ptiles[b0:b0 + 2]
                st_, stb = self.r_st.next()
                st = st_[:, :].rearrange("p (c t) -> p c t", c=8)
                for j, (k0, nk) in enumerate(pblk):
                    xt, xb = self.r_x.next()
                    self.dma(xt[:, :], dr["cache_diff_k"][k0:k0 + 128, :], [], [xb])
                    ub, ubb = self.r_ub.next()
                    self.cp("dve", ub[:, :], xt[:, :], [xb], [ubb])
                    tp, tpb = self.ps_tp.next()
                    tpv = tp[:, :].bitcast(BF16)
                    for c in range(8):
                        self.tr(tpv[:, c * 128:(c + 1) * 128], ub[:, c * 128:(c + 1) * 128], self.identb[:, :],
                                [ubb, self.identb_b], [tpb])
                    self.cp("act", st[:, :, j * 128:(j + 1) * 128], tpv.rearrange("p (c t) -> p c t", c=8), [tpb], [stb])
                    vt, vtb = self.r_x.next()
                    self.dma(vt[:, :], dr["cache_diff_v"][k0:k0 + 128, :], [], [vtb])
                    vb, vbb = self.r_vb.next()
                    vb3 = vb[:, 0:1032].rearrange("p (h c) -> p h c", h=8)
                    self.memset("pool", vb3[:, :, 128:129], 1.0, [vbb])
                    self.cp("dve", vb3[:, :, 0:128], vt[:, :].rearrange("p (h c) -> p h c", h=8), [vtb], [vbb])
                    self.dma(VAd[k0:k0 + 128, 0:1032], vb[:, 0:1032], [vbb], [nb_(("pv", k0))])
                kp, nbk = pblk[0][0], 128 * len(pblk)
                for two in range(2):
                    self.dma(KTv[two, 0:64, :, kp:kp + nbk], st[two * 64:(two + 1) * 64, :, 0:nbk], [stb],
                             [nb_(("pk", kp, two))])
                self.dma(KTd[:, 64:66, kp:kp + nbk], ones2[:, :, 0:nbk], [ones2b], [nb_(("p1", kp))])
            for blk in g.blocks:
                uT, uTb, nb = self.make_uT(g, blk, first)
                t0b = blk[0][0]
                kpos = g.past + t0b
                for which, dst in ((0, QTv), (1, KTv)):
                    pos = t0b if which == 0 else kpos
                    for hf in range(2):
                        st_, stb = self.r_st.next()
                        st = st_[:, :].rearrange("p (c t) -> p c t", c=4)
                        for c4 in range(4):
                            c = hf * 4 + c4
                            ps, pb = self.proj_fm(uT, uTb, nb, which * D + c * 128, which * D + (c + 1) * 128)
                            if which == 0:
                                self.act(st[:, c4, 0:nb], ps[:, 0:nb], AF.Copy, [pb], [stb], scale=0.125)
                            else:
                                self.cp("dve", st[:, c4, 0:nb], ps[:, 0:nb], [pb], [stb])
                        for two in range(2):
                            self.dma(dst[two, 0:64, hf * 4:hf * 4 + 4, pos:pos + nb], st[two * 64:(two + 1) * 64, :, 0:nb],
                                     [stb], [nb_(("qk", which, t0b, two, hf))])
                self.dma(KTd[:, 64:66, kpos:kpos + nb], ones2[:, :, 0:nb], [ones2b], [nb_(("k1", t0b))])
                QTm = QTd.rearrange("(h m) r s -> m h r s", m=2)
                for m_ in range(2):
                    self.dma(QTm[m_, :, 64:66, t0b:t0b + nb], crow[m_ * 8:(m_ + 1) * 8, :, 0:nb], [crowb],
                             [nb_(("qc", t0b, m_))])
                off = 0
                for (t0, n) in blk:
                    for which, oname in ((1, "diff_k_"), (2, "diff_v_")):
                        ft, fb = self.r_f.next()
                        for cg in range(2):
                            ps, pb = self.proj_tm(uT, uTb, off, n, which * D + cg * 512, which * D + (cg + 1) * 512)
                            self.cp("act" if cg == 0 else "dve", ft[:n, cg * 512:(cg + 1) * 512], ps[:n, :], [pb], [fb])
                        self.dma(dr[oname + nm][t0:t0 + n, :], ft[:n, :], [fb], [self.buf((oname, g.gi, t0))])
                        if which == 2:
                            vb, vbb = self.r_vb.next()
                            vb3 = vb[:n, 0:1032].rearrange("p (h c) -> p h c", h=8)
                            self.memset("pool", vb3[:, :, 128:129], 1.0, [vbb])
                            self.cp("dve", vb3[:, :, 0:128], ft[:n, :].rearrange("p (h c) -> p h c", h=8), [fb], [vbb])
                            self.dma(VAd[g.past + t0:g.past + t0 + n, 0:1032], vb[:n, 0:1032], [vbb], [nb_(("v", t0))])
                    zt, ztb = self.r_h.next()
                    for cg in range(2):
                        ps, pb = self.proj_tm(uT, uTb, off, n, 3 * D + cg * 512, 3 * D + (cg + 1) * 512)
                        self.act(zt[:n, cg * 512:(cg + 1) * 512], ps[:n, :], AF.Silu, [pb], [ztb])
                    self.dma(ZSd[t0:t0 + n, 0:D], zt[:n, 0:D], [ztb], [nb_(("z", t0))])
                    off += n
            ogw = []
            for vh in range(16):
                h, m = vh // 2, vh % 2

                def extras(ki, k0, nk, qi, q0a, n, h=h):
                    if k0 == q0a:
                        hi, hib, lo, lob = Hh[h][0]
                        return [(hi[:, 0:nk], J[:, 0:n], [hib, Jb]), (lo[:, 0:nk], J[:, 0:n], [lob, Jb]),
                                (self.identb[:nk, :nk], dmk[:nk, :n], [self.identb_b, dmkb])]
                    if k0 == q0a - 128:
                        hi, hib, lo, lob = Hh[h][1]
                        return [(hi[:, 0:nk], J[:, 0:n], [hib, Jb]), (lo[:, 0:nk], J[:, 0:n], [lob, Jb])]
                    return []

                def finish(bi, blk, accs, per_bank, w, h=h, m=m, vh=vh):
                    og, ogb = self.r_x.next()
                    for qi, (q0, n) in enumerate(blk):
                        acc, accb = accs[qi // per_bank]
                        col = (qi % per_bank) * w
                        rd, rdb = self.r_small.next()
                        self.recip(rd[:n, 0:1], acc[:n, col + 128:col + 129], [accb], [rdb])
                        self.ts("dve", og[:n, qi * 128:(qi + 1) * 128], acc[:n, col:col + 128], rd[:n, 0:1], None,
                                ALU.mult, None, [accb, rdb], [ogb])
                    q0b = blk[0][0]
                    kb = self.buf(("diff_og", g.gi, L, vh, bi))
                    ogw.append(kb)
                    c0 = m * D + h * 128
                    if blk[0][1] == 128:
                        nt = len(blk)
                        self.dma(OGd[q0b:q0b + nt * 128, c0:c0 + 128].rearrange("(qi p) c -> p qi c", p=128),
                                 og[:, 0:nt * 128].rearrange("p (qi c) -> p qi c", c=128), [ogb], [kb])
                    else:
                        n = blk[0][1]
                        self.dma(OGd[q0b:q0b + n, c0:c0 + 128], og[:n, 0:128], [ogb], [kb])
                self.attn_head(g, KTd[vh][0:66, :], QTd[vh][0:66, :], VAd, h * 129, 128, 66, wb, extras, finish)
            def mkprep(t0, n):
                def prep():
                        o1, o1b = self.r_x.next()
                        self.dma(o1[:n, :], OGd[t0:t0 + n, 0:D], ogw, [o1b])
                        o2, o2b = self.r_f.next()
                        self.dma(o2[:n, :], OGd[t0:t0 + n, D:2 * D], ogw, [o2b])
                        self.stt(o1[:n, :], o2[:n, :], nlam[:n, 0:1], o1[:n, :], ALU.mult, ALU.add, [o2b, nlamb, o1b], [o1b])
                        self.tt("dve", o2[:n, :], o1[:n, :], o1[:n, :], ALU.mult, [o1b], [o2b])
                        ss, ssb = self.r_small.next()
                        self.P.op("dve", lambda e, ss=ss, o2=o2, n=n: e.tensor_reduce(
                            out=ss[:n, 0:8], in_=o2[:n, :].rearrange("p (h c) -> p h c", h=8), axis=AX.X, op=ALU.add),
                            [o2b], [ssb])
                        self.rsqrt(ss[:n, 0:8], ss[:n, 0:8], [ssb], [ssb], scale=1.0 / 128, eps=NORM_EPS)
                        self.tt("dve", o1[:n, :].rearrange("p (h c) -> p h c", h=8), o1[:n, :].rearrange("p (h c) -> p h c", h=8),
                                ss[:n, 0:8].unsqueeze(2).to_broadcast([n, 8, 128]), ALU.mult, [o1b, ssb], [o1b])
                        self.tt("dve", o1[:n, :].rearrange("p (h c) -> p h c", h=8), o1[:n, :].rearrange("p (h c) -> p h c", h=8),
                                slw[:n, :].unsqueeze(1).to_broadcast([n, 8, 128]), ALU.mult, [o1b, slwb], [o1b])
                        zs, zsb = self.r_h.next()
                        self.dma(zs[:n, 0:D], ZSd[t0:t0 + n, 0:D], wb, [zsb])
                        og, ogb = self.r_h.next()
                        self.tt("dve", og[:n, 0:D], o1[:n, :], zs[:n, 0:D], ALU.mult, [o1b, zsb], [ogb])
                        return og, ogb
                return prep
            self.pass_c_run([(g, t0, n, mkprep(t0, n)) for (t0, n) in g.tiles], first, last, 8)

    def layer_ret(self, L, first, last):
        dr = self.dr
        P = self.P
        A = self.arena
        LG = [math.log1p(-2.0 ** (-5.0 - h)) for h in range(4)]
        P.barrier(); A.reset()
        self.set_psum(4, 2, 2)
        self.load_w(self.W, self.Wb, dr["ret_w_in"], 4096)
        r_st = Ring(self, "rst", [128, 8, 512], BF16, 1, arena=A)
        r_cs = Ring(self, "rcs", [128, 2, 512], F32, 1, arena=A)
        r_vt = Ring(self, "rvt", [128, 2048], BF16, 2, arena=A)
        wbs = {}
        for g in self.groups:
            nm = g.name
            wb = wbs[g.gi] = []

            def nb_(key, wb=wb, g=g):
                b = self.buf(("ret", g.gi) + key)
                wb.append(b)
                return b
            for blk in g.blocks:
                uT, uTb, nb = self.make_uT(g, blk, first)
                t0b = blk[0][0]
                cs, csb = r_cs.next()
                self.dma(cs[:, 0, 0:nb], dr["rot_cos_" + nm][:, t0b:t0b + nb], [], [csb])
                self.dma(cs[:, 1, 0:nb], dr["rot_sin_" + nm][:, t0b:t0b + nb], [], [csb])
                for which, dname in ((0, "QR_"), (1, "KR_")):
                    st, stb = r_st.next()
                    for h in range(4):
                        c0 = which * D + h * 256
                        pe_, peb = self.proj_fm(uT, uTb, nb, c0, c0 + 128)
                        po_, pob = self.proj_fm(uT, uTb, nb, c0 + 128, c0 + 256)
                        xe, xeb = self.r_f.next()
                        sc = 0.0625 if which == 0 else 1.0
                        self.act(xe[:, 0:nb], pe_[:, 0:nb], AF.Copy, [peb], [xeb], scale=sc)
                        self.act(xe[:, 512:512 + nb], po_[:, 0:nb], AF.Copy, [pob], [xeb], scale=sc)
                        tm, tmb = self.r_x.next()
                        self.tt("dve", tm[:, 0:nb], xe[:, 0:nb], cs[:, 0, 0:nb], ALU.mult, [xeb, csb], [tmb])
                        self.tt("dve", tm[:, 512:512 + nb], xe[:, 512:512 + nb], cs[:, 1, 0:nb], ALU.mult, [xeb, csb], [tmb])
                        self.tt("dve", st[:, 2 * h, 0:nb], tm[:, 0:nb], tm[:, 512:512 + nb], ALU.subtract, [tmb], [stb])
                        tm, tmb = self.r_x.next()
                        self.tt("dve", tm[:, 0:nb], xe[:, 0:nb], cs[:, 1, 0:nb], ALU.mult, [xeb, csb], [tmb])
                        self.tt("dve", tm[:, 512:512 + nb], xe[:, 512:512 + nb], cs[:, 0, 0:nb], ALU.mult, [xeb, csb], [tmb])
                        self.tt("dve", st[:, 2 * h + 1, 0:nb], tm[:, 0:nb], tm[:, 512:512 + nb], ALU.add, [tmb], [stb])
                    self.dma(dr[dname + nm].rearrange("he p t -> p he t")[:, :, t0b:t0b + nb], st[:, :, 0:nb], [stb],
                             [nb_((dname, t0b))])
                off = 0
                for (t0, n) in blk:
                    vt, vtb = r_vt.next()
                    for cg in range(4):
                        ps, pb = self.proj_tm(uT, uTb, off, n, 2 * D + cg * 512, 2 * D + (cg + 1) * 512)
                        self.cp("act" if cg % 2 == 0 else "dve", vt[:n, cg * 512:(cg + 1) * 512], ps[:n, :], [pb], [vtb])
                    self.dma(dr["OG_" + nm + "v"][t0:t0 + n, :], vt[:n, :], [vtb], [nb_(("v", t0))])
                    off += n
        self.load_w(self.W, self.Wb, dr["ret_w_in"], 2048, c_off=4096)
        for g in self.groups:
            nm = g.name
            wb = wbs[g.gi]
            for blk in g.blocks:
                uT, uTb, nb = self.make_uT(g, blk, first)
                off = 0
                for (t0, n) in blk:
                    zt, ztb = r_vt.next()
                    for cg in range(4):
                        ps, pb = self.proj_tm(uT, uTb, off, n, cg * 512, (cg + 1) * 512)
                        self.act(zt[:n, cg * 512:(cg + 1) * 512], ps[:n, :], AF.Silu, [pb], [ztb])
                    b = self.buf(("ret", g.gi, "z", t0))
                    wb.append(b)
                    self.dma(dr["ZS_" + nm][t0:t0 + n, :], zt[:n, :], [ztb], [b])
                    off += n
        P.barrier(); A.reset()
        A2 = self.arenaW
        A2.reset()
        self.load_w(self.WO, self.WOb, dr["ret_w_out"], D)
        dmk, dmkb = A.alloc([128, 4, 128], F32), Buf()
        self.dma(dmk[:, :, :], dr["ret_dmaskT"][:, :, :], [], [dmkb])
        qdc, qdcb = A.alloc([128, 4, 128], F32), Buf()
        self.dma(qdc[:, :, :], dr["ret_qdec"][:, :, :], [], [qdcb])
        kdc, kdcb = A.alloc([128, 8], F32), Buf()
        self.dma(kdc[:, :], dr["ret_kdec"][:, :], [], [kdcb])
        S, Sb = A.alloc([128, 8, 512], F32), Buf()
        Sbf, Sbfb = A.alloc([128, 8, 512], BF16), Buf()
        r_q = Ring(self, "rq", [128, 8, 128], BF16, 2, arena=A)
        r_k = Ring(self, "rk", [128, 8, 128], BF16, 2, arena=A)
        r_v = Ring(self, "rv", [128, 2048], BF16, 2, arena=A)
        r_qd = Ring(self, "rqd", [128, 8, 128], BF16, 2, arena=A2)
        r_at = Ring(self, "rat", [128, 4, 128], BF16, 2, arena=A2)
        r_kd = Ring(self, "rkd", [128, 8, 128], BF16, 2, arena=A2)
        r_o = Ring(self, "ro", [128, 2048], BF16, 2, arena=A2)
        r_s = Ring(self, "rs", [128, 32], F32, 3, arena=A2)
        ogws = {}
        for g in self.groups:
            nm = g.name
            wb = wbs[g.gi]
            ogw = ogws[g.gi] = []
            QRd = dr["QR_" + nm].rearrange("he p t -> p he t")
            KRd = dr["KR_" + nm].rearrange("he p t -> p he t")
            VRd = dr["OG_" + nm + "v"]
            if g.gi == 0:
                self.memset("pool", S[:, :, :], 0.0, [Sb])
            else:
                self.dma(S[:, :, :], dr["state_ret"].rearrange("h (e p) v -> p (h e) v", e=2), [], [Sb])
            self.cp("act", Sbf[:, 0:4, :], S[:, 0:4, :], [Sb], [Sbfb])
            self.cp("dve", Sbf[:, 4:8, :], S[:, 4:8, :], [Sb], [Sbfb])
            for (t0, n) in g.tiles:
                qt, qtb = r_q.next()
                self.dma(qt[:, :, 0:n], QRd[:, :, t0:t0 + n], wb, [qtb])
                kt, ktb = r_k.next()
                self.dma(kt[:, :, 0:n], KRd[:, :, t0:t0 + n], wb, [ktb])
                vt, vtb = r_v.next()
                self.dma(vt[:n, :], VRd[t0:t0 + n, :], wb, [vtb])
                kc0 = 0 if n == 128 else 4
                ps, pb = self.psb[0]
                for h in range(4):
                    for e in range(2):
                        self.mm(ps[:n, h * 128:h * 128 + n], kt[:, 2 * h + e, 0:n], qt[:, 2 * h + e, 0:n], e == 0, e == 1,
                                [ktb, qtb], [pb], skip=True)
                at, atb = r_at.next()
                self.tt("dve", at[:n, :, 0:n], ps[:n, :].rearrange("p (h i) -> p h i", h=4)[:, :, 0:n], dmk[:n, :, 0:n],
                        ALU.mult, [pb, dmkb], [atb])
                qd, qdb = r_qd.next()
                self.tt("dve", qd[:, :, 0:n].rearrange("p (h e) i -> p h e i", e=2),
                        qt[:, :, 0:n].rearrange("p (h e) i -> p h e i", e=2),
                        qdc[:, :, 0:n].unsqueeze(2).to_broadcast([128, 4, 2, n]), ALU.mult, [qtb, qdcb], [qdb])
                tp, tpb = self.psb[5]
                tpv = tp[:, :].bitcast(BF16)
                for he in range(8):
                    self.tr(tpv[:n, he * 128:(he + 1) * 128], kt[:, he, 0:n], self.identb[:, :], [ktb, self.identb_b], [tpb])
                kd, kdb = r_kd.next()
                self.tt("dve", kd[:n, :, :].rearrange("p (h e) d -> p h (e d)", e=2),
                        tpv[:n, :].rearrange("p (h x) -> p h x", h=4),
                        kdc[:n, kc0:kc0 + 4].unsqueeze(2).to_broadcast([n, 4, 256]), ALU.mult, [tpb, kdcb], [kdb])
                pos = []
                for h in range(4):
                    po, pob = self.psb[1 + h]
                    self.mm(po[:n, :], at[:n, h, 0:n], vt[:n, h * 512:(h + 1) * 512], True, False, [atb, vtb], [pob])
                    for e in range(2):
                        self.mm(po[:n, :], qd[:, 2 * h + e, 0:n], Sbf[:, 2 * h + e, :], False, e == 1, [qdb, Sbfb], [pob])
                    pos.append((po, pob))
                for h in range(4):
                    cdec = math.exp(LG[h] * n)
                    for e in range(2):
                        pu, pub = self.psb[6 + e]
                        self.mm(pu[:, :], kd[:n, 2 * h + e, :], vt[:n, h * 512:(h + 1) * 512], True, True, [kdb, vtb], [pub])
                        self.stt(S[:, 2 * h + e, :], S[:, 2 * h + e, :], cdec, pu[:, :], ALU.mult, ALU.add, [Sb, pub], [Sb])
                    self.cp("act", Sbf[:, 2 * h:2 * h + 2, :], S[:, 2 * h:2 * h + 2, :], [Sb], [Sbfb])
                st, stb = r_s.next()
                for h in range(4):
                    po, pob = pos[h]
                    P.op("dve", lambda e_, st=st, po=po, n=n, h=h: e_.bn_stats(out=st[:n, 8 + 6 * h:14 + 6 * h], in_=po[:n, :]),
                         [pob], [stb])
                    P.op("dve", lambda e_, st=st, n=n, h=h: e_.bn_aggr(out=st[:n, 2 * h:2 * h + 2], in_=st[:n, 8 + 6 * h:14 + 6 * h]),
                         [stb], [stb])
                mv = st[:n, 0:8].rearrange("p (h t) -> p h t", t=2)
                rs, rsb = r_s.next()
                self.rsqrt(rs[:n, 0:4], mv[:, :, 1], [stb], [rsb], eps=LN_EPS)
                self.stt(rs[:n, 4:8], mv[:, :, 0], -1.0, rs[:n, 0:4], ALU.mult, ALU.mult, [stb, rsb], [rsb])
                ot, otb = r_o.next()
                for h in range(4):
                    po, pob = pos[h]
                    self.act(ot[:n, h * 512:(h + 1) * 512], po[:n, :], AF.Identity, [pob, rsb], [otb],
                             bias=rs[:n, 4 + h:5 + h], scale=rs[:n, h:h + 1])
                b = self.buf(("ret_og", g.gi, t0))
                ogw.append(b)
                self.dma(dr["OG_" + nm][t0:t0 + n, :], ot[:n, :], [otb], [b])
            self.dma(dr["ret_state_" + nm].rearrange("h (e p) v -> p (h e) v", e=2), S[:, :, :], [Sb],
                     [self.buf(("ret_state", g.gi))])
        P.barrier()
        self.load_w(self.W, self.Wb, dr["ret_w_out"][D:2 * D, :], D)
        P.barrier(); A.reset()
        gnw, gnwb = A.alloc([128, 2048], F32), Buf()
        self.dma(gnw[:, :], dr["ret_gn_w"][0:1, :].to_broadcast([128, 2048]), [], [gnwb])
        r_og = Ring(self, "rog2", [128, 2048], BF16, 2, arena=A)
        r_zs = Ring(self, "rzs2", [128, 2048], BF16, 2, arena=A)
        r_gz = Ring(self, "rgz2", [128, 2048], BF16, 2, arena=A)

        def wo(k):
            if k < 8:
                return self.WO[:, k, :], self.WOb
            return self.W[:, k - 8, 0:D], self.Wb
        items = []
        for g in self.groups:
            def mkprep(g, t0, n):
                def prep():
                    nm = g.name
                    og, ogb = r_og.next()
                    self.dma(og[:n, :], dr["OG_" + nm][t0:t0 + n, :], ogws[g.gi], [ogb])
                    zs, zsb = r_zs.next()
                    self.dma(zs[:n, :], dr["ZS_" + nm][t0:t0 + n, :], wbs[g.gi], [zsb])
                    gz, gzb = r_gz.next()
                    self.tt("dve", gz[:n, :], og[:n, :], gnw[:n, :], ALU.mult, [ogb, gnwb], [gzb])
                    self.tt("dve", gz[:n, :], gz[:n, :], zs[:n, :], ALU.mult, [gzb, zsb], [gzb])
                    return gz, gzb
                return prep
            items += [(g, t0, n, mkprep(g, t0, n)) for (t0, n) in g.tiles]
        self.pass_c_run(items, first, last, 16, wo=wo)

    def layer_gdn(self, L, first, last):
        dr = self.dr
        P = self.P
        A = self.arena
        P.barrier(); A.reset()
        self.set_psum(4, 2, 2)
        self.load_w(self.W, self.Wb, dr["gdn_w_in"], 4112)
        self.load_w(self.WO, self.WOb, dr["gdn_w_out"], D)
        cw, cwb = A.alloc([128, 24, 4], F32), Buf()
        self.dma(cw[:, :, :], dr["gdn_convw"][:, :, :], [], [cwb])
        halo, halob = A.alloc([128, 24, 3], F32), Buf()
        onesb, onesbb = A.alloc([128, 128], BF16), Buf()
        self.memset("pool", onesb[:, :], 1.0, [onesbb])
        sc8, sc8b = A.alloc([8, 4], F32), Buf()
        self.dma(sc8[:, 0:1], dr["gdn_dt_bias"][:, :], [], [sc8b])
        self.dma(sc8[:, 2:3], dr["gdn_a_log"][:, :], [], [sc8b])
        self.act(sc8[:, 3:4], sc8[:, 2:3], AF.Exp, [sc8b], [sc8b])
        self.ts("dve", sc8[:, 1:2], sc8[:, 3:4], -1.0, None, ALU.mult, None, [sc8b], [sc8b])
        r_xr = Ring(self, "rxr", [128, 516], F32, 4, arena=A)
        r_y = Ring(self, "ry", [128, 512], F32, 4, arena=A)
        r_rn = Ring(self, "rrn", [128, 512], F32, 4, arena=A)
        r_sq = Ring(self, "rsq", [128, 512], BF16, 4, arena=A)
        r_st = Ring(self, "rst", [128, 8, 512], BF16, 1, arena=A)
        r_g8 = Ring(self, "rg8", [8, 512], F32, 6, arena=A)
        wbs = {}
        for g in self.groups:
            nm = g.name
            wb = wbs[g.gi] = []

            def nb_(key, wb=wb, g=g):
                b = self.buf(("gdn", g.gi) + key)
                wb.append(b)
                return b
            if g.gi == 0:
                self.memset("pool", halo[:, :, :], 0.0, [halob])
            else:
                self.dma(halo[:, :, :], dr["gdn_conv_in"][:, :, :], [], [halob])
            for blk in g.blocks:
                uT, uTb, nb = self.make_uT(g, blk, first)
                t0b = blk[0][0]
                for which, dname in ((0, "QG_"), (1, "KG_"), (2, "VG_")):
                    st, stb = r_st.next()
                    for hg in (0, 4):
                        hs = list(range(hg, hg + 4))
                        xrs, ys, rns, sqs, p2s = {}, {}, {}, {}, {}
                        for h in hs:
                            c = which * 8 + h
                            ps, pb = self.proj_fm(uT, uTb, nb, c * 128, (c + 1) * 128)
                            xr, xrb = r_xr.next()
                            self.cp("act", xr[:, 0:3], halo[:, c, :], [halob], [xrb])
                            self.cp("act", xr[:, 3:3 + nb], ps[:, 0:nb], [pb], [xrb])
                            self.cp("act", halo[:, c, :], xr[:, nb:nb + 3], [xrb], [halob])
                            xrs[h] = (xr, xrb)
                        for h in hs:
                            c = which * 8 + h
                            xr, xrb = xrs[h]
                            y, yb = r_y.next()
                            self.ts("dve", y[:, 0:nb], xr[:, 0:nb], cw[:, c, 0:1], None, ALU.mult, None, [xrb, cwb], [yb])
                            for j in range(1, 4):
                                self.stt(y[:, 0:nb], xr[:, j:j + nb], cw[:, c, j:j + 1], y[:, 0:nb], ALU.mult, ALU.add,
                                         [xrb, cwb, yb], [yb])
                            ys[h] = (y, yb)
                        for h in hs:
                            y, yb = ys[h]
                            self.act(y[:, 0:nb], y[:, 0:nb], AF.Silu, [yb], [yb])
                        if which == 2:
                            for h in hs:
                                y, yb = ys[h]
                                self.cp("dve", st[:, h, 0:nb], y[:, 0:nb], [yb], [stb])
                            continue
                        for h in hs:
                            y, yb = ys[h]
                            sq, sqb = r_sq.next()
                            self.tt("dve", sq[:, 0:nb], y[:, 0:nb], y[:, 0:nb], ALU.mult, [yb], [sqb])
                            p2, p2b = self.ps_mm.next()
                            self.mm(p2[:, 0:nb], onesb[:, :], sq[:, 0:nb], True, True, [onesbb, sqb], [p2b])
                            p2s[h] = (p2, p2b)
                        for h in hs:
                            p2, p2b = p2s[h]
                            rn, rnb = r_rn.next()
                            self.rsqrt(rn[:, 0:nb], p2[:, 0:nb], [p2b], [rnb], eps=NORM_EPS)
                            rns[h] = (rn, rnb)
                        for h in hs:
                            y, yb = ys[h]
                            rn, rnb = rns[h]
                            if which == 0:
                                self.stt(st[:, h, 0:nb], y[:, 0:nb], 128.0 ** -0.5, rn[:, 0:nb], ALU.mult, ALU.mult,
                                         [yb, rnb], [stb])
                            else:
                                self.tt("dve", st[:, h, 0:nb], y[:, 0:nb], rn[:, 0:nb], ALU.mult, [yb, rnb], [stb])
                    self.dma(dr[dname + nm].rearrange("h p t -> p h t")[:, :, t0b:t0b + nb], st[:, :, 0:nb], [stb],
                             [nb_((dname, t0b))])
                ps, pb = self.proj_fm(uT, uTb, nb, 3072, 3080)
                bt, btb = r_g8.next()
                self.act(bt[:, 0:nb], ps[0:8, 0:nb], AF.Exp, [pb], [btb], scale=-1.0)
                self.ts("dve", bt[:, 0:nb], bt[:, 0:nb], 1.0, None, ALU.add, None, [btb], [btb])
                self.recip(bt[:, 0:nb], bt[:, 0:nb], [btb], [btb])
                ps, pb = self.proj_fm(uT, uTb, nb, 3080, 3088)
                gt, gtb = r_g8.next()
                self.act(gt[:, 0:nb], ps[0:8, 0:nb], AF.Exp, [pb, sc8b], [gtb], bias=sc8[:, 0:1])
                self.act(gt[:, 0:nb], gt[:, 0:nb], AF.Ln, [gtb], [gtb], bias=1.0)
                self.ts("dve", gt[:, 0:nb], gt[:, 0:nb], sc8[:, 1:2], None, ALU.mult, None, [gtb, sc8b], [gtb])
                Gt, Gtb = r_g8.next()
                on8, on8b = r_g8.next()
                self.memset("pool", on8[:, 0:128], 1.0, [on8b])
                for off in range(0, nb, 64):
                    n = min(64, nb - off)
                    P.op("dve", lambda e, Gt=Gt, gt=gt, on8=on8, off=off, n=n: e.tensor_tensor_scan(
                        out=Gt[:, off:off + n], data0=on8[:, 0:n], data1=gt[:, off:off + n], initial=0.0,
                        op0=ALU.mult, op1=ALU.add), [gtb, on8b], [Gtb])
                btb16, btb16b = r_g8.next()
                bt16v = btb16[:, 0:256].bitcast(BF16)
                self.cp("act", bt16v[:, 0:nb], bt[:, 0:nb], [btb], [btb16b])
                self.dma(dr["BTb_" + nm][:, t0b:t0b + nb], bt16v[:, 0:nb], [btb16b], [nb_(("btb", t0b))])
                self.dma(dr["GT_" + nm][:, t0b:t0b + nb], Gt[:, 0:nb], [Gtb], [nb_(("gt", t0b))])
                off = 0
                for ti, (t0, n) in enumerate(blk):
                    tp, tpb = self.ps_tp.next()
                    self.tr(tp[0:n, 0:8], bt[:, off:off + n], self.identf[0:8, 0:8], [btb, self.identf_b], [tpb])
                    tp2, tp2b = self.ps_tp.next()
                    self.tr(tp2[0:n, 0:8], Gt[:, off:off + n], self.identf[0:8, 0:8], [Gtb, self.identf_b], [tp2b])
                    gb, gbb = self.r_small.next()
                    self.cp("act", gb[:n, 0:8], tp[0:n, 0:8], [tpb], [gbb])
                    self.cp("act", gb[:n, 8:16], tp2[0:n, 0:8], [tp2b], [gbb])
                    self.dma(dr["GB_" + nm][t0:t0 + n, :], gb[:n, 0:16], [gbb], [nb_(("gb", t0))])
                    zt, ztb = self.r_h.next()
                    for cg in range(2):
                        ps, pb = self.proj_tm(uT, uTb, off, n, 3088 + cg * 512, 3088 + (cg + 1) * 512)
                        self.act(zt[:n, cg * 512:(cg + 1) * 512], ps[:n, :], AF.Silu, [pb], [ztb])
                    self.dma(dr["ZS_" + nm][t0:t0 + n, 0:D], zt[:n, 0:D], [ztb], [nb_(("z", t0))])
                    off += n
                if blk is g.blocks[-1]:
                    co, cob = self.r_f.next()
                    for cg in range(6):
                        ps, pb = self.ps_mm.next()
                        for k in range(8):
                            self.mm(ps[0:3, :], uT[:, k, nb - 3:nb], self.W[:, k, cg * 512:(cg + 1) * 512], k == 0, k == 7,
                                    [uTb, self.Wb], [pb])
                        self.cp("act", co[0:3, (cg % 2) * 512:(cg % 2 + 1) * 512], ps[0:3, :], [pb], [cob])
                        if cg % 2 == 1:
                            self.dma(dr["gdn_conv_" + nm][:, (cg - 1) * 512:(cg + 1) * 512], co[0:3, :], [cob],
                                     [self.buf(("gdn_conv", g.gi, cg))])
                            if cg < 5:
                                co, cob = self.r_f.next()
        import os
        if os.environ.get("GDN_STOP") == "A":
            return
        P.barrier(); A.reset()
        A2 = self.arenaW
        A2.reset()
        msk, mskb = A.alloc([128, 3, 128], F32), Buf()
        self.dma(msk[:, :, :], dr["gdn_masks"][:, :, :], [], [mskb])
        S, Sb = A.alloc([128, 8, 128], F32), Buf()
        Sbf, Sbfb = A.alloc([128, 8, 128], BF16), Buf()
        r_qkv = Ring(self, "rqkv", [128, 3, 8, 128], BF16, 2, arena=A)
        r_gb = Ring(self, "rgb", [128, 8, 128], F32, 2, arena=A)
        r_bb = Ring(self, "rbb", [128, 8, 128], BF16, 2, arena=A)
        r_gbk = Ring(self, "rgbk", [64, 2, 16], F32, 2, arena=A)
        r_sq = Ring(self, "rsq", [64, 8, 128], F32, 1, arena=A)
        r_sc = Ring(self, "rsc", [64, 48], F32, 6, arena=A)
        r_F = Ring(self, "rF", [128, 8, 64], F32, 14, arena=A2)
        r_B = Ring(self, "rB", [128, 8, 64], BF16, 16, arena=A2)
        r_L = Ring(self, "rL", [128, 8, 64], BF16, 10, arena=A2)
        r_B2 = Ring(self, "rB2", [64, 8, 128], BF16, 8, arena=A)
        ogws = {}

        def bc(ap, shape):
            return ap.to_broadcast(shape)
        for g in self.groups:
            nm = g.name
            wb = wbs[g.gi]
            ogw = ogws[g.gi] = []
            if g.gi == 0:
                self.memset("pool", S[:, :, :], 0.0, [Sb])
            else:
                self.dma(S[:, :, :], dr["state_gdn"].rearrange("h d v -> d h v"), [], [Sb])
            self.cp("act", Sbf[:, :, :], S[:, :, :], [Sb], [Sbfb])
            for (tt0, tn) in g.tiles:
                qkv, qkvb = r_qkv.next()
                for i, dname in enumerate(("QG_", "KG_", "VG_")):
                    self.dma(qkv[:, i, :, 0:tn], dr[dname + nm].rearrange("h p t -> p h t")[:, :, tt0:tt0 + tn], wb, [qkvb])
                gb, gbb = r_gb.next()
                self.dma(gb[:, :, 0:tn], bass.AP(self.dh["GT_" + nm], tt0, [[0, 128], [g.T, 8], [1, tn]]), wb, [gbb])
                bb, bbb = r_bb.next()
                self.dma(bb[:, :, 0:tn], bass.AP(self.dh["BTb_" + nm], tt0, [[0, 128], [g.T, 8], [1, tn]]), wb, [bbb])
                gbk, gbkb = r_gbk.next()
                if tn % 64 == 0:
                    self.dma(gbk[0:64, 0:tn // 64, :], dr["GB_" + nm][tt0:tt0 + tn, :].rearrange("(c p) x -> p c x", p=64),
                             wb, [gbkb])
                else:
                    self.dma(gbk[:tn, 0, :], dr["GB_" + nm][tt0:tt0 + tn, :], wb, [gbkb])
                from types import SimpleNamespace as NS
                cks = []
                for ci, off in enumerate(range(0, tn, 64)):
                    c = NS(ci=ci, off=off, n=min(64, tn - off), t0=tt0 + off)
                    c.Qt = qkv[:, 0, :, off:off + c.n]
                    c.Kt = qkv[:, 1, :, off:off + c.n]
                    c.Vt = qkv[:, 2, :, off:off + c.n]
                    c.Gp = gbk[:c.n, ci, 8:16]
                    c.Bp = gbk[:c.n, ci, 0:8]
                    cks.append(c)
                mk = lambda c, k_: msk[:c.n, k_, 0:c.n].unsqueeze(1).to_broadcast([c.n, 8, c.n])
                v3 = lambda c, ps_: ps_[:c.n, :].rearrange("p (h c) -> p h c", h=8)[:, :, 0:c.n]
                t3 = lambda c, v_: v_[:c.n, :].rearrange("p (h c) -> p h c", h=8)
                bcs = lambda c, a_: a_.unsqueeze(2).to_broadcast([c.n, 8, 128])

                def s_bcast(c):
                    n, off = c.n, c.off
                    c.eG, c.eGb = r_F.next()
                    self.act(c.eG[:, :, 0:n], gb[:, :, off:off + n], AF.Exp, [gbb], [c.eGb])
                    c.KbT, c.KbTb = r_B.next()
                    self.tt("dve", c.KbT[:, :, 0:n], c.Kt, bb[:, :, off:off + n], ALU.mult, [qkvb, bbb], [c.KbTb])
                    c.sc, c.scb = r_sc.next()
                    self.act(c.sc[:n, 16:24], c.Gp, AF.Exp, [gbkb], [c.scb])
                    c.tD, c.tDb = r_F.next()
                    self.tt("dve", c.tD[:n, :, 0:n], gb[:n, :, off:off + n], c.Gp.unsqueeze(2).to_broadcast([n, 8, n]),
                            ALU.subtract, [gbb, gbkb], [c.tDb])

                def s_gram(c):
                    n = c.n
                    c.paA, c.paAb = self.ps_mm.next()
                    for h in range(8):
                        self.mm(c.paA[:n, h * 64:h * 64 + n], c.Kt[:, h, :], c.KbT[:, h, 0:n], True, True, [qkvb, c.KbTb], [c.paAb], skip=True)
                    c.paB, c.paBb = self.ps_mm.next()
                    for h in range(8):
                        self.mm(c.paB[:n, h * 64:h * 64 + n], c.KbT[:, h, 0:n], c.Kt[:, h, :], True, True, [qkvb, c.KbTb], [c.paBb], skip=True)
                    c.D1, c.D1b = r_F.next()
                    self.ts("dve", c.D1[:n, :, 0:n], c.tD[:n, :, 0:n], 0.0, None, ALU.min, None, [c.tDb], [c.D1b])
                    self.act(c.D1[:n, :, 0:n], c.D1[:n, :, 0:n], AF.Exp, [c.D1b], [c.D1b])
                    c.D2, c.D2b = r_F.next()
                    self.ts("dve", c.D2[:n, :, 0:n], c.tD[:n, :, 0:n], 0.0, None, ALU.max, None, [c.tDb], [c.D2b])
                    self.act(c.D2[:n, :, 0:n], c.D2[:n, :, 0:n], AF.Exp, [c.D2b], [c.D2b], scale=-1.0)

                def s_mask(c):
                    n = c.n
                    c.D1i, c.D1ib = r_F.next()
                    self.tt("dve", c.D1i[:n, :, 0:n], c.D1[:n, :, 0:n], mk(c, 2), ALU.mult, [c.D1b, mskb], [c.D1ib])
                    self.tt("dve", c.D1[:n, :, 0:n], c.D1[:n, :, 0:n], mk(c, 0), ALU.mult, [c.D1b, mskb], [c.D1b])
                    self.tt("dve", c.D2[:n, :, 0:n], c.D2[:n, :, 0:n], mk(c, 1), ALU.mult, [c.D2b, mskb], [c.D2b])
                    c.Pm, c.Pmb = r_F.next()
                    self.tt("dve", c.Pm[:n, :, 0:n], v3(c, c.paA), c.D1[:n, :, 0:n], ALU.mult, [c.paAb, c.D1b], [c.Pmb])
                    c.X, c.Xb = r_B.next()
                    self.cp("act", c.X[:n, :, 0:n], c.Pm[:n, :, 0:n], [c.Pmb], [c.Xb])
                    c.Y, c.Yb = r_B.next()
                    self.tt("dve", c.Y[:n, :, 0:n], v3(c, c.paB), c.D2[:n, :, 0:n], ALU.mult, [c.paBb, c.D2b], [c.Yb])
                    self.tt("dve", c.Pm[:n, :, 0:n], c.Pm[:n, :, 0:n],
                            self.identf[:n, 0:n].unsqueeze(1).to_broadcast([n, 8, n]), ALU.add, [c.Pmb, self.identf_b], [c.Pmb])
                    c.Pbf, c.Pbfb = r_B.next()
                    self.cp("act", c.Pbf[:n, :, 0:n], c.Pm[:n, :, 0:n], [c.Pmb], [c.Pbfb])
                    c.nsteps = max(1, int(math.ceil(math.log2(n))) - 1)

                def mk_step(s):
                    def s_sq(c):
                        n = c.n
                        if s >= c.nsteps:
                            return
                        c.lastst = s == c.nsteps - 1
                        c.pxY, c.pxYb = self.ps_mm.next()
                        for h in range(8):
                            self.mm(c.pxY[:n, h * 64:h * 64 + n], c.X[:n, h, 0:n], c.Y[:n, h, 0:n], True, True, [c.Xb, c.Yb], [c.pxYb], skip=True)
                        if not c.lastst:
                            c.pxX, c.pxXb = self.ps_mm.next()
                            for h in range(8):
                                self.mm(c.pxX[:n, h * 64:h * 64 + n], c.Y[:n, h, 0:n], c.X[:n, h, 0:n], True, True, [c.Xb, c.Yb], [c.pxXb], skip=True)
                        c.Y2, c.Y2b = r_B.next()
                        self.cp("act", c.Y2[:n, :, 0:n], v3(c, c.pxY), [c.pxYb], [c.Y2b])
                        if not c.lastst:
                            c.X2, c.X2b = r_B.next()
                            self.cp("dve", c.X2[:n, :, 0:n], v3(c, c.pxX), [c.pxXb], [c.X2b])

                    def s_acc(c):
                        n = c.n
                        if s >= c.nsteps:
                            return
                        pp, ppb = self.ps_mm.next()
                        for h in range(8):
                            self.mm(pp[:n, h * 64:h * 64 + n], c.Y2[:n, h, 0:n], c.Pbf[:n, h, 0:n], True, True, [c.Y2b, c.Pbfb], [ppb], skip=True)
                        self.tt("dve", c.Pm[:n, :, 0:n], c.Pm[:n, :, 0:n], v3(c, pp), ALU.add, [c.Pmb, ppb], [c.Pmb])
                        c.Pbf, c.Pbfb = r_L.next() if c.lastst else r_B.next()
                        self.cp("act", c.Pbf[:n, :, 0:n], c.Pm[:n, :, 0:n], [c.Pmb], [c.Pbfb])
                        c.Y, c.Yb = c.Y2, c.Y2b
                        if not c.lastst:
                            c.X, c.Xb = c.X2, c.X2b
                    return [s_sq, s_acc]

                def s_tok(c):
                    n, off = c.n, c.off
                    self.tt("dve", c.sc[:n, 0:8], c.sc[:n, 16:24], c.Bp, ALU.mult, [c.scb, gbkb], [c.scb])
                    self.tt("dve", c.sc[:n, 24:32], c.Gp, gb[:n, :, off + n - 1], ALU.subtract, [gbkb, gbb], [c.scb])
                    self.act(c.sc[:n, 8:16], c.sc[:n, 24:32], AF.Exp, [c.scb], [c.scb], scale=-1.0)
                    tpK, tpKb = self.ps_tp.next()
                    tpV, tpVb = self.ps_tp.next()
                    tKv = tpK[:, :].bitcast(BF16)
                    tVv = tpV[:, :].bitcast(BF16)
                    for h in range(8):
                        self.tr(tKv[:n, h * 128:(h + 1) * 128], c.Kt[:, h, :], self.identb[:, :], [qkvb, self.identb_b], [tpKb])
                    for h in range(8):
                        self.tr(tVv[:n, h * 128:(h + 1) * 128], c.Vt[:, h, :], self.identb[:, :], [qkvb, self.identb_b], [tpVb])
                    c.kbg, c.kbgb = r_B2.next()
                    self.tt("dve", c.kbg[:n, :, :], t3(c, tKv), bcs(c, c.sc[:n, 0:8]), ALU.mult, [tpKb, c.scb], [c.kbgb])
                    c.kdc, c.kdcb = r_B2.next()
                    self.tt("dve", c.kdc[:n, :, :], t3(c, tKv), bcs(c, c.sc[:n, 8:16]), ALU.mult, [tpKb, c.scb], [c.kdcb])
                    c.vb_, c.vbb_ = r_B2.next()
                    self.tt("dve", c.vb_[:n, :, :], t3(c, tVv), bcs(c, c.Bp), ALU.mult, [tpVb, gbkb], [c.vbb_])
                    c.qdT, c.qdTb = r_L.next()
                    self.tt("dve", c.qdT[:, :, 0:n], c.Qt, c.eG[:, :, 0:n], ALU.mult, [qkvb, c.eGb], [c.qdTb])
                    c.paQ, c.paQb = self.ps_mm.next()
                    for h in range(8):
                        self.mm(c.paQ[:n, h * 64:h * 64 + n], c.Kt[:, h, :], c.Qt[:, h, :], True, True, [qkvb], [c.paQb], skip=True)
                    c.QK, c.QKb = r_L.next()
                    self.tt("dve", c.QK[:n, :, 0:n], v3(c, c.paQ), c.D1i[:n, :, 0:n], ALU.mult, [c.paQb, c.D1ib], [c.QKb])

                def s_wd(c):
                    n = c.n
                    pw, pwb = self.ps_mm.next()
                    for h in range(8):
                        self.mm(pw[:, h * 64:h * 64 + n], c.kbg[:n, h, :], c.Pbf[:n, h, 0:n], True, True, [c.kbgb, c.Pbfb], [pwb], skip=True)
                    c.wdT, c.wdTb = r_L.next()
                    self.act(c.wdT[:, :, 0:n], pw[:, :].rearrange("p (h c) -> p h c", h=8)[:, :, 0:n], AF.Copy, [pwb], [c.wdTb],
                             scale=-1.0)

                stages = [s_bcast, s_gram, s_mask]
                for s in range(5):
                    stages += mk_step(s)
                stages += [s_tok, s_wd]
                for f in stages:
                    for c in cks:
                        f(c)

                for c in cks:
                    n, off, t0 = c.n, c.off, c.t0
                    pv = [self.ps_acc.next(), self.ps_acc.next()]
                    for h in range(8):
                        pt_, ptb_ = pv[h // 4]
                        o_ = pt_[:n, (h % 4) * 128:(h % 4 + 1) * 128]
                        self.mm(o_, c.Pbf[:n, h, 0:n], c.vb_[:n, h, :], True, False, [c.Pbfb, c.vbb_], [ptb_], skip=True)
                        self.mm(o_, c.wdT[:, h, 0:n], Sbf[:, h, :], False, True, [c.wdTb, Sbfb], [ptb_], skip=True)
                    vr, vrb = r_B2.next()
                    for hb in range(2):
                        self.cp("act", vr[:n, hb * 4:(hb + 1) * 4, :], pv[hb][0][:n, :].rearrange("p (h c) -> p h c", h=4),
                                [pv[hb][1]], [vrb])
                    self.tt("dve", S[:, :, :], S[:, :, :], c.eG[:, :, n - 1:n].to_broadcast([128, 8, 128]), ALU.mult,
                            [Sb, c.eGb], [Sb])
                    po = [self.ps_acc.next(), self.ps_acc.next()] if False else [self.ps_mm.next(), self.ps_mm.next()]
                    for h in range(8):
                        pt_, ptb_ = po[h // 4]
                        o_ = pt_[:n, (h % 4) * 128:(h % 4 + 1) * 128]
                        self.mm(o_, c.qdT[:, h, 0:n], Sbf[:, h, :], True, False, [c.qdTb, Sbfb], [ptb_], skip=True)
                        self.mm(o_, c.QK[:n, h, 0:n], vr[:n, h, :], False, True, [c.QKb, vrb], [ptb_], skip=True)
                    pu = [self.ps_mm.next(), self.ps_mm.next()]
                    for h in range(8):
                        pt_, ptb_ = pu[h // 4]
                        self.mm(pt_[:, (h % 4) * 128:(h % 4 + 1) * 128], c.kdc[:n, h, :], vr[:n, h, :], True, True,
                                [c.kdcb, vrb], [ptb_], skip=True)
                    for hb in range(2):
                        self.tt("dve", S[:, hb * 4:(hb + 1) * 4, :], S[:, hb * 4:(hb + 1) * 4, :],
                                pu[hb][0][:, :].rearrange("p (h c) -> p h c", h=4), ALU.add, [Sb, pu[hb][1]], [Sb])
                    self.cp("act", Sbf[:, :, :], S[:, :, :], [Sb], [Sbfb])
                    sq, sqb = r_sq.next()
                    for hb in range(2):
                        self.act(sq[:n, hb * 4:(hb + 1) * 4, :], po[hb][0][:n, :].rearrange("p (h c) -> p h c", h=4),
                                 AF.Square, [po[hb][1]], [sqb])
                    sc, scb = c.sc, c.scb
                    P.op("dve", lambda e, sc=sc, sq=sq, n=n: e.tensor_reduce(out=sc[:n, 32:40], in_=sq[:n, :, :], axis=AX.X,
                                                                         op=ALU.add), [sqb], [scb])
                    self.rsqrt(sc[:n, 32:40], sc[:n, 32:40], [scb], [scb], scale=1.0 / 128, eps=NORM_EPS)
                    ot, otb = r_B2.next()
                    for hb in range(2):
                        self.tt("dve", ot[:n, hb * 4:(hb + 1) * 4, :], po[hb][0][:n, :].rearrange("p (h c) -> p h c", h=4),
                                sc[:n, 32 + hb * 4:36 + hb * 4].unsqueeze(2).to_broadcast([n, 4, 128]), ALU.mult,
                                [po[hb][1], scb], [otb])
                    b = self.buf(("gdn_og", g.gi, t0))
                    ogw.append(b)
                    self.dma(dr["OG_" + nm][t0:t0 + n, 0:D], ot[:n, :, :].rearrange("p h c -> p (h c)"), [otb], [b])
            self.dma(dr["gdn_state_" + nm].rearrange("h d v -> d h v"), S[:, :, :], [Sb], [self.buf(("gdn_state", g.gi))])
        if os.environ.get("GDN_STOP") == "B":
            return
        P.barrier(); A.reset()
        r_gz = Ring(self, "rgz", [128, D], F32, 2, arena=A)
        nw, nwb = A.alloc([128, D], F32), Buf()
        for h in range(8):
            self.dma(nw[:, h * 128:(h + 1) * 128], dr["gdn_norm_w"][0:1, :].to_broadcast([128, 128]), [], [nwb])
        items = []
        for g in self.groups:
            def mkprep(g, t0, n):
                def prep():
                    nm = g.name
                    og, ogb = self.r_h.next()
                    self.dma(og[:n, 0:D], dr["OG_" + nm][t0:t0 + n, 0:D], ogws[g.gi], [ogb])
                    zs, zsb = self.r_h.next()
                    self.dma(zs[:n, 0:D], dr["ZS_" + nm][t0:t0 + n, 0:D], wbs[g.gi], [zsb])
                    gz, gzb = r_gz.next()
                    self.tt("dve", gz[:n, :], og[:n, 0:D], nw[:n, :], ALU.mult, [ogb, nwb], [gzb])
                    self.tt("dve", og[:n, 0:D], gz[:n, :], zs[:n, 0:D], ALU.mult, [gzb, zsb], [ogb])
                    return og, ogb
                return prep
            items += [(g, t0, n, mkprep(g, t0, n)) for (t0, n) in g.tiles]
        self.pass_c_run(items, first, last, 8)

    def layer_dummy(self, L, first, last):
        self.layer_mod(L)
        self.load_w(self.W, self.Wb, self.dr["fox_w_in"], 4112)
        self.load_w(self.WO, self.WOb, self.dr["fox_w_out"], D)
        for g in self.groups:
            for blk in g.blocks:
                uT, uTb, nb = self.make_uT(g, blk, first)
                off = 0
                for (t0, n) in blk:
                    og, ogb = self.r_h.next()
                    for cg in range(2):
                        ps, pb = self.proj_tm(uT, uTb, off, n, cg * 512, (cg + 1) * 512)
                        self.cp("act", og[:n, cg * 512:(cg + 1) * 512], ps[:n, :], [pb], [ogb])
                    self.pass_c_tile(g, t0, n, first, last, og, ogb, 8)
                    off += n


def build(cfg, mode="full"):
    k = K(cfg)
    k.setup()
    nl = len(cfg.layers)
    for i, L in enumerate(cfg.layers):
        if mode == "dummy":
            k.layer_dummy(L, i == 0, i == nl - 1)
        else:
            k.layer(L, i == 0, i == nl - 1)
    k.P.emit()
    return k


def t5_onehot():
    import jax
    import jax.numpy as jnp
    with jax.default_device(jax.devices("cpu")[0]):
        rel = jnp.arange(384, dtype=jnp.int32) - 255
        nb = 16
        max_exact = 8
        ret = jnp.where(rel > 0, nb, 0)
        n = jnp.abs(rel)
        large = max_exact + (jnp.log(jnp.maximum(n, 1).astype(jnp.float32) / max_exact)
                             / math.log(128 / max_exact) * (nb - max_exact)).astype(jnp.int32)
        large = jnp.minimum(large, nb - 1)
        bucket = np.asarray(ret + jnp.where(n < max_exact, n, large))
    oh = np.zeros((32, 384), np.float32)
    oh[bucket, np.arange(384)] = 1.0
    return oh


def ret_perm():
    one = np.concatenate([np.arange(0, 256, 2), np.arange(1, 256, 2)])
    return np.concatenate([h * 256 + one for h in range(4)])


_RC = {}


def ret_consts(cfg):
    key = (cfg.T, cfg.TS, cfg.PAST)
    if key in _RC:
        return _RC[key]
    import jax
    import jax.numpy as jnp
    out = {}
    with jax.default_device(jax.devices("cpu")[0]):
        inv = 10000.0 ** (-jnp.arange(0, 256, 2, dtype=jnp.float32) / 256)
        for nm, start, T in (("p", 0, cfg.T), ("s", cfg.PAST, cfg.TS)):
            pos = (start + jnp.arange(T)).astype(jnp.float32)
            ang = pos[:, None] * inv[None, :]
            out["rot_cos_" + nm] = np.ascontiguousarray(np.asarray(jnp.cos(ang)).T)
            out["rot_sin_" + nm] = np.ascontiguousarray(np.asarray(jnp.sin(ang)).T)
        lg = np.asarray(jnp.log1p(-jnp.power(2.0, -5.0 - jnp.arange(4, dtype=jnp.float32))), np.float64)
    i = np.arange(128)
    dm = np.zeros((128, 4, 128), np.float32)
    qd = np.zeros((128, 4, 128), np.float32)
    kd = np.zeros((128, 8), np.float32)
    for h in range(4):
        d = i[None, :] - i[:, None]
        dm[:, h, :] = np.where(d >= 0, np.exp(lg[h] * np.maximum(d, 0)), 0.0)
        qd[:, h, :] = np.exp(lg[h] * (i + 1.0))[None, :]
        kd[:, h] = np.exp(lg[h] * (127.0 - i))
        kd[:, 4 + h] = np.exp(lg[h] * (cfg.TS - 1.0 - i))
    out["ret_dmaskT"], out["ret_qdec"], out["ret_kdec"] = dm, qd, kd
    _RC[key] = out
    return out


def core_inputs(cfg, inp, b):
    T, PAST = cfg.T, cfg.PAST
    f = np.ascontiguousarray
    cT = np.stack([inp["c_prompt"][b], inp["c_sample"][b]], -1).reshape(8, 128, 2).transpose(1, 0, 2)
    m = {
        "xp": f(inp["x_prompt"][b, :T]), "xs": f(inp["x_sample"][b]), "cT": f(cT),
        "ada_w": inp["ada_w"], "ada_b": inp["ada_b"], "ln_g": inp["ln_g"], "ln_b": inp["ln_b"],
        "ident": np.eye(128, dtype=np.float32),
        "cmask": np.where(np.arange(128)[:, None] > np.arange(128)[None, :], -30000.0, 0.0).astype(np.float32),
        "fox_w_in": inp["fox_w_in"], "fox_b_f": f(inp["fox_b_f"].reshape(16, 1)), "fox_w_out": inp["fox_w_out"],
        "cache_fox_k": f(inp["cache_fox_k"][b, :PAST].reshape(PAST, -1)),
        "cache_fox_v": f(inp["cache_fox_v"][b, :PAST].reshape(PAST, -1)),
        "cache_fox_logf": f(inp["cache_fox_logf"][b, :PAST]),
        "diff_w_in": inp["diff_w_in"], "diff_w_out": inp["diff_w_out"], "rel_bias_table": inp["rel_bias_table"],
        "diff_lam_q1": f(inp["diff_lam_q1"].reshape(1, 64)), "diff_lam_k1": f(inp["diff_lam_k1"].reshape(1, 64)),
        "diff_lam_q2": f(inp["diff_lam_q2"].reshape(1, 64)), "diff_lam_k2": f(inp["diff_lam_k2"].reshape(1, 64)),
        "diff_subln_w": f(inp["diff_subln_w"].reshape(1, 128)), "t5_onehot": t5_onehot(),
        "antiident": np.ascontiguousarray(np.eye(128, dtype=np.float32)[::-1]),
        "dmask": np.where((np.arange(128)[:, None] >= 64) & (np.arange(128)[None, :] < 64), -30000.0, 0.0).astype(np.float32),
        "cache_diff_k": f(inp["cache_diff_k"][b, :PAST].reshape(PAST, -1)),
        "cache_diff_v": f(inp["cache_diff_v"][b, :PAST].reshape(PAST, -1)),
    }
    m.update(ret_consts(cfg))
    i = np.arange(128)
    msk = np.zeros((128, 3, 128), np.float32)
    msk[:, 0, :] = np.where(i[None, :] > i[:, None], -1.0, 0.0)
    msk[:, 1, :] = np.where(i[None, :] < i[:, None], -1.0, 0.0)
    msk[:, 2, :] = np.where(i[None, :] >= i[:, None], 1.0, 0.0)
    sel = np.zeros((8, 8, 128), np.float32)
    for h in range(8):
        sel[h, h, :] = 1.0
    m.update({
        "gdn_w_in": inp["gdn_w_in"], "gdn_w_out": inp["gdn_w_out"],
        "gdn_convw": f(inp["gdn_conv_w"].reshape(4, 24, 128).transpose(2, 1, 0)),
        "gdn_conv_in": f(inp["state_gdn_conv"][b].reshape(3, 24, 128).transpose(2, 1, 0)),
        "gdn_a_log": f(inp["gdn_a_log"].reshape(8, 1)), "gdn_dt_bias": f(inp["gdn_dt_bias"].reshape(8, 1)),
        "gdn_norm_w": f(inp["gdn_norm_w"].reshape(1, 128)), "state_gdn": f(inp["state_gdn"][b]),
        "gdn_masks": msk, "gdn_sel": sel,
    })
    perm = ret_perm()
    w = inp["ret_w_in"]
    m["ret_w_in"] = f(np.concatenate([w[:, 0:D][:, perm], w[:, D:2 * D][:, perm], w[:, 2 * D:]], axis=1))
    m["ret_w_out"] = inp["ret_w_out"]
    m["ret_gn_w"] = f(inp["ret_gn_w"].reshape(1, -1))
    m["state_ret"] = f(inp["state_ret"][b][:, perm[:256], :])
    return m


OUT_NAMES = ("y", "gdn_state_", "gdn_conv_", "fox_k_", "fox_v_", "fox_logf_", "diff_k_", "diff_v_", "ret_state_")


def gather(cfg, results):
    inv = np.argsort(ret_perm()[:256])
    outs = []
    for nm, T in (("p", cfg.T), ("s", cfg.TS)):
        g = {}
        for base in OUT_NAMES:
            key = base + nm
            g[base] = np.stack([np.asarray(r[key]) for r in results], 0)
        B = len(results)
        outs.append([
            g["y"],
            g["gdn_state_"],
            g["gdn_conv_"],
            g["fox_k_"].reshape(B, T, 16, 64),
            g["fox_v_"].reshape(B, T, 16, 64),
            g["fox_logf_"],
            g["diff_k_"].reshape(B, T, 8, 2, 64),
            g["diff_v_"].reshape(B, T, 8, 128),
            np.ascontiguousarray(g["ret_state_"][:, :, inv, :]),
        ])
    p, s = outs
    return (p[0], s[0]) + tuple(p[1:]) + tuple(s[1:])


def kernel(**inputs):
    inputs = {k: np.asarray(v) for k, v in inputs.items()}
    cfg = Cfg()
    k = build(cfg)
    in_maps = [core_inputs(cfg, inputs, b) for b in range(NCORES)]
    res = run_bass_kernel_spmd(k.nc, in_maps, core_ids=list(range(NCORES)))
    return gather(cfg, res.results)
```

```python
import math
from contextlib import ExitStack

import numpy as np
import ml_dtypes
import concourse.bass as bass
import concourse.mybir as mybir
from concourse.bass_utils import run_bass_kernel_spmd

F32 = mybir.dt.float32
BF16 = mybir.dt.bfloat16
AF = mybir.ActivationFunctionType
ALU = mybir.AluOpType
AX = mybir.AxisListType

D = 1024
DEPTH = 4
ALPHA = (2.0 * DEPTH) ** 0.25
LN_EPS = 1e-5
NORM_EPS = 1e-6
NCORES = 8


class Buf:
    __slots__ = ("w", "r", "x")

    def __init__(self, x=False):
        self.w = None
        self.r = {}
        self.x = x


class Prog:
    ENGS = ("pe", "act", "dve", "pool", "sp")
    NO_SELF_SYNC = ("pe",)

    def __init__(self, nc, es, ndma=28):
        self.nc = nc
        self.sem = {e: es.enter_context(nc.semaphore("s_" + e)) for e in self.ENGS}
        self.dsem = [es.enter_context(nc.semaphore("d%d" % i)) for i in range(ndma)]
        self.cnt = {e: 0 for e in self.ENGS}
        self.dcnt = [0] * ndma
        self.drr = 0
        self.drr_sw = 0
        self.NSW_BASE = ndma - 8
        self.seen = {e: {} for e in self.ENGS}
        self.q = {e: [] for e in self.ENGS}

    def _collect(self, reads, writes, e=None):
        deps = {}
        for b in reads:
            if b.w is not None and deps.get(b.w[0], 0) < b.w[1]:
                deps[b.w[0]] = b.w[1]
            if b.x:
                for k, v in b.r.items():
                    if k != e and deps.get(k, 0) < v:
                        deps[k] = v
        for b in writes:
            if b.w is not None and deps.get(b.w[0], 0) < b.w[1]:
                deps[b.w[0]] = b.w[1]
            for k, v in b.r.items():
                if deps.get(k, 0) < v:
                    deps[k] = v
        return deps

    def _waits(self, e, deps):
        out = []
        seen = self.seen[e]
        for k, v in deps.items():
            if k == e and e in self.NO_SELF_SYNC:
                continue
            if seen.get(k, 0) < v:
                seen[k] = v
                out.append((k, v))
        return out

    def _mark(self, tok, reads, writes):
        k, v = tok
        for b in reads:
            if b.r.get(k, 0) < v:
                b.r[k] = v
        for b in writes:
            b.w = tok
            b.r = {}

    def op(self, e, fn, reads=(), writes=()):
        waits = self._waits(e, self._collect(reads, writes, e))
        self.cnt[e] += 1
        self.q[e].append((waits, fn, None))
        self._mark((e, self.cnt[e]), reads, writes)

    def dma(self, e, fn, reads=(), writes=()):
        if e == "pool":
            s = self.NSW_BASE + self.drr_sw
            self.drr_sw = (self.drr_sw + 1) % (len(self.dsem) - self.NSW_BASE)
        else:
            s = self.drr
            self.drr = (s + 1) % self.NSW_BASE
        deps = self._collect(reads, writes, e)
        k = ("d", s)
        if self.dcnt[s] and deps.get(k, 0) < self.dcnt[s]:
            deps[k] = self.dcnt[s]
        waits = self._waits(e, deps)
        self.dcnt[s] += 16
        self.q[e].append((waits, fn, s))
        self._mark((k, self.dcnt[s]), reads, writes)

    def barrier(self):
        deps = {e: c for e, c in self.cnt.items() if c}
        for s, c in enumerate(self.dcnt):
            if c:
                deps[("d", s)] = c
        for e in self.ENGS:
            waits = self._waits(e, dict(deps))
            if waits:
                self.q[e].append((waits, None, None))

    def _semof(self, k):
        return self.sem[k] if isinstance(k, str) else self.dsem[k[1]]

    def emit(self):
        nc = self.nc
        with nc.Block() as block:
            regs = dict(sp=block.sync, pe=block.tensor, act=block.scalar, dve=block.vector, pool=block.gpsimd)
            for e in self.ENGS:
                def body(eng, e=e):
                    for waits, fn, ds in self.q[e]:
                        ws = [(self._semof(k), v) for k, v in waits]
                        if fn is None:
                            for sm, v in ws:
                                eng.wait_ge(sm, v)
                            continue
                        if ds is None and ws:
                            for sm, v in ws[:-1]:
                                eng.wait_ge(sm, v)
                            ins = fn(eng)
                            ins._wait_ge(ws[-1][0], ws[-1][1])
                        else:
                            for sm, v in ws:
                                eng.wait_ge(sm, v)
                            ins = fn(eng)
                        if ds is None:
                            ins.then_inc(self.sem[e], 1)
                        else:
                            ins.then_inc(self.dsem[ds], 16)
                    if e == "sp":
                        for s, c in enumerate(self.dcnt):
                            if c:
                                eng.wait_ge(self.dsem[s], c)
                regs[e](body)


class Ring:
    def __init__(self, K, name, shape, dtype, n, psum=False, arena=None):
        self.t = []
        for i in range(n):
            if arena is not None:
                t = arena.alloc(shape, dtype)
            else:
                f = K.nc.psum_tensor if psum else K.nc.sbuf_tensor
                t = K.es.enter_context(f("%s%d" % (name, i), shape, dtype))
            self.t.append((t, Buf(x=psum)))
        self.i = 0

    def next(self):
        r = self.t[self.i % len(self.t)]
        self.i += 1
        return r


class Arena:
    def __init__(self, K, nbytes, base=None):
        self.t = K.es.enter_context(K.nc.sbuf_tensor("arena", [128, nbytes // 4], F32)) if base is None else base
        self.nbytes = nbytes
        self.off = 0

    def reset(self):
        self.off = 0

    def alloc(self, shape, dtype):
        esz = 2 if dtype == BF16 else 4
        n = 1
        for s in shape[1:]:
            n *= s
        nb = (n * esz + 3) // 4 * 4
        assert self.off + nb <= self.nbytes, "arena overflow: need %d have %d" % (nb, self.nbytes - self.off)
        v = self.t[0:shape[0], self.off // 4:(self.off + nb) // 4]
        self.off += nb
        if dtype == BF16:
            v = v.bitcast(BF16)[:, 0:n]
        if len(shape) == 3:
            v = v.rearrange("p (a b) -> p a b", a=shape[1])
        elif len(shape) == 4:
            v = v.rearrange("p (a b c) -> p a b c", a=shape[1], b=shape[2])
        return v


class Cfg:
    def __init__(self, T=4096, TS=32, PAST=2048, layers=(0, 1, 2, 3)):
        self.T, self.TS, self.PAST, self.layers = T, TS, PAST, tuple(layers)


class Group:
    def __init__(self, gi, name, T, past):
        self.gi, self.name, self.T, self.past = gi, name, T, past
        self.S = past + T
        self.tiles = [(t0, min(128, T - t0)) for t0 in range(0, T, 128)]
        self.blocks = [self.tiles[i:i + 4] for i in range(0, len(self.tiles), 4)]
        self.ktiles = [(k0, 128) for k0 in range(0, past, 128)] + [(past + t0, n) for t0, n in self.tiles]


class K:
    def __init__(self, cfg):
        self.cfg = cfg
        self.nc = nc = bass.Bass("TRN2", target_bir_lowering=False)
        self.es = ExitStack()
        self.P = Prog(nc, self.es)
        self.bufs = {}
        self.dr = {}
        self.dh = {}
        self.all_groups = [Group(0, "p", cfg.T, 0), Group(1, "s", cfg.TS, cfg.PAST)]
        import os
        gs = os.environ.get("KGROUPS", "01")
        self.groups = [g for g in self.all_groups if str(g.gi) in gs]

    def buf(self, key):
        b = self.bufs.get(key)
        if b is None:
            b = self.bufs[key] = Buf()
        return b

    def din(self, name, shape, dtype=F32):
        t = self.nc.dram_tensor(name, list(shape), dtype, kind="ExternalInput")
        self.dh[name] = t
        self.dr[name] = t.ap()
        return self.dr[name]

    def dout(self, name, shape, dtype=F32):
        t = self.nc.dram_tensor(name, list(shape), dtype, kind="ExternalOutput")
        self.dr[name] = t.ap()
        return self.dr[name]

    def dscr(self, name, shape, dtype):
        t = self.nc.dram_tensor(name, list(shape), dtype)
        self.dh[name] = t
        self.dr[name] = t.ap()
        return self.dr[name]

    def sb(self, name, shape, dtype):
        t = self.es.enter_context(self.nc.sbuf_tensor(name, list(shape), dtype))
        return t, Buf()

    def dma(self, out, in_, rd, wr, eng="sp"):
        self.P.dma(eng, lambda e: e.dma_start(out=out, in_=in_), rd, wr)

    def mm(self, out, lhsT, rhs, start, stop, rd, wr, skip=False):
        self.P.op("pe", lambda e: e.matmul(out, lhsT, rhs, start=start, stop=stop, skip_group_check=skip), rd, wr)

    def tr(self, out, in_, ident, rd, wr):
        self.P.op("pe", lambda e: e.transpose(out, in_, ident), rd, wr)

    def act(self, out, in_, func, rd, wr, bias=0.0, scale=1.0, accum=None):
        if accum is None:
            self.P.op("act", lambda e: e.activation(out=out, in_=in_, func=func, bias=bias, scale=scale), rd, wr)
        else:
            self.P.op("act", lambda e: e.activation(out=out, in_=in_, func=func, bias=bias, scale=scale,
                                                     accum_out=accum), rd, wr)

    def tt(self, eng, out, in0, in1, op, rd, wr):
        o = self.nc.vector if eng == "dve" else self.nc.gpsimd
        self.P.op(eng, lambda e: e.tensor_tensor(out=out, in0=in0, in1=in1, op=op), rd, wr)

    def ts(self, eng, out, in0, s1, s2, op0, op1, rd, wr, accum=None):
        if op1 is None:
            self.P.op(eng, lambda e: e.tensor_scalar(out=out, in0=in0, scalar1=s1, scalar2=None, op0=op0), rd, wr)
        elif accum is None:
            self.P.op(eng, lambda e: e.tensor_scalar(out=out, in0=in0, scalar1=s1, scalar2=s2, op0=op0, op1=op1),
                      rd, wr)
        else:
            self.P.op(eng, lambda e: e.tensor_scalar(out=out, in0=in0, scalar1=s1, scalar2=s2, op0=op0, op1=op1,
                                                     accum_out=accum), rd, wr)

    def stt(self, out, in0, scalar, in1, op0, op1, rd, wr):
        self.P.op("dve", lambda e: e.scalar_tensor_tensor(out=out, in0=in0, scalar=scalar, in1=in1, op0=op0, op1=op1),
                  rd, wr)

    def cp(self, eng, out, in_, rd, wr):
        if eng == "act":
            self.P.op("act", lambda e: e.copy(out=out, in_=in_), rd, wr)
        else:
            self.P.op(eng, lambda e: e.tensor_copy(out=out, in_=in_), rd, wr)

    def rsqrt(self, out, in_, rd, wr, scale=1.0, eps=0.0):
        self.act(out, in_, AF.Ln, rd, wr, bias=eps, scale=scale)
        self.act(out, out, AF.Exp, wr, wr, scale=-0.5)

    def recip(self, out, in_, rd, wr):
        self.P.op("dve", lambda e: e.reciprocal(out=out, in_=in_), rd, wr)

    def memset(self, eng, ap, val, wr):
        self.P.op(eng, lambda e: e.memset(ap, val), (), wr)

    def set_psum(self, n_mm, n_acc, n_tp):
        def mk(lst):
            r = Ring.__new__(Ring)
            r.t = lst
            r.i = 0
            return r
        assert n_mm + n_acc + n_tp == 8
        self.ps_mm = mk(self.psb[0:n_mm])
        self.ps_acc = mk(self.psb[n_mm:n_mm + n_acc])
        self.ps_tp = mk(self.psb[n_mm + n_acc:8])

    def setup(self):
        cfg = self.cfg
        T, TS, PAST = cfg.T, cfg.TS, cfg.PAST
        din, dout, dscr = self.din, self.dout, self.dscr
        din("xp", [T, D]); din("xs", [TS, D]); din("cT", [128, 8, 2])
        din("ada_w", [4, D, 3 * D]); din("ada_b", [4, 3 * D]); din("ln_g", [4, D]); din("ln_b", [4, D])
        din("ident", [128, 128]); din("cmask", [128, 128])
        din("fox_w_in", [D, 4112]); din("fox_b_f", [16, 1]); din("fox_w_out", [D, D])
        din("cache_fox_k", [PAST, D]); din("cache_fox_v", [PAST, D]); din("cache_fox_logf", [PAST, 16])
        din("diff_w_in", [D, 4096]); din("diff_w_out", [D, D]); din("rel_bias_table", [32, 8])
        for nmv in ("diff_lam_q1", "diff_lam_k1", "diff_lam_q2", "diff_lam_k2"):
            din(nmv, [1, 64])
        din("diff_subln_w", [1, 128]); din("t5_onehot", [32, 384]); din("antiident", [128, 128]); din("dmask", [128, 128])
        din("cache_diff_k", [PAST, D]); din("cache_diff_v", [PAST, D])
        dscr("lam_scr", [2, 8], F32); dscr("fr_scr", [16, 384], F32)
        din("ret_w_in", [D, 6144]); din("ret_w_out", [2 * D, D]); din("ret_gn_w", [1, 2 * D])
        din("state_ret", [4, 256, 512]); din("ret_dmaskT", [128, 4, 128]); din("ret_qdec", [128, 4, 128])
        din("ret_kdec", [128, 8])
        for g in self.all_groups:
            din("rot_cos_" + g.name, [128, g.T]); din("rot_sin_" + g.name, [128, g.T])
            dscr("QR_" + g.name, [8, 128, g.T], BF16); dscr("KR_" + g.name, [8, 128, g.T], BF16)
            dscr("OG_" + g.name + "v", [g.T, 2 * D], BF16)
            dout("ret_state_" + g.name, [4, 256, 512])
        din("gdn_w_in", [D, 4112]); din("gdn_w_out", [D, D]); din("gdn_convw", [128, 24, 4]); din("gdn_conv_in", [128, 24, 3])
        din("gdn_a_log", [8, 1]); din("gdn_dt_bias", [8, 1]); din("gdn_norm_w", [1, 128]); din("state_gdn", [8, 128, 128])
        din("gdn_masks", [128, 3, 128]); din("gdn_sel", [8, 8, 128])
        for g in self.all_groups:
            for nmv in ("QG_", "KG_", "VG_"):
                dscr(nmv + g.name, [8, 128, g.T], BF16)
            dscr("BTb_" + g.name, [8, g.T], BF16); dscr("GT_" + g.name, [8, g.T], F32); dscr("GB_" + g.name, [g.T, 16], F32)
            dout("gdn_state_" + g.name, [8, 128, 128]); dout("gdn_conv_" + g.name, [3, 3 * D])
        dout("yp", [T, D]); dout("ys", [TS, D])
        for g in self.all_groups:
            n = g.name
            dout("fox_k_" + n, [g.T, D]); dout("fox_v_" + n, [g.T, D]); dout("fox_logf_" + n, [g.T, 16])
            dout("diff_k_" + n, [g.T, D]); dout("diff_v_" + n, [g.T, D])
            dscr("OGF_" + n, [g.T, 2 * D], F32)
        dscr("xres_p", [T, D], F32); dscr("xres_s", [TS, D], F32)
        for g in self.all_groups:
            n = g.name
            dscr("QT_" + n, [16, 70, g.T], BF16); dscr("KT_" + n, [16, 70, g.S], BF16)
            dscr("VA_" + n, [g.S, 1040], BF16)
            dscr("ZS_" + n, [g.T, 2 * D], BF16); dscr("OG_" + n, [g.T, 2 * D], BF16)

        self.W, self.Wb = self.sb("W_in", [128, 8, 4112], BF16)
        self.Wbs = [[Buf() for _ in range(8)] for _ in range(9)]
        self.WO, self.WOb = self.sb("W_out", [128, 8, D], BF16)
        self.identf, self.identf_b = self.sb("identf", [128, 128], F32)
        self.identb, self.identb_b = self.sb("identb", [128, 128], BF16)
        self.cmaskb, self.cmaskb_b = self.sb("cmaskb", [128, 128], BF16)
        self.cTs, self.cTs_b = self.sb("cTs", [128, 8, 2], F32)
        self.crep = [self.sb("crep%d" % g, [128, 8, 128], BF16) for g in range(2)]
        self.mod = [self.sb("mod_%d" % k, [128, D], F32) for k in range(3)]
        dscr("mods", [3, D], F32)
        self.lng, self.lng_b = self.sb("lng", [128, D], F32)
        self.lnb, self.lnb_b = self.sb("lnb", [128, D], F32)
        self.psb = Ring(self, "psb", [128, 512], F32, 8, psum=True).t
        self.set_psum(4, 2, 2)
        self.r_x = Ring(self, "rx", [128, D], F32, 2)
        self.r_f = Ring(self, "rf", [128, D], F32, 2)
        self.r_ub = Ring(self, "rub", [128, D], BF16, 2)
        self.r_uT = Ring(self, "ruT", [128, 8, 512], BF16, 2)
        self.r_h = Ring(self, "rh", [128, 1040], BF16, 3)
        self.r_small = Ring(self, "rsm", [128, 16], F32, 8)
        self.arena = Arena(self, 58 * 1024)
        wbytes = 8 * 4112 * 2
        self.arenaW = Arena(self, wbytes, base=self.W[:, :, :].rearrange("p a b -> p (a b)").bitcast(F32))

        self.dma(self.identf[:], self.dr["ident"][:, :], [], [self.identf_b])
        self.cp("dve", self.identb[:], self.identf[:], [self.identf_b], [self.identb_b])
        t, b = self.r_f.next()
        self.dma(t[:, 0:128], self.dr["cmask"][:, :], [], [b])
        self.cp("dve", self.cmaskb[:], t[:, 0:128], [b], [self.cmaskb_b])
        self.dma(self.cTs[:], self.dr["cT"][:, :, :], [], [self.cTs_b])
        self.act(self.cTs[:], self.cTs[:], AF.Silu, [self.cTs_b], [self.cTs_b])
        for g in range(2):
            t, b = self.crep[g]
            self.cp("dve", t[:], self.cTs[:, :, g:g + 1].to_broadcast([128, 8, 128]), [self.cTs_b], [b])

    def wdep(self, c0, c1, k):
        return [self.Wbs[cg][k] for cg in range(c0 // 512, (c1 - 1) // 512 + 1)]

    def load_w(self, dst, dstb, src, ncols, nk=8, c_off=0):
        if dstb is self.Wb:
            for a in range(0, ncols, 1024):
                b = min(ncols, a + 1024)
                for k in range(nk):
                    self.dma(dst[:, k, a:b], src[k * 128:(k + 1) * 128, c_off + a:c_off + b], [], self.wdep(a, b, k), eng="pool")
            return
        for k in range(nk):
            self.dma(dst[:, k, 0:ncols], src[k * 128:(k + 1) * 128, c_off:c_off + ncols], [], [dstb], eng="pool")

    def layer_mod(self, L):
        ada_w, ada_b = self.dr["ada_w"], self.dr["ada_b"]
        for cg in range(6):
            wt, wb = self.r_uT.next()
            self.load_w(wt, wb, ada_w[L], 512, c_off=cg * 512)
            bt_, bb = self.r_f.next()
            bt = bt_[:, 0:512]
            self.dma(bt[:, :], ada_b[L:L + 1, cg * 512:(cg + 1) * 512].to_broadcast([128, 512]), [], [bb])
            kind = cg // 2
            for g in range(2):
                ps, pb = self.ps_mm.next()
                ct, cb = self.crep[g]
                for k in range(8):
                    self.mm(ps[:, :], ct[:, k, :], wt[:, k, :], k == 0, k == 7, [cb, wb], [pb])
                if g == 0:
                    mt, mb = self.mod[kind]
                    dst = mt[:, (cg % 2) * 512:(cg % 2 + 1) * 512]
                else:
                    mt, mb = self.r_f.next()
                    dst = mt[:, 0:512]
                if kind == 0:
                    self.tt("dve", dst, ps[:, :], bt[:, :], ALU.add, [pb, bb], [mb])
                else:
                    self.stt(dst, ps[:, :], 1.0, bt[:, :], ALU.add, ALU.add, [pb, bb], [mb])
                if g == 1:
                    self.dma(self.dr["mods"][kind:kind + 1, (cg % 2) * 512:(cg % 2 + 1) * 512], mt[0:1, 0:512], [mb],
                             [self.buf(("mods", kind, cg % 2))])
        self.dma(self.lng[:], self.dr["ln_g"][L:L + 1, :].to_broadcast([128, D]), [], [self.lng_b])
        self.dma(self.lnb[:], self.dr["ln_b"][L:L + 1, :].to_broadcast([128, D]), [], [self.lnb_b])

    def modtile(self, g, kind, n):
        if g.gi == 0:
            return self.mod[kind]
        t, b = self.r_f.next()
        self.dma(t[:n, :], self.dr["mods"][kind:kind + 1, :].to_broadcast([n, D]),
                 [self.buf(("mods", kind, 0)), self.buf(("mods", kind, 1))], [b])
        return t, b

    def xsrc(self, g, first):
        if first:
            return self.dr["xp" if g.gi == 0 else "xs"], "xin%d" % g.gi
        return self.dr["xres_" + g.name], "xres%d" % g.gi

    def make_uT(self, g, blk, first):
        src, skey = self.xsrc(g, first)
        uT, uTb = self.r_uT.next()
        off = 0
        for (t0, n) in blk:
            sc, scb = self.modtile(g, 1, n)
            xt, xb = self.r_x.next()
            self.dma(xt[:n, :], src[t0:t0 + n, :], [self.buf((skey, t0))], [xb])
            ft, fb = self.r_f.next()
            self.tt("dve", ft[:n, :], xt[:n, :], sc[:n, :], ALU.mult, [xb, scb], [fb])
            sh, shb = self.modtile(g, 0, n)
            ub, ubb = self.r_ub.next()
            self.tt("dve", ub[:n, :], ft[:n, :], sh[:n, :], ALU.add, [fb, shb], [ubb])
            tp, tpb = self.ps_tp.next()
            tpv = tp[:, :].bitcast(BF16)
            for j in range(8):
                self.tr(tpv[:, j * 128:j * 128 + n], ub[:n, j * 128:(j + 1) * 128], self.identb[:n, :n],
                        [ubb, self.identb_b], [tpb])
            self.cp("act", uT[:, :, off:off + n],
                    tpv.rearrange("p (j t) -> p j t", j=8)[:, :, 0:n], [tpb], [uTb])
            off += n
        return uT, uTb, off

    def proj_tm(self, uT, uTb, off, n, c0, c1):
        ps, pb = self.ps_mm.next()
        for k in range(8):
            self.mm(ps[:n, 0:c1 - c0], uT[:, k, off:off + n], self.W[:, k, c0:c1], k == 0, k == 7,
                    [uTb] + self.wdep(c0, c1, k), [pb])
        return ps, pb

    def proj_fm(self, uT, uTb, nb, c0, c1):
        ps, pb = self.ps_mm.next()
        for k in range(8):
            self.mm(ps[:c1 - c0, 0:nb], self.W[:, k, c0:c1], uT[:, k, 0:nb], k == 0, k == 7, [uTb] + self.wdep(c0, c1, k), [pb])
        return ps, pb

    def pass_c_run(self, items, first, last, nk, wo=None):
        from types import SimpleNamespace as NS
        ctxs = [NS(g=g, t0=t0, n=n, prep=prep) for (g, t0, n, prep) in items]

        def stA(c):
            n = c.n
            ogz, ogzb = c.prep()
            c.oT, c.oTb = self.r_uT.next()
            for half in range((nk + 7) // 8):
                tp, tpb = self.ps_tp.next()
                tpv = tp[:, :].bitcast(BF16)
                for j in range(8):
                    jj = half * 8 + j
                    self.tr(tpv[:, j * 128:j * 128 + n], ogz[:n, jj * 128:(jj + 1) * 128], self.identb[:n, :n],
                            [ogzb, self.identb_b], [tpb])
                self.cp("act", c.oT[:, :, half * 128:half * 128 + n],
                        tpv.rearrange("p (j t) -> p j t", j=8)[:, :, 0:n], [tpb], [c.oTb])

        def stB(c):
            g, t0, n = c.g, c.t0, c.n
            src, skey = self.xsrc(g, first)
            hps = []
            for cg in range(2):
                ps, pb = self.ps_acc.next()
                for k in range(nk):
                    wt_, wtb_ = (self.WO[:, k, :], [self.WOb]) if wo is None else wo(k)
                    self.mm(ps[:n, :], c.oT[:, k % 8, (k // 8) * 128:(k // 8) * 128 + n],
                            wt_[:, cg * 512:(cg + 1) * 512], k == 0, k == nk - 1, [c.oTb] + wtb_, [pb])
                hps.append((ps, pb))
            xt, xb = self.r_x.next()
            self.dma(xt[:n, :], src[t0:t0 + n, :], [self.buf((skey, t0))], [xb])
            gt, gb = self.modtile(g, 2, n)
            c.rt, c.rb = self.r_f.next()
            rt, rb = c.rt, c.rb
            for cg in range(2):
                ps, pb = hps[cg]
                sl = slice(cg * 512, (cg + 1) * 512)
                self.tt("dve", rt[:n, sl], ps[:n, :], gt[:n, sl], ALU.mult, [pb, gb], [rb])
            self.stt(rt[:n, :], xt[:n, :], ALPHA, rt[:n, :], ALU.mult, ALU.add, [xb, rb], [rb])
            st, stb = self.r_small.next()
            for cg in range(2):
                self.P.op("dve", lambda e, cg=cg: e.bn_stats(out=st[:n, cg * 6:cg * 6 + 6],
                                                              in_=rt[:n, cg * 512:(cg + 1) * 512]), [rb], [stb])
            c.mv, c.mvb = self.r_small.next()
            mv, mvb = c.mv, c.mvb
            self.P.op("dve", lambda e: e.bn_aggr(out=mv[:n, 0:2], in_=st[:n, 0:12]), [stb], [mvb])
            self.rsqrt(mv[:n, 4:5], mv[:n, 1:2], [mvb], [mvb], eps=LN_EPS)
            self.stt(mv[:n, 5:6], mv[:n, 0:1], -1.0, mv[:n, 4:5], ALU.mult, ALU.mult, [mvb], [mvb])

        def stC(c):
            g, t0, n = c.g, c.t0, c.n
            dst, dkey = (self.dr["yp" if g.gi == 0 else "ys"], "y%d" % g.gi) if last else \
                (self.dr["xres_" + g.name], "xres%d" % g.gi)
            yt, yb = self.r_x.next()
            self.act(yt[:n, :], c.rt[:n, :], AF.Identity, [c.rb, c.mvb], [yb], bias=c.mv[:n, 5:6], scale=c.mv[:n, 4:5])
            self.tt("dve", yt[:n, :], yt[:n, :], self.lng[:n, :], ALU.mult, [yb, self.lng_b], [yb])
            self.tt("dve", yt[:n, :], yt[:n, :], self.lnb[:n, :], ALU.add, [yb, self.lnb_b], [yb])
            self.dma(dst[t0:t0 + n, :], yt[:n, :], [yb], [self.buf((dkey, t0))])

        stages = [stA, stB, stC]
        N, S_ = len(ctxs), len(stages)
        for slot in range(N + S_ - 1):
            for s in range(S_ - 1, -1, -1):
                i = slot - s
                if 0 <= i < N:
                    stages[s](ctxs[i])

    def pass_c_tile(self, g, t0, n, first, last, ogz, ogzb, nk, wo=None):
        self.pass_c_run([(g, t0, n, lambda: (ogz, ogzb))], first, last, nk, wo)

    def layer(self, L, first, last):
        self.layer_mod(L)
        getattr(self, ("layer_gdn", "layer_fox", "layer_diff", "layer_ret")[L % 4])(L, first, last)

    def attn_rings(self, w, lf):
        self.P.barrier()
        self.attn_skew = 3 if lf else 2
        self._attn_pend = []
        if lf:
            self.set_psum(4, 2, 2)
        else:
            self.set_psum(3, 4, 1)
        A = self.arena
        A.reset()
        smax = max(g.S for g in self.groups)
        self.r_KT = Ring(self, "rKT", [70, smax], BF16, 2, arena=A)
        nkt = max(len(g.ktiles) for g in self.groups)
        self.r_VA = Ring(self, "rVA", [128, nkt, w], BF16, 2, arena=A)
        self.r_QT = Ring(self, "rQT", [70, 512], BF16, 2, arena=A)
        self.r_PT = Ring(self, "rPT", [128, 512], BF16, 5 if lf else 3, arena=A)
        if lf:
            self.r_og = Ring(self, "rog", [128, 4, 128], BF16, 2, arena=A)
        self.r_vb = self.r_h
        self.r_st = Ring(self, "rst", [128, 2048], BF16, 1, arena=A)
        if lf:
            self.r_lf = Ring(self, "rlf", [16, 512], F32, 4, arena=A)
            self.r_pt = Ring(self, "rpt", [16, 3, 512], BF16, 2, arena=A)
            self.ones3, self.ones3_b = A.alloc([16, 3, 512], BF16), Buf()
            self.memset("pool", self.ones3[:, :, :], 1.0, [self.ones3_b])
            self.carry, self.carry_b = A.alloc([16, 1], F32), Buf()
            self.fbf, self.fbf_b = A.alloc([16, 1], F32), Buf()

    def attn_drain(self):
        while self._attn_pend:
            it_, f_ = self._attn_pend.pop(0)
            f_(it_)

    def attn_head(self, g, KTd, QTd, VAd, vcol0, dv, kc, rd_bufs, extras, finish):
        KT, KTb = self.r_KT.next()
        self.dma(KT[:kc, 0:g.S], KTd[:, :], rd_bufs, [KTb])
        VA, VAb = self.r_VA.next()
        nfull = sum(1 for (k0, nk) in g.ktiles if nk == 128)
        for k4 in range(0, nfull, 4):
            k5 = min(nfull, k4 + 4)
            self.dma(VA[:, k4:k5, 0:dv + 1],
                     VAd[k4 * 128:k5 * 128, vcol0:vcol0 + dv + 1].rearrange("(kt p) c -> p kt c", p=128), rd_bufs, [VAb])
        for i, (k0, nk) in enumerate(g.ktiles):
            if nk != 128:
                self.dma(VA[:nk, i, 0:dv + 1], VAd[k0:k0 + nk, vcol0:vcol0 + dv + 1], rd_bufs, [VAb])
        w = dv + 1
        per_bank = 512 // w
        for bi, blk in enumerate(g.blocks):
            nq = sum(n for _, n in blk)
            q0b = blk[0][0]
            QT, QTb = self.r_QT.next()
            self.dma(QT[:kc, 0:nq], QTd[:, q0b:q0b + nq], rd_bufs, [QTb])
            nbank = (len(blk) + per_bank - 1) // per_bank
            accs = [self.ps_acc.next() for _ in range(nbank)]
            started = [False] * nbank
            coff = []
            o = 0
            for (_, n) in blk:
                coff.append(o)
                o += n
            from types import SimpleNamespace as NS
            bc_ = NS(bi=bi, blk=blk, accs=accs, started=started, coff=coff, remaining=0, VA=VA, VAb=VAb)
            pend = self._attn_pend

            def emit_pv(item):
                ki, k0, nk, vis, PT, PTb, b_ = item
                for qi in vis:
                    q0, n = b_.blk[qi]
                    bk = qi // per_bank
                    acc, accb = b_.accs[bk]
                    col = (qi % per_bank) * w
                    last = (k0 == g.past + q0)
                    self.mm(acc[:n, col:col + w], PT[:nk, b_.coff[qi]:b_.coff[qi] + n], b_.VA[:nk, ki, 0:w],
                            not b_.started[bk], last, [PTb, b_.VAb], [accb], skip=True)
                    b_.started[bk] = True
                b_.remaining -= 1
                if b_.remaining == 0 and b_.closed:
                    b_.fin(b_.bi, b_.blk, b_.accs, per_bank, w)
            bc_.closed = False
            bc_.fin = finish
            for ki, (k0, nk) in enumerate(g.ktiles):
                vis = [qi for qi, (q0, n) in enumerate(blk) if g.past + q0 >= k0]
                if not vis or k0 > g.past + blk[-1][0]:
                    continue
                fi = vis[0]
                c0 = coff[fi]
                ST, STb = self.ps_mm.next()
                ex = []
                for qi in vis:
                    q0, n = blk[qi]
                    for (l, r, bl) in extras(ki, k0, nk, qi, g.past + q0, n):
                        ex.append((qi, n, l, r, bl))
                self.mm(ST[:nk, c0:nq], KT[:kc, k0:k0 + nk], QT[:kc, c0:nq], True, not ex, [KTb, QTb], [STb])
                for j, (qi, n, l, r, bl) in enumerate(ex):
                    self.mm(ST[:nk, coff[qi]:coff[qi] + n], l, r, False, j == len(ex) - 1, bl, [STb], skip=True)
                PT, PTb = self.r_PT.next()
                self.act(PT[:nk, c0:nq], ST[:nk, c0:nq], AF.Exp, [STb], [PTb])
                bc_.remaining += 1
                pend.append(((ki, k0, nk, vis, PT, PTb, bc_), emit_pv))
                if len(pend) > self.attn_skew:
                    it_, f_ = pend.pop(0)
                    f_(it_)
            bc_.closed = True
            if bc_.remaining == 0:
                finish(bi, blk, accs, per_bank, w)

    def layer_fox(self, L, first, last):
        dr = self.dr
        self.attn_rings(65, True)
        self.load_w(self.W, self.Wb, dr["fox_w_in"], 4112)
        self.load_w(self.WO, self.WOb, dr["fox_w_out"], D)
        bf, bfb = self.fbf, self.fbf_b
        self.dma(bf[:, :], dr["fox_b_f"][:, :], [], [bfb])
        self.ts("dve", bf[:, :], bf[:, :], -1.0, None, ALU.mult, None, [bfb], [bfb])
        for g in self.groups:
            nm = g.name
            KTd, QTd, VAd, ZSd, OGd = dr["KT_" + nm], dr["QT_" + nm], dr["VA_" + nm], dr["ZS_" + nm], dr["OG_" + nm]
            KTv = KTd.rearrange("(hp two) r s -> two r hp s", two=2)
            QTv = QTd.rearrange("(hp two) r s -> two r hp s", two=2)
            wb = []

            def nb_(key):
                b = self.buf(("fox", g.gi, L) + key)
                wb.append(b)
                return b
            self.memset("pool", self.carry[:, :], 0.0, [self.carry_b])

            def cum_block(lf, lfb, nb, kpos, qpos, tag):
                cum, cumb = self.r_lf.next()
                self.P.op("dve", lambda e: e.tensor_tensor_scan(out=cum[:, 0:nb], data0=self.ones3[:, 0, 0:nb],
                                                                data1=lf[:, 0:nb], initial=self.carry[:, 0:1],
                                                                op0=ALU.mult, op1=ALU.add),
                          [lfb, self.ones3_b, self.carry_b], [cumb])
                self.cp("act", self.carry[:, 0:1], cum[:, nb - 1:nb], [cumb], [self.carry_b])
                pt, ptb = self.r_pt.next()
                t32, t32b = self.r_lf.next()
                r1, r1b = self.r_lf.next()
                self.cp("dve", pt[:, 0, 0:nb], cum[:, 0:nb], [cumb], [ptb])
                self.cp("dve", t32[:, 0:nb], pt[:, 0, 0:nb], [ptb], [t32b])
                self.tt("dve", r1[:, 0:nb], cum[:, 0:nb], t32[:, 0:nb], ALU.subtract, [cumb, t32b], [r1b])
                self.cp("dve", pt[:, 1, 0:nb], r1[:, 0:nb], [r1b], [ptb])
                self.cp("dve", t32[:, 0:nb], pt[:, 1, 0:nb], [ptb], [t32b])
                self.tt("dve", r1[:, 0:nb], r1[:, 0:nb], t32[:, 0:nb], ALU.subtract, [r1b, t32b], [r1b])
                self.cp("dve", pt[:, 2, 0:nb], r1[:, 0:nb], [r1b], [ptb])
                npt, nptb = self.r_pt.next()
                self.ts("dve", npt[:, :, 0:nb], pt[:, :, 0:nb], -1.0, None, ALU.mult, None, [ptb], [nptb])
                self.dma(KTd[:, 64:67, kpos:kpos + nb], npt[:, :, 0:nb], [nptb], [nb_((tag, "kc"))])
                self.dma(KTd[:, 67:70, kpos:kpos + nb], self.ones3[:, :, 0:nb], [self.ones3_b], [nb_((tag, "k1"))])
                if qpos is not None:
                    self.dma(QTd[:, 64:67, qpos:qpos + nb], self.ones3[:, :, 0:nb], [self.ones3_b],
                             [nb_((tag, "q1"))])
                    self.dma(QTd[:, 67:70, qpos:qpos + nb], pt[:, :, 0:nb], [ptb], [nb_((tag, "qc"))])


            ptiles = [(k0, nk) for (k0, nk) in g.ktiles if k0 < g.past]
            for b0 in range(0, len(ptiles), 2):
                pblk = ptiles[b0:b0 + 2]
                lf, lfb = self.r_lf.next()
                st_, stb = self.r_st.next()
                st = st_[:, :].rearrange("p (c t) -> p c t", c=8)
                for j, (k0, nk) in enumerate(pblk):
                    xt, xb = self.r_x.next()
                    self.dma(xt[:, :], dr["cache_fox_k"][k0:k0 + 128, :], [], [xb])
                    ub, ubb = self.r_ub.next()
                    self.cp("dve", ub[:, :], xt[:, :], [xb], [ubb])
                    tp, tpb = self.ps_tp.next()
                    tpv = tp[:, :].bitcast(BF16)
                    for c in range(8):
                        self.tr(tpv[:, c * 128:(c + 1) * 128], ub[:, c * 128:(c + 1) * 128], self.identb[:, :],
                                [ubb, self.identb_b], [tpb])
                    self.cp("act", st[:, :, j * 128:(j + 1) * 128], tpv.rearrange("p (c t) -> p c t", c=8), [tpb], [stb])
                    vt, vtb = self.r_x.next()
                    self.dma(vt[:, :], dr["cache_fox_v"][k0:k0 + 128, :], [], [vtb])
                    vb, vbb = self.r_vb.next()
                    self.memset("pool", vb[:, :].rearrange("p (h c) -> p h c", h=16)[:, :, 64:65], 1.0, [vbb])
                    self.cp("dve", vb[:, :].rearrange("p (h c) -> p h c", h=16)[:, :, 0:64],
                            vt[:, :].rearrange("p (h c) -> p h c", h=16), [vtb], [vbb])
                    self.dma(VAd[k0:k0 + 128, :], vb[:, :], [vbb], [nb_(("pv", k0))])
                    lt, ltb = self.r_small.next()
                    self.dma(lt[:, 0:16], dr["cache_fox_logf"][k0:k0 + 128, :], [], [ltb])
                    tp2, tp2b = self.ps_tp.next()
                    self.tr(tp2[0:16, 0:128], lt[:, 0:16], self.identf[:, :], [ltb, self.identf_b], [tp2b])
                    self.cp("act", lf[:, j * 128:(j + 1) * 128], tp2[0:16, 0:128], [tp2b], [lfb])
                kp, nbk = pblk[0][0], 128 * len(pblk)
                for two in range(2):
                    self.dma(KTv[two, 0:64, :, kp:kp + nbk], st[two * 64:(two + 1) * 64, :, 0:nbk], [stb],
                             [nb_(("pk", kp, two))])
                cum_block(lf, lfb, nbk, kp, None, ("pc", kp))

            for blk in g.blocks:
                uT, uTb, nb = self.make_uT(g, blk, first)
                t0b = blk[0][0]
                kpos = g.past + t0b
                for which, dst in ((0, QTv), (1, KTv)):
                    pos = t0b if which == 0 else kpos
                    for hf in range(2):
                        st_, stb = self.r_st.next()
                        st = st_[:, :].rearrange("p (c t) -> p c t", c=4)
                        for c4 in range(4):
                            c = hf * 4 + c4
                            ps, pb = self.proj_fm(uT, uTb, nb, which * D + c * 128, which * D + (c + 1) * 128)
                            if which == 0:
                                self.act(st[:, c4, 0:nb], ps[:, 0:nb], AF.Copy, [pb], [stb], scale=0.125)
                            else:
                                self.cp("dve", st[:, c4, 0:nb], ps[:, 0:nb], [pb], [stb])
                        for two in range(2):
                            self.dma(dst[two, 0:64, hf * 4:hf * 4 + 4, pos:pos + nb], st[two * 64:(two + 1) * 64, :, 0:nb],
                                     [stb], [nb_(("qk", which, t0b, two, hf))])
                ps, pb = self.proj_fm(uT, uTb, nb, 4 * D, 4 * D + 16)
                e1, e1b = self.r_lf.next()
                self.act(e1[:, 0:nb], ps[0:16, 0:nb], AF.Exp, [pb, bfb], [e1b], bias=bf[:, 0:1], scale=-1.0)
                lf, lfb = self.r_lf.next()
                self.act(lf[:, 0:nb], e1[:, 0:nb], AF.Ln, [e1b], [lfb], bias=1.0)
                self.ts("dve", lf[:, 0:nb], lf[:, 0:nb], -1.0, None, ALU.mult, None, [lfb], [lfb])
                off = 0
                for (t0, n) in blk:
                    tp2, tp2b = self.ps_tp.next()
                    self.tr(tp2[0:n, 0:16], lf[:, off:off + n], self.identf[0:16, 0:16], [lfb, self.identf_b], [tp2b])
                    lo, lob = self.r_small.next()
                    self.cp("act", lo[:n, 0:16], tp2[0:n, 0:16], [tp2b], [lob])
                    self.dma(dr["fox_logf_" + nm][t0:t0 + n, :], lo[:n, 0:16], [lob], [self.buf(("flo", g.gi, t0))])
                    for which, oname in ((1, "fox_k_"), (2, "fox_v_")):
                        ft, fb = self.r_f.next()
                        for cg in range(2):
                            ps, pb = self.proj_tm(uT, uTb, off, n, which * D + cg * 512, which * D + (cg + 1) * 512)
                            self.cp("act" if cg == 0 else "dve", ft[:n, cg * 512:(cg + 1) * 512], ps[:n, :], [pb], [fb])
                        self.dma(dr[oname + nm][t0:t0 + n, :], ft[:n, :], [fb], [self.buf((oname, g.gi, t0))])
                        if which == 2:
                            vb, vbb = self.r_vb.next()
                            self.memset("pool", vb[:n, :].rearrange("p (h c) -> p h c", h=16)[:, :, 64:65], 1.0, [vbb])
                            self.cp("dve", vb[:n, :].rearrange("p (h c) -> p h c", h=16)[:, :, 0:64],
                                    ft[:n, :].rearrange("p (h c) -> p h c", h=16), [fb], [vbb])
                            self.dma(VAd[g.past + t0:g.past + t0 + n, :], vb[:n, :], [vbb], [nb_(("v", t0))])
                    zt, ztb = self.r_h.next()
                    for cg in range(2):
                        ps, pb = self.proj_tm(uT, uTb, off, n, 3 * D + cg * 512, 3 * D + (cg + 1) * 512)
                        self.act(zt[:n, cg * 512:(cg + 1) * 512], ps[:n, :], AF.Silu, [pb], [ztb])
                    self.dma(ZSd[t0:t0 + n, 0:D], zt[:n, 0:D], [ztb], [nb_(("z", t0))])
                    off += n
                cum_block(lf, lfb, nb, kpos, t0b, ("c", t0b))

            ogw = []

            def extras(ki, k0, nk, qi, q0a, n):
                if k0 == q0a:
                    return [(self.identb[:nk, :nk], self.cmaskb[:nk, :n], [self.identb_b, self.cmaskb_b])]
                return []

            for h in range(16):
                def finish(bi, blk, accs, per_bank, w, h=h):
                    og, ogb = self.r_og.next()
                    for qi, (q0, n) in enumerate(blk):
                        acc, accb = accs[qi // per_bank]
                        col = (qi % per_bank) * w
                        rd, rdb = self.r_small.next()
                        self.recip(rd[:n, 0:1], acc[:n, col + 64:col + 65], [accb], [rdb])
                        self.ts("dve", og[:n, qi, 0:64], acc[:n, col:col + 64], rd[:n, 0:1], None, ALU.mult, None,
                                [accb, rdb], [ogb])
                    q0b = blk[0][0]
                    kb = self.buf(("fox_og", g.gi, L, h, bi))
                    ogw.append(kb)
                    if len(blk) > 1 or blk[0][1] == 128:
                        nt = len(blk)
                        self.dma(OGd[q0b:q0b + nt * 128, h * 64:(h + 1) * 64].rearrange("(qi p) c -> p qi c", p=128),
                                 og[:, 0:nt, 0:64], [ogb], [kb])
                    else:
                        n = blk[0][1]
                        self.dma(OGd[q0b:q0b + n, h * 64:(h + 1) * 64], og[:n, 0, 0:64], [ogb], [kb])
                self.attn_head(g, KTd[h], QTd[h], VAd, h * 65, 64, 70, wb, extras, finish)
            self.attn_drain()

            def mkprep(t0, n):
                def prep():
                    og, ogb = self.r_h.next()
                    self.dma(og[:n, 0:D], OGd[t0:t0 + n, 0:D], ogw, [ogb])
                    zs, zsb = self.r_h.next()
                    self.dma(zs[:n, 0:D], ZSd[t0:t0 + n, 0:D], wb, [zsb])
                    self.tt("dve", og[:n, 0:D], og[:n, 0:D], zs[:n, 0:D], ALU.mult, [ogb, zsb], [ogb])
                    return og, ogb
                return prep
            self.pass_c_run([(g, t0, n, mkprep(t0, n)) for (t0, n) in g.tiles], first, last, 8)

    def layer_diff(self, L, first, last):
        dr = self.dr
        LAM_INIT = 0.8 - 0.6 * math.exp(-0.3 * 2)
        self.attn_rings(129, False)
        A = self.arena
        self.load_w(self.W, self.Wb, dr["diff_w_in"], 4096)
        self.load_w(self.WO, self.WOb, dr["diff_w_out"], D)
        lv, lvb = A.alloc([1, 4, 64], F32), Buf()
        for i, nmv in enumerate(("diff_lam_q1", "diff_lam_k1", "diff_lam_q2", "diff_lam_k2")):
            self.dma(lv[:, i, :], dr[nmv][:, :], [], [lvb])
        l2, l2b = A.alloc([1, 8], F32), Buf()
        pr, prb = A.alloc([1, 2, 64], F32), Buf()
        lvv = lv[:, :, :].rearrange("p (a b) c -> p a b c", b=2)
        self.tt("dve", pr[:, :, :], lvv[:, :, 0, :], lvv[:, :, 1, :], ALU.mult, [lvb], [prb])
        self.P.op("dve", lambda e: e.tensor_reduce(out=l2[:, 0:2], in_=pr[:, :, :], axis=AX.X, op=ALU.add), [prb], [l2b])
        self.act(l2[:, 2:4], l2[:, 0:2], AF.Exp, [l2b], [l2b])
        self.tt("dve", l2[:, 4:5], l2[:, 3:4], l2[:, 2:3], ALU.subtract, [l2b], [l2b])
        self.ts("dve", l2[:, 5:6], l2[:, 4:5], -LAM_INIT, None, ALU.add, None, [l2b], [l2b])
        self.dma(dr["lam_scr"][0:1, 0:1], l2[:, 5:6], [l2b], [self.buf("lam_scr")])
        nlam, nlamb = A.alloc([128, 1], F32), Buf()
        self.dma(nlam[:, :], dr["lam_scr"][0:1, 0:1].to_broadcast([128, 1]), [self.buf("lam_scr")], [nlamb])
        tb, tbb = A.alloc([32, 2, 8], F32), Buf()
        for m_ in range(2):
            self.dma(tb[:, m_, :], dr["rel_bias_table"][:, :], [], [tbb])
        oh, ohb = self.r_x.next()
        self.dma(oh[0:32, 0:384], dr["t5_onehot"][:, :], [], [ohb])
        tbs, tbsb = A.alloc([32, 3, 16], BF16), Buf()
        ohb16, ohb16b = A.alloc([32, 384], BF16), Buf()
        self.cp("dve", ohb16[:, :], oh[0:32, 0:384], [ohb], [ohb16b])
        tbf = tb[:, :, :].rearrange("p a b -> p (a b)")
        tmp32, tmp32b = self.r_small.next()
        self.cp("dve", tbs[:, 0, :], tbf, [tbb], [tbsb])
        self.cp("dve", tmp32[0:32, 0:16], tbs[:, 0, :], [tbsb], [tmp32b])
        self.tt("dve", tmp32[0:32, 0:16], tbf, tmp32[0:32, 0:16], ALU.subtract, [tbb, tmp32b], [tmp32b])
        self.cp("dve", tbs[:, 1, :], tmp32[0:32, 0:16], [tmp32b], [tbsb])
        tmp33, tmp33b = self.r_small.next()
        self.cp("dve", tmp33[0:32, 0:16], tbs[:, 1, :], [tbsb], [tmp33b])
        self.tt("dve", tmp33[0:32, 0:16], tmp32[0:32, 0:16], tmp33[0:32, 0:16], ALU.subtract, [tmp32b, tmp33b], [tmp33b])
        self.cp("dve", tbs[:, 2, :], tmp33[0:32, 0:16], [tmp33b], [tbsb])
        ps, pb = self.ps_mm.next()
        for j_ in range(3):
            self.mm(ps[0:16, 0:384], tbs[:, j_, :], ohb16[:, :], j_ == 0, j_ == 2, [tbsb, ohb16b], [pb])
        fr, frb = self.r_f.next()
        c16, c16b = self.r_small.next()
        self.cp("dve", c16[0:16, 0:1], ps[0:16, 0:1], [pb], [c16b])
        self.ts("dve", fr[0:16, 0:384], ps[0:16, 0:384], c16[0:16, 0:1], None, ALU.subtract, None, [pb, c16b], [frb])
        self.dma(dr["fr_scr"][:, :], fr[0:16, 0:384], [frb], [self.buf("fr_scr")])
        crow, crowb = A.alloc([16, 2, 512], BF16), Buf()
        chi, chib = self.r_small.next()
        cbf = chi[0:16, 0:4].bitcast(BF16)
        self.cp("dve", cbf[:, 0:1], c16[0:16, 0:1], [c16b], [chib])
        self.cp("dve", chi[0:16, 4:5], cbf[:, 0:1], [chib], [chib])
        self.tt("dve", chi[0:16, 5:6], c16[0:16, 0:1], chi[0:16, 4:5], ALU.subtract, [c16b, chib], [chib])
        self.cp("dve", cbf[:, 1:2], chi[0:16, 5:6], [chib], [chib])
        for j in range(2):
            self.cp("dve", crow[:, j, :], cbf[:, j:j + 1].to_broadcast([16, 512]), [chib], [crowb])
        ones2, ones2b = A.alloc([16, 2, 512], BF16), Buf()
        self.memset("pool", ones2[:, :, :], 1.0, [ones2b])
        J, Jb = A.alloc([128, 128], BF16), Buf()
        dmk, dmkb = A.alloc([128, 128], BF16), Buf()
        t, b = self.r_f.next()
        self.dma(t[:, 0:128], dr["antiident"][:, :], [], [b])
        self.cp("dve", J[:, :], t[:, 0:128], [b], [Jb])
        t, b = self.r_f.next()
        self.dma(t[:, 0:128], dr["dmask"][:, :], [], [b])
        self.cp("dve", dmk[:, :], t[:, 0:128], [b], [dmkb])
        Hh = [[None, None] for _ in range(8)]
        for h in range(8):
            for ti, c in enumerate((128, 0)):
                t, b = self.r_f.next()
                src_ap = bass.AP(self.dh["fr_scr"], h * 384 + c, [[1, 128], [1, 128]])
                self.dma(t[:, 0:128], src_ap, [self.buf("fr_scr")], [b])
                hi, hib = A.alloc([128, 128], BF16), Buf()
                lo, lob = A.alloc([128, 128], BF16), Buf()
                self.cp("dve", hi[:, :], t[:, 0:128], [b], [hib])
                self.cp("dve", t[:, 128:256], hi[:, :], [hib], [b])
                self.tt("dve", t[:, 256:384], t[:, 0:128], t[:, 128:256], ALU.subtract, [b], [b])
                self.cp("dve", lo[:, :], t[:, 256:384], [b], [lob])
                Hh[h][ti] = (hi, hib, lo, lob)

        slw, slwb = A.alloc([128, 128], F32), Buf()
        self.dma(slw[:, :], dr["diff_subln_w"][0:1, :].to_broadcast([128, 128]), [], [slwb])
        self.P.op("act", lambda e: e.mul(out=slw[:, :], in_=slw[:, :], mul=1.0 - LAM_INIT), [slwb], [slwb])
        for g in self.groups:
            nm = g.name
            KTd, QTd, VAd, ZSd, OGd = dr["KT_" + nm], dr["QT_" + nm], dr["VA_" + nm], dr["ZS_" + nm], dr["OGF_" + nm]
            KTv = KTd.rearrange("(hp two) r s -> two r hp s", two=2)
            QTv = QTd.rearrange("(hp two) r s -> two r hp s", two=2)
            wb = []

            def nb_(key):
                b = self.buf(("diff", g.gi, L) + key)
                wb.append(b)
                return b
            ptiles = [(k0, nk) for (k0, nk) in g.ktiles if k0 < g.past]
            for b0 in range(0, len(ptiles), 2):
                pblk = ptiles[b0:b0 + 2]
                st_, stb = self.r_st.next()
                st = st_[:, :].rearrange("p (c t) -> p c t", c=8)
                for j, (k0, nk) in enumerate(pblk):
                    xt, xb = self.r_x.next()
                    self.dma(xt[:, :], dr["cache_diff_k"][k0:k0 + 128, :], [], [xb])
                    ub, ubb = self.r_ub.next()
                    self.cp("dve", ub[:, :], xt[:, :], [xb], [ubb])
                    tp, tpb = self.ps_tp.next()
                    tpv = tp[:, :].bitcast(BF16)
                    for c in range(8):
                        self.tr(tpv[:, c * 128:(c + 1) * 128], ub[:, c * 128:(c + 1) * 128], self.identb[:, :],
                                [ubb, self.identb_b], [tpb])
                    self.cp("act", st[:, :, j * 128:(j + 1) * 128], tpv.rearrange("p (c t) -> p c t", c=8), [tpb], [stb])
                    vt, vtb = self.r_x.next()
                    self.dma(vt[:, :], dr["cache_diff_v"][k0:k0 + 128, :], [], [vtb])
                    vb, vbb = self.r_vb.next()
                    vb3 = vb[:, 0:1032].rearrange("p (h c) -> p h c", h=8)
                    self.memset("pool", vb3[:, :, 128:129], 1.0, [vbb])
                    self.cp("dve", vb3[:, :, 0:128], vt[:, :].rearrange("p (h c) -> p h c", h=8), [vtb], [vbb])
                    self.dma(VAd[k0:k0 + 128, 0:1032], vb[:, 0:1032], [vbb], [nb_(("pv", k0))])
                kp, nbk = pblk[0][0], 128 * len(pblk)
                for two in range(2):
                    self.dma(KTv[two, 0:64, :, kp:kp + nbk], st[two * 64:(two + 1) * 64, :, 0:nbk], [stb],
                             [nb_(("pk", kp, two))])
                self.dma(KTd[:, 64:66, kp:kp + nbk], ones2[:, :, 0:nbk], [ones2b], [nb_(("p1", kp))])
            for blk in g.blocks:
                uT, uTb, nb = self.make_uT(g, blk, first)
                t0b = blk[0][0]
                kpos = g.past + t0b
                for which, dst in ((0, QTv), (1, KTv)):
                    pos = t0b if which == 0 else kpos
                    for hf in range(2):
                        st_, stb = self.r_st.next()
                        st = st_[:, :].rearrange("p (c t) -> p c t", c=4)
                        for c4 in range(4):
                            c = hf * 4 + c4
                            ps, pb = self.proj_fm(uT, uTb, nb, which * D + c * 128, which * D + (c + 1) * 128)
                            if which == 0:
                                self.act(st[:, c4, 0:nb], ps[:, 0:nb], AF.Copy, [pb], [stb], scale=0.125)
                            else:
                                self.cp("dve", st[:, c4, 0:nb], ps[:, 0:nb], [pb], [stb])
                        for two in range(2):
                            self.dma(dst[two, 0:64, hf * 4:hf * 4 + 4, pos:pos + nb], st[two * 64:(two + 1) * 64, :, 0:nb],
                                     [stb], [nb_(("qk", which, t0b, two, hf))])
                self.dma(KTd[:, 64:66, kpos:kpos + nb], ones2[:, :, 0:nb], [ones2b], [nb_(("k1", t0b))])
                QTm = QTd.rearrange("(h m) r s -> m h r s", m=2)
                for m_ in range(2):
                    self.dma(QTm[m_, :, 64:66, t0b:t0b + nb], crow[m_ * 8:(m_ + 1) * 8, :, 0:nb], [crowb],
                             [nb_(("qc", t0b, m_))])
                off = 0
                for (t0, n) in blk:
                    for which, oname in ((1, "diff_k_"), (2, "diff_v_")):
                        ft, fb = self.r_f.next()
                        for cg in range(2):
                            ps, pb = self.proj_tm(uT, uTb, off, n, which * D + cg * 512, which * D + (cg + 1) * 512)
                            self.cp("act" if cg == 0 else "dve", ft[:n, cg * 512:(cg + 1) * 512], ps[:n, :], [pb], [fb])
                        self.dma(dr[oname + nm][t0:t0 + n, :], ft[:n, :], [fb], [self.buf((oname, g.gi, t0))])
                        if which == 2:
                            vb, vbb = self.r_vb.next()
                            vb3 = vb[:n, 0:1032].rearrange("p (h c) -> p h c", h=8)
                            self.memset("pool", vb3[:, :, 128:129], 1.0, [vbb])
                            self.cp("dve", vb3[:, :, 0:128], ft[:n, :].rearrange("p (h c) -> p h c", h=8), [fb], [vbb])
                            self.dma(VAd[g.past + t0:g.past + t0 + n, 0:1032], vb[:n, 0:1032], [vbb], [nb_(("v", t0))])
                    zt, ztb = self.r_h.next()
                    for cg in range(2):
                        ps, pb = self.proj_tm(uT, uTb, off, n, 3 * D + cg * 512, 3 * D + (cg + 1) * 512)
                        self.act(zt[:n, cg * 512:(cg + 1) * 512], ps[:n, :], AF.Silu, [pb], [ztb])
                    self.dma(ZSd[t0:t0 + n, 0:D], zt[:n, 0:D], [ztb], [nb_(("z", t0))])
                    off += n
            ogw = []
            for vh in range(16):
                h, m = vh // 2, vh % 2

                def extras(ki, k0, nk, qi, q0a, n, h=h):
                    if k0 == q0a:
                        hi, hib, lo, lob = Hh[h][0]
                        return [(hi[:, 0:nk], J[:, 0:n], [hib, Jb]), (lo[:, 0:nk], J[:, 0:n], [lob, Jb]),
                                (self.identb[:nk, :nk], dmk[:nk, :n], [self.identb_b, dmkb])]
                    if k0 == q0a - 128:
                        hi, hib, lo, lob = Hh[h][1]
                        return [(hi[:, 0:nk], J[:, 0:n], [hib, Jb]), (lo[:, 0:nk], J[:, 0:n], [lob, Jb])]
                    return []

                def finish(bi, blk, accs, per_bank, w, h=h, m=m, vh=vh):
                    og, ogb = self.r_x.next()
                    for qi, (q0, n) in enumerate(blk):
                        acc, accb = accs[qi // per_bank]
                        col = (qi % per_bank) * w
                        rd, rdb = self.r_small.next()
                        self.recip(rd[:n, 0:1], acc[:n, col + 128:col + 129], [accb], [rdb])
                        self.ts("dve", og[:n, qi * 128:(qi + 1) * 128], acc[:n, col:col + 128], rd[:n, 0:1], None,
                                ALU.mult, None, [accb, rdb], [ogb])
                    q0b = blk[0][0]
                    kb = self.buf(("diff_og", g.gi, L, vh, bi))
                    ogw.append(kb)
                    c0 = m * D + h * 128
                    if blk[0][1] == 128:
                        nt = len(blk)
                        self.dma(OGd[q0b:q0b + nt * 128, c0:c0 + 128].rearrange("(qi p) c -> p qi c", p=128),
                                 og[:, 0:nt * 128].rearrange("p (qi c) -> p qi c", c=128), [ogb], [kb])
                    else:
                        n = blk[0][1]
                        self.dma(OGd[q0b:q0b + n, c0:c0 + 128], og[:n, 0:128], [ogb], [kb])
                self.attn_head(g, KTd[vh][0:66, :], QTd[vh][0:66, :], VAd, h * 129, 128, 66, wb, extras, finish)
            self.attn_drain()
            def mkprep(t0, n):
                def prep():
                        o1, o1b = self.r_x.next()
                        self.dma(o1[:n, :], OGd[t0:t0 + n, 0:D], ogw, [o1b])
                        o2, o2b = self.r_f.next()
                        self.dma(o2[:n, :], OGd[t0:t0 + n, D:2 * D], ogw, [o2b])
                        self.stt(o1[:n, :], o2[:n, :], nlam[:n, 0:1], o1[:n, :], ALU.mult, ALU.add, [o2b, nlamb, o1b], [o1b])
                        self.tt("dve", o2[:n, :], o1[:n, :], o1[:n, :], ALU.mult, [o1b], [o2b])
                        ss, ssb = self.r_small.next()
                        self.P.op("dve", lambda e, ss=ss, o2=o2, n=n: e.tensor_reduce(
                            out=ss[:n, 0:8], in_=o2[:n, :].rearrange("p (h c) -> p h c", h=8), axis=AX.X, op=ALU.add),
                            [o2b], [ssb])
                        self.rsqrt(ss[:n, 0:8], ss[:n, 0:8], [ssb], [ssb], scale=1.0 / 128, eps=NORM_EPS)
                        self.tt("dve", o1[:n, :].rearrange("p (h c) -> p h c", h=8), o1[:n, :].rearrange("p (h c) -> p h c", h=8),
                                ss[:n, 0:8].unsqueeze(2).to_broadcast([n, 8, 128]), ALU.mult, [o1b, ssb], [o1b])
                        self.tt("dve", o1[:n, :].rearrange("p (h c) -> p h c", h=8), o1[:n, :].rearrange("p (h c) -> p h c", h=8),
                                slw[:n, :].unsqueeze(1).to_broadcast([n, 8, 128]), ALU.mult, [o1b, slwb], [o1b])
                        zs, zsb = self.r_h.next()
                        self.dma(zs[:n, 0:D], ZSd[t0:t0 + n, 0:D], wb, [zsb])
                        og, ogb = self.r_h.next()
                        self.tt("dve", og[:n, 0:D], o1[:n, :], zs[:n, 0:D], ALU.mult, [o1b, zsb], [ogb])
                        return og, ogb
                return prep
            self.pass_c_run([(g, t0, n, mkprep(t0, n)) for (t0, n) in g.tiles], first, last, 8)

    def layer_ret(self, L, first, last):
        dr = self.dr
        P = self.P
        A = self.arena
        LG = [math.log1p(-2.0 ** (-5.0 - h)) for h in range(4)]
        P.barrier(); A.reset()
        self.set_psum(4, 2, 2)
        self.load_w(self.W, self.Wb, dr["ret_w_in"], 4096)
        r_st = Ring(self, "rst", [128, 8, 512], BF16, 1, arena=A)
        r_cs = Ring(self, "rcs", [128, 2, 512], F32, 1, arena=A)
        r_vt = Ring(self, "rvt", [128, 2048], BF16, 2, arena=A)
        wbs = {}
        for g in self.groups:
            nm = g.name
            wb = wbs[g.gi] = []

            def nb_(key, wb=wb, g=g):
                b = self.buf(("ret", g.gi) + key)
                wb.append(b)
                return b
            for blk in g.blocks:
                uT, uTb, nb = self.make_uT(g, blk, first)
                t0b = blk[0][0]
                cs, csb = r_cs.next()
                self.dma(cs[:, 0, 0:nb], dr["rot_cos_" + nm][:, t0b:t0b + nb], [], [csb])
                self.dma(cs[:, 1, 0:nb], dr["rot_sin_" + nm][:, t0b:t0b + nb], [], [csb])
                for which, dname in ((0, "QR_"), (1, "KR_")):
                    st, stb = r_st.next()
                    for h in range(4):
                        c0 = which * D + h * 256
                        pe_, peb = self.proj_fm(uT, uTb, nb, c0, c0 + 128)
                        po_, pob = self.proj_fm(uT, uTb, nb, c0 + 128, c0 + 256)
                        xe, xeb = self.r_f.next()
                        sc = 0.0625 if which == 0 else 1.0
                        self.act(xe[:, 0:nb], pe_[:, 0:nb], AF.Copy, [peb], [xeb], scale=sc)
                        self.act(xe[:, 512:512 + nb], po_[:, 0:nb], AF.Copy, [pob], [xeb], scale=sc)
                        tm, tmb = self.r_x.next()
                        self.tt("dve", tm[:, 0:nb], xe[:, 0:nb], cs[:, 0, 0:nb], ALU.mult, [xeb, csb], [tmb])
                        self.tt("dve", tm[:, 512:512 + nb], xe[:, 512:512 + nb], cs[:, 1, 0:nb], ALU.mult, [xeb, csb], [tmb])
                        self.tt("dve", st[:, 2 * h, 0:nb], tm[:, 0:nb], tm[:, 512:512 + nb], ALU.subtract, [tmb], [stb])
                        tm, tmb = self.r_x.next()
                        self.tt("dve", tm[:, 0:nb], xe[:, 0:nb], cs[:, 1, 0:nb], ALU.mult, [xeb, csb], [tmb])
                        self.tt("dve", tm[:, 512:512 + nb], xe[:, 512:512 + nb], cs[:, 0, 0:nb], ALU.mult, [xeb, csb], [tmb])
                        self.tt("dve", st[:, 2 * h + 1, 0:nb], tm[:, 0:nb], tm[:, 512:512 + nb], ALU.add, [tmb], [stb])
                    self.dma(dr[dname + nm].rearrange("he p t -> p he t")[:, :, t0b:t0b + nb], st[:, :, 0:nb], [stb],
                             [nb_((dname, t0b))])
                off = 0
                for (t0, n) in blk:
                    vt, vtb = r_vt.next()
                    for cg in range(4):
                        ps, pb = self.proj_tm(uT, uTb, off, n, 2 * D + cg * 512, 2 * D + (cg + 1) * 512)
                        self.cp("act" if cg % 2 == 0 else "dve", vt[:n, cg * 512:(cg + 1) * 512], ps[:n, :], [pb], [vtb])
                    self.dma(dr["OG_" + nm + "v"][t0:t0 + n, :], vt[:n, :], [vtb], [nb_(("v", t0))])
                    off += n
        self.load_w(self.W, self.Wb, dr["ret_w_in"], 2048, c_off=4096)
        for g in self.groups:
            nm = g.name
            wb = wbs[g.gi]
            for blk in g.blocks:
                uT, uTb, nb = self.make_uT(g, blk, first)
                off = 0
                for (t0, n) in blk:
                    zt, ztb = r_vt.next()
                    for cg in range(4):
                        ps, pb = self.proj_tm(uT, uTb, off, n, cg * 512, (cg + 1) * 512)
                        self.act(zt[:n, cg * 512:(cg + 1) * 512], ps[:n, :], AF.Silu, [pb], [ztb])
                    b = self.buf(("ret", g.gi, "z", t0))
                    wb.append(b)
                    self.dma(dr["ZS_" + nm][t0:t0 + n, :], zt[:n, :], [ztb], [b])
                    off += n
        P.barrier(); A.reset()
        A2 = self.arenaW
        A2.reset()
        self.load_w(self.WO, self.WOb, dr["ret_w_out"], D)
        dmk, dmkb = A.alloc([128, 4, 128], F32), Buf()
        self.dma(dmk[:, :, :], dr["ret_dmaskT"][:, :, :], [], [dmkb])
        qdc, qdcb = A.alloc([128, 4, 128], F32), Buf()
        self.dma(qdc[:, :, :], dr["ret_qdec"][:, :, :], [], [qdcb])
        kdc, kdcb = A.alloc([128, 8], F32), Buf()
        self.dma(kdc[:, :], dr["ret_kdec"][:, :], [], [kdcb])
        S, Sb = A.alloc([128, 8, 512], F32), Buf()
        Sbf, Sbfb = A.alloc([128, 8, 512], BF16), Buf()
        r_q = Ring(self, "rq", [128, 8, 128], BF16, 2, arena=A)
        r_k = Ring(self, "rk", [128, 8, 128], BF16, 2, arena=A)
        r_v = Ring(self, "rv", [128, 2048], BF16, 2, arena=A)
        r_qd = Ring(self, "rqd", [128, 8, 128], BF16, 2, arena=A2)
        r_at = Ring(self, "rat", [128, 4, 128], BF16, 2, arena=A2)
        r_kd = Ring(self, "rkd", [128, 8, 128], BF16, 2, arena=A2)
        r_o = Ring(self, "ro", [128, 2048], BF16, 2, arena=A2)
        r_s = Ring(self, "rs", [128, 32], F32, 3, arena=A2)
        ogws = {}
        for g in self.groups:
            nm = g.name
            wb = wbs[g.gi]
            ogw = ogws[g.gi] = []
            QRd = dr["QR_" + nm].rearrange("he p t -> p he t")
            KRd = dr["KR_" + nm].rearrange("he p t -> p he t")
            VRd = dr["OG_" + nm + "v"]
            if g.gi == 0:
                self.memset("pool", S[:, :, :], 0.0, [Sb])
            else:
                self.dma(S[:, :, :], dr["state_ret"].rearrange("h (e p) v -> p (h e) v", e=2), [], [Sb])
            self.cp("act", Sbf[:, 0:4, :], S[:, 0:4, :], [Sb], [Sbfb])
            self.cp("dve", Sbf[:, 4:8, :], S[:, 4:8, :], [Sb], [Sbfb])
            for (t0, n) in g.tiles:
                qt, qtb = r_q.next()
                self.dma(qt[:, :, 0:n], QRd[:, :, t0:t0 + n], wb, [qtb])
                kt, ktb = r_k.next()
                self.dma(kt[:, :, 0:n], KRd[:, :, t0:t0 + n], wb, [ktb])
                vt, vtb = r_v.next()
                self.dma(vt[:n, :], VRd[t0:t0 + n, :], wb, [vtb])
                kc0 = 0 if n == 128 else 4
                ps, pb = self.psb[0]
                for h in range(4):
                    for e in range(2):
                        self.mm(ps[:n, h * 128:h * 128 + n], kt[:, 2 * h + e, 0:n], qt[:, 2 * h + e, 0:n], e == 0, e == 1,
                                [ktb, qtb], [pb], skip=True)
                at, atb = r_at.next()
                self.tt("dve", at[:n, :, 0:n], ps[:n, :].rearrange("p (h i) -> p h i", h=4)[:, :, 0:n], dmk[:n, :, 0:n],
                        ALU.mult, [pb, dmkb], [atb])
                qd, qdb = r_qd.next()
                self.tt("dve", qd[:, :, 0:n].rearrange("p (h e) i -> p h e i", e=2),
                        qt[:, :, 0:n].rearrange("p (h e) i -> p h e i", e=2),
                        qdc[:, :, 0:n].unsqueeze(2).to_broadcast([128, 4, 2, n]), ALU.mult, [qtb, qdcb], [qdb])
                tp, tpb = self.psb[5]
                tpv = tp[:, :].bitcast(BF16)
                for he in range(8):
                    self.tr(tpv[:n, he * 128:(he + 1) * 128], kt[:, he, 0:n], self.identb[:, :], [ktb, self.identb_b], [tpb])
                kd, kdb = r_kd.next()
                self.tt("dve", kd[:n, :, :].rearrange("p (h e) d -> p h (e d)", e=2),
                        tpv[:n, :].rearrange("p (h x) -> p h x", h=4),
                        kdc[:n, kc0:kc0 + 4].unsqueeze(2).to_broadcast([n, 4, 256]), ALU.mult, [tpb, kdcb], [kdb])
                pos = []
                for h in range(4):
                    po, pob = self.psb[1 + h]
                    self.mm(po[:n, :], at[:n, h, 0:n], vt[:n, h * 512:(h + 1) * 512], True, False, [atb, vtb], [pob])
                    for e in range(2):
                        self.mm(po[:n, :], qd[:, 2 * h + e, 0:n], Sbf[:, 2 * h + e, :], False, e == 1, [qdb, Sbfb], [pob])
                    pos.append((po, pob))
                for h in range(4):
                    cdec = math.exp(LG[h] * n)
                    for e in range(2):
                        pu, pub = self.psb[6 + e]
                        self.mm(pu[:, :], kd[:n, 2 * h + e, :], vt[:n, h * 512:(h + 1) * 512], True, True, [kdb, vtb], [pub])
                        self.stt(S[:, 2 * h + e, :], S[:, 2 * h + e, :], cdec, pu[:, :], ALU.mult, ALU.add, [Sb, pub], [Sb])
                    self.cp("act", Sbf[:, 2 * h:2 * h + 2, :], S[:, 2 * h:2 * h + 2, :], [Sb], [Sbfb])
                st, stb = r_s.next()
                for h in range(4):
                    po, pob = pos[h]
                    P.op("dve", lambda e_, st=st, po=po, n=n, h=h: e_.bn_stats(out=st[:n, 8 + 6 * h:14 + 6 * h], in_=po[:n, :]),
                         [pob], [stb])
                    P.op("dve", lambda e_, st=st, n=n, h=h: e_.bn_aggr(out=st[:n, 2 * h:2 * h + 2], in_=st[:n, 8 + 6 * h:14 + 6 * h]),
                         [stb], [stb])
                mv = st[:n, 0:8].rearrange("p (h t) -> p h t", t=2)
                rs, rsb = r_s.next()
                self.rsqrt(rs[:n, 0:4], mv[:, :, 1], [stb], [rsb], eps=LN_EPS)
                self.stt(rs[:n, 4:8], mv[:, :, 0], -1.0, rs[:n, 0:4], ALU.mult, ALU.mult, [stb, rsb], [rsb])
                ot, otb = r_o.next()
                for h in range(4):
                    po, pob = pos[h]
                    self.act(ot[:n, h * 512:(h + 1) * 512], po[:n, :], AF.Identity, [pob, rsb], [otb],
                             bias=rs[:n, 4 + h:5 + h], scale=rs[:n, h:h + 1])
                b = self.buf(("ret_og", g.gi, t0))
                ogw.append(b)
                self.dma(dr["OG_" + nm][t0:t0 + n, :], ot[:n, :], [otb], [b])
            self.dma(dr["ret_state_" + nm].rearrange("h (e p) v -> p (h e) v", e=2), S[:, :, :], [Sb],
                     [self.buf(("ret_state", g.gi))])
        P.barrier()
        self.load_w(self.W, self.Wb, dr["ret_w_out"][D:2 * D, :], D)
        P.barrier(); A.reset()
        gnw, gnwb = A.alloc([128, 2048], F32), Buf()
        self.dma(gnw[:, :], dr["ret_gn_w"][0:1, :].to_broadcast([128, 2048]), [], [gnwb])
        r_og = Ring(self, "rog2", [128, 2048], BF16, 2, arena=A)
        r_zs = Ring(self, "rzs2", [128, 2048], BF16, 2, arena=A)
        r_gz = Ring(self, "rgz2", [128, 2048], BF16, 2, arena=A)

        def wo(k):
            if k < 8:
                return self.WO[:, k, :], [self.WOb]
            return self.W[:, k - 8, 0:D], self.wdep(0, D, k - 8)
        items = []
        for g in self.groups:
            def mkprep(g, t0, n):
                def prep():
                    nm = g.name
                    og, ogb = r_og.next()
                    self.dma(og[:n, :], dr["OG_" + nm][t0:t0 + n, :], ogws[g.gi], [ogb])
                    zs, zsb = r_zs.next()
                    self.dma(zs[:n, :], dr["ZS_" + nm][t0:t0 + n, :], wbs[g.gi], [zsb])
                    gz, gzb = r_gz.next()
                    self.tt("dve", gz[:n, :], og[:n, :], gnw[:n, :], ALU.mult, [ogb, gnwb], [gzb])
                    self.tt("dve", gz[:n, :], gz[:n, :], zs[:n, :], ALU.mult, [gzb, zsb], [gzb])
                    return gz, gzb
                return prep
            items += [(g, t0, n, mkprep(g, t0, n)) for (t0, n) in g.tiles]
        self.pass_c_run(items, first, last, 16, wo=wo)

    def layer_gdn(self, L, first, last):
        dr = self.dr
        P = self.P
        A = self.arena
        P.barrier(); A.reset()
        self.set_psum(4, 2, 2)
        self.load_w(self.W, self.Wb, dr["gdn_w_in"], 4112)
        self.load_w(self.WO, self.WOb, dr["gdn_w_out"], D)
        cw, cwb = A.alloc([128, 24, 4], F32), Buf()
        self.dma(cw[:, :, :], dr["gdn_convw"][:, :, :], [], [cwb])
        halo, halob = A.alloc([128, 24, 3], F32), Buf()
        onesb, onesbb = A.alloc([128, 128], BF16), Buf()
        self.memset("pool", onesb[:, :], 1.0, [onesbb])
        sc8, sc8b = A.alloc([8, 4], F32), Buf()
        self.dma(sc8[:, 0:1], dr["gdn_dt_bias"][:, :], [], [sc8b])
        self.dma(sc8[:, 2:3], dr["gdn_a_log"][:, :], [], [sc8b])
        self.act(sc8[:, 3:4], sc8[:, 2:3], AF.Exp, [sc8b], [sc8b])
        self.ts("dve", sc8[:, 1:2], sc8[:, 3:4], -1.0, None, ALU.mult, None, [sc8b], [sc8b])
        r_xr = Ring(self, "rxr", [128, 516], F32, 4, arena=A)
        r_y = Ring(self, "ry", [128, 512], F32, 4, arena=A)
        r_rn = Ring(self, "rrn", [128, 512], F32, 4, arena=A)
        r_sq = Ring(self, "rsq", [128, 512], BF16, 4, arena=A)
        r_st = Ring(self, "rst", [128, 8, 512], BF16, 1, arena=A)
        r_g8 = Ring(self, "rg8", [8, 512], F32, 6, arena=A)
        wbs = {}
        for g in self.groups:
            nm = g.name
            wb = wbs[g.gi] = []

            def nb_(key, wb=wb, g=g):
                b = self.buf(("gdn", g.gi) + key)
                wb.append(b)
                return b
            if g.gi == 0:
                self.memset("pool", halo[:, :, :], 0.0, [halob])
            else:
                self.dma(halo[:, :, :], dr["gdn_conv_in"][:, :, :], [], [halob])
            for blk in g.blocks:
                uT, uTb, nb = self.make_uT(g, blk, first)
                t0b = blk[0][0]
                for which, dname in ((0, "QG_"), (1, "KG_"), (2, "VG_")):
                    st, stb = r_st.next()
                    for hg in (0, 4):
                        hs = list(range(hg, hg + 4))
                        xrs, ys, rns, sqs, p2s = {}, {}, {}, {}, {}
                        for h in hs:
                            c = which * 8 + h
                            ps, pb = self.proj_fm(uT, uTb, nb, c * 128, (c + 1) * 128)
                            xr, xrb = r_xr.next()
                            self.cp("act", xr[:, 0:3], halo[:, c, :], [halob], [xrb])
                            self.cp("act", xr[:, 3:3 + nb], ps[:, 0:nb], [pb], [xrb])
                            self.cp("act", halo[:, c, :], xr[:, nb:nb + 3], [xrb], [halob])
                            xrs[h] = (xr, xrb)
                        for h in hs:
                            c = which * 8 + h
                            xr, xrb = xrs[h]
                            y, yb = r_y.next()
                            self.ts("dve", y[:, 0:nb], xr[:, 0:nb], cw[:, c, 0:1], None, ALU.mult, None, [xrb, cwb], [yb])
                            for j in range(1, 4):
                                self.stt(y[:, 0:nb], xr[:, j:j + nb], cw[:, c, j:j + 1], y[:, 0:nb], ALU.mult, ALU.add,
                                         [xrb, cwb, yb], [yb])
                            ys[h] = (y, yb)
                        for h in hs:
                            y, yb = ys[h]
                            self.act(y[:, 0:nb], y[:, 0:nb], AF.Silu, [yb], [yb])
                        if which == 2:
                            for h in hs:
                                y, yb = ys[h]
                                self.cp("dve", st[:, h, 0:nb], y[:, 0:nb], [yb], [stb])
                            continue
                        for h in hs:
                            y, yb = ys[h]
                            sq, sqb = r_sq.next()
                            self.tt("dve", sq[:, 0:nb], y[:, 0:nb], y[:, 0:nb], ALU.mult, [yb], [sqb])
                            p2, p2b = self.ps_mm.next()
                            self.mm(p2[:, 0:nb], onesb[:, :], sq[:, 0:nb], True, True, [onesbb, sqb], [p2b])
                            p2s[h] = (p2, p2b)
                        for h in hs:
                            p2, p2b = p2s[h]
                            rn, rnb = r_rn.next()
                            self.rsqrt(rn[:, 0:nb], p2[:, 0:nb], [p2b], [rnb], eps=NORM_EPS)
                            rns[h] = (rn, rnb)
                        for h in hs:
                            y, yb = ys[h]
                            rn, rnb = rns[h]
                            if which == 0:
                                self.stt(st[:, h, 0:nb], y[:, 0:nb], 128.0 ** -0.5, rn[:, 0:nb], ALU.mult, ALU.mult,
                                         [yb, rnb], [stb])
                            else:
                                self.tt("dve", st[:, h, 0:nb], y[:, 0:nb], rn[:, 0:nb], ALU.mult, [yb, rnb], [stb])
                    self.dma(dr[dname + nm].rearrange("h p t -> p h t")[:, :, t0b:t0b + nb], st[:, :, 0:nb], [stb],
                             [nb_((dname, t0b))])
                ps, pb = self.proj_fm(uT, uTb, nb, 3072, 3080)
                bt, btb = r_g8.next()
                self.act(bt[:, 0:nb], ps[0:8, 0:nb], AF.Exp, [pb], [btb], scale=-1.0)
                self.ts("dve", bt[:, 0:nb], bt[:, 0:nb], 1.0, None, ALU.add, None, [btb], [btb])
                self.recip(bt[:, 0:nb], bt[:, 0:nb], [btb], [btb])
                ps, pb = self.proj_fm(uT, uTb, nb, 3080, 3088)
                gt, gtb = r_g8.next()
                self.act(gt[:, 0:nb], ps[0:8, 0:nb], AF.Exp, [pb, sc8b], [gtb], bias=sc8[:, 0:1])
                self.act(gt[:, 0:nb], gt[:, 0:nb], AF.Ln, [gtb], [gtb], bias=1.0)
                self.ts("dve", gt[:, 0:nb], gt[:, 0:nb], sc8[:, 1:2], None, ALU.mult, None, [gtb, sc8b], [gtb])
                Gt, Gtb = r_g8.next()
                on8, on8b = r_g8.next()
                self.memset("pool", on8[:, 0:128], 1.0, [on8b])
                for off in range(0, nb, 64):
                    n = min(64, nb - off)
                    P.op("dve", lambda e, Gt=Gt, gt=gt, on8=on8, off=off, n=n: e.tensor_tensor_scan(
                        out=Gt[:, off:off + n], data0=on8[:, 0:n], data1=gt[:, off:off + n], initial=0.0,
                        op0=ALU.mult, op1=ALU.add), [gtb, on8b], [Gtb])
                btb16, btb16b = r_g8.next()
                bt16v = btb16[:, 0:256].bitcast(BF16)
                self.cp("act", bt16v[:, 0:nb], bt[:, 0:nb], [btb], [btb16b])
                self.dma(dr["BTb_" + nm][:, t0b:t0b + nb], bt16v[:, 0:nb], [btb16b], [nb_(("btb", t0b))])
                self.dma(dr["GT_" + nm][:, t0b:t0b + nb], Gt[:, 0:nb], [Gtb], [nb_(("gt", t0b))])
                off = 0
                for ti, (t0, n) in enumerate(blk):
                    tp, tpb = self.ps_tp.next()
                    self.tr(tp[0:n, 0:8], bt[:, off:off + n], self.identf[0:8, 0:8], [btb, self.identf_b], [tpb])
                    tp2, tp2b = self.ps_tp.next()
                    self.tr(tp2[0:n, 0:8], Gt[:, off:off + n], self.identf[0:8, 0:8], [Gtb, self.identf_b], [tp2b])
                    gb, gbb = self.r_small.next()
                    self.cp("act", gb[:n, 0:8], tp[0:n, 0:8], [tpb], [gbb])
                    self.cp("act", gb[:n, 8:16], tp2[0:n, 0:8], [tp2b], [gbb])
                    self.dma(dr["GB_" + nm][t0:t0 + n, :], gb[:n, 0:16], [gbb], [nb_(("gb", t0))])
                    zt, ztb = self.r_h.next()
                    for cg in range(2):
                        ps, pb = self.proj_tm(uT, uTb, off, n, 3088 + cg * 512, 3088 + (cg + 1) * 512)
                        self.act(zt[:n, cg * 512:(cg + 1) * 512], ps[:n, :], AF.Silu, [pb], [ztb])
                    self.dma(dr["ZS_" + nm][t0:t0 + n, 0:D], zt[:n, 0:D], [ztb], [nb_(("z", t0))])
                    off += n
                if blk is g.blocks[-1]:
                    co, cob = self.r_f.next()
                    for cg in range(6):
                        ps, pb = self.ps_mm.next()
                        for k in range(8):
                            self.mm(ps[0:3, :], uT[:, k, nb - 3:nb], self.W[:, k, cg * 512:(cg + 1) * 512], k == 0, k == 7,
                                    [uTb] + self.wdep(cg * 512, (cg + 1) * 512, k), [pb])
                        self.cp("act", co[0:3, (cg % 2) * 512:(cg % 2 + 1) * 512], ps[0:3, :], [pb], [cob])
                        if cg % 2 == 1:
                            self.dma(dr["gdn_conv_" + nm][:, (cg - 1) * 512:(cg + 1) * 512], co[0:3, :], [cob],
                                     [self.buf(("gdn_conv", g.gi, cg))])
                            if cg < 5:
                                co, cob = self.r_f.next()
        import os
        if os.environ.get("GDN_STOP") == "A":
            return
        P.barrier(); A.reset()
        A2 = self.arenaW
        A2.reset()
        msk, mskb = A.alloc([128, 3, 128], F32), Buf()
        self.dma(msk[:, :, :], dr["gdn_masks"][:, :, :], [], [mskb])
        S, Sb = A.alloc([128, 8, 128], F32), Buf()
        Sbf, Sbfb = A.alloc([128, 8, 128], BF16), Buf()
        r_qkv = Ring(self, "rqkv", [128, 3, 8, 128], BF16, 2, arena=A)
        r_gb = Ring(self, "rgb", [128, 8, 128], F32, 2, arena=A)
        r_bb = Ring(self, "rbb", [128, 8, 128], BF16, 2, arena=A)
        r_gbk = Ring(self, "rgbk", [64, 2, 16], F32, 2, arena=A)
        r_sq = Ring(self, "rsq", [64, 8, 128], F32, 1, arena=A)
        r_sc = Ring(self, "rsc", [64, 48], F32, 6, arena=A)
        r_F = Ring(self, "rF", [128, 8, 64], F32, 14, arena=A2)
        r_B = Ring(self, "rB", [128, 8, 64], BF16, 16, arena=A2)
        r_L = Ring(self, "rL", [128, 8, 64], BF16, 10, arena=A2)
        r_B2 = Ring(self, "rB2", [64, 8, 128], BF16, 8, arena=A)
        ogws = {}

        def bc(ap, shape):
            return ap.to_broadcast(shape)
        for g in self.groups:
            nm = g.name
            wb = wbs[g.gi]
            ogw = ogws[g.gi] = []
            if g.gi == 0:
                self.memset("pool", S[:, :, :], 0.0, [Sb])
            else:
                self.dma(S[:, :, :], dr["state_gdn"].rearrange("h d v -> d h v"), [], [Sb])
            self.cp("act", Sbf[:, :, :], S[:, :, :], [Sb], [Sbfb])
            for (tt0, tn) in g.tiles:
                qkv, qkvb = r_qkv.next()
                for i, dname in enumerate(("QG_", "KG_", "VG_")):
                    self.dma(qkv[:, i, :, 0:tn], dr[dname + nm].rearrange("h p t -> p h t")[:, :, tt0:tt0 + tn], wb, [qkvb])
                gb, gbb = r_gb.next()
                self.dma(gb[:, :, 0:tn], bass.AP(self.dh["GT_" + nm], tt0, [[0, 128], [g.T, 8], [1, tn]]), wb, [gbb])
                bb, bbb = r_bb.next()
                self.dma(bb[:, :, 0:tn], bass.AP(self.dh["BTb_" + nm], tt0, [[0, 128], [g.T, 8], [1, tn]]), wb, [bbb])
                gbk, gbkb = r_gbk.next()
                if tn % 64 == 0:
                    self.dma(gbk[0:64, 0:tn // 64, :], dr["GB_" + nm][tt0:tt0 + tn, :].rearrange("(c p) x -> p c x", p=64),
                             wb, [gbkb])
                else:
                    self.dma(gbk[:tn, 0, :], dr["GB_" + nm][tt0:tt0 + tn, :], wb, [gbkb])
                from types import SimpleNamespace as NS
                cks = []
                for ci, off in enumerate(range(0, tn, 64)):
                    c = NS(ci=ci, off=off, n=min(64, tn - off), t0=tt0 + off)
                    c.Qt = qkv[:, 0, :, off:off + c.n]
                    c.Kt = qkv[:, 1, :, off:off + c.n]
                    c.Vt = qkv[:, 2, :, off:off + c.n]
                    c.Gp = gbk[:c.n, ci, 8:16]
                    c.Bp = gbk[:c.n, ci, 0:8]
                    cks.append(c)
                mk = lambda c, k_: msk[:c.n, k_, 0:c.n].unsqueeze(1).to_broadcast([c.n, 8, c.n])
                v3 = lambda c, ps_: ps_[:c.n, :].rearrange("p (h c) -> p h c", h=8)[:, :, 0:c.n]
                t3 = lambda c, v_: v_[:c.n, :].rearrange("p (h c) -> p h c", h=8)
                bcs = lambda c, a_: a_.unsqueeze(2).to_broadcast([c.n, 8, 128])

                def s_bcast(c):
                    n, off = c.n, c.off
                    c.eG, c.eGb = r_F.next()
                    self.act(c.eG[:, :, 0:n], gb[:, :, off:off + n], AF.Exp, [gbb], [c.eGb])
                    c.KbT, c.KbTb = r_B.next()
                    self.tt("dve", c.KbT[:, :, 0:n], c.Kt, bb[:, :, off:off + n], ALU.mult, [qkvb, bbb], [c.KbTb])
                    c.sc, c.scb = r_sc.next()
                    self.act(c.sc[:n, 16:24], c.Gp, AF.Exp, [gbkb], [c.scb])
                    c.tD, c.tDb = r_F.next()
                    self.tt("dve", c.tD[:n, :, 0:n], gb[:n, :, off:off + n], c.Gp.unsqueeze(2).to_broadcast([n, 8, n]),
                            ALU.subtract, [gbb, gbkb], [c.tDb])

                def s_gram(c):
                    n = c.n
                    c.paA, c.paAb = self.ps_mm.next()
                    for h in range(8):
                        self.mm(c.paA[:n, h * 64:h * 64 + n], c.Kt[:, h, :], c.KbT[:, h, 0:n], True, True, [qkvb, c.KbTb], [c.paAb], skip=True)
                    c.paB, c.paBb = self.ps_mm.next()
                    for h in range(8):
                        self.mm(c.paB[:n, h * 64:h * 64 + n], c.KbT[:, h, 0:n], c.Kt[:, h, :], True, True, [qkvb, c.KbTb], [c.paBb], skip=True)
                    c.D1, c.D1b = r_F.next()
                    self.ts("dve", c.D1[:n, :, 0:n], c.tD[:n, :, 0:n], 0.0, None, ALU.min, None, [c.tDb], [c.D1b])
                    self.act(c.D1[:n, :, 0:n], c.D1[:n, :, 0:n], AF.Exp, [c.D1b], [c.D1b])
                    c.D2, c.D2b = r_F.next()
                    self.ts("dve", c.D2[:n, :, 0:n], c.tD[:n, :, 0:n], 0.0, None, ALU.max, None, [c.tDb], [c.D2b])
                    self.act(c.D2[:n, :, 0:n], c.D2[:n, :, 0:n], AF.Exp, [c.D2b], [c.D2b], scale=-1.0)

                def s_mask(c):
                    n = c.n
                    c.D1i, c.D1ib = r_F.next()
                    self.tt("dve", c.D1i[:n, :, 0:n], c.D1[:n, :, 0:n], mk(c, 2), ALU.mult, [c.D1b, mskb], [c.D1ib])
                    self.tt("dve", c.D1[:n, :, 0:n], c.D1[:n, :, 0:n], mk(c, 0), ALU.mult, [c.D1b, mskb], [c.D1b])
                    self.tt("dve", c.D2[:n, :, 0:n], c.D2[:n, :, 0:n], mk(c, 1), ALU.mult, [c.D2b, mskb], [c.D2b])
                    c.Pm, c.Pmb = r_F.next()
                    self.tt("dve", c.Pm[:n, :, 0:n], v3(c, c.paA), c.D1[:n, :, 0:n], ALU.mult, [c.paAb, c.D1b], [c.Pmb])
                    c.X, c.Xb = r_B.next()
                    self.cp("act", c.X[:n, :, 0:n], c.Pm[:n, :, 0:n], [c.Pmb], [c.Xb])
                    c.Y, c.Yb = r_B.next()
                    self.tt("dve", c.Y[:n, :, 0:n], v3(c, c.paB), c.D2[:n, :, 0:n], ALU.mult, [c.paBb, c.D2b], [c.Yb])
                    self.tt("dve", c.Pm[:n, :, 0:n], c.Pm[:n, :, 0:n],
                            self.identf[:n, 0:n].unsqueeze(1).to_broadcast([n, 8, n]), ALU.add, [c.Pmb, self.identf_b], [c.Pmb])
                    c.Pbf, c.Pbfb = r_B.next()
                    self.cp("act", c.Pbf[:n, :, 0:n], c.Pm[:n, :, 0:n], [c.Pmb], [c.Pbfb])
                    c.nsteps = max(1, int(math.ceil(math.log2(n))) - 1)

                def mk_step(s):
                    def s_sq(c):
                        n = c.n
                        if s >= c.nsteps:
                            return
                        c.lastst = s == c.nsteps - 1
                        c.pxY, c.pxYb = self.ps_mm.next()
                        for h in range(8):
                            self.mm(c.pxY[:n, h * 64:h * 64 + n], c.X[:n, h, 0:n], c.Y[:n, h, 0:n], True, True, [c.Xb, c.Yb], [c.pxYb], skip=True)
                        if not c.lastst:
                            c.pxX, c.pxXb = self.ps_mm.next()
                            for h in range(8):
                                self.mm(c.pxX[:n, h * 64:h * 64 + n], c.Y[:n, h, 0:n], c.X[:n, h, 0:n], True, True, [c.Xb, c.Yb], [c.pxXb], skip=True)
                        c.Y2, c.Y2b = r_B.next()
                        self.cp("act", c.Y2[:n, :, 0:n], v3(c, c.pxY), [c.pxYb], [c.Y2b])
                        if not c.lastst:
                            c.X2, c.X2b = r_B.next()
                            self.cp("dve", c.X2[:n, :, 0:n], v3(c, c.pxX), [c.pxXb], [c.X2b])

                    def s_acc(c):
                        n = c.n
                        if s >= c.nsteps:
                            return
                        pp, ppb = self.ps_mm.next()
                        for h in range(8):
                            self.mm(pp[:n, h * 64:h * 64 + n], c.Y2[:n, h, 0:n], c.Pbf[:n, h, 0:n], True, True, [c.Y2b, c.Pbfb], [ppb], skip=True)
                        self.tt("dve", c.Pm[:n, :, 0:n], c.Pm[:n, :, 0:n], v3(c, pp), ALU.add, [c.Pmb, ppb], [c.Pmb])
                        c.Pbf, c.Pbfb = r_L.next() if c.lastst else r_B.next()
                        self.cp("act", c.Pbf[:n, :, 0:n], c.Pm[:n, :, 0:n], [c.Pmb], [c.Pbfb])
                        c.Y, c.Yb = c.Y2, c.Y2b
                        if not c.lastst:
                            c.X, c.Xb = c.X2, c.X2b
                    return [s_sq, s_acc]

                def s_tok(c):
                    n, off = c.n, c.off
                    self.tt("dve", c.sc[:n, 0:8], c.sc[:n, 16:24], c.Bp, ALU.mult, [c.scb, gbkb], [c.scb])
                    self.tt("dve", c.sc[:n, 24:32], c.Gp, gb[:n, :, off + n - 1], ALU.subtract, [gbkb, gbb], [c.scb])
                    self.act(c.sc[:n, 8:16], c.sc[:n, 24:32], AF.Exp, [c.scb], [c.scb], scale=-1.0)
                    tpK, tpKb = self.ps_tp.next()
                    tpV, tpVb = self.ps_tp.next()
                    tKv = tpK[:, :].bitcast(BF16)
                    tVv = tpV[:, :].bitcast(BF16)
                    for h in range(8):
                        self.tr(tKv[:n, h * 128:(h + 1) * 128], c.Kt[:, h, :], self.identb[:, :], [qkvb, self.identb_b], [tpKb])
                    for h in range(8):
                        self.tr(tVv[:n, h * 128:(h + 1) * 128], c.Vt[:, h, :], self.identb[:, :], [qkvb, self.identb_b], [tpVb])
                    c.kbg, c.kbgb = r_B2.next()
                    self.tt("dve", c.kbg[:n, :, :], t3(c, tKv), bcs(c, c.sc[:n, 0:8]), ALU.mult, [tpKb, c.scb], [c.kbgb])
                    c.kdc, c.kdcb = r_B2.next()
                    self.tt("dve", c.kdc[:n, :, :], t3(c, tKv), bcs(c, c.sc[:n, 8:16]), ALU.mult, [tpKb, c.scb], [c.kdcb])
                    c.vb_, c.vbb_ = r_B2.next()
                    self.tt("dve", c.vb_[:n, :, :], t3(c, tVv), bcs(c, c.Bp), ALU.mult, [tpVb, gbkb], [c.vbb_])
                    c.qdT, c.qdTb = r_L.next()
                    self.tt("dve", c.qdT[:, :, 0:n], c.Qt, c.eG[:, :, 0:n], ALU.mult, [qkvb, c.eGb], [c.qdTb])
                    c.paQ, c.paQb = self.ps_mm.next()
                    for h in range(8):
                        self.mm(c.paQ[:n, h * 64:h * 64 + n], c.Kt[:, h, :], c.Qt[:, h, :], True, True, [qkvb], [c.paQb], skip=True)
                    c.QK, c.QKb = r_L.next()
                    self.tt("dve", c.QK[:n, :, 0:n], v3(c, c.paQ), c.D1i[:n, :, 0:n], ALU.mult, [c.paQb, c.D1ib], [c.QKb])

                def s_wd(c):
                    n = c.n
                    pw, pwb = self.ps_mm.next()
                    for h in range(8):
                        self.mm(pw[:, h * 64:h * 64 + n], c.kbg[:n, h, :], c.Pbf[:n, h, 0:n], True, True, [c.kbgb, c.Pbfb], [pwb], skip=True)
                    c.wdT, c.wdTb = r_L.next()
                    self.act(c.wdT[:, :, 0:n], pw[:, :].rearrange("p (h c) -> p h c", h=8)[:, :, 0:n], AF.Copy, [pwb], [c.wdTb],
                             scale=-1.0)

                stages = [s_bcast, s_gram, s_mask]
                for s in range(5):
                    stages += mk_step(s)
                stages += [s_tok, s_wd]
                for f in stages:
                    for c in cks:
                        f(c)

                for c in cks:
                    n, off, t0 = c.n, c.off, c.t0
                    pv = [self.ps_acc.next(), self.ps_acc.next()]
                    for h in range(8):
                        pt_, ptb_ = pv[h // 4]
                        o_ = pt_[:n, (h % 4) * 128:(h % 4 + 1) * 128]
                        self.mm(o_, c.Pbf[:n, h, 0:n], c.vb_[:n, h, :], True, False, [c.Pbfb, c.vbb_], [ptb_], skip=True)
                        self.mm(o_, c.wdT[:, h, 0:n], Sbf[:, h, :], False, True, [c.wdTb, Sbfb], [ptb_], skip=True)
                    vr, vrb = r_B2.next()
                    for hb in range(2):
                        self.cp("act", vr[:n, hb * 4:(hb + 1) * 4, :], pv[hb][0][:n, :].rearrange("p (h c) -> p h c", h=4),
                                [pv[hb][1]], [vrb])
                    self.tt("dve", S[:, :, :], S[:, :, :], c.eG[:, :, n - 1:n].to_broadcast([128, 8, 128]), ALU.mult,
                            [Sb, c.eGb], [Sb])
                    po = [self.ps_acc.next(), self.ps_acc.next()] if False else [self.ps_mm.next(), self.ps_mm.next()]
                    for h in range(8):
                        pt_, ptb_ = po[h // 4]
                        o_ = pt_[:n, (h % 4) * 128:(h % 4 + 1) * 128]
                        self.mm(o_, c.qdT[:, h, 0:n], Sbf[:, h, :], True, False, [c.qdTb, Sbfb], [ptb_], skip=True)
                        self.mm(o_, c.QK[:n, h, 0:n], vr[:n, h, :], False, True, [c.QKb, vrb], [ptb_], skip=True)
                    pu = [self.ps_mm.next(), self.ps_mm.next()]
                    for h in range(8):
                        pt_, ptb_ = pu[h // 4]
                        self.mm(pt_[:, (h % 4) * 128:(h % 4 + 1) * 128], c.kdc[:n, h, :], vr[:n, h, :], True, True,
                                [c.kdcb, vrb], [ptb_], skip=True)
                    for hb in range(2):
                        self.tt("dve", S[:, hb * 4:(hb + 1) * 4, :], S[:, hb * 4:(hb + 1) * 4, :],
                                pu[hb][0][:, :].rearrange("p (h c) -> p h c", h=4), ALU.add, [Sb, pu[hb][1]], [Sb])
                    self.cp("act", Sbf[:, :, :], S[:, :, :], [Sb], [Sbfb])
                    sq, sqb = r_sq.next()
                    for hb in range(2):
                        self.act(sq[:n, hb * 4:(hb + 1) * 4, :], po[hb][0][:n, :].rearrange("p (h c) -> p h c", h=4),
                                 AF.Square, [po[hb][1]], [sqb])
                    sc, scb = c.sc, c.scb
                    P.op("dve", lambda e, sc=sc, sq=sq, n=n: e.tensor_reduce(out=sc[:n, 32:40], in_=sq[:n, :, :], axis=AX.X,
                                                                         op=ALU.add), [sqb], [scb])
                    self.rsqrt(sc[:n, 32:40], sc[:n, 32:40], [scb], [scb], scale=1.0 / 128, eps=NORM_EPS)
                    ot, otb = r_B2.next()
                    for hb in range(2):
                        self.tt("dve", ot[:n, hb * 4:(hb + 1) * 4, :], po[hb][0][:n, :].rearrange("p (h c) -> p h c", h=4),
                                sc[:n, 32 + hb * 4:36 + hb * 4].unsqueeze(2).to_broadcast([n, 4, 128]), ALU.mult,
                                [po[hb][1], scb], [otb])
                    b = self.buf(("gdn_og", g.gi, t0))
                    ogw.append(b)
                    self.dma(dr["OG_" + nm][t0:t0 + n, 0:D], ot[:n, :, :].rearrange("p h c -> p (h c)"), [otb], [b])
            self.dma(dr["gdn_state_" + nm].rearrange("h d v -> d h v"), S[:, :, :], [Sb], [self.buf(("gdn_state", g.gi))])
        if os.environ.get("GDN_STOP") == "B":
            return
        P.barrier(); A.reset()
        r_gz = Ring(self, "rgz", [128, D], F32, 2, arena=A)
        nw, nwb = A.alloc([128, D], F32), Buf()
        for h in range(8):
            self.dma(nw[:, h * 128:(h + 1) * 128], dr["gdn_norm_w"][0:1, :].to_broadcast([128, 128]), [], [nwb])
        items = []
        for g in self.groups:
            def mkprep(g, t0, n):
                def prep():
                    nm = g.name
                    og, ogb = self.r_h.next()
                    self.dma(og[:n, 0:D], dr["OG_" + nm][t0:t0 + n, 0:D], ogws[g.gi], [ogb])
                    zs, zsb = self.r_h.next()
                    self.dma(zs[:n, 0:D], dr["ZS_" + nm][t0:t0 + n, 0:D], wbs[g.gi], [zsb])
                    gz, gzb = r_gz.next()
                    self.tt("dve", gz[:n, :], og[:n, 0:D], nw[:n, :], ALU.mult, [ogb, nwb], [gzb])
                    self.tt("dve", og[:n, 0:D], gz[:n, :], zs[:n, 0:D], ALU.mult, [gzb, zsb], [ogb])
                    return og, ogb
                return prep
            items += [(g, t0, n, mkprep(g, t0, n)) for (t0, n) in g.tiles]
        self.pass_c_run(items, first, last, 8)

    def layer_dummy(self, L, first, last):
        self.layer_mod(L)
        self.load_w(self.W, self.Wb, self.dr["fox_w_in"], 4112)
        self.load_w(self.WO, self.WOb, self.dr["fox_w_out"], D)
        for g in self.groups:
            for blk in g.blocks:
                uT, uTb, nb = self.make_uT(g, blk, first)
                off = 0
                for (t0, n) in blk:
                    og, ogb = self.r_h.next()
                    for cg in range(2):
                        ps, pb = self.proj_tm(uT, uTb, off, n, cg * 512, (cg + 1) * 512)
                        self.cp("act", og[:n, cg * 512:(cg + 1) * 512], ps[:n, :], [pb], [ogb])
                    self.pass_c_tile(g, t0, n, first, last, og, ogb, 8)
                    off += n


def build(cfg, mode="full"):
    k = K(cfg)
    k.setup()
    nl = len(cfg.layers)
    for i, L in enumerate(cfg.layers):
        if mode == "dummy":
            k.layer_dummy(L, i == 0, i == nl - 1)
        else:
            k.layer(L, i == 0, i == nl - 1)
    k.P.emit()
    return k


def t5_onehot():
    import jax
    import jax.numpy as jnp
    with jax.default_device(jax.devices("cpu")[0]):
        rel = jnp.arange(384, dtype=jnp.int32) - 255
        nb = 16
        max_exact = 8
        ret = jnp.where(rel > 0, nb, 0)
        n = jnp.abs(rel)
        large = max_exact + (jnp.log(jnp.maximum(n, 1).astype(jnp.float32) / max_exact)
                             / math.log(128 / max_exact) * (nb - max_exact)).astype(jnp.int32)
        large = jnp.minimum(large, nb - 1)
        bucket = np.asarray(ret + jnp.where(n < max_exact, n, large))
    oh = np.zeros((32, 384), np.float32)
    oh[bucket, np.arange(384)] = 1.0
    return oh


def ret_perm():
    one = np.concatenate([np.arange(0, 256, 2), np.arange(1, 256, 2)])
    return np.concatenate([h * 256 + one for h in range(4)])


_RC = {}


def ret_consts(cfg):
    key = (cfg.T, cfg.TS, cfg.PAST)
    if key in _RC:
        return _RC[key]
    import jax
    import jax.numpy as jnp
    out = {}
    with jax.default_device(jax.devices("cpu")[0]):
        inv = 10000.0 ** (-jnp.arange(0, 256, 2, dtype=jnp.float32) / 256)
        for nm, start, T in (("p", 0, cfg.T), ("s", cfg.PAST, cfg.TS)):
            pos = (start + jnp.arange(T)).astype(jnp.float32)
            ang = pos[:, None] * inv[None, :]
            out["rot_cos_" + nm] = np.ascontiguousarray(np.asarray(jnp.cos(ang)).T)
            out["rot_sin_" + nm] = np.ascontiguousarray(np.asarray(jnp.sin(ang)).T)
        lg = np.asarray(jnp.log1p(-jnp.power(2.0, -5.0 - jnp.arange(4, dtype=jnp.float32))), np.float64)
    i = np.arange(128)
    dm = np.zeros((128, 4, 128), np.float32)
    qd = np.zeros((128, 4, 128), np.float32)
    kd = np.zeros((128, 8), np.float32)
    for h in range(4):
        d = i[None, :] - i[:, None]
        dm[:, h, :] = np.where(d >= 0, np.exp(lg[h] * np.maximum(d, 0)), 0.0)
        qd[:, h, :] = np.exp(lg[h] * (i + 1.0))[None, :]
        kd[:, h] = np.exp(lg[h] * (127.0 - i))
        kd[:, 4 + h] = np.exp(lg[h] * (cfg.TS - 1.0 - i))
    out["ret_dmaskT"], out["ret_qdec"], out["ret_kdec"] = dm, qd, kd
    _RC[key] = out
    return out


def core_inputs(cfg, inp, b):
    T, PAST = cfg.T, cfg.PAST
    f = np.ascontiguousarray
    cT = np.stack([inp["c_prompt"][b], inp["c_sample"][b]], -1).reshape(8, 128, 2).transpose(1, 0, 2)
    m = {
        "xp": f(inp["x_prompt"][b, :T]), "xs": f(inp["x_sample"][b]), "cT": f(cT),
        "ada_w": inp["ada_w"], "ada_b": inp["ada_b"], "ln_g": inp["ln_g"], "ln_b": inp["ln_b"],
        "ident": np.eye(128, dtype=np.float32),
        "cmask": np.where(np.arange(128)[:, None] > np.arange(128)[None, :], -30000.0, 0.0).astype(np.float32),
        "fox_w_in": inp["fox_w_in"], "fox_b_f": f(inp["fox_b_f"].reshape(16, 1)), "fox_w_out": inp["fox_w_out"],
        "cache_fox_k": f(inp["cache_fox_k"][b, :PAST].reshape(PAST, -1)),
        "cache_fox_v": f(inp["cache_fox_v"][b, :PAST].reshape(PAST, -1)),
        "cache_fox_logf": f(inp["cache_fox_logf"][b, :PAST]),
        "diff_w_in": inp["diff_w_in"], "diff_w_out": inp["diff_w_out"], "rel_bias_table": inp["rel_bias_table"],
        "diff_lam_q1": f(inp["diff_lam_q1"].reshape(1, 64)), "diff_lam_k1": f(inp["diff_lam_k1"].reshape(1, 64)),
        "diff_lam_q2": f(inp["diff_lam_q2"].reshape(1, 64)), "diff_lam_k2": f(inp["diff_lam_k2"].reshape(1, 64)),
        "diff_subln_w": f(inp["diff_subln_w"].reshape(1, 128)), "t5_onehot": t5_onehot(),
        "antiident": np.ascontiguousarray(np.eye(128, dtype=np.float32)[::-1]),
        "dmask": np.where((np.arange(128)[:, None] >= 64) & (np.arange(128)[None, :] < 64), -30000.0, 0.0).astype(np.float32),
        "cache_diff_k": f(inp["cache_diff_k"][b, :PAST].reshape(PAST, -1)),
        "cache_diff_v": f(inp["cache_diff_v"][b, :PAST].reshape(PAST, -1)),
    }
    m.update(ret_consts(cfg))
    i = np.arange(128)
    msk = np.zeros((128, 3, 128), np.float32)
    msk[:, 0, :] = np.where(i[None, :] > i[:, None], -1.0, 0.0)
    msk[:, 1, :] = np.where(i[None, :] < i[:, None], -1.0, 0.0)
    msk[:, 2, :] = np.where(i[None, :] >= i[:, None], 1.0, 0.0)
    sel = np.zeros((8, 8, 128), np.float32)
    for h in range(8):
        sel[h, h, :] = 1.0
    m.update({
        "gdn_w_in": inp["gdn_w_in"], "gdn_w_out": inp["gdn_w_out"],
        "gdn_convw": f(inp["gdn_conv_w"].reshape(4, 24, 128).transpose(2, 1, 0)),
        "gdn_conv_in": f(inp["state_gdn_conv"][b].reshape(3, 24, 128).transpose(2, 1, 0)),
        "gdn_a_log": f(inp["gdn_a_log"].reshape(8, 1)), "gdn_dt_bias": f(inp["gdn_dt_bias"].reshape(8, 1)),
        "gdn_norm_w": f(inp["gdn_norm_w"].reshape(1, 128)), "state_gdn": f(inp["state_gdn"][b]),
        "gdn_masks": msk, "gdn_sel": sel,
    })
    perm = ret_perm()
    w = inp["ret_w_in"]
    m["ret_w_in"] = f(np.concatenate([w[:, 0:D][:, perm], w[:, D:2 * D][:, perm], w[:, 2 * D:]], axis=1))
    m["ret_w_out"] = inp["ret_w_out"]
    m["ret_gn_w"] = f(inp["ret_gn_w"].reshape(1, -1))
    m["state_ret"] = f(inp["state_ret"][b][:, perm[:256], :])
    return m


OUT_NAMES = ("y", "gdn_state_", "gdn_conv_", "fox_k_", "fox_v_", "fox_logf_", "diff_k_", "diff_v_", "ret_state_")


def gather(cfg, results):
    inv = np.argsort(ret_perm()[:256])
    outs = []
    for nm, T in (("p", cfg.T), ("s", cfg.TS)):
        g = {}
        for base in OUT_NAMES:
            key = base + nm
            g[base] = np.stack([np.asarray(r[key]) for r in results], 0)
        B = len(results)
        outs.append([
            g["y"],
            g["gdn_state_"],
            g["gdn_conv_"],
            g["fox_k_"].reshape(B, T, 16, 64),
            g["fox_v_"].reshape(B, T, 16, 64),
            g["fox_logf_"],
            g["diff_k_"].reshape(B, T, 8, 2, 64),
            g["diff_v_"].reshape(B, T, 8, 128),
            np.ascontiguousarray(g["ret_state_"][:, :, inv, :]),
        ])
    p, s = outs
    return (p[0], s[0]) + tuple(p[1:]) + tuple(s[1:])


def kernel(**inputs):
    inputs = {k: np.asarray(v) for k, v in inputs.items()}
    cfg = Cfg()
    k = build(cfg)
    in_maps = [core_inputs(cfg, inputs, b) for b in range(NCORES)]
    res = run_bass_kernel_spmd(k.nc, in_maps, core_ids=list(range(NCORES)))
    return gather(cfg, res.results)
```

```python
import math
from contextlib import ExitStack

import numpy as np
import ml_dtypes
import concourse.bass as bass
import concourse.mybir as mybir
from concourse.bass_utils import run_bass_kernel_spmd

F32 = mybir.dt.float32
BF16 = mybir.dt.bfloat16
AF = mybir.ActivationFunctionType
ALU = mybir.AluOpType
AX = mybir.AxisListType

D = 1024
DEPTH = 4
ALPHA = (2.0 * DEPTH) ** 0.25
LN_EPS = 1e-5
NORM_EPS = 1e-6
NCORES = 8


class Buf:
    __slots__ = ("w", "r", "x")

    def __init__(self, x=False):
        self.w = None
        self.r = {}
        self.x = x


class Prog:
    ENGS = ("pe", "act", "dve", "pool", "sp")
    NO_SELF_SYNC = ("pe",)

    def __init__(self, nc, es, ndma=28):
        self.nc = nc
        self.sem = {e: es.enter_context(nc.semaphore("s_" + e)) for e in self.ENGS}
        self.dsem = [es.enter_context(nc.semaphore("d%d" % i)) for i in range(ndma)]
        self.cnt = {e: 0 for e in self.ENGS}
        self.dcnt = [0] * ndma
        self.drr = 0
        self.drr_sw = 0
        self.NSW_BASE = ndma - 8
        self.seen = {e: {} for e in self.ENGS}
        self.q = {e: [] for e in self.ENGS}

    def _collect(self, reads, writes, e=None):
        deps = {}
        for b in reads:
            if b.w is not None and deps.get(b.w[0], 0) < b.w[1]:
                deps[b.w[0]] = b.w[1]
            if b.x:
                for k, v in b.r.items():
                    if k != e and deps.get(k, 0) < v:
                        deps[k] = v
        for b in writes:
            if b.w is not None and deps.get(b.w[0], 0) < b.w[1]:
                deps[b.w[0]] = b.w[1]
            for k, v in b.r.items():
                if deps.get(k, 0) < v:
                    deps[k] = v
        return deps

    def _waits(self, e, deps):
        out = []
        seen = self.seen[e]
        for k, v in deps.items():
            if k == e and e in self.NO_SELF_SYNC:
                continue
            if seen.get(k, 0) < v:
                seen[k] = v
                out.append((k, v))
        return out

    def _mark(self, tok, reads, writes):
        k, v = tok
        for b in reads:
            if b.r.get(k, 0) < v:
                b.r[k] = v
        for b in writes:
            b.w = tok
            b.r = {}

    def op(self, e, fn, reads=(), writes=()):
        waits = self._waits(e, self._collect(reads, writes, e))
        self.cnt[e] += 1
        self.q[e].append((waits, fn, None))
        self._mark((e, self.cnt[e]), reads, writes)

    def dma(self, e, fn, reads=(), writes=()):
        if e == "pool":
            s = self.NSW_BASE + self.drr_sw
            self.drr_sw = (self.drr_sw + 1) % (len(self.dsem) - self.NSW_BASE)
        else:
            s = self.drr
            self.drr = (s + 1) % self.NSW_BASE
        deps = self._collect(reads, writes, e)
        k = ("d", s)
        if self.dcnt[s] and deps.get(k, 0) < self.dcnt[s]:
            deps[k] = self.dcnt[s]
        waits = self._waits(e, deps)
        self.dcnt[s] += 16
        self.q[e].append((waits, fn, s))
        self._mark((k, self.dcnt[s]), reads, writes)

    def barrier(self):
        deps = {e: c for e, c in self.cnt.items() if c}
        for s, c in enumerate(self.dcnt):
            if c:
                deps[("d", s)] = c
        for e in self.ENGS:
            waits = self._waits(e, dict(deps))
            if waits:
                self.q[e].append((waits, None, None))

    def _semof(self, k):
        return self.sem[k] if isinstance(k, str) else self.dsem[k[1]]

    def emit(self):
        nc = self.nc
        with nc.Block() as block:
            regs = dict(sp=block.sync, pe=block.tensor, act=block.scalar, dve=block.vector, pool=block.gpsimd)
            for e in self.ENGS:
                def body(eng, e=e):
                    for waits, fn, ds in self.q[e]:
                        ws = [(self._semof(k), v) for k, v in waits]
                        if fn is None:
                            for sm, v in ws:
                                eng.wait_ge(sm, v)
                            continue
                        if ds is None and ws:
                            for sm, v in ws[:-1]:
                                eng.wait_ge(sm, v)
                            ins = fn(eng)
                            ins._wait_ge(ws[-1][0], ws[-1][1])
                        else:
                            for sm, v in ws:
                                eng.wait_ge(sm, v)
                            ins = fn(eng)
                        if ds is None:
                            ins.then_inc(self.sem[e], 1)
                        else:
                            ins.then_inc(self.dsem[ds], 16)
                    if e == "sp":
                        for s, c in enumerate(self.dcnt):
                            if c:
                                eng.wait_ge(self.dsem[s], c)
                regs[e](body)


class Ring:
    def __init__(self, K, name, shape, dtype, n, psum=False, arena=None):
        self.t = []
        for i in range(n):
            if arena is not None:
                t = arena.alloc(shape, dtype)
            else:
                f = K.nc.psum_tensor if psum else K.nc.sbuf_tensor
                t = K.es.enter_context(f("%s%d" % (name, i), shape, dtype))
            self.t.append((t, Buf(x=psum)))
        self.i = 0

    def next(self):
        r = self.t[self.i % len(self.t)]
        self.i += 1
        return r


class Arena:
    def __init__(self, K, nbytes, base=None):
        self.t = K.es.enter_context(K.nc.sbuf_tensor("arena", [128, nbytes // 4], F32)) if base is None else base
        self.nbytes = nbytes
        self.off = 0

    def reset(self):
        self.off = 0

    def alloc(self, shape, dtype):
        esz = 2 if dtype == BF16 else 4
        n = 1
        for s in shape[1:]:
            n *= s
        nb = (n * esz + 3) // 4 * 4
        assert self.off + nb <= self.nbytes, "arena overflow: need %d have %d" % (nb, self.nbytes - self.off)
        v = self.t[0:shape[0], self.off // 4:(self.off + nb) // 4]
        self.off += nb
        if dtype == BF16:
            v = v.bitcast(BF16)[:, 0:n]
        if len(shape) == 3:
            v = v.rearrange("p (a b) -> p a b", a=shape[1])
        elif len(shape) == 4:
            v = v.rearrange("p (a b c) -> p a b c", a=shape[1], b=shape[2])
        return v


class Cfg:
    def __init__(self, T=4096, TS=32, PAST=2048, layers=(0, 1, 2, 3)):
        self.T, self.TS, self.PAST, self.layers = T, TS, PAST, tuple(layers)


class Group:
    def __init__(self, gi, name, T, past):
        self.gi, self.name, self.T, self.past = gi, name, T, past
        self.S = past + T
        self.tiles = [(t0, min(128, T - t0)) for t0 in range(0, T, 128)]
        self.blocks = [self.tiles[i:i + 4] for i in range(0, len(self.tiles), 4)]
        self.ktiles = [(k0, 128) for k0 in range(0, past, 128)] + [(past + t0, n) for t0, n in self.tiles]


class K:
    def __init__(self, cfg):
        self.cfg = cfg
        self.nc = nc = bass.Bass("TRN2", target_bir_lowering=False)
        self.es = ExitStack()
        self.P = Prog(nc, self.es)
        self.bufs = {}
        self.dr = {}
        self.dh = {}
        self.all_groups = [Group(0, "p", cfg.T, 0), Group(1, "s", cfg.TS, cfg.PAST)]
        import os
        gs = os.environ.get("KGROUPS", "01")
        self.groups = [g for g in self.all_groups if str(g.gi) in gs]

    def buf(self, key):
        b = self.bufs.get(key)
        if b is None:
            b = self.bufs[key] = Buf()
        return b

    def din(self, name, shape, dtype=F32):
        t = self.nc.dram_tensor(name, list(shape), dtype, kind="ExternalInput")
        self.dh[name] = t
        self.dr[name] = t.ap()
        return self.dr[name]

    def dout(self, name, shape, dtype=F32):
        t = self.nc.dram_tensor(name, list(shape), dtype, kind="ExternalOutput")
        self.dr[name] = t.ap()
        return self.dr[name]

    def dscr(self, name, shape, dtype):
        t = self.nc.dram_tensor(name, list(shape), dtype)
        self.dh[name] = t
        self.dr[name] = t.ap()
        return self.dr[name]

    def sb(self, name, shape, dtype):
        t = self.es.enter_context(self.nc.sbuf_tensor(name, list(shape), dtype))
        return t, Buf()

    def dma(self, out, in_, rd, wr, eng="sp"):
        self.P.dma(eng, lambda e: e.dma_start(out=out, in_=in_), rd, wr)

    def mm(self, out, lhsT, rhs, start, stop, rd, wr, skip=False):
        self.P.op("pe", lambda e: e.matmul(out, lhsT, rhs, start=start, stop=stop, skip_group_check=skip), rd, wr)

    def tr(self, out, in_, ident, rd, wr):
        self.P.op("pe", lambda e: e.transpose(out, in_, ident), rd, wr)

    def act(self, out, in_, func, rd, wr, bias=0.0, scale=1.0, accum=None):
        if accum is None:
            self.P.op("act", lambda e: e.activation(out=out, in_=in_, func=func, bias=bias, scale=scale), rd, wr)
        else:
            self.P.op("act", lambda e: e.activation(out=out, in_=in_, func=func, bias=bias, scale=scale,
                                                     accum_out=accum), rd, wr)

    def tt(self, eng, out, in0, in1, op, rd, wr):
        o = self.nc.vector if eng == "dve" else self.nc.gpsimd
        self.P.op(eng, lambda e: e.tensor_tensor(out=out, in0=in0, in1=in1, op=op), rd, wr)

    def ts(self, eng, out, in0, s1, s2, op0, op1, rd, wr, accum=None):
        if op1 is None:
            self.P.op(eng, lambda e: e.tensor_scalar(out=out, in0=in0, scalar1=s1, scalar2=None, op0=op0), rd, wr)
        elif accum is None:
            self.P.op(eng, lambda e: e.tensor_scalar(out=out, in0=in0, scalar1=s1, scalar2=s2, op0=op0, op1=op1),
                      rd, wr)
        else:
            self.P.op(eng, lambda e: e.tensor_scalar(out=out, in0=in0, scalar1=s1, scalar2=s2, op0=op0, op1=op1,
                                                     accum_out=accum), rd, wr)

    def stt(self, out, in0, scalar, in1, op0, op1, rd, wr):
        self.P.op("dve", lambda e: e.scalar_tensor_tensor(out=out, in0=in0, scalar=scalar, in1=in1, op0=op0, op1=op1),
                  rd, wr)

    def cp(self, eng, out, in_, rd, wr):
        if eng == "act":
            self.P.op("act", lambda e: e.copy(out=out, in_=in_), rd, wr)
        else:
            self.P.op(eng, lambda e: e.tensor_copy(out=out, in_=in_), rd, wr)

    def rsqrt(self, out, in_, rd, wr, scale=1.0, eps=0.0):
        self.act(out, in_, AF.Ln, rd, wr, bias=eps, scale=scale)
        self.act(out, out, AF.Exp, wr, wr, scale=-0.5)

    def recip(self, out, in_, rd, wr):
        self.P.op("dve", lambda e: e.reciprocal(out=out, in_=in_), rd, wr)

    def memset(self, eng, ap, val, wr):
        self.P.op(eng, lambda e: e.memset(ap, val), (), wr)

    def set_psum(self, n_mm, n_acc, n_tp):
        def mk(lst):
            r = Ring.__new__(Ring)
            r.t = lst
            r.i = 0
            return r
        assert n_mm + n_acc + n_tp == 8
        self.ps_mm = mk(self.psb[0:n_mm])
        self.ps_acc = mk(self.psb[n_mm:n_mm + n_acc])
        self.ps_tp = mk(self.psb[n_mm + n_acc:8])

    def setup(self):
        cfg = self.cfg
        T, TS, PAST = cfg.T, cfg.TS, cfg.PAST
        din, dout, dscr = self.din, self.dout, self.dscr
        din("xp", [T, D]); din("xs", [TS, D]); din("cT", [128, 8, 2])
        din("ada_w", [4, D, 3 * D]); din("ada_b", [4, 3 * D]); din("ln_g", [4, D]); din("ln_b", [4, D])
        din("ident", [128, 128]); din("cmask", [128, 128])
        din("fox_w_in", [D, 4112]); din("fox_b_f", [16, 1]); din("fox_w_out", [D, D])
        din("cache_fox_k", [PAST, D]); din("cache_fox_v", [PAST, D]); din("cache_fox_logf", [PAST, 16])
        din("diff_w_in", [D, 4096]); din("diff_w_out", [D, D]); din("rel_bias_table", [32, 8])
        for nmv in ("diff_lam_q1", "diff_lam_k1", "diff_lam_q2", "diff_lam_k2"):
            din(nmv, [1, 64])
        din("diff_subln_w", [1, 128]); din("t5_onehot", [32, 384]); din("antiident", [128, 128]); din("dmask", [128, 128])
        din("cache_diff_k", [PAST, D]); din("cache_diff_v", [PAST, D])
        dscr("lam_scr", [2, 8], F32); dscr("fr_scr", [16, 384], F32)
        din("ret_w_in", [D, 6144]); din("ret_w_out", [2 * D, D]); din("ret_gn_w", [1, 2 * D])
        din("state_ret", [4, 256, 512]); din("ret_dmaskT", [128, 4, 128]); din("ret_qdec", [128, 4, 128])
        din("ret_kdec", [128, 8])
        for g in self.all_groups:
            din("rot_cos_" + g.name, [128, g.T]); din("rot_sin_" + g.name, [128, g.T])
            dscr("QR_" + g.name, [8, 128, g.T], BF16); dscr("KR_" + g.name, [8, 128, g.T], BF16)
            dscr("OG_" + g.name + "v", [g.T, 2 * D], BF16)
            dout("ret_state_" + g.name, [4, 256, 512])
        din("gdn_w_in", [D, 4112]); din("gdn_w_out", [D, D]); din("gdn_convw", [128, 24, 4]); din("gdn_conv_in", [128, 24, 3])
        din("gdn_a_log", [8, 1]); din("gdn_dt_bias", [8, 1]); din("gdn_norm_w", [1, 128]); din("state_gdn", [8, 128, 128])
        din("gdn_masks", [128, 3, 128]); din("gdn_sel", [8, 8, 128])
        for g in self.all_groups:
            for nmv in ("QG_", "KG_", "VG_"):
                dscr(nmv + g.name, [8, 128, g.T], BF16)
            dscr("BTb_" + g.name, [8, g.T], BF16); dscr("GT_" + g.name, [8, g.T], F32); dscr("GB_" + g.name, [g.T, 16], F32)
            dout("gdn_state_" + g.name, [8, 128, 128]); dout("gdn_conv_" + g.name, [3, 3 * D])
        dout("yp", [T, D]); dout("ys", [TS, D])
        for g in self.all_groups:
            n = g.name
            dout("fox_k_" + n, [g.T, D]); dout("fox_v_" + n, [g.T, D]); dout("fox_logf_" + n, [g.T, 16])
            dout("diff_k_" + n, [g.T, D]); dout("diff_v_" + n, [g.T, D])
            dscr("OGF_" + n, [g.T, 2 * D], F32)
        dscr("xres_p", [T, D], F32); dscr("xres_s", [TS, D], F32)
        for g in self.all_groups:
            n = g.name
            dscr("QT_" + n, [16, 70, g.T], BF16); dscr("KT_" + n, [16, 70, g.S], BF16)
            dscr("VA_" + n, [g.S, 1040], BF16)
            dscr("ZS_" + n, [g.T, 2 * D], BF16); dscr("OG_" + n, [g.T, 2 * D], BF16)

        self.W, self.Wb = self.sb("W_in", [128, 8, 4112], BF16)
        self.Wbs = [[Buf() for _ in range(8)] for _ in range(9)]
        self.WO, self.WOb = self.sb("W_out", [128, 8, D], BF16)
        self.identf, self.identf_b = self.sb("identf", [128, 128], F32)
        self.identb, self.identb_b = self.sb("identb", [128, 128], BF16)
        self.cmaskb, self.cmaskb_b = self.sb("cmaskb", [128, 128], BF16)
        self.cTs, self.cTs_b = self.sb("cTs", [128, 8, 2], F32)
        self.crep = [self.sb("crep%d" % g, [128, 8, 128], BF16) for g in range(2)]
        self.mod = [self.sb("mod_%d" % k, [128, D], F32) for k in range(3)]
        dscr("mods", [3, D], F32)
        self.lng, self.lng_b = self.sb("lng", [128, D], F32)
        self.lnb, self.lnb_b = self.sb("lnb", [128, D], F32)
        self.psb = Ring(self, "psb", [128, 512], F32, 8, psum=True).t
        self.set_psum(4, 2, 2)
        self.r_x = Ring(self, "rx", [128, D], F32, 2)
        self.r_f = Ring(self, "rf", [128, D], F32, 2)
        self.r_ub = Ring(self, "rub", [128, D], BF16, 2)
        self.r_uT = Ring(self, "ruT", [128, 8, 512], BF16, 2)
        self.r_h = Ring(self, "rh", [128, 1040], BF16, 3)
        self.r_small = Ring(self, "rsm", [128, 16], F32, 8)
        self.arena = Arena(self, 58 * 1024)
        wbytes = 8 * 4112 * 2
        self.arenaW = Arena(self, wbytes, base=self.W[:, :, :].rearrange("p a b -> p (a b)").bitcast(F32))

        self.dma(self.identf[:], self.dr["ident"][:, :], [], [self.identf_b])
        self.cp("dve", self.identb[:], self.identf[:], [self.identf_b], [self.identb_b])
        t, b = self.r_f.next()
        self.dma(t[:, 0:128], self.dr["cmask"][:, :], [], [b])
        self.cp("dve", self.cmaskb[:], t[:, 0:128], [b], [self.cmaskb_b])
        self.dma(self.cTs[:], self.dr["cT"][:, :, :], [], [self.cTs_b])
        self.act(self.cTs[:], self.cTs[:], AF.Silu, [self.cTs_b], [self.cTs_b])
        for g in range(2):
            t, b = self.crep[g]
            self.cp("dve", t[:], self.cTs[:, :, g:g + 1].to_broadcast([128, 8, 128]), [self.cTs_b], [b])

    def wdep(self, c0, c1, k):
        return [self.Wbs[cg][k] for cg in range(c0 // 512, (c1 - 1) // 512 + 1)]

    def load_w(self, dst, dstb, src, ncols, nk=8, c_off=0):
        if dstb is self.Wb:
            for a in range(0, ncols, 1024):
                b = min(ncols, a + 1024)
                for k in range(nk):
                    self.dma(dst[:, k, a:b], src[k * 128:(k + 1) * 128, c_off + a:c_off + b], [], self.wdep(a, b, k), eng="pool")
            return
        for k in range(nk):
            self.dma(dst[:, k, 0:ncols], src[k * 128:(k + 1) * 128, c_off:c_off + ncols], [], [dstb], eng="pool")

    def layer_mod(self, L):
        ada_w, ada_b = self.dr["ada_w"], self.dr["ada_b"]
        for cg in range(6):
            wt, wb = self.r_uT.next()
            self.load_w(wt, wb, ada_w[L], 512, c_off=cg * 512)
            bt_, bb = self.r_f.next()
            bt = bt_[:, 0:512]
            self.dma(bt[:, :], ada_b[L:L + 1, cg * 512:(cg + 1) * 512].to_broadcast([128, 512]), [], [bb])
            kind = cg // 2
            for g in range(2):
                ps, pb = self.ps_mm.next()
                ct, cb = self.crep[g]
                for k in range(8):
                    self.mm(ps[:, :], ct[:, k, :], wt[:, k, :], k == 0, k == 7, [cb, wb], [pb])
                if g == 0:
                    mt, mb = self.mod[kind]
                    dst = mt[:, (cg % 2) * 512:(cg % 2 + 1) * 512]
                else:
                    mt, mb = self.r_f.next()
                    dst = mt[:, 0:512]
                if kind == 0:
                    self.tt("dve", dst, ps[:, :], bt[:, :], ALU.add, [pb, bb], [mb])
                else:
                    self.stt(dst, ps[:, :], 1.0, bt[:, :], ALU.add, ALU.add, [pb, bb], [mb])
                if g == 1:
                    self.dma(self.dr["mods"][kind:kind + 1, (cg % 2) * 512:(cg % 2 + 1) * 512], mt[0:1, 0:512], [mb],
                             [self.buf(("mods", kind, cg % 2))])
        self.dma(self.lng[:], self.dr["ln_g"][L:L + 1, :].to_broadcast([128, D]), [], [self.lng_b])
        self.dma(self.lnb[:], self.dr["ln_b"][L:L + 1, :].to_broadcast([128, D]), [], [self.lnb_b])

    def modtile(self, g, kind, n):
        if g.gi == 0:
            return self.mod[kind]
        t, b = self.r_f.next()
        self.dma(t[:n, :], self.dr["mods"][kind:kind + 1, :].to_broadcast([n, D]),
                 [self.buf(("mods", kind, 0)), self.buf(("mods", kind, 1))], [b])
        return t, b

    def xsrc(self, g, first):
        if first:
            return self.dr["xp" if g.gi == 0 else "xs"], "xin%d" % g.gi
        return self.dr["xres_" + g.name], "xres%d" % g.gi

    def make_uT(self, g, blk, first):
        src, skey = self.xsrc(g, first)
        uT, uTb = self.r_uT.next()
        off = 0
        for (t0, n) in blk:
            sc, scb = self.modtile(g, 1, n)
            xt, xb = self.r_x.next()
            self.dma(xt[:n, :], src[t0:t0 + n, :], [self.buf((skey, t0))], [xb])
            ft, fb = self.r_f.next()
            self.tt("dve", ft[:n, :], xt[:n, :], sc[:n, :], ALU.mult, [xb, scb], [fb])
            sh, shb = self.modtile(g, 0, n)
            ub, ubb = self.r_ub.next()
            self.tt("dve", ub[:n, :], ft[:n, :], sh[:n, :], ALU.add, [fb, shb], [ubb])
            tp, tpb = self.ps_tp.next()
            tpv = tp[:, :].bitcast(BF16)
            for j in range(8):
                self.tr(tpv[:, j * 128:j * 128 + n], ub[:n, j * 128:(j + 1) * 128], self.identb[:n, :n],
                        [ubb, self.identb_b], [tpb])
            self.cp("act", uT[:, :, off:off + n],
                    tpv.rearrange("p (j t) -> p j t", j=8)[:, :, 0:n], [tpb], [uTb])
            off += n
        return uT, uTb, off

    def proj_tm(self, uT, uTb, off, n, c0, c1):
        ps, pb = self.ps_mm.next()
        for k in range(8):
            self.mm(ps[:n, 0:c1 - c0], uT[:, k, off:off + n], self.W[:, k, c0:c1], k == 0, k == 7,
                    [uTb] + self.wdep(c0, c1, k), [pb])
        return ps, pb

    def proj_fm(self, uT, uTb, nb, c0, c1):
        ps, pb = self.ps_mm.next()
        for k in range(8):
            self.mm(ps[:c1 - c0, 0:nb], self.W[:, k, c0:c1], uT[:, k, 0:nb], k == 0, k == 7, [uTb] + self.wdep(c0, c1, k), [pb])
        return ps, pb

    def pass_c_run(self, items, first, last, nk, wo=None):
        from types import SimpleNamespace as NS
        ctxs = [NS(g=g, t0=t0, n=n, prep=prep) for (g, t0, n, prep) in items]

        def stA(c):
            n = c.n
            ogz, ogzb = c.prep()
            c.oT, c.oTb = self.r_uT.next()
            for half in range((nk + 7) // 8):
                tp, tpb = self.ps_tp.next()
                tpv = tp[:, :].bitcast(BF16)
                for j in range(8):
                    jj = half * 8 + j
                    self.tr(tpv[:, j * 128:j * 128 + n], ogz[:n, jj * 128:(jj + 1) * 128], self.identb[:n, :n],
                            [ogzb, self.identb_b], [tpb])
                self.cp("act", c.oT[:, :, half * 128:half * 128 + n],
                        tpv.rearrange("p (j t) -> p j t", j=8)[:, :, 0:n], [tpb], [c.oTb])

        def stB(c):
            g, t0, n = c.g, c.t0, c.n
            src, skey = self.xsrc(g, first)
            hps = []
            for cg in range(2):
                ps, pb = self.ps_acc.next()
                for k in range(nk):
                    wt_, wtb_ = (self.WO[:, k, :], [self.WOb]) if wo is None else wo(k)
                    self.mm(ps[:n, :], c.oT[:, k % 8, (k // 8) * 128:(k // 8) * 128 + n],
                            wt_[:, cg * 512:(cg + 1) * 512], k == 0, k == nk - 1, [c.oTb] + wtb_, [pb])
                hps.append((ps, pb))
            xt, xb = self.r_x.next()
            self.dma(xt[:n, :], src[t0:t0 + n, :], [self.buf((skey, t0))], [xb])
            gt, gb = self.modtile(g, 2, n)
            c.rt, c.rb = self.r_f.next()
            rt, rb = c.rt, c.rb
            for cg in range(2):
                ps, pb = hps[cg]
                sl = slice(cg * 512, (cg + 1) * 512)
                self.tt("dve", rt[:n, sl], ps[:n, :], gt[:n, sl], ALU.mult, [pb, gb], [rb])
            self.stt(rt[:n, :], xt[:n, :], ALPHA, rt[:n, :], ALU.mult, ALU.add, [xb, rb], [rb])
            st, stb = self.r_small.next()
            for cg in range(2):
                self.P.op("dve", lambda e, cg=cg: e.bn_stats(out=st[:n, cg * 6:cg * 6 + 6],
                                                              in_=rt[:n, cg * 512:(cg + 1) * 512]), [rb], [stb])
            c.mv, c.mvb = self.r_small.next()
            mv, mvb = c.mv, c.mvb
            self.P.op("dve", lambda e: e.bn_aggr(out=mv[:n, 0:2], in_=st[:n, 0:12]), [stb], [mvb])
            self.rsqrt(mv[:n, 4:5], mv[:n, 1:2], [mvb], [mvb], eps=LN_EPS)
            self.stt(mv[:n, 5:6], mv[:n, 0:1], -1.0, mv[:n, 4:5], ALU.mult, ALU.mult, [mvb], [mvb])

        def stC(c):
            g, t0, n = c.g, c.t0, c.n
            dst, dkey = (self.dr["yp" if g.gi == 0 else "ys"], "y%d" % g.gi) if last else \
                (self.dr["xres_" + g.name], "xres%d" % g.gi)
            yt, yb = self.r_x.next()
            self.act(yt[:n, :], c.rt[:n, :], AF.Identity, [c.rb, c.mvb], [yb], bias=c.mv[:n, 5:6], scale=c.mv[:n, 4:5])
            self.tt("dve", yt[:n, :], yt[:n, :], self.lng[:n, :], ALU.mult, [yb, self.lng_b], [yb])
            self.tt("dve", yt[:n, :], yt[:n, :], self.lnb[:n, :], ALU.add, [yb, self.lnb_b], [yb])
            self.dma(dst[t0:t0 + n, :], yt[:n, :], [yb], [self.buf((dkey, t0))])

        stages = [stA, stB, stC]
        N, S_ = len(ctxs), len(stages)
        for slot in range(N + S_ - 1):
            for s in range(S_ - 1, -1, -1):
                i = slot - s
                if 0 <= i < N:
                    stages[s](ctxs[i])

    def pass_c_tile(self, g, t0, n, first, last, ogz, ogzb, nk, wo=None):
        self.pass_c_run([(g, t0, n, lambda: (ogz, ogzb))], first, last, nk, wo)

    def layer(self, L, first, last):
        self.layer_mod(L)
        getattr(self, ("layer_gdn", "layer_fox", "layer_diff", "layer_ret")[L % 4])(L, first, last)

    def attn_rings(self, w, lf):
        self.P.barrier()
        self.attn_skew = 3 if lf else 2
        self._attn_pend = []
        if lf:
            self.set_psum(4, 2, 2)
        else:
            self.set_psum(3, 4, 1)
        A = self.arena
        A.reset()
        smax = max(g.S for g in self.groups)
        self.r_KT = Ring(self, "rKT", [70, smax], BF16, 2, arena=A)
        nkt = max(len(g.ktiles) for g in self.groups)
        self.r_VA = Ring(self, "rVA", [128, nkt, w], BF16, 2, arena=A)
        self.r_QT = Ring(self, "rQT", [70, 512], BF16, 2, arena=A)
        self.r_PT = Ring(self, "rPT", [128, 512], BF16, 5 if lf else 3, arena=A)
        if lf:
            self.r_og = Ring(self, "rog", [128, 4, 128], BF16, 2, arena=A)
        self.r_vb = self.r_h
        self.r_st = Ring(self, "rst", [128, 2048], BF16, 1, arena=A)
        if lf:
            self.r_lf = Ring(self, "rlf", [16, 512], F32, 4, arena=A)
            self.r_pt = Ring(self, "rpt", [16, 3, 512], BF16, 2, arena=A)
            self.ones3, self.ones3_b = A.alloc([16, 3, 512], BF16), Buf()
            self.memset("pool", self.ones3[:, :, :], 1.0, [self.ones3_b])
            self.carry, self.carry_b = A.alloc([16, 1], F32), Buf()
            self.fbf, self.fbf_b = A.alloc([16, 1], F32), Buf()

    def attn_drain(self):
        while self._attn_pend:
            it_, f_ = self._attn_pend.pop(0)
            f_(it_)

    def attn_head(self, g, KTd, QTd, VAd, vcol0, dv, kc, rd_bufs, extras, finish):
        KT, KTb = self.r_KT.next()
        self.dma(KT[:kc, 0:g.S], KTd[:, :], rd_bufs, [KTb])
        VA, VAb = self.r_VA.next()
        nfull = sum(1 for (k0, nk) in g.ktiles if nk == 128)
        for k4 in range(0, nfull, 4):
            k5 = min(nfull, k4 + 4)
            self.dma(VA[:, k4:k5, 0:dv + 1],
                     VAd[k4 * 128:k5 * 128, vcol0:vcol0 + dv + 1].rearrange("(kt p) c -> p kt c", p=128), rd_bufs, [VAb])
        for i, (k0, nk) in enumerate(g.ktiles):
            if nk != 128:
                self.dma(VA[:nk, i, 0:dv + 1], VAd[k0:k0 + nk, vcol0:vcol0 + dv + 1], rd_bufs, [VAb])
        w = dv + 1
        per_bank = 512 // w
        for bi, blk in enumerate(g.blocks):
            nq = sum(n for _, n in blk)
            q0b = blk[0][0]
            QT, QTb = self.r_QT.next()
            self.dma(QT[:kc, 0:nq], QTd[:, q0b:q0b + nq], rd_bufs, [QTb])
            nbank = (len(blk) + per_bank - 1) // per_bank
            accs = [self.ps_acc.next() for _ in range(nbank)]
            started = [False] * nbank
            coff = []
            o = 0
            for (_, n) in blk:
                coff.append(o)
                o += n
            from types import SimpleNamespace as NS
            bc_ = NS(bi=bi, blk=blk, accs=accs, started=started, coff=coff, remaining=0, VA=VA, VAb=VAb)
            pend = self._attn_pend

            def emit_pv(item):
                ki, k0, nk, vis, PT, PTb, b_ = item
                for qi in vis:
                    q0, n = b_.blk[qi]
                    bk = qi // per_bank
                    acc, accb = b_.accs[bk]
                    col = (qi % per_bank) * w
                    last = (k0 == g.past + q0)
                    self.mm(acc[:n, col:col + w], PT[:nk, b_.coff[qi]:b_.coff[qi] + n], b_.VA[:nk, ki, 0:w],
                            not b_.started[bk], last, [PTb, b_.VAb], [accb], skip=True)
                    b_.started[bk] = True
                b_.remaining -= 1
                if b_.remaining == 0 and b_.closed:
                    b_.fin(b_.bi, b_.blk, b_.accs, per_bank, w)
            bc_.closed = False
            bc_.fin = finish
            for ki, (k0, nk) in enumerate(g.ktiles):
                vis = [qi for qi, (q0, n) in enumerate(blk) if g.past + q0 >= k0]
                if not vis or k0 > g.past + blk[-1][0]:
                    continue
                fi = vis[0]
                c0 = coff[fi]
                ST, STb = self.ps_mm.next()
                ex = []
                for qi in vis:
                    q0, n = blk[qi]
                    for (l, r, bl) in extras(ki, k0, nk, qi, g.past + q0, n):
                        ex.append((qi, n, l, r, bl))
                self.mm(ST[:nk, c0:nq], KT[:kc, k0:k0 + nk], QT[:kc, c0:nq], True, not ex, [KTb, QTb], [STb])
                for j, (qi, n, l, r, bl) in enumerate(ex):
                    self.mm(ST[:nk, coff[qi]:coff[qi] + n], l, r, False, j == len(ex) - 1, bl, [STb], skip=True)
                PT, PTb = self.r_PT.next()
                self.act(PT[:nk, c0:nq], ST[:nk, c0:nq], AF.Exp, [STb], [PTb])
                bc_.remaining += 1
                pend.append(((ki, k0, nk, vis, PT, PTb, bc_), emit_pv))
                if len(pend) > self.attn_skew:
                    it_, f_ = pend.pop(0)
                    f_(it_)
            bc_.closed = True
            if bc_.remaining == 0:
                finish(bi, blk, accs, per_bank, w)

    def layer_fox(self, L, first, last):
        dr = self.dr
        self.attn_rings(65, True)
        self.load_w(self.W, self.Wb, dr["fox_w_in"], 4112)
        self.load_w(self.WO, self.WOb, dr["fox_w_out"], D)
        bf, bfb = self.fbf, self.fbf_b
        self.dma(bf[:, :], dr["fox_b_f"][:, :], [], [bfb])
        self.ts("dve", bf[:, :], bf[:, :], -1.0, None, ALU.mult, None, [bfb], [bfb])
        for g in self.groups:
            nm = g.name
            KTd, QTd, VAd, ZSd, OGd = dr["KT_" + nm], dr["QT_" + nm], dr["VA_" + nm], dr["ZS_" + nm], dr["OG_" + nm]
            KTv = KTd.rearrange("(hp two) r s -> two r hp s", two=2)
            QTv = QTd.rearrange("(hp two) r s -> two r hp s", two=2)
            wb = []

            def nb_(key):
                b = self.buf(("fox", g.gi, L) + key)
                wb.append(b)
                return b
            self.memset("pool", self.carry[:, :], 0.0, [self.carry_b])

            def cum_block(lf, lfb, nb, kpos, qpos, tag):
                cum, cumb = self.r_lf.next()
                self.P.op("dve", lambda e: e.tensor_tensor_scan(out=cum[:, 0:nb], data0=self.ones3[:, 0, 0:nb],
                                                                data1=lf[:, 0:nb], initial=self.carry[:, 0:1],
                                                                op0=ALU.mult, op1=ALU.add),
                          [lfb, self.ones3_b, self.carry_b], [cumb])
                self.cp("act", self.carry[:, 0:1], cum[:, nb - 1:nb], [cumb], [self.carry_b])
                pt, ptb = self.r_pt.next()
                t32, t32b = self.r_lf.next()
                r1, r1b = self.r_lf.next()
                self.cp("dve", pt[:, 0, 0:nb], cum[:, 0:nb], [cumb], [ptb])
                self.cp("dve", t32[:, 0:nb], pt[:, 0, 0:nb], [ptb], [t32b])
                self.tt("dve", r1[:, 0:nb], cum[:, 0:nb], t32[:, 0:nb], ALU.subtract, [cumb, t32b], [r1b])
                self.cp("dve", pt[:, 1, 0:nb], r1[:, 0:nb], [r1b], [ptb])
                self.cp("dve", t32[:, 0:nb], pt[:, 1, 0:nb], [ptb], [t32b])
                self.tt("dve", r1[:, 0:nb], r1[:, 0:nb], t32[:, 0:nb], ALU.subtract, [r1b, t32b], [r1b])
                self.cp("dve", pt[:, 2, 0:nb], r1[:, 0:nb], [r1b], [ptb])
                npt, nptb = self.r_pt.next()
                self.ts("dve", npt[:, :, 0:nb], pt[:, :, 0:nb], -1.0, None, ALU.mult, None, [ptb], [nptb])
                self.dma(KTd[:, 64:67, kpos:kpos + nb], npt[:, :, 0:nb], [nptb], [nb_((tag, "kc"))])
                self.dma(KTd[:, 67:70, kpos:kpos + nb], self.ones3[:, :, 0:nb], [self.ones3_b], [nb_((tag, "k1"))])
                if qpos is not None:
                    self.dma(QTd[:, 64:67, qpos:qpos + nb], self.ones3[:, :, 0:nb], [self.ones3_b],
                             [nb_((tag, "q1"))])
                    self.dma(QTd[:, 67:70, qpos:qpos + nb], pt[:, :, 0:nb], [ptb], [nb_((tag, "qc"))])


            ptiles = [(k0, nk) for (k0, nk) in g.ktiles if k0 < g.past]
            for b0 in range(0, len(ptiles), 2):
                pblk = ptiles[b0:b0 + 2]
                lf, lfb = self.r_lf.next()
                st_, stb = self.r_st.next()
                st = st_[:, :].rearrange("p (c t) -> p c t", c=8)
                for j, (k0, nk) in enumerate(pblk):
                    xt, xb = self.r_x.next()
                    self.dma(xt[:, :], dr["cache_fox_k"][k0:k0 + 128, :], [], [xb])
                    ub, ubb = self.r_ub.next()
                    self.cp("dve", ub[:, :], xt[:, :], [xb], [ubb])
                    tp, tpb = self.ps_tp.next()
                    tpv = tp[:, :].bitcast(BF16)
                    for c in range(8):
                        self.tr(tpv[:, c * 128:(c + 1) * 128], ub[:, c * 128:(c + 1) * 128], self.identb[:, :],
                                [ubb, self.identb_b], [tpb])
                    self.cp("act", st[:, :, j * 128:(j + 1) * 128], tpv.rearrange("p (c t) -> p c t", c=8), [tpb], [stb])
                    vt, vtb = self.r_x.next()
                    self.dma(vt[:, :], dr["cache_fox_v"][k0:k0 + 128, :], [], [vtb])
                    vb, vbb = self.r_vb.next()
                    self.memset("pool", vb[:, :].rearrange("p (h c) -> p h c", h=16)[:, :, 64:65], 1.0, [vbb])
                    self.cp("dve", vb[:, :].rearrange("p (h c) -> p h c", h=16)[:, :, 0:64],
                            vt[:, :].rearrange("p (h c) -> p h c", h=16), [vtb], [vbb])
                    self.dma(VAd[k0:k0 + 128, :], vb[:, :], [vbb], [nb_(("pv", k0))])
                    lt, ltb = self.r_small.next()
                    self.dma(lt[:, 0:16], dr["cache_fox_logf"][k0:k0 + 128, :], [], [ltb])
                    tp2, tp2b = self.ps_tp.next()
                    self.tr(tp2[0:16, 0:128], lt[:, 0:16], self.identf[:, :], [ltb, self.identf_b], [tp2b])
                    self.cp("act", lf[:, j * 128:(j + 1) * 128], tp2[0:16, 0:128], [tp2b], [lfb])
                kp, nbk = pblk[0][0], 128 * len(pblk)
                for two in range(2):
                    self.dma(KTv[two, 0:64, :, kp:kp + nbk], st[two * 64:(two + 1) * 64, :, 0:nbk], [stb],
                             [nb_(("pk", kp, two))])
                cum_block(lf, lfb, nbk, kp, None, ("pc", kp))

            for blk in g.blocks:
                uT, uTb, nb = self.make_uT(g, blk, first)
                t0b = blk[0][0]
                kpos = g.past + t0b
                for which, dst in ((0, QTv), (1, KTv)):
                    pos = t0b if which == 0 else kpos
                    for hf in range(2):
                        st_, stb = self.r_st.next()
                        st = st_[:, :].rearrange("p (c t) -> p c t", c=4)
                        for c4 in range(4):
                            c = hf * 4 + c4
                            ps, pb = self.proj_fm(uT, uTb, nb, which * D + c * 128, which * D + (c + 1) * 128)
                            if which == 0:
                                self.act(st[:, c4, 0:nb], ps[:, 0:nb], AF.Copy, [pb], [stb], scale=0.125)
                            else:
                                self.cp("dve", st[:, c4, 0:nb], ps[:, 0:nb], [pb], [stb])
                        for two in range(2):
                            self.dma(dst[two, 0:64, hf * 4:hf * 4 + 4, pos:pos + nb], st[two * 64:(two + 1) * 64, :, 0:nb],
                                     [stb], [nb_(("qk", which, t0b, two, hf))])
                ps, pb = self.proj_fm(uT, uTb, nb, 4 * D, 4 * D + 16)
                e1, e1b = self.r_lf.next()
                self.act(e1[:, 0:nb], ps[0:16, 0:nb], AF.Exp, [pb, bfb], [e1b], bias=bf[:, 0:1], scale=-1.0)
                lf, lfb = self.r_lf.next()
                self.act(lf[:, 0:nb], e1[:, 0:nb], AF.Ln, [e1b], [lfb], bias=1.0)
                self.ts("dve", lf[:, 0:nb], lf[:, 0:nb], -1.0, None, ALU.mult, None, [lfb], [lfb])
                off = 0
                for (t0, n) in blk:
                    tp2, tp2b = self.ps_tp.next()
                    self.tr(tp2[0:n, 0:16], lf[:, off:off + n], self.identf[0:16, 0:16], [lfb, self.identf_b], [tp2b])
                    lo, lob = self.r_small.next()
                    self.cp("act", lo[:n, 0:16], tp2[0:n, 0:16], [tp2b], [lob])
                    self.dma(dr["fox_logf_" + nm][t0:t0 + n, :], lo[:n, 0:16], [lob], [self.buf(("flo", g.gi, t0))])
                    for which, oname in ((1, "fox_k_"), (2, "fox_v_")):
                        ft, fb = self.r_f.next()
                        for cg in range(2):
                            ps, pb = self.proj_tm(uT, uTb, off, n, which * D + cg * 512, which * D + (cg + 1) * 512)
                            self.cp("act" if cg == 0 else "dve", ft[:n, cg * 512:(cg + 1) * 512], ps[:n, :], [pb], [fb])
                        self.dma(dr[oname + nm][t0:t0 + n, :], ft[:n, :], [fb], [self.buf((oname, g.gi, t0))])
                        if which == 2:
                            vb, vbb = self.r_vb.next()
                            self.memset("pool", vb[:n, :].rearrange("p (h c) -> p h c", h=16)[:, :, 64:65], 1.0, [vbb])
                            self.cp("dve", vb[:n, :].rearrange("p (h c) -> p h c", h=16)[:, :, 0:64],
                                    ft[:n, :].rearrange("p (h c) -> p h c", h=16), [fb], [vbb])
                            self.dma(VAd[g.past + t0:g.past + t0 + n, :], vb[:n, :], [vbb], [nb_(("v", t0))])
                    zt, ztb = self.r_h.next()
                    for cg in range(2):
                        ps, pb = self.proj_tm(uT, uTb, off, n, 3 * D + cg * 512, 3 * D + (cg + 1) * 512)
                        self.act(zt[:n, cg * 512:(cg + 1) * 512], ps[:n, :], AF.Silu, [pb], [ztb])
                    self.dma(ZSd[t0:t0 + n, 0:D], zt[:n, 0:D], [ztb], [nb_(("z", t0))])
                    off += n
                cum_block(lf, lfb, nb, kpos, t0b, ("c", t0b))

            ogw = []

            def extras(ki, k0, nk, qi, q0a, n):
                if k0 == q0a:
                    return [(self.identb[:nk, :nk], self.cmaskb[:nk, :n], [self.identb_b, self.cmaskb_b])]
                return []

            for h in range(16):
                def finish(bi, blk, accs, per_bank, w, h=h):
                    og, ogb = self.r_og.next()
                    for qi, (q0, n) in enumerate(blk):
                        acc, accb = accs[qi // per_bank]
                        col = (qi % per_bank) * w
                        rd, rdb = self.r_small.next()
                        self.recip(rd[:n, 0:1], acc[:n, col + 64:col + 65], [accb], [rdb])
                        self.ts("dve", og[:n, qi, 0:64], acc[:n, col:col + 64], rd[:n, 0:1], None, ALU.mult, None,
                                [accb, rdb], [ogb])
                    q0b = blk[0][0]
                    kb = self.buf(("fox_og", g.gi, L, h, bi))
                    ogw.append(kb)
                    if len(blk) > 1 or blk[0][1] == 128:
                        nt = len(blk)
                        self.dma(OGd[q0b:q0b + nt * 128, h * 64:(h + 1) * 64].rearrange("(qi p) c -> p qi c", p=128),
                                 og[:, 0:nt, 0:64], [ogb], [kb])
                    else:
                        n = blk[0][1]
                        self.dma(OGd[q0b:q0b + n, h * 64:(h + 1) * 64], og[:n, 0, 0:64], [ogb], [kb])
                self.attn_head(g, KTd[h], QTd[h], VAd, h * 65, 64, 70, wb, extras, finish)
            self.attn_drain()

            def mkprep(t0, n):
                def prep():
                    og, ogb = self.r_h.next()
                    self.dma(og[:n, 0:D], OGd[t0:t0 + n, 0:D], ogw, [ogb])
                    zs, zsb = self.r_h.next()
                    self.dma(zs[:n, 0:D], ZSd[t0:t0 + n, 0:D], wb, [zsb])
                    self.tt("dve", og[:n, 0:D], og[:n, 0:D], zs[:n, 0:D], ALU.mult, [ogb, zsb], [ogb])
                    return og, ogb
                return prep
            self.pass_c_run([(g, t0, n, mkprep(t0, n)) for (t0, n) in g.tiles], first, last, 8)

    def layer_diff(self, L, first, last):
        dr = self.dr
        LAM_INIT = 0.8 - 0.6 * math.exp(-0.3 * 2)
        self.attn_rings(129, False)
        A = self.arena
        self.load_w(self.W, self.Wb, dr["diff_w_in"], 4096)
        self.load_w(self.WO, self.WOb, dr["diff_w_out"], D)
        lv, lvb = A.alloc([1, 4, 64], F32), Buf()
        for i, nmv in enumerate(("diff_lam_q1", "diff_lam_k1", "diff_lam_q2", "diff_lam_k2")):
            self.dma(lv[:, i, :], dr[nmv][:, :], [], [lvb])
        l2, l2b = A.alloc([1, 8], F32), Buf()
        pr, prb = A.alloc([1, 2, 64], F32), Buf()
        lvv = lv[:, :, :].rearrange("p (a b) c -> p a b c", b=2)
        self.tt("dve", pr[:, :, :], lvv[:, :, 0, :], lvv[:, :, 1, :], ALU.mult, [lvb], [prb])
        self.P.op("dve", lambda e: e.tensor_reduce(out=l2[:, 0:2], in_=pr[:, :, :], axis=AX.X, op=ALU.add), [prb], [l2b])
        self.act(l2[:, 2:4], l2[:, 0:2], AF.Exp, [l2b], [l2b])
        self.tt("dve", l2[:, 4:5], l2[:, 3:4], l2[:, 2:3], ALU.subtract, [l2b], [l2b])
        self.ts("dve", l2[:, 5:6], l2[:, 4:5], -LAM_INIT, None, ALU.add, None, [l2b], [l2b])
        self.dma(dr["lam_scr"][0:1, 0:1], l2[:, 5:6], [l2b], [self.buf("lam_scr")])
        nlam, nlamb = A.alloc([128, 1], F32), Buf()
        self.dma(nlam[:, :], dr["lam_scr"][0:1, 0:1].to_broadcast([128, 1]), [self.buf("lam_scr")], [nlamb])
        tb, tbb = A.alloc([32, 2, 8], F32), Buf()
        for m_ in range(2):
            self.dma(tb[:, m_, :], dr["rel_bias_table"][:, :], [], [tbb])
        oh, ohb = self.r_x.next()
        self.dma(oh[0:32, 0:384], dr["t5_onehot"][:, :], [], [ohb])
        tbs, tbsb = A.alloc([32, 3, 16], BF16), Buf()
        ohb16, ohb16b = A.alloc([32, 384], BF16), Buf()
        self.cp("dve", ohb16[:, :], oh[0:32, 0:384], [ohb], [ohb16b])
        tbf = tb[:, :, :].rearrange("p a b -> p (a b)")
        tmp32, tmp32b = self.r_small.next()
        self.cp("dve", tbs[:, 0, :], tbf, [tbb], [tbsb])
        self.cp("dve", tmp32[0:32, 0:16], tbs[:, 0, :], [tbsb], [tmp32b])
        self.tt("dve", tmp32[0:32, 0:16], tbf, tmp32[0:32, 0:16], ALU.subtract, [tbb, tmp32b], [tmp32b])
        self.cp("dve", tbs[:, 1, :], tmp32[0:32, 0:16], [tmp32b], [tbsb])
        tmp33, tmp33b = self.r_small.next()
        self.cp("dve", tmp33[0:32, 0:16], tbs[:, 1, :], [tbsb], [tmp33b])
        self.tt("dve", tmp33[0:32, 0:16], tmp32[0:32, 0:16], tmp33[0:32, 0:16], ALU.subtract, [tmp32b, tmp33b], [tmp33b])
        self.cp("dve", tbs[:, 2, :], tmp33[0:32, 0:16], [tmp33b], [tbsb])
        ps, pb = self.ps_mm.next()
        for j_ in range(3):
            self.mm(ps[0:16, 0:384], tbs[:, j_, :], ohb16[:, :], j_ == 0, j_ == 2, [tbsb, ohb16b], [pb])
        fr, frb = self.r_f.next()
        c16, c16b = self.r_small.next()
        self.cp("dve", c16[0:16, 0:1], ps[0:16, 0:1], [pb], [c16b])
        self.ts("dve", fr[0:16, 0:384], ps[0:16, 0:384], c16[0:16, 0:1], None, ALU.subtract, None, [pb, c16b], [frb])
        self.dma(dr["fr_scr"][:, :], fr[0:16, 0:384], [frb], [self.buf("fr_scr")])
        crow, crowb = A.alloc([16, 2, 512], BF16), Buf()
        chi, chib = self.r_small.next()
        cbf = chi[0:16, 0:4].bitcast(BF16)
        self.cp("dve", cbf[:, 0:1], c16[0:16, 0:1], [c16b], [chib])
        self.cp("dve", chi[0:16, 4:5], cbf[:, 0:1], [chib], [chib])
        self.tt("dve", chi[0:16, 5:6], c16[0:16, 0:1], chi[0:16, 4:5], ALU.subtract, [c16b, chib], [chib])
        self.cp("dve", cbf[:, 1:2], chi[0:16, 5:6], [chib], [chib])
        for j in range(2):
            self.cp("dve", crow[:, j, :], cbf[:, j:j + 1].to_broadcast([16, 512]), [chib], [crowb])
        ones2, ones2b = A.alloc([16, 2, 512], BF16), Buf()
        self.memset("pool", ones2[:, :, :], 1.0, [ones2b])
        J, Jb = A.alloc([128, 128], BF16), Buf()
        dmk, dmkb = A.alloc([128, 128], BF16), Buf()
        t, b = self.r_f.next()
        self.dma(t[:, 0:128], dr["antiident"][:, :], [], [b])
        self.cp("dve", J[:, :], t[:, 0:128], [b], [Jb])
        t, b = self.r_f.next()
        self.dma(t[:, 0:128], dr["dmask"][:, :], [], [b])
        self.cp("dve", dmk[:, :], t[:, 0:128], [b], [dmkb])
        Hh = [[None, None] for _ in range(8)]
        for h in range(8):
            for ti, c in enumerate((128, 0)):
                t, b = self.r_f.next()
                src_ap = bass.AP(self.dh["fr_scr"], h * 384 + c, [[1, 128], [1, 128]])
                self.dma(t[:, 0:128], src_ap, [self.buf("fr_scr")], [b])
                hi, hib = A.alloc([128, 128], BF16), Buf()
                lo, lob = A.alloc([128, 128], BF16), Buf()
                self.cp("dve", hi[:, :], t[:, 0:128], [b], [hib])
                self.cp("dve", t[:, 128:256], hi[:, :], [hib], [b])
                self.tt("dve", t[:, 256:384], t[:, 0:128], t[:, 128:256], ALU.subtract, [b], [b])
                self.cp("dve", lo[:, :], t[:, 256:384], [b], [lob])
                Hh[h][ti] = (hi, hib, lo, lob)

        slw, slwb = A.alloc([128, 128], F32), Buf()
        self.dma(slw[:, :], dr["diff_subln_w"][0:1, :].to_broadcast([128, 128]), [], [slwb])
        self.P.op("act", lambda e: e.mul(out=slw[:, :], in_=slw[:, :], mul=1.0 - LAM_INIT), [slwb], [slwb])
        for g in self.groups:
            nm = g.name
            KTd, QTd, VAd, ZSd, OGd = dr["KT_" + nm], dr["QT_" + nm], dr["VA_" + nm], dr["ZS_" + nm], dr["OGF_" + nm]
            KTv = KTd.rearrange("(hp two) r s -> two r hp s", two=2)
            QTv = QTd.rearrange("(hp two) r s -> two r hp s", two=2)
            wb = []

            def nb_(key):
                b = self.buf(("diff", g.gi, L) + key)
                wb.append(b)
                return b
            ptiles = [(k0, nk) for (k0, nk) in g.ktiles if k0 < g.past]
            for b0 in range(0, len(ptiles), 2):
                pblk = ptiles[b0:b0 + 2]
                st_, stb = self.r_st.next()
                st = st_[:, :].rearrange("p (c t) -> p c t", c=8)
                for j, (k0, nk) in enumerate(pblk):
                    xt, xb = self.r_x.next()
                    self.dma(xt[:, :], dr["cache_diff_k"][k0:k0 + 128, :], [], [xb])
                    ub, ubb = self.r_ub.next()
                    self.cp("dve", ub[:, :], xt[:, :], [xb], [ubb])
                    tp, tpb = self.ps_tp.next()
                    tpv = tp[:, :].bitcast(BF16)
                    for c in range(8):
                        self.tr(tpv[:, c * 128:(c + 1) * 128], ub[:, c * 128:(c + 1) * 128], self.identb[:, :],
                                [ubb, self.identb_b], [tpb])
                    self.cp("act", st[:, :, j * 128:(j + 1) * 128], tpv.rearrange("p (c t) -> p c t", c=8), [tpb], [stb])
                    vt, vtb = self.r_x.next()
                    self.dma(vt[:, :], dr["cache_diff_v"][k0:k0 + 128, :], [], [vtb])
                    vb, vbb = self.r_vb.next()
                    vb3 = vb[:, 0:1032].rearrange("p (h c) -> p h c", h=8)
                    self.memset("pool", vb3[:, :, 128:129], 1.0, [vbb])
                    self.cp("dve", vb3[:, :, 0:128], vt[:, :].rearrange("p (h c) -> p h c", h=8), [vtb], [vbb])
                    self.dma(VAd[k0:k0 + 128, 0:1032], vb[:, 0:1032], [vbb], [nb_(("pv", k0))])
                kp, nbk = pblk[0][0], 128 * len(pblk)
                for two in range(2):
                    self.dma(KTv[two, 0:64, :, kp:kp + nbk], st[two * 64:(two + 1) * 64, :, 0:nbk], [stb],
                             [nb_(("pk", kp, two))])
                self.dma(KTd[:, 64:66, kp:kp + nbk], ones2[:, :, 0:nbk], [ones2b], [nb_(("p1", kp))])
            for blk in g.blocks:
                uT, uTb, nb = self.make_uT(g, blk, first)
                t0b = blk[0][0]
                kpos = g.past + t0b
                for which, dst in ((0, QTv), (1, KTv)):
                    pos = t0b if which == 0 else kpos
                    for hf in range(2):
                        st_, stb = self.r_st.next()
                        st = st_[:, :].rearrange("p (c t) -> p c t", c=4)
                        for c4 in range(4):
                            c = hf * 4 + c4
                            ps, pb = self.proj_fm(uT, uTb, nb, which * D + c * 128, which * D + (c + 1) * 128)
                            if which == 0:
                                self.act(st[:, c4, 0:nb], ps[:, 0:nb], AF.Copy, [pb], [stb], scale=0.125)
                            else:
                                self.cp("dve", st[:, c4, 0:nb], ps[:, 0:nb], [pb], [stb])
                        for two in range(2):
                            self.dma(dst[two, 0:64, hf * 4:hf * 4 + 4, pos:pos + nb], st[two * 64:(two + 1) * 64, :, 0:nb],
                                     [stb], [nb_(("qk", which, t0b, two, hf))])
                self.dma(KTd[:, 64:66, kpos:kpos + nb], ones2[:, :, 0:nb], [ones2b], [nb_(("k1", t0b))])
                QTm = QTd.rearrange("(h m) r s -> m h r s", m=2)
                for m_ in range(2):
                    self.dma(QTm[m_, :, 64:66, t0b:t0b + nb], crow[m_ * 8:(m_ + 1) * 8, :, 0:nb], [crowb],
                             [nb_(("qc", t0b, m_))])
                off = 0
                for (t0, n) in blk:
                    for which, oname in ((1, "diff_k_"), (2, "diff_v_")):
                        ft, fb = self.r_f.next()
                        for cg in range(2):
                            ps, pb = self.proj_tm(uT, uTb, off, n, which * D + cg * 512, which * D + (cg + 1) * 512)
                            self.cp("act" if cg == 0 else "dve", ft[:n, cg * 512:(cg + 1) * 512], ps[:n, :], [pb], [fb])
                        self.dma(dr[oname + nm][t0:t0 + n, :], ft[:n, :], [fb], [self.buf((oname, g.gi, t0))])
                        if which == 2:
                            vb, vbb = self.r_vb.next()
                            vb3 = vb[:n, 0:1032].rearrange("p (h c) -> p h c", h=8)
                            self.memset("pool", vb3[:, :, 128:129], 1.0, [vbb])
                            self.cp("dve", vb3[:, :, 0:128], ft[:n, :].rearrange("p (h c) -> p h c", h=8), [fb], [vbb])
                            self.dma(VAd[g.past + t0:g.past + t0 + n, 0:1032], vb[:n, 0:1032], [vbb], [nb_(("v", t0))])
                    zt, ztb = self.r_h.next()
                    for cg in range(2):
                        ps, pb = self.proj_tm(uT, uTb, off, n, 3 * D + cg * 512, 3 * D + (cg + 1) * 512)
                        self.act(zt[:n, cg * 512:(cg + 1) * 512], ps[:n, :], AF.Silu, [pb], [ztb])
                    self.dma(ZSd[t0:t0 + n, 0:D], zt[:n, 0:D], [ztb], [nb_(("z", t0))])
                    off += n
            ogw = []
            for vh in range(16):
                h, m = vh // 2, vh % 2

                def extras(ki, k0, nk, qi, q0a, n, h=h):
                    if k0 == q0a:
                        hi, hib, lo, lob = Hh[h][0]
                        return [(hi[:, 0:nk], J[:, 0:n], [hib, Jb]), (lo[:, 0:nk], J[:, 0:n], [lob, Jb]),
                                (self.identb[:nk, :nk], dmk[:nk, :n], [self.identb_b, dmkb])]
                    if k0 == q0a - 128:
                        hi, hib, lo, lob = Hh[h][1]
                        return [(hi[:, 0:nk], J[:, 0:n], [hib, Jb]), (lo[:, 0:nk], J[:, 0:n], [lob, Jb])]
                    return []

                def finish(bi, blk, accs, per_bank, w, h=h, m=m, vh=vh):
                    og, ogb = self.r_x.next()
                    for qi, (q0, n) in enumerate(blk):
                        acc, accb = accs[qi // per_bank]
                        col = (qi % per_bank) * w
                        rd, rdb = self.r_small.next()
                        self.recip(rd[:n, 0:1], acc[:n, col + 128:col + 129], [accb], [rdb])
                        self.ts("dve", og[:n, qi * 128:(qi + 1) * 128], acc[:n, col:col + 128], rd[:n, 0:1], None,
                                ALU.mult, None, [accb, rdb], [ogb])
                    q0b = blk[0][0]
                    kb = self.buf(("diff_og", g.gi, L, vh, bi))
                    ogw.append(kb)
                    c0 = m * D + h * 128
                    if blk[0][1] == 128:
                        nt = len(blk)
                        self.dma(OGd[q0b:q0b + nt * 128, c0:c0 + 128].rearrange("(qi p) c -> p qi c", p=128),
                                 og[:, 0:nt * 128].rearrange("p (qi c) -> p qi c", c=128), [ogb], [kb])
                    else:
                        n = blk[0][1]
                        self.dma(OGd[q0b:q0b + n, c0:c0 + 128], og[:n, 0:128], [ogb], [kb])
                self.attn_head(g, KTd[vh][0:66, :], QTd[vh][0:66, :], VAd, h * 129, 128, 66, wb, extras, finish)
            self.attn_drain()
            def mkprep(t0, n):
                def prep():
                        o1, o1b = self.r_x.next()
                        self.dma(o1[:n, :], OGd[t0:t0 + n, 0:D], ogw, [o1b])
                        o2, o2b = self.r_f.next()
                        self.dma(o2[:n, :], OGd[t0:t0 + n, D:2 * D], ogw, [o2b])
                        self.stt(o1[:n, :], o2[:n, :], nlam[:n, 0:1], o1[:n, :], ALU.mult, ALU.add, [o2b, nlamb, o1b], [o1b])
                        self.tt("dve", o2[:n, :], o1[:n, :], o1[:n, :], ALU.mult, [o1b], [o2b])
                        ss, ssb = self.r_small.next()
                        self.P.op("dve", lambda e, ss=ss, o2=o2, n=n: e.tensor_reduce(
                            out=ss[:n, 0:8], in_=o2[:n, :].rearrange("p (h c) -> p h c", h=8), axis=AX.X, op=ALU.add),
                            [o2b], [ssb])
                        self.rsqrt(ss[:n, 0:8], ss[:n, 0:8], [ssb], [ssb], scale=1.0 / 128, eps=NORM_EPS)
                        self.tt("dve", o1[:n, :].rearrange("p (h c) -> p h c", h=8), o1[:n, :].rearrange("p (h c) -> p h c", h=8),
                                ss[:n, 0:8].unsqueeze(2).to_broadcast([n, 8, 128]), ALU.mult, [o1b, ssb], [o1b])
                        self.tt("dve", o1[:n, :].rearrange("p (h c) -> p h c", h=8), o1[:n, :].rearrange("p (h c) -> p h c", h=8),
                                slw[:n, :].unsqueeze(1).to_broadcast([n, 8, 128]), ALU.mult, [o1b, slwb], [o1b])
                        zs, zsb = self.r_h.next()
                        self.dma(zs[:n, 0:D], ZSd[t0:t0 + n, 0:D], wb, [zsb])
                        og, ogb = self.r_h.next()
                        self.tt("dve", og[:n, 0:D], o1[:n, :], zs[:n, 0:D], ALU.mult, [o1b, zsb], [ogb])
                        return og, ogb
                return prep
            self.pass_c_run([(g, t0, n, mkprep(t0, n)) for (t0, n) in g.tiles], first, last, 8)

    def layer_ret(self, L, first, last):
        dr = self.dr
        P = self.P
        A = self.arena
        LG = [math.log1p(-2.0 ** (-5.0 - h)) for h in range(4)]
        P.barrier(); A.reset()
        self.set_psum(4, 2, 2)
        self.load_w(self.W, self.Wb, dr["ret_w_in"], 4096)
        r_st = Ring(self, "rst", [128, 8, 512], BF16, 1, arena=A)
        r_cs = Ring(self, "rcs", [128, 2, 512], F32, 1, arena=A)
        r_vt = Ring(self, "rvt", [128, 2048], BF16, 2, arena=A)
        wbs = {}
        for g in self.groups:
            nm = g.name
            wb = wbs[g.gi] = []

            def nb_(key, wb=wb, g=g):
                b = self.buf(("ret", g.gi) + key)
                wb.append(b)
                return b
            for blk in g.blocks:
                uT, uTb, nb = self.make_uT(g, blk, first)
                t0b = blk[0][0]
                cs, csb = r_cs.next()
                self.dma(cs[:, 0, 0:nb], dr["rot_cos_" + nm][:, t0b:t0b + nb], [], [csb])
                self.dma(cs[:, 1, 0:nb], dr["rot_sin_" + nm][:, t0b:t0b + nb], [], [csb])
                for which, dname in ((0, "QR_"), (1, "KR_")):
                    st, stb = r_st.next()
                    for h in range(4):
                        c0 = which * D + h * 256
                        pe_, peb = self.proj_fm(uT, uTb, nb, c0, c0 + 128)
                        po_, pob = self.proj_fm(uT, uTb, nb, c0 + 128, c0 + 256)
                        xe, xeb = self.r_f.next()
                        sc = 0.0625 if which == 0 else 1.0
                        self.act(xe[:, 0:nb], pe_[:, 0:nb], AF.Copy, [peb], [xeb], scale=sc)
                        self.act(xe[:, 512:512 + nb], po_[:, 0:nb], AF.Copy, [pob], [xeb], scale=sc)
                        tm, tmb = self.r_x.next()
                        self.tt("dve", tm[:, 0:nb], xe[:, 0:nb], cs[:, 0, 0:nb], ALU.mult, [xeb, csb], [tmb])
                        self.tt("dve", tm[:, 512:512 + nb], xe[:, 512:512 + nb], cs[:, 1, 0:nb], ALU.mult, [xeb, csb], [tmb])
                        self.tt("dve", st[:, 2 * h, 0:nb], tm[:, 0:nb], tm[:, 512:512 + nb], ALU.subtract, [tmb], [stb])
                        tm, tmb = self.r_x.next()
                        self.tt("dve", tm[:, 0:nb], xe[:, 0:nb], cs[:, 1, 0:nb], ALU.mult, [xeb, csb], [tmb])
                        self.tt("dve", tm[:, 512:512 + nb], xe[:, 512:512 + nb], cs[:, 0, 0:nb], ALU.mult, [xeb, csb], [tmb])
                        self.tt("dve", st[:, 2 * h + 1, 0:nb], tm[:, 0:nb], tm[:, 512:512 + nb], ALU.add, [tmb], [stb])
                    self.dma(dr[dname + nm].rearrange("he p t -> p he t")[:, :, t0b:t0b + nb], st[:, :, 0:nb], [stb],
                             [nb_((dname, t0b))])
                off = 0
                for (t0, n) in blk:
                    vt, vtb = r_vt.next()
                    for cg in range(4):
                        ps, pb = self.proj_tm(uT, uTb, off, n, 2 * D + cg * 512, 2 * D + (cg + 1) * 512)
                        self.cp("act" if cg % 2 == 0 else "dve", vt[:n, cg * 512:(cg + 1) * 512], ps[:n, :], [pb], [vtb])
                    self.dma(dr["OG_" + nm + "v"][t0:t0 + n, :], vt[:n, :], [vtb], [nb_(("v", t0))])
                    off += n
        self.load_w(self.W, self.Wb, dr["ret_w_in"], 2048, c_off=4096)
        for g in self.groups:
            nm = g.name
            wb = wbs[g.gi]
            for blk in g.blocks:
                uT, uTb, nb = self.make_uT(g, blk, first)
                off = 0
                for (t0, n) in blk:
                    zt, ztb = r_vt.next()
                    for cg in range(4):
                        ps, pb = self.proj_tm(uT, uTb, off, n, cg * 512, (cg + 1) * 512)
                        self.act(zt[:n, cg * 512:(cg + 1) * 512], ps[:n, :], AF.Silu, [pb], [ztb])
                    b = self.buf(("ret", g.gi, "z", t0))
                    wb.append(b)
                    self.dma(dr["ZS_" + nm][t0:t0 + n, :], zt[:n, :], [ztb], [b])
                    off += n
        P.barrier(); A.reset()
        A2 = self.arenaW
        A2.reset()
        self.load_w(self.WO, self.WOb, dr["ret_w_out"], D)
        dmk, dmkb = A.alloc([128, 4, 128], F32), Buf()
        self.dma(dmk[:, :, :], dr["ret_dmaskT"][:, :, :], [], [dmkb])
        qdc, qdcb = A.alloc([128, 4, 128], F32), Buf()
        self.dma(qdc[:, :, :], dr["ret_qdec"][:, :, :], [], [qdcb])
        kdc, kdcb = A.alloc([128, 8], F32), Buf()
        self.dma(kdc[:, :], dr["ret_kdec"][:, :], [], [kdcb])
        S, Sb = A.alloc([128, 8, 512], F32), Buf()
        Sbf, Sbfb = A.alloc([128, 8, 512], BF16), Buf()
        r_q = Ring(self, "rq", [128, 8, 128], BF16, 2, arena=A)
        r_k = Ring(self, "rk", [128, 8, 128], BF16, 2, arena=A)
        r_v = Ring(self, "rv", [128, 2048], BF16, 2, arena=A)
        r_qd = Ring(self, "rqd", [128, 8, 128], BF16, 2, arena=A2)
        r_at = Ring(self, "rat", [128, 4, 128], BF16, 2, arena=A2)
        r_kd = Ring(self, "rkd", [128, 8, 128], BF16, 2, arena=A2)
        r_o = Ring(self, "ro", [128, 2048], BF16, 2, arena=A2)
        r_s = Ring(self, "rs", [128, 32], F32, 3, arena=A2)
        ogws = {}
        for g in self.groups:
            nm = g.name
            wb = wbs[g.gi]
            ogw = ogws[g.gi] = []
            QRd = dr["QR_" + nm].rearrange("he p t -> p he t")
            KRd = dr["KR_" + nm].rearrange("he p t -> p he t")
            VRd = dr["OG_" + nm + "v"]
            if g.gi == 0:
                self.memset("pool", S[:, :, :], 0.0, [Sb])
            else:
                self.dma(S[:, :, :], dr["state_ret"].rearrange("h (e p) v -> p (h e) v", e=2), [], [Sb])
            self.cp("act", Sbf[:, 0:4, :], S[:, 0:4, :], [Sb], [Sbfb])
            self.cp("dve", Sbf[:, 4:8, :], S[:, 4:8, :], [Sb], [Sbfb])
            from types import SimpleNamespace as NS

            def r_prep(t0, n):
                c = NS(t0=t0, n=n)
                qt, qtb = r_q.next()
                self.dma(qt[:, :, 0:n], QRd[:, :, t0:t0 + n], wb, [qtb])
                kt, ktb = r_k.next()
                self.dma(kt[:, :, 0:n], KRd[:, :, t0:t0 + n], wb, [ktb])
                vt, vtb = r_v.next()
                self.dma(vt[:n, :], VRd[t0:t0 + n, :], wb, [vtb])
                kc0 = 0 if n == 128 else 4
                ps, pb = self.psb[0]
                for h in range(4):
                    for e in range(2):
                        self.mm(ps[:n, h * 128:h * 128 + n], kt[:, 2 * h + e, 0:n], qt[:, 2 * h + e, 0:n], e == 0, e == 1,
                                [ktb, qtb], [pb], skip=True)
                at, atb = r_at.next()
                self.tt("dve", at[:n, :, 0:n], ps[:n, :].rearrange("p (h i) -> p h i", h=4)[:, :, 0:n], dmk[:n, :, 0:n],
                        ALU.mult, [pb, dmkb], [atb])
                qd, qdb = r_qd.next()
                self.tt("dve", qd[:, :, 0:n].rearrange("p (h e) i -> p h e i", e=2),
                        qt[:, :, 0:n].rearrange("p (h e) i -> p h e i", e=2),
                        qdc[:, :, 0:n].unsqueeze(2).to_broadcast([128, 4, 2, n]), ALU.mult, [qtb, qdcb], [qdb])
                tp, tpb = self.psb[5]
                tpv = tp[:, :].bitcast(BF16)
                for he in range(8):
                    self.tr(tpv[:n, he * 128:(he + 1) * 128], kt[:, he, 0:n], self.identb[:, :], [ktb, self.identb_b], [tpb])
                kd, kdb = r_kd.next()
                self.tt("dve", kd[:n, :, :].rearrange("p (h e) d -> p h (e d)", e=2),
                        tpv[:n, :].rearrange("p (h x) -> p h x", h=4),
                        kdc[:n, kc0:kc0 + 4].unsqueeze(2).to_broadcast([n, 4, 256]), ALU.mult, [tpb, kdcb], [kdb])
                c.__dict__.update(qt=qt, qtb=qtb, kt=kt, ktb=ktb, vt=vt, vtb=vtb, at=at, atb=atb, qd=qd, qdb=qdb, kd=kd, kdb=kdb)
                return c

            def r_rec(c):
                t0, n = c.t0, c.n
                qt, qtb, kt, ktb, vt, vtb, at, atb, qd, qdb, kd, kdb = (c.qt, c.qtb, c.kt, c.ktb, c.vt, c.vtb, c.at, c.atb,
                                                                        c.qd, c.qdb, c.kd, c.kdb)
                pos = []
                for h in range(4):
                    po, pob = self.psb[1 + h]
                    self.mm(po[:n, :], at[:n, h, 0:n], vt[:n, h * 512:(h + 1) * 512], True, False, [atb, vtb], [pob])
                    for e in range(2):
                        self.mm(po[:n, :], qd[:, 2 * h + e, 0:n], Sbf[:, 2 * h + e, :], False, e == 1, [qdb, Sbfb], [pob])
                    pos.append((po, pob))
                for h in range(4):
                    cdec = math.exp(LG[h] * n)
                    for e in range(2):
                        pu, pub = self.psb[6 + e]
                        self.mm(pu[:, :], kd[:n, 2 * h + e, :], vt[:n, h * 512:(h + 1) * 512], True, True, [kdb, vtb], [pub])
                        self.stt(S[:, 2 * h + e, :], S[:, 2 * h + e, :], cdec, pu[:, :], ALU.mult, ALU.add, [Sb, pub], [Sb])
                    self.cp("act", Sbf[:, 2 * h:2 * h + 2, :], S[:, 2 * h:2 * h + 2, :], [Sb], [Sbfb])
                st, stb = r_s.next()
                for h in range(4):
                    po, pob = pos[h]
                    P.op("dve", lambda e_, st=st, po=po, n=n, h=h: e_.bn_stats(out=st[:n, 8 + 6 * h:14 + 6 * h], in_=po[:n, :]),
                         [pob], [stb])
                    P.op("dve", lambda e_, st=st, n=n, h=h: e_.bn_aggr(out=st[:n, 2 * h:2 * h + 2], in_=st[:n, 8 + 6 * h:14 + 6 * h]),
                         [stb], [stb])
                mv = st[:n, 0:8].rearrange("p (h t) -> p h t", t=2)
                rs, rsb = r_s.next()
                self.rsqrt(rs[:n, 0:4], mv[:, :, 1], [stb], [rsb], eps=LN_EPS)
                self.stt(rs[:n, 4:8], mv[:, :, 0], -1.0, rs[:n, 0:4], ALU.mult, ALU.mult, [stb, rsb], [rsb])
                ot, otb = r_o.next()
                for h in range(4):
                    po, pob = pos[h]
                    self.act(ot[:n, h * 512:(h + 1) * 512], po[:n, :], AF.Identity, [pob, rsb], [otb],
                             bias=rs[:n, 4 + h:5 + h], scale=rs[:n, h:h + 1])
                b = self.buf(("ret_og", g.gi, t0))
                ogw.append(b)
                self.dma(dr["OG_" + nm][t0:t0 + n, :], ot[:n, :], [otb], [b])

            prev = None
            for (t0, n) in g.tiles:
                cur = r_prep(t0, n)
                if prev is not None:
                    r_rec(prev)
                prev = cur
            r_rec(prev)
            self.dma(dr["ret_state_" + nm].rearrange("h (e p) v -> p (h e) v", e=2), S[:, :, :], [Sb],
                     [self.buf(("ret_state", g.gi))])
        P.barrier()
        self.load_w(self.W, self.Wb, dr["ret_w_out"][D:2 * D, :], D)
        P.barrier(); A.reset()
        gnw, gnwb = A.alloc([128, 2048], F32), Buf()
        self.dma(gnw[:, :], dr["ret_gn_w"][0:1, :].to_broadcast([128, 2048]), [], [gnwb])
        r_og = Ring(self, "rog2", [128, 2048], BF16, 2, arena=A)
        r_zs = Ring(self, "rzs2", [128, 2048], BF16, 2, arena=A)
        r_gz = Ring(self, "rgz2", [128, 2048], BF16, 2, arena=A)

        def wo(k):
            if k < 8:
                return self.WO[:, k, :], [self.WOb]
            return self.W[:, k - 8, 0:D], self.wdep(0, D, k - 8)
        items = []
        for g in self.groups:
            def mkprep(g, t0, n):
                def prep():
                    nm = g.name
                    og, ogb = r_og.next()
                    self.dma(og[:n, :], dr["OG_" + nm][t0:t0 + n, :], ogws[g.gi], [ogb])
                    zs, zsb = r_zs.next()
                    self.dma(zs[:n, :], dr["ZS_" + nm][t0:t0 + n, :], wbs[g.gi], [zsb])
                    gz, gzb = r_gz.next()
                    self.tt("dve", gz[:n, :], og[:n, :], gnw[:n, :], ALU.mult, [ogb, gnwb], [gzb])
                    self.tt("dve", gz[:n, :], gz[:n, :], zs[:n, :], ALU.mult, [gzb, zsb], [gzb])
                    return gz, gzb
                return prep
            items += [(g, t0, n, mkprep(g, t0, n)) for (t0, n) in g.tiles]
        self.pass_c_run(items, first, last, 16, wo=wo)

    def layer_gdn(self, L, first, last):
        dr = self.dr
        P = self.P
        A = self.arena
        P.barrier(); A.reset()
        self.set_psum(4, 2, 2)
        self.load_w(self.W, self.Wb, dr["gdn_w_in"], 4112)
        self.load_w(self.WO, self.WOb, dr["gdn_w_out"], D)
        cw, cwb = A.alloc([128, 24, 4], F32), Buf()
        self.dma(cw[:, :, :], dr["gdn_convw"][:, :, :], [], [cwb])
        halo, halob = A.alloc([128, 24, 3], F32), Buf()
        onesb, onesbb = A.alloc([128, 128], BF16), Buf()
        self.memset("pool", onesb[:, :], 1.0, [onesbb])
        sc8, sc8b = A.alloc([8, 4], F32), Buf()
        self.dma(sc8[:, 0:1], dr["gdn_dt_bias"][:, :], [], [sc8b])
        self.dma(sc8[:, 2:3], dr["gdn_a_log"][:, :], [], [sc8b])
        self.act(sc8[:, 3:4], sc8[:, 2:3], AF.Exp, [sc8b], [sc8b])
        self.ts("dve", sc8[:, 1:2], sc8[:, 3:4], -1.0, None, ALU.mult, None, [sc8b], [sc8b])
        r_xr = Ring(self, "rxr", [128, 516], F32, 4, arena=A)
        r_y = Ring(self, "ry", [128, 512], F32, 4, arena=A)
        r_rn = Ring(self, "rrn", [128, 512], F32, 4, arena=A)
        r_sq = Ring(self, "rsq", [128, 512], BF16, 4, arena=A)
        r_st = Ring(self, "rst", [128, 8, 512], BF16, 1, arena=A)
        r_g8 = Ring(self, "rg8", [8, 512], F32, 6, arena=A)
        wbs = {}
        for g in self.groups:
            nm = g.name
            wb = wbs[g.gi] = []

            def nb_(key, wb=wb, g=g):
                b = self.buf(("gdn", g.gi) + key)
                wb.append(b)
                return b
            if g.gi == 0:
                self.memset("pool", halo[:, :, :], 0.0, [halob])
            else:
                self.dma(halo[:, :, :], dr["gdn_conv_in"][:, :, :], [], [halob])
            for blk in g.blocks:
                uT, uTb, nb = self.make_uT(g, blk, first)
                t0b = blk[0][0]
                for which, dname in ((0, "QG_"), (1, "KG_"), (2, "VG_")):
                    st, stb = r_st.next()
                    for hg in (0, 4):
                        hs = list(range(hg, hg + 4))
                        xrs, ys, rns, sqs, p2s = {}, {}, {}, {}, {}
                        for h in hs:
                            c = which * 8 + h
                            ps, pb = self.proj_fm(uT, uTb, nb, c * 128, (c + 1) * 128)
                            xr, xrb = r_xr.next()
                            self.cp("act", xr[:, 0:3], halo[:, c, :], [halob], [xrb])
                            self.cp("act", xr[:, 3:3 + nb], ps[:, 0:nb], [pb], [xrb])
                            self.cp("act", halo[:, c, :], xr[:, nb:nb + 3], [xrb], [halob])
                            xrs[h] = (xr, xrb)
                        for h in hs:
                            c = which * 8 + h
                            xr, xrb = xrs[h]
                            y, yb = r_y.next()
                            self.ts("dve", y[:, 0:nb], xr[:, 0:nb], cw[:, c, 0:1], None, ALU.mult, None, [xrb, cwb], [yb])
                            for j in range(1, 4):
                                self.stt(y[:, 0:nb], xr[:, j:j + nb], cw[:, c, j:j + 1], y[:, 0:nb], ALU.mult, ALU.add,
                                         [xrb, cwb, yb], [yb])
                            ys[h] = (y, yb)
                        for h in hs:
                            y, yb = ys[h]
                            self.act(y[:, 0:nb], y[:, 0:nb], AF.Silu, [yb], [yb])
                        if which == 2:
                            for h in hs:
                                y, yb = ys[h]
                                self.cp("dve", st[:, h, 0:nb], y[:, 0:nb], [yb], [stb])
                            continue
                        for h in hs:
                            y, yb = ys[h]
                            sq, sqb = r_sq.next()
                            self.tt("dve", sq[:, 0:nb], y[:, 0:nb], y[:, 0:nb], ALU.mult, [yb], [sqb])
                            p2, p2b = self.ps_mm.next()
                            self.mm(p2[:, 0:nb], onesb[:, :], sq[:, 0:nb], True, True, [onesbb, sqb], [p2b])
                            p2s[h] = (p2, p2b)
                        for h in hs:
                            p2, p2b = p2s[h]
                            rn, rnb = r_rn.next()
                            self.rsqrt(rn[:, 0:nb], p2[:, 0:nb], [p2b], [rnb], eps=NORM_EPS)
                            rns[h] = (rn, rnb)
                        for h in hs:
                            y, yb = ys[h]
                            rn, rnb = rns[h]
                            if which == 0:
                                self.stt(st[:, h, 0:nb], y[:, 0:nb], 128.0 ** -0.5, rn[:, 0:nb], ALU.mult, ALU.mult,
                                         [yb, rnb], [stb])
                            else:
                                self.tt("dve", st[:, h, 0:nb], y[:, 0:nb], rn[:, 0:nb], ALU.mult, [yb, rnb], [stb])
                    self.dma(dr[dname + nm].rearrange("h p t -> p h t")[:, :, t0b:t0b + nb], st[:, :, 0:nb], [stb],
                             [nb_((dname, t0b))])
                ps, pb = self.proj_fm(uT, uTb, nb, 3072, 3080)
                bt, btb = r_g8.next()
                self.act(bt[:, 0:nb], ps[0:8, 0:nb], AF.Exp, [pb], [btb], scale=-1.0)
                self.ts("dve", bt[:, 0:nb], bt[:, 0:nb], 1.0, None, ALU.add, None, [btb], [btb])
                self.recip(bt[:, 0:nb], bt[:, 0:nb], [btb], [btb])
                ps, pb = self.proj_fm(uT, uTb, nb, 3080, 3088)
                gt, gtb = r_g8.next()
                self.act(gt[:, 0:nb], ps[0:8, 0:nb], AF.Exp, [pb, sc8b], [gtb], bias=sc8[:, 0:1])
                self.act(gt[:, 0:nb], gt[:, 0:nb], AF.Ln, [gtb], [gtb], bias=1.0)
                self.ts("dve", gt[:, 0:nb], gt[:, 0:nb], sc8[:, 1:2], None, ALU.mult, None, [gtb, sc8b], [gtb])
                Gt, Gtb = r_g8.next()
                on8, on8b = r_g8.next()
                self.memset("pool", on8[:, 0:128], 1.0, [on8b])
                for off in range(0, nb, 64):
                    n = min(64, nb - off)
                    P.op("dve", lambda e, Gt=Gt, gt=gt, on8=on8, off=off, n=n: e.tensor_tensor_scan(
                        out=Gt[:, off:off + n], data0=on8[:, 0:n], data1=gt[:, off:off + n], initial=0.0,
                        op0=ALU.mult, op1=ALU.add), [gtb, on8b], [Gtb])
                btb16, btb16b = r_g8.next()
                bt16v = btb16[:, 0:256].bitcast(BF16)
                self.cp("act", bt16v[:, 0:nb], bt[:, 0:nb], [btb], [btb16b])
                self.dma(dr["BTb_" + nm][:, t0b:t0b + nb], bt16v[:, 0:nb], [btb16b], [nb_(("btb", t0b))])
                self.dma(dr["GT_" + nm][:, t0b:t0b + nb], Gt[:, 0:nb], [Gtb], [nb_(("gt", t0b))])
                off = 0
                for ti, (t0, n) in enumerate(blk):
                    tp, tpb = self.ps_tp.next()
                    self.tr(tp[0:n, 0:8], bt[:, off:off + n], self.identf[0:8, 0:8], [btb, self.identf_b], [tpb])
                    tp2, tp2b = self.ps_tp.next()
                    self.tr(tp2[0:n, 0:8], Gt[:, off:off + n], self.identf[0:8, 0:8], [Gtb, self.identf_b], [tp2b])
                    gb, gbb = self.r_small.next()
                    self.cp("act", gb[:n, 0:8], tp[0:n, 0:8], [tpb], [gbb])
                    self.cp("act", gb[:n, 8:16], tp2[0:n, 0:8], [tp2b], [gbb])
                    self.dma(dr["GB_" + nm][t0:t0 + n, :], gb[:n, 0:16], [gbb], [nb_(("gb", t0))])
                    zt, ztb = self.r_h.next()
                    for cg in range(2):
                        ps, pb = self.proj_tm(uT, uTb, off, n, 3088 + cg * 512, 3088 + (cg + 1) * 512)
                        self.act(zt[:n, cg * 512:(cg + 1) * 512], ps[:n, :], AF.Silu, [pb], [ztb])
                    self.dma(dr["ZS_" + nm][t0:t0 + n, 0:D], zt[:n, 0:D], [ztb], [nb_(("z", t0))])
                    off += n
                if blk is g.blocks[-1]:
                    co, cob = self.r_f.next()
                    for cg in range(6):
                        ps, pb = self.ps_mm.next()
                        for k in range(8):
                            self.mm(ps[0:3, :], uT[:, k, nb - 3:nb], self.W[:, k, cg * 512:(cg + 1) * 512], k == 0, k == 7,
                                    [uTb] + self.wdep(cg * 512, (cg + 1) * 512, k), [pb])
                        self.cp("act", co[0:3, (cg % 2) * 512:(cg % 2 + 1) * 512], ps[0:3, :], [pb], [cob])
                        if cg % 2 == 1:
                            self.dma(dr["gdn_conv_" + nm][:, (cg - 1) * 512:(cg + 1) * 512], co[0:3, :], [cob],
                                     [self.buf(("gdn_conv", g.gi, cg))])
                            if cg < 5:
                                co, cob = self.r_f.next()
        import os
        if os.environ.get("GDN_STOP") == "A":
            return
        P.barrier(); A.reset()
        A2 = self.arenaW
        A2.reset()
        msk, mskb = A.alloc([128, 3, 128], F32), Buf()
        self.dma(msk[:, :, :], dr["gdn_masks"][:, :, :], [], [mskb])
        S, Sb = A.alloc([128, 8, 128], F32), Buf()
        Sbf, Sbfb = A.alloc([128, 8, 128], BF16), Buf()
        r_qkv = Ring(self, "rqkv", [128, 3, 8, 128], BF16, 2, arena=A)
        r_gb = Ring(self, "rgb", [128, 8, 128], F32, 2, arena=A)
        r_bb = Ring(self, "rbb", [128, 8, 128], BF16, 2, arena=A)
        r_gbk = Ring(self, "rgbk", [64, 2, 16], F32, 2, arena=A)
        r_sq = Ring(self, "rsq", [64, 8, 128], F32, 1, arena=A)
        r_sc = Ring(self, "rsc", [64, 48], F32, 6, arena=A)
        r_F = Ring(self, "rF", [128, 8, 64], F32, 14, arena=A2)
        r_B = Ring(self, "rB", [128, 8, 64], BF16, 16, arena=A2)
        r_L = Ring(self, "rL", [128, 8, 64], BF16, 10, arena=A2)
        r_B2 = Ring(self, "rB2", [64, 8, 128], BF16, 8, arena=A)
        ogws = {}

        def bc(ap, shape):
            return ap.to_broadcast(shape)
        for g in self.groups:
            nm = g.name
            wb = wbs[g.gi]
            ogw = ogws[g.gi] = []
            if g.gi == 0:
                self.memset("pool", S[:, :, :], 0.0, [Sb])
            else:
                self.dma(S[:, :, :], dr["state_gdn"].rearrange("h d v -> d h v"), [], [Sb])
            self.cp("act", Sbf[:, :, :], S[:, :, :], [Sb], [Sbfb])
            for (tt0, tn) in g.tiles:
                qkv, qkvb = r_qkv.next()
                for i, dname in enumerate(("QG_", "KG_", "VG_")):
                    self.dma(qkv[:, i, :, 0:tn], dr[dname + nm].rearrange("h p t -> p h t")[:, :, tt0:tt0 + tn], wb, [qkvb])
                gb, gbb = r_gb.next()
                self.dma(gb[:, :, 0:tn], bass.AP(self.dh["GT_" + nm], tt0, [[0, 128], [g.T, 8], [1, tn]]), wb, [gbb])
                bb, bbb = r_bb.next()
                self.dma(bb[:, :, 0:tn], bass.AP(self.dh["BTb_" + nm], tt0, [[0, 128], [g.T, 8], [1, tn]]), wb, [bbb])
                gbk, gbkb = r_gbk.next()
                if tn % 64 == 0:
                    self.dma(gbk[0:64, 0:tn // 64, :], dr["GB_" + nm][tt0:tt0 + tn, :].rearrange("(c p) x -> p c x", p=64),
                             wb, [gbkb])
                else:
                    self.dma(gbk[:tn, 0, :], dr["GB_" + nm][tt0:tt0 + tn, :], wb, [gbkb])
                from types import SimpleNamespace as NS
                cks = []
                for ci, off in enumerate(range(0, tn, 64)):
                    c = NS(ci=ci, off=off, n=min(64, tn - off), t0=tt0 + off)
                    c.Qt = qkv[:, 0, :, off:off + c.n]
                    c.Kt = qkv[:, 1, :, off:off + c.n]
                    c.Vt = qkv[:, 2, :, off:off + c.n]
                    c.Gp = gbk[:c.n, ci, 8:16]
                    c.Bp = gbk[:c.n, ci, 0:8]
                    cks.append(c)
                mk = lambda c, k_: msk[:c.n, k_, 0:c.n].unsqueeze(1).to_broadcast([c.n, 8, c.n])
                v3 = lambda c, ps_: ps_[:c.n, :].rearrange("p (h c) -> p h c", h=8)[:, :, 0:c.n]
                t3 = lambda c, v_: v_[:c.n, :].rearrange("p (h c) -> p h c", h=8)
                bcs = lambda c, a_: a_.unsqueeze(2).to_broadcast([c.n, 8, 128])

                def s_bcast(c):
                    n, off = c.n, c.off
                    c.eG, c.eGb = r_F.next()
                    self.act(c.eG[:, :, 0:n], gb[:, :, off:off + n], AF.Exp, [gbb], [c.eGb])
                    c.KbT, c.KbTb = r_B.next()
                    self.tt("dve", c.KbT[:, :, 0:n], c.Kt, bb[:, :, off:off + n], ALU.mult, [qkvb, bbb], [c.KbTb])
                    c.sc, c.scb = r_sc.next()
                    self.act(c.sc[:n, 16:24], c.Gp, AF.Exp, [gbkb], [c.scb])
                    c.tD, c.tDb = r_F.next()
                    self.tt("dve", c.tD[:n, :, 0:n], gb[:n, :, off:off + n], c.Gp.unsqueeze(2).to_broadcast([n, 8, n]),
                            ALU.subtract, [gbb, gbkb], [c.tDb])

                def s_gram(c):
                    n = c.n
                    c.paA, c.paAb = self.ps_mm.next()
                    for h in range(8):
                        self.mm(c.paA[:n, h * 64:h * 64 + n], c.Kt[:, h, :], c.KbT[:, h, 0:n], True, True, [qkvb, c.KbTb], [c.paAb], skip=True)
                    c.paB, c.paBb = self.ps_mm.next()
                    for h in range(8):
                        self.mm(c.paB[:n, h * 64:h * 64 + n], c.KbT[:, h, 0:n], c.Kt[:, h, :], True, True, [qkvb, c.KbTb], [c.paBb], skip=True)
                    c.D1, c.D1b = r_F.next()
                    self.ts("dve", c.D1[:n, :, 0:n], c.tD[:n, :, 0:n], 0.0, None, ALU.min, None, [c.tDb], [c.D1b])
                    self.act(c.D1[:n, :, 0:n], c.D1[:n, :, 0:n], AF.Exp, [c.D1b], [c.D1b])
                    c.D2, c.D2b = r_F.next()
                    self.ts("dve", c.D2[:n, :, 0:n], c.tD[:n, :, 0:n], 0.0, None, ALU.max, None, [c.tDb], [c.D2b])
                    self.act(c.D2[:n, :, 0:n], c.D2[:n, :, 0:n], AF.Exp, [c.D2b], [c.D2b], scale=-1.0)

                def s_mask(c):
                    n = c.n
                    c.D1i, c.D1ib = r_F.next()
                    self.tt("dve", c.D1i[:n, :, 0:n], c.D1[:n, :, 0:n], mk(c, 2), ALU.mult, [c.D1b, mskb], [c.D1ib])
                    self.tt("dve", c.D1[:n, :, 0:n], c.D1[:n, :, 0:n], mk(c, 0), ALU.mult, [c.D1b, mskb], [c.D1b])
                    self.tt("dve", c.D2[:n, :, 0:n], c.D2[:n, :, 0:n], mk(c, 1), ALU.mult, [c.D2b, mskb], [c.D2b])
                    c.Pm, c.Pmb = r_F.next()
                    self.tt("dve", c.Pm[:n, :, 0:n], v3(c, c.paA), c.D1[:n, :, 0:n], ALU.mult, [c.paAb, c.D1b], [c.Pmb])
                    c.X, c.Xb = r_B.next()
                    self.cp("act", c.X[:n, :, 0:n], c.Pm[:n, :, 0:n], [c.Pmb], [c.Xb])
                    c.Y, c.Yb = r_B.next()
                    self.tt("dve", c.Y[:n, :, 0:n], v3(c, c.paB), c.D2[:n, :, 0:n], ALU.mult, [c.paBb, c.D2b], [c.Yb])
                    self.tt("dve", c.Pm[:n, :, 0:n], c.Pm[:n, :, 0:n],
                            self.identf[:n, 0:n].unsqueeze(1).to_broadcast([n, 8, n]), ALU.add, [c.Pmb, self.identf_b], [c.Pmb])
                    c.Pbf, c.Pbfb = r_B.next()
                    self.cp("act", c.Pbf[:n, :, 0:n], c.Pm[:n, :, 0:n], [c.Pmb], [c.Pbfb])
                    c.nsteps = max(1, int(math.ceil(math.log2(n))) - 1)

                def mk_step(s):
                    def s_sq(c):
                        n = c.n
                        if s >= c.nsteps:
                            return
                        c.lastst = s == c.nsteps - 1
                        c.pxY, c.pxYb = self.ps_mm.next()
                        for h in range(8):
                            self.mm(c.pxY[:n, h * 64:h * 64 + n], c.X[:n, h, 0:n], c.Y[:n, h, 0:n], True, True, [c.Xb, c.Yb], [c.pxYb], skip=True)
                        if not c.lastst:
                            c.pxX, c.pxXb = self.ps_mm.next()
                            for h in range(8):
                                self.mm(c.pxX[:n, h * 64:h * 64 + n], c.Y[:n, h, 0:n], c.X[:n, h, 0:n], True, True, [c.Xb, c.Yb], [c.pxXb], skip=True)
                        c.Y2, c.Y2b = r_B.next()
                        self.cp("act", c.Y2[:n, :, 0:n], v3(c, c.pxY), [c.pxYb], [c.Y2b])
                        if not c.lastst:
                            c.X2, c.X2b = r_B.next()
                            self.cp("dve", c.X2[:n, :, 0:n], v3(c, c.pxX), [c.pxXb], [c.X2b])

                    def s_acc(c):
                        n = c.n
                        if s >= c.nsteps:
                            return
                        pp, ppb = self.ps_mm.next()
                        for h in range(8):
                            self.mm(pp[:n, h * 64:h * 64 + n], c.Y2[:n, h, 0:n], c.Pbf[:n, h, 0:n], True, True, [c.Y2b, c.Pbfb], [ppb], skip=True)
                        self.tt("dve", c.Pm[:n, :, 0:n], c.Pm[:n, :, 0:n], v3(c, pp), ALU.add, [c.Pmb, ppb], [c.Pmb])
                        c.Pbf, c.Pbfb = r_L.next() if c.lastst else r_B.next()
                        self.cp("act", c.Pbf[:n, :, 0:n], c.Pm[:n, :, 0:n], [c.Pmb], [c.Pbfb])
                        c.Y, c.Yb = c.Y2, c.Y2b
                        if not c.lastst:
                            c.X, c.Xb = c.X2, c.X2b
                    return [s_sq, s_acc]

                def s_tok(c):
                    n, off = c.n, c.off
                    self.tt("dve", c.sc[:n, 0:8], c.sc[:n, 16:24], c.Bp, ALU.mult, [c.scb, gbkb], [c.scb])
                    self.tt("dve", c.sc[:n, 24:32], c.Gp, gb[:n, :, off + n - 1], ALU.subtract, [gbkb, gbb], [c.scb])
                    self.act(c.sc[:n, 8:16], c.sc[:n, 24:32], AF.Exp, [c.scb], [c.scb], scale=-1.0)
                    tpK, tpKb = self.ps_tp.next()
                    tpV, tpVb = self.ps_tp.next()
                    tKv = tpK[:, :].bitcast(BF16)
                    tVv = tpV[:, :].bitcast(BF16)
                    for h in range(8):
                        self.tr(tKv[:n, h * 128:(h + 1) * 128], c.Kt[:, h, :], self.identb[:, :], [qkvb, self.identb_b], [tpKb])
                    for h in range(8):
                        self.tr(tVv[:n, h * 128:(h + 1) * 128], c.Vt[:, h, :], self.identb[:, :], [qkvb, self.identb_b], [tpVb])
                    c.kbg, c.kbgb = r_B2.next()
                    self.tt("dve", c.kbg[:n, :, :], t3(c, tKv), bcs(c, c.sc[:n, 0:8]), ALU.mult, [tpKb, c.scb], [c.kbgb])
                    c.kdc, c.kdcb = r_B2.next()
                    self.tt("dve", c.kdc[:n, :, :], t3(c, tKv), bcs(c, c.sc[:n, 8:16]), ALU.mult, [tpKb, c.scb], [c.kdcb])
                    c.vb_, c.vbb_ = r_B2.next()
                    self.tt("dve", c.vb_[:n, :, :], t3(c, tVv), bcs(c, c.Bp), ALU.mult, [tpVb, gbkb], [c.vbb_])
                    c.qdT, c.qdTb = r_L.next()
                    self.tt("dve", c.qdT[:, :, 0:n], c.Qt, c.eG[:, :, 0:n], ALU.mult, [qkvb, c.eGb], [c.qdTb])
                    c.paQ, c.paQb = self.ps_mm.next()
                    for h in range(8):
                        self.mm(c.paQ[:n, h * 64:h * 64 + n], c.Kt[:, h, :], c.Qt[:, h, :], True, True, [qkvb], [c.paQb], skip=True)
                    c.QK, c.QKb = r_L.next()
                    self.tt("dve", c.QK[:n, :, 0:n], v3(c, c.paQ), c.D1i[:n, :, 0:n], ALU.mult, [c.paQb, c.D1ib], [c.QKb])

                def s_wd(c):
                    n = c.n
                    pw, pwb = self.ps_mm.next()
                    for h in range(8):
                        self.mm(pw[:, h * 64:h * 64 + n], c.kbg[:n, h, :], c.Pbf[:n, h, 0:n], True, True, [c.kbgb, c.Pbfb], [pwb], skip=True)
                    c.wdT, c.wdTb = r_L.next()
                    self.act(c.wdT[:, :, 0:n], pw[:, :].rearrange("p (h c) -> p h c", h=8)[:, :, 0:n], AF.Copy, [pwb], [c.wdTb],
                             scale=-1.0)

                stages = [s_bcast, s_gram, s_mask]
                for s in range(5):
                    stages += mk_step(s)
                stages += [s_tok, s_wd]
                for f in stages:
                    for c in cks:
                        f(c)

                for c in cks:
                    n, off, t0 = c.n, c.off, c.t0
                    pv = [self.ps_acc.next(), self.ps_acc.next()]
                    for h in range(8):
                        pt_, ptb_ = pv[h // 4]
                        o_ = pt_[:n, (h % 4) * 128:(h % 4 + 1) * 128]
                        self.mm(o_, c.Pbf[:n, h, 0:n], c.vb_[:n, h, :], True, False, [c.Pbfb, c.vbb_], [ptb_], skip=True)
                        self.mm(o_, c.wdT[:, h, 0:n], Sbf[:, h, :], False, True, [c.wdTb, Sbfb], [ptb_], skip=True)
                    vr, vrb = r_B2.next()
                    for hb in range(2):
                        self.cp("act", vr[:n, hb * 4:(hb + 1) * 4, :], pv[hb][0][:n, :].rearrange("p (h c) -> p h c", h=4),
                                [pv[hb][1]], [vrb])
                    self.tt("dve", S[:, :, :], S[:, :, :], c.eG[:, :, n - 1:n].to_broadcast([128, 8, 128]), ALU.mult,
                            [Sb, c.eGb], [Sb])
                    po = [self.ps_acc.next(), self.ps_acc.next()] if False else [self.ps_mm.next(), self.ps_mm.next()]
                    for h in range(8):
                        pt_, ptb_ = po[h // 4]
                        o_ = pt_[:n, (h % 4) * 128:(h % 4 + 1) * 128]
                        self.mm(o_, c.qdT[:, h, 0:n], Sbf[:, h, :], True, False, [c.qdTb, Sbfb], [ptb_], skip=True)
                        self.mm(o_, c.QK[:n, h, 0:n], vr[:n, h, :], False, True, [c.QKb, vrb], [ptb_], skip=True)
                    pu = [self.ps_mm.next(), self.ps_mm.next()]
                    for h in range(8):
                        pt_, ptb_ = pu[h // 4]
                        self.mm(pt_[:, (h % 4) * 128:(h % 4 + 1) * 128], c.kdc[:n, h, :], vr[:n, h, :], True, True,
                                [c.kdcb, vrb], [ptb_], skip=True)
                    for hb in range(2):
                        self.tt("dve", S[:, hb * 4:(hb + 1) * 4, :], S[:, hb * 4:(hb + 1) * 4, :],
                                pu[hb][0][:, :].rearrange("p (h c) -> p h c", h=4), ALU.add, [Sb, pu[hb][1]], [Sb])
                    self.cp("act", Sbf[:, :, :], S[:, :, :], [Sb], [Sbfb])
                    sq, sqb = r_sq.next()
                    for hb in range(2):
                        self.act(sq[:n, hb * 4:(hb + 1) * 4, :], po[hb][0][:n, :].rearrange("p (h c) -> p h c", h=4),
                                 AF.Square, [po[hb][1]], [sqb])
                    sc, scb = c.sc, c.scb
                    P.op("dve", lambda e, sc=sc, sq=sq, n=n: e.tensor_reduce(out=sc[:n, 32:40], in_=sq[:n, :, :], axis=AX.X,
                                                                         op=ALU.add), [sqb], [scb])
                    self.rsqrt(sc[:n, 32:40], sc[:n, 32:40], [scb], [scb], scale=1.0 / 128, eps=NORM_EPS)
                    ot, otb = r_B2.next()
                    for hb in range(2):
                        self.tt("dve", ot[:n, hb * 4:(hb + 1) * 4, :], po[hb][0][:n, :].rearrange("p (h c) -> p h c", h=4),
                                sc[:n, 32 + hb * 4:36 + hb * 4].unsqueeze(2).to_broadcast([n, 4, 128]), ALU.mult,
                                [po[hb][1], scb], [otb])
                    b = self.buf(("gdn_og", g.gi, t0))
                    ogw.append(b)
                    self.dma(dr["OG_" + nm][t0:t0 + n, 0:D], ot[:n, :, :].rearrange("p h c -> p (h c)"), [otb], [b])
            self.dma(dr["gdn_state_" + nm].rearrange("h d v -> d h v"), S[:, :, :], [Sb], [self.buf(("gdn_state", g.gi))])
        if os.environ.get("GDN_STOP") == "B":
            return
        P.barrier(); A.reset()
        r_gz = Ring(self, "rgz", [128, D], F32, 2, arena=A)
        nw, nwb = A.alloc([128, D], F32), Buf()
        for h in range(8):
            self.dma(nw[:, h * 128:(h + 1) * 128], dr["gdn_norm_w"][0:1, :].to_broadcast([128, 128]), [], [nwb])
        items = []
        for g in self.groups:
            def mkprep(g, t0, n):
                def prep():
                    nm = g.name
                    og, ogb = self.r_h.next()
                    self.dma(og[:n, 0:D], dr["OG_" + nm][t0:t0 + n, 0:D], ogws[g.gi], [ogb])
                    zs, zsb = self.r_h.next()
                    self.dma(zs[:n, 0:D], dr["ZS_" + nm][t0:t0 + n, 0:D], wbs[g.gi], [zsb])
                    gz, gzb = r_gz.next()
                    self.tt("dve", gz[:n, :], og[:n, 0:D], nw[:n, :], ALU.mult, [ogb, nwb], [gzb])
                    self.tt("dve", og[:n, 0:D], gz[:n, :], zs[:n, 0:D], ALU.mult, [gzb, zsb], [ogb])
                    return og, ogb
                return prep
            items += [(g, t0, n, mkprep(g, t0, n)) for (t0, n) in g.tiles]
        self.pass_c_run(items, first, last, 8)

    def layer_dummy(self, L, first, last):
        self.layer_mod(L)
        self.load_w(self.W, self.Wb, self.dr["fox_w_in"], 4112)
        self.load_w(self.WO, self.WOb, self.dr["fox_w_out"], D)
        for g in self.groups:
            for blk in g.blocks:
                uT, uTb, nb = self.make_uT(g, blk, first)
                off = 0
                for (t0, n) in blk:
                    og, ogb = self.r_h.next()
                    for cg in range(2):
                        ps, pb = self.proj_tm(uT, uTb, off, n, cg * 512, (cg + 1) * 512)
                        self.cp("act", og[:n, cg * 512:(cg + 1) * 512], ps[:n, :], [pb], [ogb])
                    self.pass_c_tile(g, t0, n, first, last, og, ogb, 8)
                    off += n


def build(cfg, mode="full"):
    k = K(cfg)
    k.setup()
    nl = len(cfg.layers)
    for i, L in enumerate(cfg.layers):
        if mode == "dummy":
            k.layer_dummy(L, i == 0, i == nl - 1)
        else:
            k.layer(L, i == 0, i == nl - 1)
    k.P.emit()
    return k


def t5_onehot():
    import jax
    import jax.numpy as jnp
    with jax.default_device(jax.devices("cpu")[0]):
        rel = jnp.arange(384, dtype=jnp.int32) - 255
        nb = 16
        max_exact = 8
        ret = jnp.where(rel > 0, nb, 0)
        n = jnp.abs(rel)
        large = max_exact + (jnp.log(jnp.maximum(n, 1).astype(jnp.float32) / max_exact)
                             / math.log(128 / max_exact) * (nb - max_exact)).astype(jnp.int32)
        large = jnp.minimum(large, nb - 1)
        bucket = np.asarray(ret + jnp.where(n < max_exact, n, large))
    oh = np.zeros((32, 384), np.float32)
    oh[bucket, np.arange(384)] = 1.0
    return oh


def ret_perm():
    one = np.concatenate([np.arange(0, 256, 2), np.arange(1, 256, 2)])
    return np.concatenate([h * 256 + one for h in range(4)])


_RC = {}


def ret_consts(cfg):
    key = (cfg.T, cfg.TS, cfg.PAST)
    if key in _RC:
        return _RC[key]
    import jax
    import jax.numpy as jnp
    out = {}
    with jax.default_device(jax.devices("cpu")[0]):
        inv = 10000.0 ** (-jnp.arange(0, 256, 2, dtype=jnp.float32) / 256)
        for nm, start, T in (("p", 0, cfg.T), ("s", cfg.PAST, cfg.TS)):
            pos = (start + jnp.arange(T)).astype(jnp.float32)
            ang = pos[:, None] * inv[None, :]
            out["rot_cos_" + nm] = np.ascontiguousarray(np.asarray(jnp.cos(ang)).T)
            out["rot_sin_" + nm] = np.ascontiguousarray(np.asarray(jnp.sin(ang)).T)
        lg = np.asarray(jnp.log1p(-jnp.power(2.0, -5.0 - jnp.arange(4, dtype=jnp.float32))), np.float64)
    i = np.arange(128)
    dm = np.zeros((128, 4, 128), np.float32)
    qd = np.zeros((128, 4, 128), np.float32)
    kd = np.zeros((128, 8), np.float32)
    for h in range(4):
        d = i[None, :] - i[:, None]
        dm[:, h, :] = np.where(d >= 0, np.exp(lg[h] * np.maximum(d, 0)), 0.0)
        qd[:, h, :] = np.exp(lg[h] * (i + 1.0))[None, :]
        kd[:, h] = np.exp(lg[h] * (127.0 - i))
        kd[:, 4 + h] = np.exp(lg[h] * (cfg.TS - 1.0 - i))
    out["ret_dmaskT"], out["ret_qdec"], out["ret_kdec"] = dm, qd, kd
    _RC[key] = out
    return out


def core_inputs(cfg, inp, b):
    T, PAST = cfg.T, cfg.PAST
    f = np.ascontiguousarray
    cT = np.stack([inp["c_prompt"][b], inp["c_sample"][b]], -1).reshape(8, 128, 2).transpose(1, 0, 2)
    m = {
        "xp": f(inp["x_prompt"][b, :T]), "xs": f(inp["x_sample"][b]), "cT": f(cT),
        "ada_w": inp["ada_w"], "ada_b": inp["ada_b"], "ln_g": inp["ln_g"], "ln_b": inp["ln_b"],
        "ident": np.eye(128, dtype=np.float32),
        "cmask": np.where(np.arange(128)[:, None] > np.arange(128)[None, :], -30000.0, 0.0).astype(np.float32),
        "fox_w_in": inp["fox_w_in"], "fox_b_f": f(inp["fox_b_f"].reshape(16, 1)), "fox_w_out": inp["fox_w_out"],
        "cache_fox_k": f(inp["cache_fox_k"][b, :PAST].reshape(PAST, -1)),
        "cache_fox_v": f(inp["cache_fox_v"][b, :PAST].reshape(PAST, -1)),
        "cache_fox_logf": f(inp["cache_fox_logf"][b, :PAST]),
        "diff_w_in": inp["diff_w_in"], "diff_w_out": inp["diff_w_out"], "rel_bias_table": inp["rel_bias_table"],
        "diff_lam_q1": f(inp["diff_lam_q1"].reshape(1, 64)), "diff_lam_k1": f(inp["diff_lam_k1"].reshape(1, 64)),
        "diff_lam_q2": f(inp["diff_lam_q2"].reshape(1, 64)), "diff_lam_k2": f(inp["diff_lam_k2"].reshape(1, 64)),
        "diff_subln_w": f(inp["diff_subln_w"].reshape(1, 128)), "t5_onehot": t5_onehot(),
        "antiident": np.ascontiguousarray(np.eye(128, dtype=np.float32)[::-1]),
        "dmask": np.where((np.arange(128)[:, None] >= 64) & (np.arange(128)[None, :] < 64), -30000.0, 0.0).astype(np.float32),
        "cache_diff_k": f(inp["cache_diff_k"][b, :PAST].reshape(PAST, -1)),
        "cache_diff_v": f(inp["cache_diff_v"][b, :PAST].reshape(PAST, -1)),
    }
    m.update(ret_consts(cfg))
    i = np.arange(128)
    msk = np.zeros((128, 3, 128), np.float32)
    msk[:, 0, :] = np.where(i[None, :] > i[:, None], -1.0, 0.0)
    msk[:, 1, :] = np.where(i[None, :] < i[:, None], -1.0, 0.0)
    msk[:, 2, :] = np.where(i[None, :] >= i[:, None], 1.0, 0.0)
    sel = np.zeros((8, 8, 128), np.float32)
    for h in range(8):
        sel[h, h, :] = 1.0
    m.update({
        "gdn_w_in": inp["gdn_w_in"], "gdn_w_out": inp["gdn_w_out"],
        "gdn_convw": f(inp["gdn_conv_w"].reshape(4, 24, 128).transpose(2, 1, 0)),
        "gdn_conv_in": f(inp["state_gdn_conv"][b].reshape(3, 24, 128).transpose(2, 1, 0)),
        "gdn_a_log": f(inp["gdn_a_log"].reshape(8, 1)), "gdn_dt_bias": f(inp["gdn_dt_bias"].reshape(8, 1)),
        "gdn_norm_w": f(inp["gdn_norm_w"].reshape(1, 128)), "state_gdn": f(inp["state_gdn"][b]),
        "gdn_masks": msk, "gdn_sel": sel,
    })
    perm = ret_perm()
    w = inp["ret_w_in"]
    m["ret_w_in"] = f(np.concatenate([w[:, 0:D][:, perm], w[:, D:2 * D][:, perm], w[:, 2 * D:]], axis=1))
    m["ret_w_out"] = inp["ret_w_out"]
    m["ret_gn_w"] = f(inp["ret_gn_w"].reshape(1, -1))
    m["state_ret"] = f(inp["state_ret"][b][:, perm[:256], :])
    return m


OUT_NAMES = ("y", "gdn_state_", "gdn_conv_", "fox_k_", "fox_v_", "fox_logf_", "diff_k_", "diff_v_", "ret_state_")


def gather(cfg, results):
    inv = np.argsort(ret_perm()[:256])
    outs = []
    for nm, T in (("p", cfg.T), ("s", cfg.TS)):
        g = {}
        for base in OUT_NAMES:
            key = base + nm
            g[base] = np.stack([np.asarray(r[key]) for r in results], 0)
        B = len(results)
        outs.append([
            g["y"],
            g["gdn_state_"],
            g["gdn_conv_"],
            g["fox_k_"].reshape(B, T, 16, 64),
            g["fox_v_"].reshape(B, T, 16, 64),
            g["fox_logf_"],
            g["diff_k_"].reshape(B, T, 8, 2, 64),
            g["diff_v_"].reshape(B, T, 8, 128),
            np.ascontiguousarray(g["ret_state_"][:, :, inv, :]),
        ])
    p, s = outs
    return (p[0], s[0]) + tuple(p[1:]) + tuple(s[1:])


def kernel(**inputs):
    inputs = {k: np.asarray(v) for k, v in inputs.items()}
    cfg = Cfg()
    k = build(cfg)
    in_maps = [core_inputs(cfg, inputs, b) for b in range(NCORES)]
    res = run_bass_kernel_spmd(k.nc, in_maps, core_ids=list(range(NCORES)))
    return gather(cfg, res.results)
```
